# Optimizing a Trainium2 kernel written in Bass

```python
import jax, jax.numpy as jnp
from jax import lax
import numpy as np

D_MODEL = 1024
BATCH = 16
SEQ = 2048
DEPTH = 1

GRID_W = 64
CTX_LEN = 256
FOURIER_WIDTH = 512
FOURIER_GROUPS = 4
FOURIER_GROUP_DIM = FOURIER_WIDTH // FOURIER_GROUPS
RWKV_WIDTH = 512
RWKV_HEAD_DIM = 64
RWKV_HEADS = RWKV_WIDTH // RWKV_HEAD_DIM
DECAY_LORA = 64
AAA_LORA = 64
GATE_LORA = 128
N_BRANCHES = 2
RWKV_SPLIT = (RWKV_WIDTH, RWKV_WIDTH, RWKV_WIDTH, DECAY_LORA, DECAY_LORA, AAA_LORA, AAA_LORA, GATE_LORA)
RWKV_COLS = sum(RWKV_SPLIT)
FOURIER_START = RWKV_COLS
GATE_START = RWKV_COLS + FOURIER_WIDTH
IN_COLS = GATE_START + N_BRANCHES * D_MODEL
D_FF = -(-8 * D_MODEL // (3 * 256)) * 256
NORM_EPS = 1e-6
GN_EPS = 64e-5

kernel_name = "hybrid_fourier_rwkv7_dit_block"


def _split(u, sizes):
    idx = [int(i) for i in np.cumsum(sizes)[:-1]]
    return jnp.split(u, idx, axis=-1)


def rmsnorm(u, g):
    uf = u.astype(jnp.float32)
    uf = uf * lax.rsqrt(jnp.mean(jnp.square(uf), axis=-1, keepdims=True) + NORM_EPS)
    return (uf * g.astype(jnp.float32)).astype(u.dtype)


def modulate(h, shift, scale):
    return h * (1.0 + scale) + shift


def centred_shift(u, mu_prev, mu_next):
    prev = jnp.pad(u, ((0, 0), (1, 0), (0, 0)))[:, :-1]
    nxt = jnp.pad(u, ((0, 0), (0, 1), (0, 0)))[:, 1:]
    return u + mu_prev * (prev - u) + mu_next * (nxt - u)


def _heads(t):
    return t.reshape(t.shape[0], t.shape[1], RWKV_HEADS, RWKV_HEAD_DIM)


def rwkv_inputs(u, lp):
    r, k, v, wd_f, wd_b, ad_f, ad_b, gd = _split(u, RWKV_SPLIT)
    kk = _heads(k * lp["k_k"]).astype(jnp.float32)
    kk = kk * lax.rsqrt(jnp.maximum(jnp.sum(jnp.square(kk), -1, keepdims=True), 1e-24))

    def direction(wd, w0, w2, ad, a0, a2):
        w = -jax.nn.softplus(-(w0 + jnp.tanh(wd) @ w2)) - 0.5
        decay = jnp.exp(-jnp.exp(w.astype(jnp.float32)))
        a = jax.nn.sigmoid(a0 + ad @ a2)
        kd = k * (1.0 + (a - 1.0) * lp["k_a"])
        return _heads(decay), _heads(a), _heads(kd)

    fwd = direction(wd_f, lp["w0_f"], lp["w2_f"], ad_f, lp["a0_f"], lp["a2_f"])
    bwd = direction(wd_b, lp["w0_b"], lp["w2_b"], ad_b, lp["a0_b"], lp["a2_b"])
    g = jax.nn.sigmoid(gd) @ lp["g2"]
    return {"r": _heads(r), "v": _heads(v), "kk": kk, "fwd": fwd, "bwd": bwd, "g": g}


def wkv_scan(r, decay, kd, v, kk, a, S0, reverse, emit):
    to_time = lambda t: jnp.moveaxis(t.astype(jnp.float32), 1, 0)
    xs = (to_time(r), to_time(decay), to_time(kd), to_time(v), to_time(kk), to_time(a))

    def step(S, inp):
        r_t, w_t, k_t, v_t, kk_t, a_t = inp
        sa = jnp.einsum("bhvk,bhk->bhv", S, -kk_t)
        S = (S * w_t[:, :, None, :] + sa[..., None] * (kk_t * a_t)[:, :, None, :]
             + v_t[..., None] * k_t[:, :, None, :])
        y = jnp.einsum("bhvk,bhk->bhv", S, r_t) if emit else None
        return S, y

    S, ys = lax.scan(step, S0, xs, reverse=reverse)
    return (jnp.moveaxis(ys, 0, 1) if emit else None), S


def bidir_wkv(q, Sf0, Sb0, emit):
    df, af, kf = q["fwd"]
    db, ab, kb = q["bwd"]
    yf, Sf = wkv_scan(q["r"], df, kf, q["v"], q["kk"], af, Sf0, False, emit)
    yb, Sb = wkv_scan(q["r"], db, kb, q["v"], q["kk"], ab, Sb0, True, emit)
    y = yf + yb if emit else None
    return y, Sf, Sb


def rwkv_output(y, q, lp):
    mean = jnp.mean(y, -1, keepdims=True)
    var = jnp.mean(jnp.square(y - mean), -1, keepdims=True)
    o = (y - mean) * lax.rsqrt(var + GN_EPS)
    o = o * lp["lnx_g"].reshape(RWKV_HEADS, RWKV_HEAD_DIM) + lp["lnx_b"].reshape(RWKV_HEADS, RWKV_HEAD_DIM)
    kd_sum = q["fwd"][2] + q["bwd"][2]
    bonus = jnp.sum(q["r"] * kd_sum * lp["r_k"], -1, keepdims=True) * q["v"]
    o = (o + bonus).astype(q["g"].dtype)
    o = o.reshape(o.shape[0], o.shape[1], RWKV_WIDTH) * q["g"]
    return o @ lp["w_up_r"]


def fourier_latent(u, rows):
    B = u.shape[0]
    uf = u.astype(jnp.float32).reshape(B, rows, GRID_W, FOURIER_GROUPS, FOURIER_GROUP_DIM)
    f = jnp.real(jnp.fft.fftn(uf, axes=(1, 2, 4), norm="ortho"))
    return f.reshape(B, rows * GRID_W, FOURIER_WIDTH).astype(u.dtype)


def fourier_context(u):
    B, L = u.shape[0], u.shape[1]
    uf = u.astype(jnp.float32).reshape(B, L, FOURIER_GROUPS, FOURIER_GROUP_DIM)
    f = jnp.real(jnp.fft.fftn(uf, axes=(1, 3), norm="ortho"))
    return f.reshape(B, L, FOURIER_WIDTH).astype(u.dtype)


def branch_merge(p, f_out, r_out, lp):
    gate_f, gate_r = _split(p[..., GATE_START:], (D_MODEL, D_MODEL))
    m = jax.nn.sigmoid(gate_f) * (f_out @ lp["w_up_f"]) + jax.nn.sigmoid(gate_r) * r_out
    return m @ lp["w_out"]


def swiglu(h, w_gu, w_down):
    gate, up = jnp.split(h @ w_gu, 2, axis=-1)
    return (jax.nn.silu(gate) * up) @ w_down


def trunk_layer(x, ctx, mod_x, mod_c, lp, update_ctx):
    rows = x.shape[1] // GRID_W
    B = x.shape[0]
    sh1, sc1, ga1, sh2, sc2, ga2 = jnp.split(mod_x, 6, axis=-1)
    csh1, csc1, cga1, csh2, csc2, cga2 = jnp.split(mod_c, 6, axis=-1)

    hx = modulate(rmsnorm(x, lp["norm1_g"]), sh1, sc1)
    px = hx @ lp["w_in"]
    qx = rwkv_inputs(centred_shift(px[..., :RWKV_COLS], lp["mu_prev"], lp["mu_next"]), lp)

    hc = modulate(rmsnorm(ctx, lp["norm1_g"]), csh1, csc1)
    pc = hc @ (lp["w_in"] if update_ctx else lp["w_in"][:, :RWKV_COLS])
    qc = rwkv_inputs(centred_shift(pc[..., :RWKV_COLS], lp["mu_prev"], lp["mu_next"]), lp)
    S0 = jnp.zeros((B, RWKV_HEADS, RWKV_HEAD_DIM, RWKV_HEAD_DIM), jnp.float32)
    yc, Sf, Sb = bidir_wkv(qc, S0, S0, update_ctx)
    yx, _, _ = bidir_wkv(qx, Sf, Sb, True)

    fx = fourier_latent(px[..., FOURIER_START:GATE_START], rows)
    x = x + ga1 * branch_merge(px, fx, rwkv_output(yx, qx, lp), lp)
    hx2 = modulate(rmsnorm(x, lp["norm2_g"]), sh2, sc2)
    x = x + ga2 * swiglu(hx2, lp["w_gu"], lp["w_down"])

    if update_ctx:
        fc = fourier_context(pc[..., FOURIER_START:GATE_START])
        ctx = ctx + cga1 * branch_merge(pc, fc, rwkv_output(yc, qc, lp), lp)
        hc2 = modulate(rmsnorm(ctx, lp["norm2_g"]), csh2, csc2)
        ctx = ctx + cga2 * swiglu(hc2, lp["w_gu"], lp["w_down"])
    return x, ctx


def setup_inputs(seed: int = 0) -> dict:
    key = jax.random.key(seed)
    ks = iter(jax.random.split(key, 40))
    L, D = DEPTH, D_MODEL

    def nrm(shape, scale):
        return scale * jax.random.normal(next(ks), shape, jnp.float32)

    def gain(shape):
        return 1.0 + nrm(shape, 0.02)

    ratio = jnp.linspace(0.0, 1.0, RWKV_WIDTH, dtype=jnp.float32)
    w0_base = -6.0 + 5.0 * ratio ** 0.9
    return {
        "x": nrm((BATCH, SEQ, D), 1.0),
        "c": nrm((BATCH, D), 1.0),
        "ctx": nrm((BATCH, CTX_LEN, D), 1.0),
        "c_ctx": nrm((D,), 1.0),
        "norm1_g": gain((L, D)),
        "norm2_g": gain((L, D)),
        "w_ada": nrm((L, D, 6 * D), D ** -0.5),
        "b_ada": nrm((L, 6 * D), 0.01),
        "w_in": nrm((L, D, IN_COLS), D ** -0.5),
        "mu_prev": jax.random.uniform(next(ks), (L, RWKV_COLS), jnp.float32, 0.0, 0.5),
        "mu_next": jax.random.uniform(next(ks), (L, RWKV_COLS), jnp.float32, 0.0, 0.5),
        "w0_f": w0_base + nrm((L, RWKV_WIDTH), 0.1),
        "w2_f": nrm((L, DECAY_LORA, RWKV_WIDTH), 0.5 * DECAY_LORA ** -0.5),
        "a0_f": nrm((L, RWKV_WIDTH), 0.1),
        "a2_f": nrm((L, AAA_LORA, RWKV_WIDTH), AAA_LORA ** -0.5),
        "w0_b": w0_base + nrm((L, RWKV_WIDTH), 0.1),
        "w2_b": nrm((L, DECAY_LORA, RWKV_WIDTH), 0.5 * DECAY_LORA ** -0.5),
        "a0_b": nrm((L, RWKV_WIDTH), 0.1),
        "a2_b": nrm((L, AAA_LORA, RWKV_WIDTH), AAA_LORA ** -0.5),
        "g2": nrm((L, GATE_LORA, RWKV_WIDTH), GATE_LORA ** -0.5),
        "k_k": 0.85 + nrm((L, RWKV_WIDTH), 0.02),
        "k_a": 1.0 + nrm((L, RWKV_WIDTH), 0.02),
        "r_k": nrm((L, RWKV_HEADS, RWKV_HEAD_DIM), 0.1),
        "lnx_g": gain((L, RWKV_WIDTH)),
        "lnx_b": nrm((L, RWKV_WIDTH), 0.01),
        "w_up_r": nrm((L, RWKV_WIDTH, D), RWKV_WIDTH ** -0.5),
        "w_up_f": nrm((L, FOURIER_WIDTH, D), FOURIER_WIDTH ** -0.5),
        "w_out": nrm((L, D, D), D ** -0.5),
        "w_gu": nrm((L, D, 2 * D_FF), D ** -0.5),
        "w_down": nrm((L, D_FF, D), D_FF ** -0.5),
        "final_norm_g": gain((D,)),
    }


def reference(x, c, ctx, c_ctx, norm1_g, norm2_g, w_ada, b_ada, w_in, mu_prev, mu_next,
              w0_f, w2_f, a0_f, a2_f, w0_b, w2_b, a0_b, a2_b, g2, k_k, k_a, r_k,
              lnx_g, lnx_b, w_up_r, w_up_f, w_out, w_gu, w_down, final_norm_g):
    for layer in range(DEPTH):
        lp = {
            "norm1_g": norm1_g[layer], "norm2_g": norm2_g[layer], "w_in": w_in[layer],
            "mu_prev": mu_prev[layer], "mu_next": mu_next[layer],
            "w0_f": w0_f[layer], "w2_f": w2_f[layer], "a0_f": a0_f[layer], "a2_f": a2_f[layer],
            "w0_b": w0_b[layer], "w2_b": w2_b[layer], "a0_b": a0_b[layer], "a2_b": a2_b[layer],
            "g2": g2[layer], "k_k": k_k[layer], "k_a": k_a[layer], "r_k": r_k[layer],
            "lnx_g": lnx_g[layer], "lnx_b": lnx_b[layer], "w_up_r": w_up_r[layer],
            "w_up_f": w_up_f[layer], "w_out": w_out[layer], "w_gu": w_gu[layer],
            "w_down": w_down[layer],
        }
        mod_x = (jax.nn.silu(c) @ w_ada[layer] + b_ada[layer])[:, None, :]
        mod_c = (jax.nn.silu(c_ctx) @ w_ada[layer] + b_ada[layer])[None, None, :]
        x, ctx = trunk_layer(x, ctx, mod_x, mod_c, lp, layer < DEPTH - 1)
    return rmsnorm(x, final_norm_g)
```

```python
import numpy as np
import concourse.bass as bass
import concourse.mybir as mybir

F32 = mybir.dt.float32
BF16 = mybir.dt.bfloat16
I32 = mybir.dt.int32
AF = mybir.ActivationFunctionType
ALU = mybir.AluOpType
AX = mybir.AxisListType

SEM_LIMIT = 30000


class T:
    __slots__ = ("name", "w", "r")

    def __init__(self, name=""):
        self.name = name
        self.w = None
        self.r = {}


class V:
    __slots__ = ("ts", "ap", "x")

    def __init__(self, ts, ap, x=False):
        self.ts = ts
        self.ap = ap
        self.x = x


class Buf:
    def __init__(self, tensor, name, nsplit=None, sdim=1):
        self.t = tensor
        self.name = name
        self.nsplit = nsplit
        self.sdim = sdim
        if nsplit is None:
            self.ts = [T(name)]
        else:
            self.ts = [T(f"{name}{i}") for i in range(nsplit)]

    def __getitem__(self, idx):
        ap = self.t[idx]
        if self.nsplit is None:
            return V(self.ts, ap)
        if not isinstance(idx, tuple):
            idx = (idx,)
        if len(idx) <= self.sdim:
            return V(self.ts, ap)
        s = idx[self.sdim]
        if isinstance(s, int):
            return V([self.ts[s]], ap)
        if isinstance(s, slice):
            st, sp, _ = s.indices(self.nsplit)
            return V(self.ts[st:sp], ap)
        return V(self.ts, ap)

    def v(self, ap, which=None):
        if which is None:
            return V(self.ts, ap)
        return V([self.ts[i] for i in which], ap)


class Sync:
    def __init__(self, nc, ndma=24):
        self.nc = nc
        self.engs = {"pe": nc.tensor, "act": nc.scalar, "dve": nc.vector,
                     "pool": nc.gpsimd, "sp": nc.sync}
        self.semh = {}
        self.cur = {}
        self.cnt = {}
        self.nsem = 0
        for k in self.engs:
            self._new_sem(k)
        self.seen = {k: {} for k in self.engs}
        self.ndma = ndma
        self.dma_key = []
        self.dma_val = []
        for i in range(ndma):
            key = self._alloc(f"dma{i}")
            self.dma_key.append(key)
            self.dma_val.append(0)
        self.dma_rr = 0
        self.ninst = {k: 0 for k in self.engs}

    def _alloc(self, name):
        key = f"{name}_{self.nsem}"
        self.nsem += 1
        self.semh[key] = self.nc.alloc_semaphore(name=key)
        return key

    def _new_sem(self, ek):
        self.cur[ek] = self._alloc(f"e_{ek}")
        self.cnt[ek] = 0

    def _deps(self, outs, ins):
        deps = {}

        def add(ev):
            if ev is None:
                return
            k, val = ev
            if deps.get(k, 0) < val:
                deps[k] = val
        for v in ins:
            for t in v.ts:
                add(t.w)
        for v in outs:
            for t in v.ts:
                add(t.w)
                for k, val in t.r.items():
                    add((k, val))
        return deps

    def _wait(self, ek, deps):
        eng = self.engs[ek]
        seen = self.seen[ek]
        for k, val in deps.items():
            if ek == "pe" and k.startswith("e_pe"):
                continue
            if seen.get(k, 0) < val:
                eng.wait_ge(self.semh[k], val)
                seen[k] = val
                self.ninst[ek] += 1

    def _mark(self, ev, outs, ins):
        k, val = ev
        for v in ins:
            for t in v.ts:
                if t.r.get(k, 0) < val:
                    t.r[k] = val
        for v in outs:
            for t in v.ts:
                t.w = ev
                t.r = {}

    def op(self, ek, fn, outs, ins):
        outs = list(outs) + [v for v in ins if v.x]
        self._wait(ek, self._deps(outs, ins))
        if self.cnt[ek] >= SEM_LIMIT:
            self._new_sem(ek)
        inst = fn(self.engs[ek])
        self.cnt[ek] += 1
        inst.then_inc(self.semh[self.cur[ek]], 1)
        self.ninst[ek] += 1
        self._mark((self.cur[ek], self.cnt[ek]), outs, ins)

    def dma(self, ek, out, in_, **kw):
        i = self.dma_rr
        self.dma_rr = (self.dma_rr + 1) % self.ndma
        if self.dma_val[i] >= SEM_LIMIT:
            self.dma_key[i] = self._alloc(f"dma{i}")
            self.dma_val[i] = 0
        deps = self._deps([out], [in_])
        key = self.dma_key[i]
        if self.dma_val[i] > 0:
            if deps.get(key, 0) < self.dma_val[i]:
                deps[key] = self.dma_val[i]
        self._wait(ek, deps)
        inst = self.engs[ek].dma_start(out=out.ap, in_=in_.ap, **kw)
        self.dma_val[i] += 16
        inst.then_inc(self.semh[key], 16)
        self.ninst[ek] += 1
        self._mark((key, self.dma_val[i]), [out], [in_])

    def wait_all(self, ek, views):
        self._wait(ek, self._deps([], views))

    def mm(self, out, lhsT, rhs, start=True, stop=True, **kw):
        self.op("pe", lambda e: e.matmul(out.ap, lhsT.ap, rhs.ap, start=start, stop=stop, **kw),
                [out], [lhsT, rhs] + ([] if start else [out]))

    def tr(self, out, in_, ident):
        self.op("pe", lambda e: e.transpose(out.ap, in_.ap, ident.ap), [out], [in_, ident])

    def act(self, out, in_, func, bias=None, scale=1.0, ek="act", accum=None):
        ins = [in_]
        kw = {}
        if bias is not None:
            if isinstance(bias, V):
                ins.append(bias)
                kw["bias"] = bias.ap
            else:
                kw["bias"] = bias
        if isinstance(scale, V):
            ins.append(scale)
            kw["scale"] = scale.ap
        else:
            kw["scale"] = scale
        outs = [out]
        if accum is not None:
            outs.append(accum)
            kw["accum_out"] = accum.ap
        self.op(ek, lambda e: e.activation(out.ap, in_.ap, func, **kw), outs, ins)

    def tt(self, out, a, b, op, ek="dve"):
        self.op(ek, lambda e: e.tensor_tensor(out.ap, a.ap, b.ap, op), [out], [a, b])

    def ts(self, out, a, s1, op0, s2=None, op1=None, ek="dve"):
        ins = [a]
        x1 = s1
        if isinstance(s1, V):
            ins.append(s1)
            x1 = s1.ap
        x2 = s2
        if isinstance(s2, V):
            ins.append(s2)
            x2 = s2.ap
        if op1 is None:
            self.op(ek, lambda e: e.tensor_scalar(out.ap, a.ap, x1, None, op0), [out], ins)
        else:
            self.op(ek, lambda e: e.tensor_scalar(out.ap, a.ap, x1, x2, op0, op1), [out], ins)

    def stt(self, out, a, s, b, op0, op1):
        ins = [a, b]
        x = s
        if isinstance(s, V):
            ins.append(s)
            x = s.ap
        self.op("dve", lambda e: e.scalar_tensor_tensor(out.ap, a.ap, x, b.ap, op0, op1), [out], ins)

    def copy(self, out, in_, ek="dve"):
        self.op(ek, lambda e: e.tensor_copy(out.ap, in_.ap), [out], [in_])

    def memset(self, out, val, ek="pool"):
        self.op(ek, lambda e: e.memset(out.ap, val), [out], [])

    def reduce(self, out, in_, op, axis, ek="dve"):
        self.op(ek, lambda e: e.tensor_reduce(out.ap, in_.ap, axis, op), [out], [in_])

from contextlib import ExitStack

NB = 2
D = 1024
TL = 2048
TC = 256
NIN = 4480
DFF = 2816
NORM_EPS = 1e-6
import os
EVAC_ACT_ONLY = bool(int(os.environ.get('EVAC_ACT_ONLY', '0')))
GN_EPS = 64e-5
SC = -float(np.exp(-0.5))


class Scope:
    uid = 0

    def __init__(self, nc):
        self.nc = nc
        self.es = ExitStack()
        self.n = 0

    def sb(self, name, shape, dt, nsplit=None, sdim=1):
        Scope.uid += 1
        t = self.es.enter_context(self.nc.sbuf_tensor(f"s{Scope.uid}_{name}", shape, dt))
        return Buf(t, name, nsplit, sdim)

    def close(self):
        self.es.close()


class PBank:
    def __init__(self, nc, name, dt, width):
        self.t = nc.alloc_psum_tensor(name, [128, width], dt)
        self.blk = width
        self.ts = [T(f"{name}{i}") for i in range(max(1, width // self.blk))]

    def c(self, a, b, rows=slice(None)):
        q0 = a // self.blk
        q1 = (b - 1) // self.blk
        return V(self.ts[q0:q1 + 1], self.t[rows, a:b], True)

    def v(self, ap):
        return V(self.ts, ap, True)


def barrier(S):
    for ek, eng in S.engs.items():
        deps = {}
        for k2 in S.engs:
            if S.cnt[k2] > 0:
                deps[S.cur[k2]] = S.cnt[k2]
        for i in range(S.ndma):
            if S.dma_val[i] > 0:
                deps[S.dma_key[i]] = S.dma_val[i]
        seen = S.seen[ek]
        for k, val in deps.items():
            if seen.get(k, 0) < val:
                eng.wait_ge(S.semh[k], val)
                seen[k] = val


def host_consts():
    c = {}
    i = np.arange(128)
    row = i[:, None]
    col = i[None, :]
    SL = (col < row).astype(np.float32)
    SU = (col > row).astype(np.float32)
    IL = (col <= row).astype(np.float32)
    IU = (col >= row).astype(np.float32)
    mk = np.zeros((128, 4, 256), np.float32)
    mk[:, 0, :128] = SL; mk[:, 0, 128:] = SL
    mk[:, 1, :128] = SU; mk[:, 1, 128:] = IU
    mk[:, 2, :128] = SU; mk[:, 2, 128:] = SU
    mk[:, 3, :128] = SL; mk[:, 3, 128:] = IL
    c["mkp"] = mk
    BDm = np.zeros((128, 128), np.float32)
    BDm[:64, :64] = 1.0
    BDm[64:, 64:] = 1.0
    mk2 = np.zeros((128, 2, 5, 128), np.float32)
    mk2[:, 0, 0] = SL * BDm; mk2[:, 0, 1] = SL; mk2[:, 0, 2] = SU * BDm; mk2[:, 0, 3] = IU; mk2[:, 0, 4] = SU * (1 - BDm)
    mk2[:, 1, 0] = SU * BDm; mk2[:, 1, 1] = SU; mk2[:, 1, 2] = SL * BDm; mk2[:, 1, 3] = IL; mk2[:, 1, 4] = SL * (1 - BDm)
    c["mk2"] = mk2
    c["identf"] = np.eye(128, dtype=np.float32)
    th = 2 * np.pi * np.outer(i, i) / 128.0
    cs = np.zeros((128, 256), np.float32)
    cs[:, :128] = np.cos(th) / np.sqrt(128.0)
    cs[:, 128:] = np.sin(th) / np.sqrt(128.0)
    c["cs128"] = cs
    j = np.arange(64)
    th64 = 2 * np.pi * np.outer(j, j) / 64.0
    C64 = np.cos(th64) / 8.0
    S64 = np.sin(th64) / 8.0
    bd = np.zeros((128, 3, 128), np.float32)
    for r in range(2):
        bd[r * 64:(r + 1) * 64, 0, r * 64:(r + 1) * 64] = C64
        bd[r * 64:(r + 1) * 64, 1, r * 64:(r + 1) * 64] = S64
        bd[r * 64:(r + 1) * 64, 2, r * 64:(r + 1) * 64] = -S64
    c["bd64"] = bd
    k32 = np.arange(32)
    th32 = 2 * np.pi * np.outer(k32, k32) / 32.0
    c32 = np.zeros((32, 2, 32), np.float32)
    c32[:, 0, :] = np.cos(th32) / np.sqrt(32.0)
    c32[:, 1, :] = -np.sin(th32) / np.sqrt(32.0)
    c["c32"] = c32
    ones_bd = np.zeros((128, 128), np.float32)
    ones_bd[:64, :64] = 1.0
    ones_bd[64:, 64:] = 1.0
    c["onesbd"] = ones_bd
    e2 = np.zeros((128, 2), np.float32)
    e2[:64, 0] = 1.0
    e2[64:, 1] = 1.0
    c["e2"] = e2
    c["ones"] = np.ones((128, 128), np.float32)
    return c


CONST_SHAPES = {"mk2": [128, 2, 5, 128], "mkp": [128, 4, 256], "identf": [128, 128], "cs128": [128, 256], "bd64": [128, 3, 128],
                "c32": [32, 2, 32], "onesbd": [128, 128], "e2": [128, 2], "ones": [128, 128]}

IN_SHAPES = {
    "x": [NB, TL, D], "ctx": [NB, TC, D], "cT": [128, 8, 3], "b_adaT": [128, 48], "b_ada_row": [1, 6144],
    "n1g": [128, 8], "n2g": [128, 8], "fng_row": [1, D],
    "w_ada": [D, 6144], "w_in": [D, NIN], "mu3": [128, 3, 15],
    "w2": [128, 512], "a2": [128, 512], "g2": [128, 512],
    "pvec": [128, 9, 4],
    "lnx_row": [1, 1024],
    "w_up_r": [512, D], "w_up_f": [512, D], "w_out": [D, D], "w_gu": [D, 2 * DFF], "w_down": [DFF, D],
}
IN_SHAPES.update(CONST_SHAPES)


class StopBuild(Exception):
    pass


def build(debug=(), stop_after=None):
    nc = bass.Bass("TRN2", target_bir_lowering=False)
    S = Sync(nc)
    try:
        return _build(nc, S, debug, stop_after)
    except StopBuild:
        barrier(S)
        return nc, S


def _build(nc, S, debug, stop_after):
    def ck(name):
        if stop_after == name:
            raise StopBuild()
    dr = {}
    for name, shp in IN_SHAPES.items():
        dr[name] = Buf(nc.dram_tensor(name, shp, F32, kind="ExternalInput"), name)
    out = Buf(nc.dram_tensor("out", [NB, TL, D], F32, kind="ExternalOutput"), "out", nsplit=NB, sdim=0)

    def scratch(name, shape, nsplit=None, sdim=0):
        kind = "ExternalOutput" if name in debug else "Internal"
        return Buf(nc.dram_tensor(name, shape, F32, kind=kind), name, nsplit, sdim)

    U_lat = [scratch(f"U_lat{b}", [15, 128, TL], 15, 0) for b in range(NB)]
    U_ctx = [scratch(f"U_ctx{b}", [15, 128, TC], 15, 0) for b in range(NB)]
    SG = [scratch(f"SG{b}", [16, 128, TL], 16, 0) for b in range(NB)]
    DD = [scratch(f"DD{b}", [2, TL, 512], 2, 0) for b in range(NB)]
    YD = [[scratch(f"YD{b}_{d}", [TL, 512]) for d in range(2)] for b in range(NB)]
    VT = [scratch(f"VT{b}", [TL, 512]) for b in range(NB)]
    GT = [scratch(f"GT{b}", [TL, 512]) for b in range(NB)]
    BON = [[scratch(f"BON{b}_{d}", [TL, 8]) for d in range(2)] for b in range(NB)]
    X1 = [scratch(f"X1_{b}", [TL, D]) for b in range(NB)]
    MODROW = scratch("MODROW", [3, 2, 1024])

    if os.environ.get("PTFIRST") == "1":
        PT = PBank(nc, "pst", BF16, 1024)
        PS = [PBank(nc, f"ps{i}", F32, 512) for i in range(7)]
    else:
        PS = [PBank(nc, f"ps{i}", F32, 512) for i in range(7)]
        PT = PBank(nc, "pst", BF16, 1024)

    G = Scope(nc)
    cst = {}

    def load_const(sc, name):
        shp = CONST_SHAPES[name]
        cst[name] = sc.sb("c_" + name, shp, F32)
        S.dma("sp", cst[name].v(cst[name].t[tuple(slice(None) for _ in shp)]),
              dr[name].v(dr[name].t[tuple(slice(None) for _ in shp)]))
        return cst[name]

    load_const(G, "identf")
    identf = cst["identf"]
    identb = G.sb("identb", [128, 128], BF16)
    S.copy(identb[:, :], identf[:, :])
    mu3 = G.sb("mu3", [128, 3, 15], F32)
    S.dma("sp", mu3[:, :, :], dr["mu3"][:, :, :])
    c0 = G.sb("c0", [128, 15], F32)
    S.tt(c0[:, :], mu3[:, 0, :], mu3[:, 1, :], ALU.add)
    S.ts(c0[:, :], c0[:, :], -1.0, ALU.mult, 1.0, ALU.add)
    pvec = G.sb("pvec", [128, 9, 4], F32)
    S.dma("sp", pvec[:, :, :], dr["pvec"][:, :, :])
    S.ts(pvec[:, 2, :], pvec[:, 1, :], -1.0, ALU.mult, 1.0, ALU.add)
    PK_KK, PK_KA, PK_OMKA, PK_RK, PK_W0, PK_A0 = 0, 1, 2, 3, 4, 6
    n1g = G.sb("n1g", [128, 8], F32)
    n2g = G.sb("n2g", [128, 8], F32)
    S.dma("sp", n1g[:, :], dr["n1g"][:, :])
    S.dma("sp", n2g[:, :], dr["n2g"][:, :])
    smallw = {}
    for name in ("w2", "a2", "g2"):
        smallw[name] = G.sb("bf_" + name, [128, 512], BF16)
    W2w, A2w, G2w = smallw["w2"], smallw["a2"], smallw["g2"]

    if stop_after == "G":
        barrier(S)
        return nc, S
    modT = G.sb("modT", [128, 48, 3], F32)
    A1 = G.sb("A1", [128, 8, 3], F32)
    A2 = G.sb("A2", [128, 8, 3], F32)
    M = Scope(nc)
    garow = M.sb("garow", [3, 2, 1024], F32)
    stg_small = [M.sb(f"stg_small{i}", [128, 512], F32) for i in range(3)]
    for i, name in enumerate(("w2", "a2", "g2")):
        S.dma("sp", stg_small[i][:, :], dr[name][:, :])
        S.copy(smallw[name][:, :], stg_small[i][:, :])
    cT = M.sb("cT", [128, 8, 3], F32)
    sT = M.sb("sT", [128, 8, 3], F32)
    badaT = M.sb("badaT", [128, 48], F32)
    brow = M.sb("brow", [3, 6144], F32)
    S.dma("sp", cT[:, :, :], dr["cT"][:, :, :])
    S.dma("sp", badaT[:, :], dr["b_adaT"][:, :])
    S.dma("sp", brow[:, :], dr["b_ada_row"].v(dr["b_ada_row"].t.ap().partition_broadcast(3)))
    S.act(sT[:, :, :], cT[:, :, :], AF.Silu)
    wa = [M.sb(f"wa{i}", [128, 8, 1024], F32) for i in range(2)]
    w_ada_v = dr["w_ada"].t.ap().rearrange("(k p) n -> p k n", p=128)
    for sec in range(6):
        wb = wa[sec % 2]
        S.dma("sp" if sec % 2 == 0 else "pool", wb[:, :, :], dr["w_ada"].v(w_ada_v[:, :, sec * 1024:(sec + 1) * 1024]))
        for nn in range(8):
            n = sec * 8 + nn
            o = PS[0].c(n * 3, n * 3 + 3)
            for k in range(8):
                S.mm(o, wb[:, k, nn * 128:(nn + 1) * 128], sT[:, k, :], start=(k == 0), stop=(k == 7))
        if sec in (2, 5):
            gi_ = 0 if sec == 2 else 1
            for half in range(2):
                o = PS[1].c(0, 512, rows=slice(0, 3))
                for k in range(8):
                    S.mm(o, sT[:, k, :], wb[:, k, half * 512:(half + 1) * 512], start=(k == 0), stop=(k == 7))
                S.tt(garow[:, gi_, half * 512:(half + 1) * 512], o,
                     brow[:, sec * 1024 + half * 512: sec * 1024 + (half + 1) * 512], ALU.add)
    S.tt(modT[:, :, :], PS[0].v(PS[0].t[:, 0:144].rearrange("p (n j) -> p n j", j=3)),
         V(badaT.ts, badaT.t[:, :].unsqueeze(2).broadcast_to([128, 48, 3])), ALU.add)
    S.dma("sp", MODROW[:, :, :], garow[:, :, :])
    S.ts(A1[:, :, :], modT[:, 8:16, :], 1.0, ALU.add)
    S.tt(A1[:, :, :], A1[:, :, :], V(n1g.ts, n1g.t[:, :].unsqueeze(2).broadcast_to([128, 8, 3])), ALU.mult)
    S.ts(A2[:, :, :], modT[:, 32:40, :], 1.0, ALU.add)
    S.tt(A2[:, :, :], A2[:, :, :], V(n2g.ts, n2g.t[:, :].unsqueeze(2).broadcast_to([128, 8, 3])), ALU.mult)
    barrier(S)
    M.close()
    if stop_after == "M":
        if "modT" in debug:
            dbgm = nc.dram_tensor("dbg_modT", [128, 48, 3], F32, kind="ExternalOutput")
            S.dma("sp", V([T("x")], dbgm[:, :, :]), modT[:, :, :])
        barrier(S)
        return nc, S

    def nt_bufs(sc, tag, nbuf=2):
        xt = [sc.sb(f"xt{tag}{i}", [128, 1024], F32) for i in range(nbuf)]
        xn = [sc.sb(f"xn{tag}{i}", [128, 1024], BF16) for i in range(nbuf)]
        junk = sc.sb(f"junk{tag}", [128, 1024], BF16)
        st = [sc.sb(f"st{tag}{i}", [128, 4], F32) for i in range(2)]
        return xt, xn, junk, st

    def norm_transpose(bufs, src_buf, src_rows_fn, ntiles, Aap, Bfn, j, hT, col0):
        xt, xn, junk, st = bufs
        for i in range(ntiles):
            if i == 1:
                ck("A0b")
            if i == 3:
                ck("A0c")
            x_t = xt[i % len(xt)]
            x_n = xn[i % len(xn)]
            s_ = st[i % 2]
            S.dma("sp", x_t[:, :], src_buf.v(src_rows_fn(i)))
            ck("A0a")
            S.act(junk[:, :], x_t[:, :], AF.Square, accum=s_[:, 0:1])
            S.act(s_[:, 1:2], s_[:, 0:1], AF.Sqrt, scale=1.0 / D, bias=eps_t[:, 0:1])
            S.op("dve", lambda e: e.reciprocal(s_.t[:, 2:3], s_.t[:, 1:2]), [s_[:, 2:3]], [s_[:, 1:2]])
            S.act(x_n[:, :], x_t[:, :], AF.Copy, scale=s_[:, 2:3])
            ck("A0a2")
            for k in range(8):
                S.tr(PT.c(k * 128, (k + 1) * 128), x_n[:, k * 128:(k + 1) * 128], identb[:, :])
            ck("A0a3")
            for k in range(8):
                o = hT[:, k, col0 + i * 128: col0 + (i + 1) * 128]
                if k == 1:
                    ck("A0a4")
                if os.environ.get("DBGV") == "1":
                    o = junk[:, 0:128]
                if os.environ.get("DBGV") == "2":
                    S.act(o, PT.c(k * 128, (k + 1) * 128), AF.Identity, scale=0.5, bias=eps_t[:, 0:1])
                    continue
                if os.environ.get("DBGV") == "3":
                    S.act(o, PT.c(k * 128, (k + 1) * 128), AF.Copy)
                    continue
                if k == 2:
                    ck("A0a5")
                if k % 2 == 0 and not EVAC_ACT_ONLY:
                    S.ts(o, PT.c(k * 128, (k + 1) * 128), Aap[:, k, j:j + 1], ALU.mult, Bfn(k), ALU.add)
                else:
                    S.act(o, PT.c(k * 128, (k + 1) * 128), AF.Identity, scale=Aap[:, k, j:j + 1],
                          bias=Bfn(k))

    eps_t = G.sb("eps_t", [128, 2], F32)
    S.memset(eps_t[:, 0:1], NORM_EPS)
    S.memset(eps_t[:, 1:2], GN_EPS)

    def load_weight_bf16(sc, dst, dram_buf, kchunks, c0_, c1_, stg, ei=[0]):
        v = dram_buf.t.ap().rearrange("(k p) n -> p k n", p=128)
        cc = c0_
        while cc < c1_:
            w = min(256, c1_ - cc)
            sg_ = stg[ei[0] % 2]
            ei[0] += 1
            S.dma("pool" if ei[0] % 2 else "sp", sg_[:, 0:kchunks, 0:w], dram_buf.v(v[:, :, cc:cc + w]))
            S.copy(dst[:, 0:kchunks, cc - c0_: cc - c0_ + w], sg_[:, 0:kchunks, 0:w], ek="pool")
            cc += w

    Bsh1 = modT

    def phase_A(b):
        A = Scope(nc)
        load_const(A, "cs128")
        load_const(A, "bd64")
        stg = [A.sb(f"stgA{i}", [128, 8, 256], F32) for i in range(2)]
        hT = A.sb("hT", [128, 8, TL + 2], BF16)
        hTc = A.sb("hTc", [128, 8, TC + 2], BF16)
        Win = A.sb("Win", [128, 8, 2432], BF16)
        S.memset(hT[:, :, 0:1], 0.0)
        S.memset(hT[:, :, TL + 1:TL + 2], 0.0)
        S.memset(hTc[:, :, 0:1], 0.0)
        S.memset(hTc[:, :, TC + 1:TC + 2], 0.0)
        xv = dr["x"].t
        cv = dr["ctx"].t
        ck("A0")
        ntb = nt_bufs(A, "A")
        norm_transpose(ntb, dr["x"], lambda i: xv[b, i * 128:(i + 1) * 128, :], TL // 128, A1,
                       lambda k: modT[:, k, b:b + 1], b, hT, 1)
        norm_transpose(ntb, dr["ctx"], lambda i: cv[b, i * 128:(i + 1) * 128, :], TC // 128, A1,
                       lambda k: modT[:, k, 2:3], 2, hTc, 1)
        ck("A1")
        pb = [A.sb(f"pb{i}", [128, 514], F32) for i in range(2)]
        ub = [A.sb(f"ub{i}", [128, 512], F32) for i in range(3)]
        uf = A.sb("uf", [128, 4, 512], F32, nsplit=4)
        Abuf = [A.sb(f"Abuf{i}", [128, 4, 256], F32) for i in range(2)]
        Dout = [A.sb(f"Dout{i}", [128, 2, 512], F32) for i in range(2)]
        cnt = [0]

        def gemm(hbuf, T_, TT, n_lo, n_hi, wofs, Udst):
            for t0 in range(0, T_, TT):
                for n in range(n_lo, n_hi):
                    q = cnt[0]
                    cnt[0] += 1
                    bank = PS[q % 2]
                    o = bank.c(0, TT)
                    for k in range(8):
                        S.mm(o, Win[:, k, (n - wofs) * 128:(n - wofs + 1) * 128], hbuf[:, k, 1 + t0:1 + t0 + TT],
                             start=(k == 0), stop=(k == 7))
                    if n == 1:
                        ck("A2")
                    if n == 16:
                        ck("A3")
                    if n < 15:
                        oh = PS[2 + q % 2].c(0, 2)
                        for k in range(8):
                            S.mm(oh, Win[:, k, (n - wofs) * 128:(n - wofs + 1) * 128],
                                 hbuf[:, k, t0:t0 + TT + 2:TT + 1], start=(k == 0), stop=(k == 7))
                        p_ = pb[q % 2]
                        u_ = ub[q % 3]
                        S.act(p_[:, 1:TT + 1], o, AF.Copy)
                        S.copy(p_[:, 0:TT + 2:TT + 1], oh)
                        S.act(u_[:, 0:TT], p_[:, 1:TT + 1], AF.Copy, scale=c0[:, n:n + 1])
                        S.stt(u_[:, 0:TT], p_[:, 0:TT], mu3[:, 0, n:n + 1], u_[:, 0:TT], ALU.mult, ALU.add)
                        S.stt(u_[:, 0:TT], p_[:, 2:TT + 2], mu3[:, 1, n:n + 1], u_[:, 0:TT], ALU.mult, ALU.add)
                        S.dma("pool", Udst[n, :, t0:t0 + TT], u_[:, 0:TT])
                    elif n < 19:
                        S.act(uf[:, n - 15, 0:TT], o, AF.Copy)
                    else:
                        u_ = ub[q % 3]
                        S.act(u_[:, 0:TT], o, AF.Sigmoid)
                        S.dma("pool", SG[b][n - 19, :, t0:t0 + TT], u_[:, 0:TT])
                if n_lo <= 15 and n_hi >= 19 and T_ == TL:
                    for ch in range(4):
                        ab = Abuf[ch % 2]
                        do = Dout[ch % 2]
                        for g in range(4):
                            bank = PS[4 + (g // 2)]
                            S.mm(bank.c((g % 2) * 256, (g % 2) * 256 + 256), uf[:, g, ch * 128:(ch + 1) * 128],
                                 cst["cs128"][:, :])
                        S.act(ab[:, 0:2, :], PS[4].v(PS[4].t[:, :].rearrange("p (g c) -> p g c", c=256)), AF.Copy)
                        S.copy(ab[:, 2:4, :], PS[5].v(PS[5].t[:, :].rearrange("p (g c) -> p g c", c=256)))
                        Ac = ab[:, :, 0:128]
                        As = ab[:, :, 128:256]
                        d1 = PS[6].c(0, 512)
                        S.mm(d1, cst["bd64"][:, 0, :], Ac, start=True, stop=False)
                        S.mm(d1, cst["bd64"][:, 2, :], As, start=False, stop=True)
                        S.act(do[:, 0, :], d1, AF.Copy)
                        d2 = PS[6].c(0, 512)
                        S.mm(d2, cst["bd64"][:, 0, :], As, start=True, stop=False)
                        S.mm(d2, cst["bd64"][:, 1, :], Ac, start=False, stop=True)
                        S.copy(do[:, 1, :], d2)
                        tt0 = t0 + ch * 128
                        S.dma("pool", DD[b].v(DD[b].t.ap()[:, tt0:tt0 + 128, :].rearrange("a t c -> t a c")),
                              do[:, :, :])

        load_weight_bf16(A, Win, dr["w_in"], 8, 0, 2432, stg)
        ck("A1b")
        gemm(hT, TL, 512, 0, 19, 0, U_lat[b])
        ck("A4")
        gemm(hTc, TC, 256, 0, 15, 0, U_ctx[b])
        load_weight_bf16(A, Win, dr["w_in"], 8, 2432, 4480, stg)
        gemm(hT, TL, 512, 19, 35, 19, None)
        barrier(S)
        A.close()

    for b in range(NB):
        phase_A(b)
    if stop_after == "A":
        barrier(S)
        return nc, S


    def pv(row, m):
        return pvec[:, row, m:m + 1]

    def phase_B():
        B = Scope(nc)
        mk2 = load_const(B, "mk2")
        load_const(B, "onesbd")
        load_const(B, "e2")
        load_const(B, "ones")
        uc = [B.sb(f"uc{i}", [128, 15, 128], F32) for i in range(2)]
        Hf = [B.sb(f"Hf{d}", [128, NB * 4 * 64], F32) for d in range(2)]
        Hb = [B.sb(f"Hb{d}", [128, NB, 4, 64], BF16) for d in range(2)]
        FM = [B.sb(f"FM{d}", [128, NB, 4, 4, 128], BF16, nsplit=NB, sdim=1) for d in range(2)]
        TOK = [B.sb(f"TOK{d}", [128, NB, 4, 3, 128], BF16, nsplit=NB, sdim=1) for d in range(2)]
        VM = [B.sb(f"VM{d}", [128, NB, 512], BF16, nsplit=NB, sdim=1) for d in range(2)]
        GC = [B.sb(f"GC{d}", [128, NB, 4], F32) for d in range(2)]
        tmp = {}
        for nm in ("kkraw", "sq", "rn", "kk", "a", "sg", "cf", "incl", "excl", "gi", "ge", "ginv", "kap",
                   "beta", "t1"):
            tmp[nm] = B.sb("t_" + nm, [128, 4, 128], F32)
        adb = B.sb("adb", [128, 128], BF16)
        twd = B.sb("twd", [128, 128], BF16)
        sgd = B.sb("sgd", [128, 128], BF16)
        vtok = [B.sb(f"vtok{i}", [128, 512], F32) for i in range(2)]
        gtok = [B.sb(f"gtok{i}", [128, 512], F32) for i in range(2)]
        bont = [B.sb(f"bont{i}", [128, 8], F32) for i in range(2)]
        NSLOT = 8
        FB = [B.sb(f"FB{i}", [128, 576], F32) for i in range(NSLOT)]
        PPf = [B.sb(f"PPf{i}", [128, 256], F32) for i in range(NSLOT)]
        TTf = [B.sb(f"TTf{i}", [128, 128], F32) for i in range(NSLOT)]
        Yf_ = [B.sb(f"Yf{i}", [128, 192], F32) for i in range(NSLOT)]
        Xf_ = [B.sb(f"Xf{i}", [128, 192], F32) for i in range(NSLOT)]
        ARB = [B.sb(f"ARB{i}", [128, 128], BF16) for i in range(NSLOT)]
        ARK = [B.sb(f"ARK{i}", [128, 128], BF16) for i in range(NSLOT)]
        Gb = [B.sb(f"Gb{i}", [128, 128], BF16) for i in range(NSLOT)]
        GTs = [B.sb(f"GTs{i}", [128, 128], BF16) for i in range(NSLOT)]
        WP = [B.sb(f"WP{i}", [128, 128], BF16) for i in range(NSLOT // 2)]
        WTs = [B.sb(f"WTs{i}", [128, 128], BF16) for i in range(NSLOT // 2)]
        Ub = B.sb("Ub", [128, NB, 512], BF16, nsplit=NB, sdim=1)
        Ysb = [B.sb(f"Ysb{i}", [128, 512], F32) for i in range(2)]
        htmp = B.sb("htmp", [128, NB * 4 * 64], F32)
        for d in range(2):
            S.memset(Hf[d][:, :], 0.0)
            S.memset(Hb[d][:, :, :, :], 0.0)
        ctr = [0]

        def prep(d, b, Usrc, t0, latent):
            q = ctr[0]
            ctr[0] += 1
            U = uc[q % 2]
            S.dma("sp", U[:, :, :], Usrc.v(Usrc.t.ap()[:, :, t0:t0 + 128].rearrange("n p t -> p n t")))
            Pd = slice(d * 64, d * 64 + 64)
            r = U[:, 0:4, :]
            k = U[:, 4:8, :]
            t = tmp
            for m in range(4):
                S.ts(t["kkraw"][:, m, :], U[:, 4 + m, :], pv(PK_KK, m), ALU.mult, ek="pool")
            S.tt(t["sq"][:, :, :], t["kkraw"][:, :, :], t["kkraw"][:, :, :], ALU.mult, ek="pool")
            ssp = PS[0].c(0, 512)
            S.mm(ssp, cst["onesbd"][:, :], t["sq"][:, :, :])
            S.ts(t["rn"][:, :, :], PS[0].v(PS[0].t[:, :].rearrange("p (m t) -> p m t", t=128)), 1e-24, ALU.max)
            S.act(t["rn"][:, :, :], t["rn"][:, :, :], AF.Ln)
            S.act(t["rn"][:, :, :], t["rn"][:, :, :], AF.Exp, scale=-0.5)
            S.tt(t["kk"][:, :, :], t["kkraw"][:, :, :], t["rn"][:, :, :], ALU.mult)
            ck("B1a")
            S.copy(adb[Pd, :], U[Pd, 13, :], ek="pool")
            for m in range(4):
                S.mm(PS[1].c(m * 128, (m + 1) * 128), A2w[Pd, m * 128:(m + 1) * 128], adb[Pd, :])
            for m in range(4):
                S.act(t["a"][:, m, :], PS[1].c(m * 128, (m + 1) * 128), AF.Sigmoid, bias=pv(PK_A0 + d, m))
            S.act(twd[Pd, :], U[Pd, 12, :], AF.Tanh)
            for m in range(4):
                S.mm(PS[2].c(m * 128, (m + 1) * 128), W2w[Pd, m * 128:(m + 1) * 128], twd[Pd, :])
            for m in range(4):
                S.act(t["sg"][:, m, :], PS[2].c(m * 128, (m + 1) * 128), AF.Sigmoid, bias=pv(PK_W0 + d, m))
            ck("B1b")
            for m in range(4):
                S.op("dve", lambda e: e.tensor_tensor_scan(t["cf"].t[:, m, :], cst["ones"].t[:, :], t["sg"].t[:, m, :],
                                                           0.0, ALU.mult, ALU.add),
                     [t["cf"][:, m, :]], [cst["ones"][:, :], t["sg"][:, m, :]])
            ck("B1c")
            if d == 0:
                incl = t["cf"]
                S.tt(t["excl"][:, :, :], t["cf"][:, :, :], t["sg"][:, :, :], ALU.subtract)
                excl = t["excl"]
            else:
                for m in range(4):
                    S.ts(t["excl"][:, m, :], t["cf"][:, m, :], -1.0, ALU.mult, t["cf"][:, m, 127:128], ALU.add)
                S.tt(t["incl"][:, :, :], t["excl"][:, :, :], t["sg"][:, :, :], ALU.add)
                incl = t["incl"]
                excl = t["excl"]
            S.act(t["gi"][:, :, :], incl[:, :, :], AF.Exp, scale=SC)
            S.act(t["ge"][:, :, :], excl[:, :, :], AF.Exp, scale=SC)
            S.act(t["ginv"][:, :, :], incl[:, :, :], AF.Exp, scale=-SC)
            S.act(GC[d][:, b, :], t["cf"][:, :, 127], AF.Exp, scale=SC)
            for m in range(4):
                S.ts(t["t1"][:, m, :], t["a"][:, m, :], pv(PK_KA, m), ALU.mult, pv(PK_OMKA, m), ALU.add, ek="pool")
            S.tt(t["kap"][:, :, :], k, t["t1"][:, :, :], ALU.mult, ek="pool")
            S.tt(t["beta"][:, :, :], t["kk"][:, :, :], t["a"][:, :, :], ALU.mult, ek="pool")
            fm = FM[d]
            S.tt(t["sq"][:, :, :], t["kk"][:, :, :], t["ge"][:, :, :], ALU.mult)
            S.act(fm[:, b, :, 0, :], t["sq"][:, :, :], AF.Copy, scale=-1.0)
            S.tt(fm[:, b, :, 1, :], r, t["gi"][:, :, :], ALU.mult)
            S.tt(fm[:, b, :, 2, :], t["beta"][:, :, :], t["ginv"][:, :, :], ALU.mult)
            S.tt(fm[:, b, :, 3, :], t["kap"][:, :, :], t["ginv"][:, :, :], ALU.mult)
            ck("B1d")
            for m in range(4):
                for xi, x in enumerate((0, 2, 3)):
                    S.tr(PT.c(xi * 128, (xi + 1) * 128), fm[:, b, m, x, :], identb[:, :])
                S.act(TOK[d][:, b, m, :, :], PT.v(PT.t[:, 0:384].rearrange("p (x c) -> p x c", c=128)),
                      AF.Copy)
            ck("B1e")
            for m in range(4):
                S.tr(PS[3].c(m * 128, (m + 1) * 128), U[:, 8 + m, :], identf[:, :])
            S.act(VM[d][:, b, :], PS[3].c(0, 512), AF.Copy)
            if latent:
                if d == 0:
                    vt = vtok[(q // 2) % 2]
                    S.copy(vt[:, :], PS[3].c(0, 512))
                    S.dma("pool", VT[b][t0:t0 + 128, :], vt[:, :])
                    S.act(sgd[:, :], U[:, 14, :], AF.Sigmoid)
                    S.mm(PS[4].c(0, 512), sgd[:, :], G2w[:, :])
                    gt = gtok[(q // 2) % 2]
                    S.act(gt[:, :], PS[4].c(0, 512), AF.Copy)
                    S.dma("pool", GT[b][t0:t0 + 128, :], gt[:, :])
                S.tt(t["t1"][:, :, :], r, t["kap"][:, :, :], ALU.mult, ek="pool")
                for m in range(4):
                    S.ts(t["t1"][:, m, :], t["t1"][:, m, :], pv(PK_RK, m), ALU.mult, ek="pool")
                for m in range(4):
                    S.mm(PS[5].c(m * 2, m * 2 + 2), t["t1"][:, m, :], cst["e2"][:, :])
                bt = bont[q % 2]
                S.copy(bt[:, :], PS[5].c(0, 8))
                S.dma("pool", BON[b][d][t0:t0 + 128, :], bt[:, :])

        def chunk_math(d, b, latent, t0):
            ck("B1")
            fm = FM[d]
            for h in range(8):
                m, hl = divmod(h, 2)
                P = slice(hl * 64, hl * 64 + 64)
                sl = h
                fb, ppf, ttf, yf_, xf_ = FB[sl], PPf[sl], TTf[sl], Yf_[sl], Xf_[sl]
                bankA = PS[h % 2]
                S.mm(bankA.c(0, 256), fm[P, b, m, 0, :], fm[P, b, m, 2:4, :])
                S.mm(bankA.c(256, 512), fm[P, b, m, 2, :], fm[P, b, m, 0:2, :])
                S.mm(PS[2].c(0, 128), fm[P, b, m, 3, :], fm[P, b, m, 1, :])
                S.tt(V(fb.ts, fb.t[:, 0:512].rearrange("p (x c) -> p x c", c=256)[:, :, 0:128]),
                     bankA.v(bankA.t[:, 0:256].rearrange("p (x c) -> p x c", c=128)), mk2[:, d, 0:2, :], ALU.mult)
                S.tt(fb[:, 128:256], bankA.c(256, 384), mk2[:, d, 2, :], ALU.mult)
                S.tt(fb[:, 448:576], bankA.c(256, 384), mk2[:, d, 4, :], ALU.mult)
                S.tt(ARB[sl][:, :], bankA.c(384, 512), mk2[:, d, 3, :], ALU.mult)
                S.tt(ARK[sl][:, :], PS[2].c(0, 128), mk2[:, d, 3, :], ALU.mult)
                S.copy(fb[:, 384:448], TOK[d][:, b, m, 0, hl * 64:(hl + 1) * 64], ek="pool")
                ck("B2")
                S.tt(ttf[:, :], fb[:, 128:256], identf[:, :], ALU.add, ek="pool")
                Pn, Pt_ = fb[:, 0:128], fb[:, 128:256]
                for lev in range(1, 6):
                    bq = PS[3]
                    S.mm(bq.c(0, 128), Pt_, Pn)
                    S.mm(bq.c(128, 256), Pn, Pt_)
                    S.act(ppf[:, :], bq.c(0, 256), AF.Copy)
                    Pn, Pt_ = ppf[:, 0:128], ppf[:, 128:256]
                    ba = PS[4]
                    S.mm(ba.c(0, 128), Pn, ttf[:, :])
                    S.tt(ttf[:, :], ba.c(0, 128), ttf[:, :], ALU.add)
                S.mm(PS[3].c(0, 192), ttf[:, :], fb[:, 256:448])
                S.act(yf_[:, :], PS[3].c(0, 192), AF.Copy)
                S.mm(PS[4].c(0, 192), fb[:, 448:576], yf_[:, :])
                S.copy(xf_[:, :], PS[4].c(0, 192))
                S.mm(PS[3].c(0, 192), ttf[:, :], xf_[:, :])
                S.tt(Gb[sl][:, :], PS[3].c(0, 128), yf_[:, 0:128], ALU.add)
                S.tt(WP[sl // 2][:, hl * 64:(hl + 1) * 64], PS[3].c(128, 192), yf_[:, 128:192], ALU.add)
                ck("B3")
                S.tr(PT.c(384, 512), Gb[sl][:, :], identb[:, :])
                S.act(GTs[sl][:, :], PT.c(384, 512), AF.Copy)
                if hl == 1:
                    S.tr(PT.c(640, 768), WP[sl // 2][:, :], identb[:, :])
                    S.copy(WTs[sl // 2][:, :], PT.c(640, 768))
            ck("B4")
            psU = PS[5].c(0, 512)
            for h in range(8):
                m, hl = divmod(h, 2)
                P = slice(hl * 64, hl * 64 + 64)
                sl = h
                o = PS[5].c(h * 64, (h + 1) * 64)
                S.mm(o, WTs[sl // 2][P, :], Hb[d][P, b, m, :], start=True, stop=False)
                S.mm(o, GTs[sl][:, :], VM[d][:, b, h * 64:(h + 1) * 64], start=False, stop=True)
            S.act(Ub[:, b, :], psU, AF.Copy)
            for h in range(8):
                m, hl = divmod(h, 2)
                P = slice(hl * 64, hl * 64 + 64)
                sl = h
                if latent:
                    o = PS[6].c(h * 64, (h + 1) * 64)
                    S.mm(o, fm[P, b, m, 1, :], Hb[d][P, b, m, :], start=True, stop=False)
                    S.mm(o, ARB[sl][:, :], Ub[:, b, h * 64:(h + 1) * 64], start=False, stop=False)
                    S.mm(o, ARK[sl][:, :], VM[d][:, b, h * 64:(h + 1) * 64], start=False, stop=True)
                o = PS[0].v(PS[0].t[P, (b * 4 + m) * 64:(b * 4 + m + 1) * 64])
                S.mm(o, TOK[d][:, b, m, 1, hl * 64:(hl + 1) * 64], Ub[:, b, h * 64:(h + 1) * 64], start=True, stop=False)
                S.mm(o, TOK[d][:, b, m, 2, hl * 64:(hl + 1) * 64], VM[d][:, b, h * 64:(h + 1) * 64], start=False, stop=True)
            ck("B5")
            if latent:
                ys = Ysb[(b + d) % 2]
                S.act(ys[:, :], PS[6].c(0, 512), AF.Copy)
                S.dma("pool", YD[b][d][t0:t0 + 128, :], ys[:, :])
            cs_ = slice(b * 256, (b + 1) * 256)
            S.tt(htmp[:, cs_], PS[0].v(PS[0].t[:, cs_]), Hf[d][:, cs_], ALU.add)
            S.tt(V(Hf[d].ts, Hf[d].t[:, cs_].rearrange("p (m v) -> p m v", v=64)),
                 V(htmp.ts, htmp.t[:, cs_].rearrange("p (m v) -> p m v", v=64)),
                 V(GC[d].ts, GC[d].t[:, b, :].unsqueeze(2).broadcast_to([128, 4, 64])), ALU.mult)
            S.copy(Hb[d][:, b, :, :], V(Hf[d].ts, Hf[d].t[:, cs_].rearrange("p (m v) -> p m v", v=64)), ek="pool")
            ck("B6")

        nsteps = 2 + TL // 128
        for s_ in range(nsteps):
            for d in range(2):
                for b in range(NB):
                    if s_ < 2:
                        ci = s_ if d == 0 else 1 - s_
                        prep(d, b, U_ctx[b], ci * 128, False)
                        chunk_math(d, b, False, ci * 128)
                    else:
                        li = s_ - 2
                        ci = li if d == 0 else (TL // 128 - 1 - li)
                        prep(d, b, U_lat[b], ci * 128, True)
                        chunk_math(d, b, True, ci * 128)
        barrier(S)
        B.close()

    phase_B()
    if stop_after == "B":
        return nc, S

    def phase_C(b):
        C = Scope(nc)
        load_const(C, "c32")
        lnx = C.sb("lnx", [128, 1024], F32)
        S.dma("sp", lnx[:, :], dr["lnx_row"].v(dr["lnx_row"].t.ap().partition_broadcast(128)))
        stg = [C.sb(f"stgC{i}", [128, 8, 256], F32) for i in range(2)]
        fT = C.sb("fT", [128, 4, TL], BF16)
        Wupf = C.sb("Wupf", [128, 4, D], BF16)
        Wupr = C.sb("Wupr", [128, 4, D], BF16)
        Wout = C.sb("Wout", [128, 8, D], BF16)
        load_weight_bf16(C, Wupf, dr["w_up_f"], 4, 0, D, stg)
        load_weight_bf16(C, Wupr, dr["w_up_r"], 4, 0, D, stg)
        load_weight_bf16(C, Wout, dr["w_out"], 8, 0, D, stg)
        ga1b = C.sb("ga1b", [128, D], F32)
        S.dma("sp", ga1b[:, :], MODROW.v(MODROW.t.ap()[b:b + 1, 0, :].partition_broadcast(128)))
        dd = [C.sb(f"dd{i}", [32, 2, 4, 512], F32) for i in range(2)]
        ddv = DD[b].t.ap().rearrange("a (r c) k -> r a c k", c=64)
        c32 = cst["c32"]
        for cb in range(16):
            dt_ = dd[cb % 2]
            S.dma("sp", dt_[:, :, :, :], DD[b].v(ddv[:, :, cb * 4:(cb + 1) * 4, :]))
            bank = PS[cb % 2]
            for cl in range(4):
                for g in range(4):
                    o = bank.c((g * 4 + cl) * 32, (g * 4 + cl) * 32 + 32)
                    S.mm(o, dt_[:, 0, cl, g * 128:(g + 1) * 128], c32[:, 0, :], start=True, stop=False)
                    S.mm(o, dt_[:, 1, cl, g * 128:(g + 1) * 128], c32[:, 1, :], start=False, stop=True)
            for g in range(4):
                src = bank.v(bank.t[:, g * 128:(g + 1) * 128].rearrange("p (c r) -> p r c", r=32))
                dst = V(fT.ts, fT.t[:, g, :].rearrange("p (r c) -> p r c", c=64)[:, :, cb * 4:(cb + 1) * 4])
                if g % 2 == 0:
                    S.act(dst, src, AF.Copy)
                else:
                    S.copy(dst, src)
        yf = [C.sb(f"yf{i}", [128, 512], F32) for i in range(2)]
        yb = [C.sb(f"yb{i}", [128, 512], F32) for i in range(2)]
        vt = [C.sb(f"vt{i}", [128, 512], F32) for i in range(2)]
        gt = [C.sb(f"gt{i}", [128, 512], F32) for i in range(2)]
        bo = [C.sb(f"bo{i}", [128, 2, 8], F32) for i in range(2)]
        ysq = C.sb("ysq", [128, 512], F32)
        stt_ = [C.sb(f"stC{i}", [128, 6, 8], F32) for i in range(2)]
        obf = C.sb("obf", [128, 512], BF16)
        oT = C.sb("oT", [128, 4, 512], BF16)
        gsb = [C.sb(f"gsb{i}", [128, 2, 512], F32) for i in range(2)]
        mT = C.sb("mT", [128, 8, 512], BF16, nsplit=8)
        mtmp = [C.sb(f"mtmp{i}", [128, 512], F32) for i in range(2)]
        xin = [C.sb(f"xin{i}", [128, D], F32) for i in range(2)]
        xo = [C.sb(f"xo{i}", [128, D], F32) for i in range(2)]

        def b3(v_, n):
            return V(v_.ts, v_.ap.unsqueeze(2).broadcast_to([128, 8, n]))

        def v3(buf):
            return V(buf.ts, buf.t[:, :].rearrange("p (h v) -> p h v", v=64))

        for tt in range(TL // 512):
            for sub in range(4):
                i = tt * 4 + sub
                t0 = i * 128
                y_, yb_, v_, g_, bo_, st_ = yf[i % 2], yb[i % 2], vt[i % 2], gt[i % 2], bo[i % 2], stt_[i % 2]
                S.dma("sp", y_[:, :], YD[b][0][t0:t0 + 128, :])
                S.dma("sp", yb_[:, :], YD[b][1][t0:t0 + 128, :])
                S.dma("sp", v_[:, :], VT[b][t0:t0 + 128, :])
                S.dma("sp", g_[:, :], GT[b][t0:t0 + 128, :])
                S.dma("sp", bo_[:, 0, :], BON[b][0][t0:t0 + 128, :])
                S.dma("sp", bo_[:, 1, :], BON[b][1][t0:t0 + 128, :])
                S.tt(y_[:, :], y_[:, :], yb_[:, :], ALU.add)
                S.reduce(st_[:, 0, :], v3(y_), ALU.add, AX.X)
                S.tt(ysq[:, :], y_[:, :], y_[:, :], ALU.mult, ek="pool")
                S.reduce(st_[:, 1, :], v3(ysq), ALU.add, AX.X)
                S.ts(st_[:, 2, :], st_[:, 0, :], 1.0 / 64, ALU.mult)
                S.tt(st_[:, 3, :], st_[:, 2, :], st_[:, 2, :], ALU.mult)
                S.stt(st_[:, 3, :], st_[:, 1, :], 1.0 / 64, st_[:, 3, :], ALU.mult, ALU.subtract)
                S.act(st_[:, 4, :], st_[:, 3, :], AF.Sqrt, bias=eps_t[:, 1:2])
                S.op("dve", lambda e: e.reciprocal(st_.t[:, 5, :], st_.t[:, 4, :]), [st_[:, 5, :]], [st_[:, 4, :]])
                S.tt(v3(y_), v3(y_), b3(st_[:, 2, :], 64), ALU.subtract)
                S.tt(v3(y_), v3(y_), b3(st_[:, 5, :], 64), ALU.mult)
                S.tt(y_[:, :], y_[:, :], lnx[:, 0:512], ALU.mult)
                S.tt(y_[:, :], y_[:, :], lnx[:, 512:1024], ALU.add, ek="pool")
                S.tt(bo_[:, 0, :], bo_[:, 0, :], bo_[:, 1, :], ALU.add, ek="pool")
                S.tt(v3(v_), v3(v_), b3(bo_[:, 0, :], 64), ALU.mult)
                S.tt(y_[:, :], y_[:, :], v_[:, :], ALU.add, ek="pool")
                S.tt(obf[:, :], y_[:, :], g_[:, :], ALU.mult)
                for kc in range(4):
                    S.tr(PT.c(kc * 128, (kc + 1) * 128), obf[:, kc * 128:(kc + 1) * 128], identb[:, :])
                S.act(oT[:, :, sub * 128:(sub + 1) * 128],
                      PT.v(PT.t[:, 0:512].rearrange("p (k t) -> p k t", t=128)), AF.Copy)
            T0 = tt * 512
            for n in range(8):
                gs = gsb[n % 2]
                S.dma("sp", gs[:, 0, :], SG[b][n, :, T0:T0 + 512])
                S.dma("sp", gs[:, 1, :], SG[b][8 + n, :, T0:T0 + 512])
                pf = PS[2].c(0, 512)
                pr = PS[3].c(0, 512)
                for kc in range(4):
                    S.mm(pf, Wupf[:, kc, n * 128:(n + 1) * 128], fT[:, kc, T0:T0 + 512], start=(kc == 0), stop=(kc == 3))
                for kc in range(4):
                    S.mm(pr, Wupr[:, kc, n * 128:(n + 1) * 128], oT[:, kc, :], start=(kc == 0), stop=(kc == 3))
                mt = mtmp[n % 2]
                S.tt(mt[:, :], pf, gs[:, 0, :], ALU.mult)
                S.tt(gs[:, 1, :], pr, gs[:, 1, :], ALU.mult)
                S.tt(mT[:, n, :], mt[:, :], gs[:, 1, :], ALU.add, ek="pool")
            for sub in range(4):
                i = tt * 4 + sub
                t0 = i * 128
                xi, xo_ = xin[i % 2], xo[i % 2]
                S.dma("sp", xi[:, :], dr["x"].v(dr["x"].t[b, t0:t0 + 128, :]))
                for half in range(2):
                    o = PS[4 + half].c(0, 512)
                    for n in range(8):
                        S.mm(o, mT[:, n, sub * 128:(sub + 1) * 128], Wout[:, n, half * 512:(half + 1) * 512],
                             start=(n == 0), stop=(n == 7))
                    hs = slice(half * 512, (half + 1) * 512)
                    S.tt(xo_[:, hs], o, ga1b[:, hs], ALU.mult)
                    S.tt(xo_[:, hs], xo_[:, hs], xi[:, hs], ALU.add, ek="pool")
                S.dma("pool", X1[b][t0:t0 + 128, :], xo_[:, :])
        barrier(S)
        C.close()

    for b in range(NB):
        phase_C(b)
    if stop_after == "C":
        return nc, S

    def phase_D():
        Dd = Scope(nc)
        fng = Dd.sb("fng", [128, 1024], F32)
        S.dma("sp", fng[:, :], dr["fng_row"].v(dr["fng_row"].t.ap().partition_broadcast(128)))
        stg = [Dd.sb(f"stgD{i}", [128, 1024], F32) for i in range(2)]
        Wgu = Dd.sb("Wgu", [128, 8, 2 * DFF], BF16)
        Wdn = Dd.sb("Wdn", [128, 22, D], BF16)
        vgu = dr["w_gu"].t.ap().rearrange("(k p) n -> p k n", p=128)
        ci = 0
        for k in range(8):
            for c0_ in range(0, 2 * DFF, 1024):
                w = min(1024, 2 * DFF - c0_)
                sg_ = stg[ci % 2]
                S.dma("sp" if ci % 2 else "pool", sg_[:, 0:w], dr["w_gu"].v(vgu[:, k, c0_:c0_ + w]))
                S.copy(Wgu[:, k, c0_:c0_ + w], sg_[:, 0:w], ek="pool")
                ci += 1
        vdn = dr["w_down"].t.ap().rearrange("(k p) n -> p k n", p=128)
        for k in range(22):
            sg_ = stg[ci % 2]
            S.dma("sp" if ci % 2 else "pool", sg_[:, :], dr["w_down"].v(vdn[:, k, :]))
            S.copy(Wdn[:, k, :], sg_[:, :], ek="pool")
            ci += 1
        TD = 256
        hT2 = Dd.sb("hT2", [128, 8, TD], BF16)
        ntb = nt_bufs(Dd, "D", nbuf=1)
        actT = Dd.sb("actT", [128, 22, TD], BF16, nsplit=22)
        sil = [Dd.sb(f"sil{i}", [128, TD], F32) for i in range(2)]
        ga2b = Dd.sb("ga2b", [128, D], F32)
        x1t = [Dd.sb(f"x1t{i}", [128, D], F32) for i in range(1)]
        x2t = [Dd.sb(f"x2t{i}", [128, D], F32) for i in range(2)]
        junk2 = ntb[2]
        stf = [Dd.sb(f"stf{i}", [128, 4], F32) for i in range(2)]
        for b in range(NB):
            S.dma("sp", ga2b[:, :], MODROW.v(MODROW.t.ap()[b:b + 1, 1, :].partition_broadcast(128)))
            for tt in range(TL // TD):
                T0 = tt * TD
                x1v = X1[b].t
                norm_transpose(ntb, X1[b], lambda i: x1v[T0 + i * 128:T0 + (i + 1) * 128, :], TD // 128, A2,
                               lambda k: modT[:, 24 + k, b:b + 1], b, hT2, 0)
                for fc in range(22):
                    pg = PS[fc % 2].c(0, TD)
                    pu = PS[2 + fc % 2].c(0, TD)
                    for k in range(8):
                        S.mm(pg, Wgu[:, k, fc * 128:(fc + 1) * 128], hT2[:, k, :], start=(k == 0), stop=(k == 7))
                    for k in range(8):
                        S.mm(pu, Wgu[:, k, DFF + fc * 128:DFF + (fc + 1) * 128], hT2[:, k, :], start=(k == 0),
                             stop=(k == 7))
                    sl_ = sil[fc % 2]
                    S.act(sl_[:, :], pg, AF.Silu)
                    S.tt(actT[:, fc, :], pu, sl_[:, :], ALU.mult)
                for sub in range(TD // 128):
                    i = tt * (TD // 128) + sub
                    t0 = T0 + sub * 128
                    x1_, x2_, sf = x1t[0], x2t[i % 2], stf[i % 2]
                    S.dma("sp", x1_[:, :], X1[b][t0:t0 + 128, :])
                    for half in range(2):
                        o = PS[4 + half].c(0, 512)
                        for fc in range(22):
                            S.mm(o, actT[:, fc, sub * 128:(sub + 1) * 128], Wdn[:, fc, half * 512:(half + 1) * 512],
                                 start=(fc == 0), stop=(fc == 21))
                        hs = slice(half * 512, (half + 1) * 512)
                        S.tt(x2_[:, hs], o, ga2b[:, hs], ALU.mult)
                        S.tt(x2_[:, hs], x2_[:, hs], x1_[:, hs], ALU.add, ek="pool")
                    S.act(junk2[:, :], x2_[:, :], AF.Square, accum=sf[:, 0:1])
                    S.act(sf[:, 1:2], sf[:, 0:1], AF.Sqrt, scale=1.0 / D, bias=eps_t[:, 0:1])
                    S.op("dve", lambda e: e.reciprocal(sf.t[:, 2:3], sf.t[:, 1:2]), [sf[:, 2:3]], [sf[:, 1:2]])
                    S.act(x2_[:, :], x2_[:, :], AF.Copy, scale=sf[:, 2:3])
                    S.tt(x2_[:, :], x2_[:, :], fng[:, :], ALU.mult)
                    S.dma("pool", out[b, t0:t0 + 128, :], x2_[:, :])
        barrier(S)
        Dd.close()

    phase_D()
    barrier(S)
    return nc, S

from concourse.bass_utils import run_bass_kernel_spmd

N_CORES = 8
_CACHE = {}


def _fm(vec, nchunk):
    return np.ascontiguousarray(np.asarray(vec, np.float32).reshape(nchunk, 128).T)


def make_in_maps(inp, cores):
    consts = host_consts()
    f = lambda a: np.ascontiguousarray(np.asarray(a, np.float32))
    shared = {
        "b_adaT": _fm(inp["b_ada"][0], 48), "b_ada_row": f(inp["b_ada"][0]).reshape(1, 6144),
        "n1g": _fm(inp["norm1_g"][0], 8), "n2g": _fm(inp["norm2_g"][0], 8),
        "fng_row": f(inp["final_norm_g"]).reshape(1, D),
        "w_ada": f(inp["w_ada"][0]), "w_in": f(inp["w_in"][0]),
        "w2": np.concatenate([f(inp["w2_f"][0]), f(inp["w2_b"][0])], 0),
        "a2": np.concatenate([f(inp["a2_f"][0]), f(inp["a2_b"][0])], 0),
        "g2": f(inp["g2"][0]),
        "lnx_row": np.concatenate([f(inp["lnx_g"][0]), f(inp["lnx_b"][0])]).reshape(1, 1024),
        "w_up_r": f(inp["w_up_r"][0]), "w_up_f": f(inp["w_up_f"][0]), "w_out": f(inp["w_out"][0]),
        "w_gu": f(inp["w_gu"][0]), "w_down": f(inp["w_down"][0]),
    }
    mu3 = np.zeros((128, 3, 15), np.float32)
    mu3[:, 0, :] = _fm(inp["mu_prev"][0], 15)
    mu3[:, 1, :] = _fm(inp["mu_next"][0], 15)
    shared["mu3"] = mu3
    pvec = np.zeros((128, 9, 4), np.float32)
    for i, nm in ((0, "k_k"), (1, "k_a"), (3, "r_k"), (4, "w0_f"), (5, "w0_b"), (6, "a0_f"), (7, "a0_b")):
        pvec[:, i, :] = _fm(np.asarray(inp[nm][0]).reshape(512), 4)
    shared["pvec"] = pvec
    shared.update(consts)
    maps = []
    for c in cores:
        m = dict(shared)
        m["x"] = f(inp["x"][NB * c:NB * (c + 1)])
        m["ctx"] = f(inp["ctx"][NB * c:NB * (c + 1)])
        cT = np.zeros((128, 8, 3), np.float32)
        for j in range(NB):
            cT[:, :, j] = _fm(inp["c"][NB * c + j], 8)
        cT[:, :, 2] = _fm(inp["c_ctx"], 8)
        m["cT"] = cT
        maps.append(m)
    return maps


def kernel(**inputs):
    if "nc" not in _CACHE:
        _CACHE["nc"] = build()[0]
    nc = _CACHE["nc"]
    maps = make_in_maps(inputs, list(range(N_CORES)))
    res = run_bass_kernel_spmd(nc, maps, core_ids=list(range(N_CORES)))
    outs = [np.asarray(res.results[c]["out"], np.float32) for c in range(N_CORES)]
    return np.concatenate(outs, axis=0)
```

```python
import numpy as np
import concourse.bass as bass
import concourse.mybir as mybir

F32 = mybir.dt.float32
BF16 = mybir.dt.bfloat16
I32 = mybir.dt.int32
AF = mybir.ActivationFunctionType
ALU = mybir.AluOpType
AX = mybir.AxisListType

SEM_LIMIT = 30000


class T:
    __slots__ = ("name", "w", "r")

    def __init__(self, name=""):
        self.name = name
        self.w = None
        self.r = {}


class V:
    __slots__ = ("ts", "ap", "x")

    def __init__(self, ts, ap, x=False):
        self.ts = ts
        self.ap = ap
        self.x = x


class Buf:
    def __init__(self, tensor, name, nsplit=None, sdim=1):
        self.t = tensor
        self.name = name
        self.nsplit = nsplit
        self.sdim = sdim
        if nsplit is None:
            self.ts = [T(name)]
        else:
            self.ts = [T(f"{name}{i}") for i in range(nsplit)]

    def __getitem__(self, idx):
        ap = self.t[idx]
        if self.nsplit is None:
            return V(self.ts, ap)
        if not isinstance(idx, tuple):
            idx = (idx,)
        if len(idx) <= self.sdim:
            return V(self.ts, ap)
        s = idx[self.sdim]
        if isinstance(s, int):
            return V([self.ts[s]], ap)
        if isinstance(s, slice):
            st, sp, _ = s.indices(self.nsplit)
            return V(self.ts[st:sp], ap)
        return V(self.ts, ap)

    def v(self, ap, which=None):
        if which is None:
            return V(self.ts, ap)
        return V([self.ts[i] for i in which], ap)


class Sync:
    def __init__(self, nc, ndma=24):
        self.nc = nc
        self.engs = {"pe": nc.tensor, "act": nc.scalar, "dve": nc.vector,
                     "pool": nc.gpsimd, "sp": nc.sync}
        self.semh = {}
        self.cur = {}
        self.cnt = {}
        self.nsem = 0
        for k in self.engs:
            self._new_sem(k)
        self.seen = {k: {} for k in self.engs}
        self.ndma = ndma
        self.dma_key = []
        self.dma_val = []
        for i in range(ndma):
            key = self._alloc(f"dma{i}")
            self.dma_key.append(key)
            self.dma_val.append(0)
        self.dma_rr = 0
        self.ninst = {k: 0 for k in self.engs}

    def _alloc(self, name):
        key = f"{name}_{self.nsem}"
        self.nsem += 1
        self.semh[key] = self.nc.alloc_semaphore(name=key)
        return key

    def _new_sem(self, ek):
        self.cur[ek] = self._alloc(f"e_{ek}")
        self.cnt[ek] = 0

    def _deps(self, outs, ins):
        deps = {}

        def add(ev):
            if ev is None:
                return
            k, val = ev
            if deps.get(k, 0) < val:
                deps[k] = val
        for v in ins:
            for t in v.ts:
                add(t.w)
        for v in outs:
            for t in v.ts:
                add(t.w)
                for k, val in t.r.items():
                    add((k, val))
        return deps

    def _wait(self, ek, deps):
        eng = self.engs[ek]
        seen = self.seen[ek]
        for k, val in deps.items():
            if ek == "pe" and k.startswith("e_pe"):
                continue
            if seen.get(k, 0) < val:
                eng.wait_ge(self.semh[k], val)
                seen[k] = val
                self.ninst[ek] += 1

    def _mark(self, ev, outs, ins):
        k, val = ev
        for v in ins:
            for t in v.ts:
                if t.r.get(k, 0) < val:
                    t.r[k] = val
        for v in outs:
            for t in v.ts:
                t.w = ev
                t.r = {}

    def op(self, ek, fn, outs, ins):
        outs = list(outs) + [v for v in ins if v.x]
        self._wait(ek, self._deps(outs, ins))
        if self.cnt[ek] >= SEM_LIMIT:
            self._new_sem(ek)
        inst = fn(self.engs[ek])
        self.cnt[ek] += 1
        inst.then_inc(self.semh[self.cur[ek]], 1)
        self.ninst[ek] += 1
        self._mark((self.cur[ek], self.cnt[ek]), outs, ins)

    def dma(self, ek, out, in_, **kw):
        i = self.dma_rr
        self.dma_rr = (self.dma_rr + 1) % self.ndma
        if self.dma_val[i] >= SEM_LIMIT:
            self.dma_key[i] = self._alloc(f"dma{i}")
            self.dma_val[i] = 0
        deps = self._deps([out], [in_])
        key = self.dma_key[i]
        if self.dma_val[i] > 0:
            if deps.get(key, 0) < self.dma_val[i]:
                deps[key] = self.dma_val[i]
        self._wait(ek, deps)
        inst = self.engs[ek].dma_start(out=out.ap, in_=in_.ap, **kw)
        self.dma_val[i] += 16
        inst.then_inc(self.semh[key], 16)
        self.ninst[ek] += 1
        self._mark((key, self.dma_val[i]), [out], [in_])

    def wait_all(self, ek, views):
        self._wait(ek, self._deps([], views))

    def mm(self, out, lhsT, rhs, start=True, stop=True, **kw):
        self.op("pe", lambda e: e.matmul(out.ap, lhsT.ap, rhs.ap, start=start, stop=stop, **kw),
                [out], [lhsT, rhs] + ([] if start else [out]))

    def tr(self, out, in_, ident):
        self.op("pe", lambda e: e.transpose(out.ap, in_.ap, ident.ap), [out], [in_, ident])

    def act(self, out, in_, func, bias=None, scale=1.0, ek="act", accum=None):
        ins = [in_]
        kw = {}
        if bias is not None:
            if isinstance(bias, V):
                ins.append(bias)
                kw["bias"] = bias.ap
            else:
                kw["bias"] = bias
        if isinstance(scale, V):
            ins.append(scale)
            kw["scale"] = scale.ap
        else:
            kw["scale"] = scale
        outs = [out]
        if accum is not None:
            outs.append(accum)
            kw["accum_out"] = accum.ap
        self.op(ek, lambda e: e.activation(out.ap, in_.ap, func, **kw), outs, ins)

    def tt(self, out, a, b, op, ek="dve"):
        self.op(ek, lambda e: e.tensor_tensor(out.ap, a.ap, b.ap, op), [out], [a, b])

    def ts(self, out, a, s1, op0, s2=None, op1=None, ek="dve"):
        ins = [a]
        x1 = s1
        if isinstance(s1, V):
            ins.append(s1)
            x1 = s1.ap
        x2 = s2
        if isinstance(s2, V):
            ins.append(s2)
            x2 = s2.ap
        if op1 is None:
            self.op(ek, lambda e: e.tensor_scalar(out.ap, a.ap, x1, None, op0), [out], ins)
        else:
            self.op(ek, lambda e: e.tensor_scalar(out.ap, a.ap, x1, x2, op0, op1), [out], ins)

    def stt(self, out, a, s, b, op0, op1):
        ins = [a, b]
        x = s
        if isinstance(s, V):
            ins.append(s)
            x = s.ap
        self.op("dve", lambda e: e.scalar_tensor_tensor(out.ap, a.ap, x, b.ap, op0, op1), [out], ins)

    def copy(self, out, in_, ek="dve"):
        self.op(ek, lambda e: e.tensor_copy(out.ap, in_.ap), [out], [in_])

    def memset(self, out, val, ek="pool"):
        self.op(ek, lambda e: e.memset(out.ap, val), [out], [])

    def reduce(self, out, in_, op, axis, ek="dve"):
        self.op(ek, lambda e: e.tensor_reduce(out.ap, in_.ap, axis, op), [out], [in_])

from contextlib import ExitStack

NB = 2
D = 1024
TL = 2048
TC = 256
NIN = 4480
DFF = 2816
NORM_EPS = 1e-6
import os
EVAC_ACT_ONLY = bool(int(os.environ.get('EVAC_ACT_ONLY', '0')))
GN_EPS = 64e-5
SC = -float(np.exp(-0.5))


class Scope:
    uid = 0

    def __init__(self, nc):
        self.nc = nc
        self.es = ExitStack()
        self.n = 0

    def sb(self, name, shape, dt, nsplit=None, sdim=1):
        Scope.uid += 1
        t = self.es.enter_context(self.nc.sbuf_tensor(f"s{Scope.uid}_{name}", shape, dt))
        return Buf(t, name, nsplit, sdim)

    def close(self):
        self.es.close()


class PBank:
    def __init__(self, nc, name, dt, width):
        self.t = nc.alloc_psum_tensor(name, [128, width], dt)
        self.blk = width
        self.ts = [T(f"{name}{i}") for i in range(max(1, width // self.blk))]

    def c(self, a, b, rows=slice(None)):
        q0 = a // self.blk
        q1 = (b - 1) // self.blk
        return V(self.ts[q0:q1 + 1], self.t[rows, a:b], True)

    def v(self, ap):
        return V(self.ts, ap, True)


def barrier(S):
    for ek, eng in S.engs.items():
        deps = {}
        for k2 in S.engs:
            if S.cnt[k2] > 0:
                deps[S.cur[k2]] = S.cnt[k2]
        for i in range(S.ndma):
            if S.dma_val[i] > 0:
                deps[S.dma_key[i]] = S.dma_val[i]
        seen = S.seen[ek]
        for k, val in deps.items():
            if seen.get(k, 0) < val:
                eng.wait_ge(S.semh[k], val)
                seen[k] = val


def host_consts():
    c = {}
    i = np.arange(128)
    row = i[:, None]
    col = i[None, :]
    SL = (col < row).astype(np.float32)
    SU = (col > row).astype(np.float32)
    IL = (col <= row).astype(np.float32)
    IU = (col >= row).astype(np.float32)
    mk = np.zeros((128, 4, 256), np.float32)
    mk[:, 0, :128] = SL; mk[:, 0, 128:] = SL
    mk[:, 1, :128] = SU; mk[:, 1, 128:] = IU
    mk[:, 2, :128] = SU; mk[:, 2, 128:] = SU
    mk[:, 3, :128] = SL; mk[:, 3, 128:] = IL
    c["mkp"] = mk
    BDm = np.zeros((128, 128), np.float32)
    BDm[:64, :64] = 1.0
    BDm[64:, 64:] = 1.0
    mk2 = np.zeros((128, 2, 5, 128), np.float32)
    mk2[:, 0, 0] = SL * BDm; mk2[:, 0, 1] = SL; mk2[:, 0, 2] = SU * BDm; mk2[:, 0, 3] = IU; mk2[:, 0, 4] = SU * (1 - BDm)
    mk2[:, 1, 0] = SU * BDm; mk2[:, 1, 1] = SU; mk2[:, 1, 2] = SL * BDm; mk2[:, 1, 3] = IL; mk2[:, 1, 4] = SL * (1 - BDm)
    c["mk2"] = mk2
    c["identf"] = np.eye(128, dtype=np.float32)
    th = 2 * np.pi * np.outer(i, i) / 128.0
    cs = np.zeros((128, 256), np.float32)
    cs[:, :128] = np.cos(th) / np.sqrt(128.0)
    cs[:, 128:] = np.sin(th) / np.sqrt(128.0)
    c["cs128"] = cs
    j = np.arange(64)
    th64 = 2 * np.pi * np.outer(j, j) / 64.0
    C64 = np.cos(th64) / 8.0
    S64 = np.sin(th64) / 8.0
    bd = np.zeros((128, 3, 128), np.float32)
    for r in range(2):
        bd[r * 64:(r + 1) * 64, 0, r * 64:(r + 1) * 64] = C64
        bd[r * 64:(r + 1) * 64, 1, r * 64:(r + 1) * 64] = S64
        bd[r * 64:(r + 1) * 64, 2, r * 64:(r + 1) * 64] = -S64
    c["bd64"] = bd
    k32 = np.arange(32)
    th32 = 2 * np.pi * np.outer(k32, k32) / 32.0
    c32 = np.zeros((32, 2, 32), np.float32)
    c32[:, 0, :] = np.cos(th32) / np.sqrt(32.0)
    c32[:, 1, :] = -np.sin(th32) / np.sqrt(32.0)
    c["c32"] = c32
    ones_bd = np.zeros((128, 128), np.float32)
    ones_bd[:64, :64] = 1.0
    ones_bd[64:, 64:] = 1.0
    c["onesbd"] = ones_bd
    e2 = np.zeros((128, 2), np.float32)
    e2[:64, 0] = 1.0
    e2[64:, 1] = 1.0
    c["e2"] = e2
    c["ones"] = np.ones((128, 128), np.float32)
    return c


CONST_SHAPES = {"mk2": [128, 2, 5, 128], "mkp": [128, 4, 256], "identf": [128, 128], "cs128": [128, 256], "bd64": [128, 3, 128],
                "c32": [32, 2, 32], "onesbd": [128, 128], "e2": [128, 2], "ones": [128, 128]}

IN_SHAPES = {
    "x": [NB, TL, D], "ctx": [NB, TC, D], "cT": [128, 8, 3], "b_adaT": [128, 48], "b_ada_row": [1, 6144],
    "n1g": [128, 8], "n2g": [128, 8], "fng_row": [1, D],
    "w_ada": [D, 6144], "w_in": [D, NIN], "mu3": [128, 3, 15],
    "w2": [128, 512], "a2": [128, 512], "g2": [128, 512],
    "pvec": [128, 9, 4],
    "lnx_row": [1, 1024],
    "w_up_r": [512, D], "w_up_f": [512, D], "w_out": [D, D], "w_gu": [D, 2 * DFF], "w_down": [DFF, D],
}
IN_SHAPES.update(CONST_SHAPES)


class StopBuild(Exception):
    pass


def build(debug=(), stop_after=None):
    nc = bass.Bass("TRN2", target_bir_lowering=False)
    S = Sync(nc)
    try:
        return _build(nc, S, debug, stop_after)
    except StopBuild:
        barrier(S)
        return nc, S


def _build(nc, S, debug, stop_after):
    def ck(name):
        if stop_after == name:
            raise StopBuild()
    dr = {}
    for name, shp in IN_SHAPES.items():
        dr[name] = Buf(nc.dram_tensor(name, shp, F32, kind="ExternalInput"), name)
    out = Buf(nc.dram_tensor("out", [NB, TL, D], F32, kind="ExternalOutput"), "out", nsplit=NB, sdim=0)

    def scratch(name, shape, nsplit=None, sdim=0):
        kind = "ExternalOutput" if name in debug else "Internal"
        return Buf(nc.dram_tensor(name, shape, F32, kind=kind), name, nsplit, sdim)

    U_lat = [scratch(f"U_lat{b}", [15, 128, TL], 15, 0) for b in range(NB)]
    U_ctx = [scratch(f"U_ctx{b}", [15, 128, TC], 15, 0) for b in range(NB)]
    SG = [scratch(f"SG{b}", [16, 128, TL], 16, 0) for b in range(NB)]
    DD = [scratch(f"DD{b}", [2, TL, 512], 2, 0) for b in range(NB)]
    YD = [[scratch(f"YD{b}_{d}", [TL, 512]) for d in range(2)] for b in range(NB)]
    VT = [scratch(f"VT{b}", [TL, 512]) for b in range(NB)]
    GT = [scratch(f"GT{b}", [TL, 512]) for b in range(NB)]
    BON = [[scratch(f"BON{b}_{d}", [TL, 8]) for d in range(2)] for b in range(NB)]
    X1 = [scratch(f"X1_{b}", [TL, D]) for b in range(NB)]
    MODROW = scratch("MODROW", [3, 2, 1024])

    if os.environ.get("PTFIRST") == "1":
        PT = PBank(nc, "pst", BF16, 1024)
        PS = [PBank(nc, f"ps{i}", F32, 512) for i in range(7)]
    else:
        PS = [PBank(nc, f"ps{i}", F32, 512) for i in range(7)]
        PT = PBank(nc, "pst", BF16, 1024)

    G = Scope(nc)
    cst = {}

    def load_const(sc, name):
        shp = CONST_SHAPES[name]
        cst[name] = sc.sb("c_" + name, shp, F32)
        S.dma("sp", cst[name].v(cst[name].t[tuple(slice(None) for _ in shp)]),
              dr[name].v(dr[name].t[tuple(slice(None) for _ in shp)]))
        return cst[name]

    load_const(G, "identf")
    identf = cst["identf"]
    identb = G.sb("identb", [128, 128], BF16)
    S.copy(identb[:, :], identf[:, :])
    mu3 = G.sb("mu3", [128, 3, 15], F32)
    S.dma("sp", mu3[:, :, :], dr["mu3"][:, :, :])
    c0 = G.sb("c0", [128, 15], F32)
    S.tt(c0[:, :], mu3[:, 0, :], mu3[:, 1, :], ALU.add)
    S.ts(c0[:, :], c0[:, :], -1.0, ALU.mult, 1.0, ALU.add)
    pvec = G.sb("pvec", [128, 9, 4], F32)
    S.dma("sp", pvec[:, :, :], dr["pvec"][:, :, :])
    S.ts(pvec[:, 2, :], pvec[:, 1, :], -1.0, ALU.mult, 1.0, ALU.add)
    PK_KK, PK_KA, PK_OMKA, PK_RK, PK_W0, PK_A0 = 0, 1, 2, 3, 4, 6
    n1g = G.sb("n1g", [128, 8], F32)
    n2g = G.sb("n2g", [128, 8], F32)
    S.dma("sp", n1g[:, :], dr["n1g"][:, :])
    S.dma("sp", n2g[:, :], dr["n2g"][:, :])
    smallw = {}
    for name in ("w2", "a2", "g2"):
        smallw[name] = G.sb("bf_" + name, [128, 512], BF16)
    W2w, A2w, G2w = smallw["w2"], smallw["a2"], smallw["g2"]

    if stop_after == "G":
        barrier(S)
        return nc, S
    modT = G.sb("modT", [128, 48, 3], F32)
    A1 = G.sb("A1", [128, 8, 3], F32)
    A2 = G.sb("A2", [128, 8, 3], F32)
    M = Scope(nc)
    garow = M.sb("garow", [3, 2, 1024], F32)
    stg_small = [M.sb(f"stg_small{i}", [128, 512], F32) for i in range(3)]
    for i, name in enumerate(("w2", "a2", "g2")):
        S.dma("sp", stg_small[i][:, :], dr[name][:, :])
        S.copy(smallw[name][:, :], stg_small[i][:, :])
    cT = M.sb("cT", [128, 8, 3], F32)
    sT = M.sb("sT", [128, 8, 3], F32)
    badaT = M.sb("badaT", [128, 48], F32)
    brow = M.sb("brow", [3, 6144], F32)
    S.dma("sp", cT[:, :, :], dr["cT"][:, :, :])
    S.dma("sp", badaT[:, :], dr["b_adaT"][:, :])
    S.dma("sp", brow[:, :], dr["b_ada_row"].v(dr["b_ada_row"].t.ap().partition_broadcast(3)))
    S.act(sT[:, :, :], cT[:, :, :], AF.Silu)
    wa = [M.sb(f"wa{i}", [128, 8, 1024], F32) for i in range(2)]
    w_ada_v = dr["w_ada"].t.ap().rearrange("(k p) n -> p k n", p=128)
    for sec in range(6):
        wb = wa[sec % 2]
        S.dma("sp" if sec % 2 == 0 else "pool", wb[:, :, :], dr["w_ada"].v(w_ada_v[:, :, sec * 1024:(sec + 1) * 1024]))
        for nn in range(8):
            n = sec * 8 + nn
            o = PS[0].c(n * 3, n * 3 + 3)
            for k in range(8):
                S.mm(o, wb[:, k, nn * 128:(nn + 1) * 128], sT[:, k, :], start=(k == 0), stop=(k == 7))
        if sec in (2, 5):
            gi_ = 0 if sec == 2 else 1
            for half in range(2):
                o = PS[1].c(0, 512, rows=slice(0, 3))
                for k in range(8):
                    S.mm(o, sT[:, k, :], wb[:, k, half * 512:(half + 1) * 512], start=(k == 0), stop=(k == 7))
                S.tt(garow[:, gi_, half * 512:(half + 1) * 512], o,
                     brow[:, sec * 1024 + half * 512: sec * 1024 + (half + 1) * 512], ALU.add)
    S.tt(modT[:, :, :], PS[0].v(PS[0].t[:, 0:144].rearrange("p (n j) -> p n j", j=3)),
         V(badaT.ts, badaT.t[:, :].unsqueeze(2).broadcast_to([128, 48, 3])), ALU.add)
    S.dma("sp", MODROW[:, :, :], garow[:, :, :])
    S.ts(A1[:, :, :], modT[:, 8:16, :], 1.0, ALU.add)
    S.tt(A1[:, :, :], A1[:, :, :], V(n1g.ts, n1g.t[:, :].unsqueeze(2).broadcast_to([128, 8, 3])), ALU.mult)
    S.ts(A2[:, :, :], modT[:, 32:40, :], 1.0, ALU.add)
    S.tt(A2[:, :, :], A2[:, :, :], V(n2g.ts, n2g.t[:, :].unsqueeze(2).broadcast_to([128, 8, 3])), ALU.mult)
    barrier(S)
    M.close()
    if stop_after == "M":
        if "modT" in debug:
            dbgm = nc.dram_tensor("dbg_modT", [128, 48, 3], F32, kind="ExternalOutput")
            S.dma("sp", V([T("x")], dbgm[:, :, :]), modT[:, :, :])
        barrier(S)
        return nc, S

    def nt_bufs(sc, tag, nbuf=2):
        xt = [sc.sb(f"xt{tag}{i}", [128, 1024], F32) for i in range(nbuf)]
        xn = [sc.sb(f"xn{tag}{i}", [128, 1024], BF16) for i in range(nbuf)]
        junk = sc.sb(f"junk{tag}", [128, 1024], BF16)
        st = [sc.sb(f"st{tag}{i}", [128, 4], F32) for i in range(2)]
        return xt, xn, junk, st

    def norm_transpose(bufs, src_buf, src_rows_fn, ntiles, Aap, Bfn, j, hT, col0):
        xt, xn, junk, st = bufs
        for i in range(ntiles):
            if i == 1:
                ck("A0b")
            if i == 3:
                ck("A0c")
            x_t = xt[i % len(xt)]
            x_n = xn[i % len(xn)]
            s_ = st[i % 2]
            S.dma("sp", x_t[:, :], src_buf.v(src_rows_fn(i)))
            ck("A0a")
            S.act(junk[:, :], x_t[:, :], AF.Square, accum=s_[:, 0:1])
            S.act(s_[:, 1:2], s_[:, 0:1], AF.Sqrt, scale=1.0 / D, bias=eps_t[:, 0:1])
            S.op("dve", lambda e: e.reciprocal(s_.t[:, 2:3], s_.t[:, 1:2]), [s_[:, 2:3]], [s_[:, 1:2]])
            S.act(x_n[:, :], x_t[:, :], AF.Copy, scale=s_[:, 2:3])
            ck("A0a2")
            for k in range(8):
                S.tr(PT.c(k * 128, (k + 1) * 128), x_n[:, k * 128:(k + 1) * 128], identb[:, :])
            ck("A0a3")
            for k in range(8):
                o = hT[:, k, col0 + i * 128: col0 + (i + 1) * 128]
                if k == 1:
                    ck("A0a4")
                if os.environ.get("DBGV") == "1":
                    o = junk[:, 0:128]
                if os.environ.get("DBGV") == "2":
                    S.act(o, PT.c(k * 128, (k + 1) * 128), AF.Identity, scale=0.5, bias=eps_t[:, 0:1])
                    continue
                if os.environ.get("DBGV") == "3":
                    S.act(o, PT.c(k * 128, (k + 1) * 128), AF.Copy)
                    continue
                if k == 2:
                    ck("A0a5")
                if k % 2 == 0 and not EVAC_ACT_ONLY:
                    S.ts(o, PT.c(k * 128, (k + 1) * 128), Aap[:, k, j:j + 1], ALU.mult, Bfn(k), ALU.add)
                else:
                    S.act(o, PT.c(k * 128, (k + 1) * 128), AF.Identity, scale=Aap[:, k, j:j + 1],
                          bias=Bfn(k))

    eps_t = G.sb("eps_t", [128, 2], F32)
    S.memset(eps_t[:, 0:1], NORM_EPS)
    S.memset(eps_t[:, 1:2], GN_EPS)

    def load_weight_bf16(sc, dst, dram_buf, kchunks, c0_, c1_, stg, ei=[0]):
        v = dram_buf.t.ap().rearrange("(k p) n -> p k n", p=128)
        cc = c0_
        while cc < c1_:
            w = min(256, c1_ - cc)
            sg_ = stg[ei[0] % 2]
            ei[0] += 1
            S.dma("pool" if ei[0] % 2 else "sp", sg_[:, 0:kchunks, 0:w], dram_buf.v(v[:, :, cc:cc + w]))
            S.copy(dst[:, 0:kchunks, cc - c0_: cc - c0_ + w], sg_[:, 0:kchunks, 0:w], ek="pool")
            cc += w

    Bsh1 = modT

    def phase_A(b):
        A = Scope(nc)
        load_const(A, "cs128")
        load_const(A, "bd64")
        stg = [A.sb(f"stgA{i}", [128, 8, 256], F32) for i in range(2)]
        hT = A.sb("hT", [128, 8, TL + 2], BF16)
        hTc = A.sb("hTc", [128, 8, TC + 2], BF16)
        Win = A.sb("Win", [128, 8, 2432], BF16)
        S.memset(hT[:, :, 0:1], 0.0)
        S.memset(hT[:, :, TL + 1:TL + 2], 0.0)
        S.memset(hTc[:, :, 0:1], 0.0)
        S.memset(hTc[:, :, TC + 1:TC + 2], 0.0)
        xv = dr["x"].t
        cv = dr["ctx"].t
        ck("A0")
        ntb = nt_bufs(A, "A")
        norm_transpose(ntb, dr["x"], lambda i: xv[b, i * 128:(i + 1) * 128, :], TL // 128, A1,
                       lambda k: modT[:, k, b:b + 1], b, hT, 1)
        norm_transpose(ntb, dr["ctx"], lambda i: cv[b, i * 128:(i + 1) * 128, :], TC // 128, A1,
                       lambda k: modT[:, k, 2:3], 2, hTc, 1)
        ck("A1")
        pb = [A.sb(f"pb{i}", [128, 514], F32) for i in range(2)]
        ub = [A.sb(f"ub{i}", [128, 512], F32) for i in range(3)]
        uf = A.sb("uf", [128, 4, 512], F32, nsplit=4)
        Abuf = [A.sb(f"Abuf{i}", [128, 4, 256], F32) for i in range(2)]
        Dout = [A.sb(f"Dout{i}", [128, 2, 512], F32) for i in range(2)]
        cnt = [0]

        def gemm(hbuf, T_, TT, n_lo, n_hi, wofs, Udst):
            for t0 in range(0, T_, TT):
                for n in range(n_lo, n_hi):
                    q = cnt[0]
                    cnt[0] += 1
                    bank = PS[q % 2]
                    o = bank.c(0, TT)
                    for k in range(8):
                        S.mm(o, Win[:, k, (n - wofs) * 128:(n - wofs + 1) * 128], hbuf[:, k, 1 + t0:1 + t0 + TT],
                             start=(k == 0), stop=(k == 7))
                    if n == 1:
                        ck("A2")
                    if n == 16:
                        ck("A3")
                    if n < 15:
                        oh = PS[2 + q % 2].c(0, 2)
                        for k in range(8):
                            S.mm(oh, Win[:, k, (n - wofs) * 128:(n - wofs + 1) * 128],
                                 hbuf[:, k, t0:t0 + TT + 2:TT + 1], start=(k == 0), stop=(k == 7))
                        p_ = pb[q % 2]
                        u_ = ub[q % 3]
                        S.act(p_[:, 1:TT + 1], o, AF.Copy)
                        S.copy(p_[:, 0:TT + 2:TT + 1], oh)
                        S.act(u_[:, 0:TT], p_[:, 1:TT + 1], AF.Copy, scale=c0[:, n:n + 1])
                        S.stt(u_[:, 0:TT], p_[:, 0:TT], mu3[:, 0, n:n + 1], u_[:, 0:TT], ALU.mult, ALU.add)
                        S.stt(u_[:, 0:TT], p_[:, 2:TT + 2], mu3[:, 1, n:n + 1], u_[:, 0:TT], ALU.mult, ALU.add)
                        S.dma("pool", Udst[n, :, t0:t0 + TT], u_[:, 0:TT])
                    elif n < 19:
                        S.act(uf[:, n - 15, 0:TT], o, AF.Copy)
                    else:
                        u_ = ub[q % 3]
                        S.act(u_[:, 0:TT], o, AF.Sigmoid)
                        S.dma("pool", SG[b][n - 19, :, t0:t0 + TT], u_[:, 0:TT])
                if n_lo <= 15 and n_hi >= 19 and T_ == TL:
                    for ch in range(4):
                        ab = Abuf[ch % 2]
                        do = Dout[ch % 2]
                        for g in range(4):
                            bank = PS[4 + (g // 2)]
                            S.mm(bank.c((g % 2) * 256, (g % 2) * 256 + 256), uf[:, g, ch * 128:(ch + 1) * 128],
                                 cst["cs128"][:, :])
                        S.act(ab[:, 0:2, :], PS[4].v(PS[4].t[:, :].rearrange("p (g c) -> p g c", c=256)), AF.Copy)
                        S.copy(ab[:, 2:4, :], PS[5].v(PS[5].t[:, :].rearrange("p (g c) -> p g c", c=256)))
                        Ac = ab[:, :, 0:128]
                        As = ab[:, :, 128:256]
                        d1 = PS[6].c(0, 512)
                        S.mm(d1, cst["bd64"][:, 0, :], Ac, start=True, stop=False)
                        S.mm(d1, cst["bd64"][:, 2, :], As, start=False, stop=True)
                        S.act(do[:, 0, :], d1, AF.Copy)
                        d2 = PS[6].c(0, 512)
                        S.mm(d2, cst["bd64"][:, 0, :], As, start=True, stop=False)
                        S.mm(d2, cst["bd64"][:, 1, :], Ac, start=False, stop=True)
                        S.copy(do[:, 1, :], d2)
                        tt0 = t0 + ch * 128
                        S.dma("pool", DD[b].v(DD[b].t.ap()[:, tt0:tt0 + 128, :].rearrange("a t c -> t a c")),
                              do[:, :, :])

        load_weight_bf16(A, Win, dr["w_in"], 8, 0, 2432, stg)
        ck("A1b")
        gemm(hT, TL, 512, 0, 19, 0, U_lat[b])
        ck("A4")
        gemm(hTc, TC, 256, 0, 15, 0, U_ctx[b])
        load_weight_bf16(A, Win, dr["w_in"], 8, 2432, 4480, stg)
        gemm(hT, TL, 512, 19, 35, 19, None)
        barrier(S)
        A.close()

    for b in range(NB):
        phase_A(b)
    if stop_after == "A":
        barrier(S)
        return nc, S


    def pv(row, m):
        return pvec[:, row, m:m + 1]

    def phase_B():
        B = Scope(nc)
        mk2 = load_const(B, "mk2")
        load_const(B, "onesbd")
        load_const(B, "e2")
        load_const(B, "ones")
        uc = [B.sb(f"uc{i}", [128, 15, 128], F32) for i in range(2)]
        Hf = [B.sb(f"Hf{d}", [128, NB * 4 * 64], F32) for d in range(2)]
        Hb = [B.sb(f"Hb{d}", [128, NB, 4, 64], BF16) for d in range(2)]
        FM = [B.sb(f"FM{d}", [128, NB, 4, 4, 128], BF16, nsplit=NB, sdim=1) for d in range(2)]
        TOK = [B.sb(f"TOK{d}", [128, NB, 4, 3, 128], BF16, nsplit=NB, sdim=1) for d in range(2)]
        VM = [B.sb(f"VM{d}", [128, NB, 512], BF16, nsplit=NB, sdim=1) for d in range(2)]
        GC = [B.sb(f"GC{d}", [128, NB, 4], F32) for d in range(2)]
        tmp = {}
        for nm in ("kkraw", "sq", "rn", "kk", "a", "sg", "cf", "incl", "excl", "gi", "ge", "ginv", "kap",
                   "beta", "t1"):
            tmp[nm] = B.sb("t_" + nm, [128, 4, 128], F32)
        adb = B.sb("adb", [128, 128], BF16)
        twd = B.sb("twd", [128, 128], BF16)
        sgd = B.sb("sgd", [128, 128], BF16)
        vtok = [B.sb(f"vtok{i}", [128, 512], F32) for i in range(2)]
        gtok = [B.sb(f"gtok{i}", [128, 512], F32) for i in range(2)]
        bont = [B.sb(f"bont{i}", [128, 8], F32) for i in range(2)]
        NSLOT = 8
        FB = [B.sb(f"FB{i}", [128, 576], F32) for i in range(NSLOT)]
        PPf = [B.sb(f"PPf{i}", [128, 256], F32) for i in range(NSLOT)]
        TTf = [B.sb(f"TTf{i}", [128, 128], F32) for i in range(NSLOT)]
        Yf_ = [B.sb(f"Yf{i}", [128, 192], F32) for i in range(NSLOT)]
        Xf_ = [B.sb(f"Xf{i}", [128, 192], F32) for i in range(NSLOT)]
        ARB = [B.sb(f"ARB{i}", [128, 128], BF16) for i in range(NSLOT)]
        ARK = [B.sb(f"ARK{i}", [128, 128], BF16) for i in range(NSLOT)]
        Gb = [B.sb(f"Gb{i}", [128, 128], BF16) for i in range(NSLOT)]
        GTs = [B.sb(f"GTs{i}", [128, 128], BF16) for i in range(NSLOT)]
        WP = [B.sb(f"WP{i}", [128, 128], BF16) for i in range(NSLOT // 2)]
        WTs = [B.sb(f"WTs{i}", [128, 128], BF16) for i in range(NSLOT // 2)]
        Ub = B.sb("Ub", [128, NB, 512], BF16, nsplit=NB, sdim=1)
        Ysb = [B.sb(f"Ysb{i}", [128, 512], F32) for i in range(2)]
        htmp = B.sb("htmp", [128, NB * 4 * 64], F32)
        for d in range(2):
            S.memset(Hf[d][:, :], 0.0)
            S.memset(Hb[d][:, :, :, :], 0.0)
        ctr = [0]

        def prep(d, b, Usrc, t0, latent):
            q = ctr[0]
            ctr[0] += 1
            U = uc[q % 2]
            S.dma("sp", U[:, :, :], Usrc.v(Usrc.t.ap()[:, :, t0:t0 + 128].rearrange("n p t -> p n t")))
            Pd = slice(d * 64, d * 64 + 64)
            r = U[:, 0:4, :]
            k = U[:, 4:8, :]
            t = tmp
            for m in range(4):
                S.ts(t["kkraw"][:, m, :], U[:, 4 + m, :], pv(PK_KK, m), ALU.mult, ek="pool")
            S.tt(t["sq"][:, :, :], t["kkraw"][:, :, :], t["kkraw"][:, :, :], ALU.mult, ek="pool")
            ssp = PS[0].c(0, 512)
            S.mm(ssp, cst["onesbd"][:, :], t["sq"][:, :, :])
            S.ts(t["rn"][:, :, :], PS[0].v(PS[0].t[:, :].rearrange("p (m t) -> p m t", t=128)), 1e-24, ALU.max)
            S.act(t["rn"][:, :, :], t["rn"][:, :, :], AF.Ln)
            S.act(t["rn"][:, :, :], t["rn"][:, :, :], AF.Exp, scale=-0.5)
            S.tt(t["kk"][:, :, :], t["kkraw"][:, :, :], t["rn"][:, :, :], ALU.mult)
            ck("B1a")
            S.copy(adb[Pd, :], U[Pd, 13, :], ek="pool")
            for m in range(4):
                S.mm(PS[1].c(m * 128, (m + 1) * 128), A2w[Pd, m * 128:(m + 1) * 128], adb[Pd, :])
            for m in range(4):
                S.act(t["a"][:, m, :], PS[1].c(m * 128, (m + 1) * 128), AF.Sigmoid, bias=pv(PK_A0 + d, m))
            S.act(twd[Pd, :], U[Pd, 12, :], AF.Tanh)
            for m in range(4):
                S.mm(PS[2].c(m * 128, (m + 1) * 128), W2w[Pd, m * 128:(m + 1) * 128], twd[Pd, :])
            for m in range(4):
                S.act(t["sg"][:, m, :], PS[2].c(m * 128, (m + 1) * 128), AF.Sigmoid, bias=pv(PK_W0 + d, m))
            ck("B1b")
            for m in range(4):
                S.op("dve", lambda e: e.tensor_tensor_scan(t["cf"].t[:, m, :], cst["ones"].t[:, :], t["sg"].t[:, m, :],
                                                           0.0, ALU.mult, ALU.add),
                     [t["cf"][:, m, :]], [cst["ones"][:, :], t["sg"][:, m, :]])
            ck("B1c")
            if d == 0:
                incl = t["cf"]
                S.tt(t["excl"][:, :, :], t["cf"][:, :, :], t["sg"][:, :, :], ALU.subtract)
                excl = t["excl"]
            else:
                for m in range(4):
                    S.ts(t["excl"][:, m, :], t["cf"][:, m, :], -1.0, ALU.mult, t["cf"][:, m, 127:128], ALU.add)
                S.tt(t["incl"][:, :, :], t["excl"][:, :, :], t["sg"][:, :, :], ALU.add)
                incl = t["incl"]
                excl = t["excl"]
            S.act(t["gi"][:, :, :], incl[:, :, :], AF.Exp, scale=SC)
            S.act(t["ge"][:, :, :], excl[:, :, :], AF.Exp, scale=SC)
            S.act(t["ginv"][:, :, :], incl[:, :, :], AF.Exp, scale=-SC)
            S.act(GC[d][:, b, :], t["cf"][:, :, 127], AF.Exp, scale=SC)
            for m in range(4):
                S.ts(t["t1"][:, m, :], t["a"][:, m, :], pv(PK_KA, m), ALU.mult, pv(PK_OMKA, m), ALU.add, ek="pool")
            S.tt(t["kap"][:, :, :], k, t["t1"][:, :, :], ALU.mult, ek="pool")
            S.tt(t["beta"][:, :, :], t["kk"][:, :, :], t["a"][:, :, :], ALU.mult, ek="pool")
            fm = FM[d]
            S.tt(t["sq"][:, :, :], t["kk"][:, :, :], t["ge"][:, :, :], ALU.mult)
            S.act(fm[:, b, :, 0, :], t["sq"][:, :, :], AF.Copy, scale=-1.0)
            S.tt(fm[:, b, :, 1, :], r, t["gi"][:, :, :], ALU.mult)
            S.tt(fm[:, b, :, 2, :], t["beta"][:, :, :], t["ginv"][:, :, :], ALU.mult)
            S.tt(fm[:, b, :, 3, :], t["kap"][:, :, :], t["ginv"][:, :, :], ALU.mult)
            ck("B1d")
            for m in range(4):
                for xi, x in enumerate((0, 2, 3)):
                    S.tr(PT.c(xi * 128, (xi + 1) * 128), fm[:, b, m, x, :], identb[:, :])
                S.act(TOK[d][:, b, m, :, :], PT.v(PT.t[:, 0:384].rearrange("p (x c) -> p x c", c=128)),
                      AF.Copy)
            ck("B1e")
            for m in range(4):
                S.tr(PS[3].c(m * 128, (m + 1) * 128), U[:, 8 + m, :], identf[:, :])
            S.act(VM[d][:, b, :], PS[3].c(0, 512), AF.Copy)
            if latent:
                if d == 0:
                    vt = vtok[(q // 2) % 2]
                    S.copy(vt[:, :], PS[3].c(0, 512))
                    S.dma("pool", VT[b][t0:t0 + 128, :], vt[:, :])
                    S.act(sgd[:, :], U[:, 14, :], AF.Sigmoid)
                    S.mm(PS[4].c(0, 512), sgd[:, :], G2w[:, :])
                    gt = gtok[(q // 2) % 2]
                    S.act(gt[:, :], PS[4].c(0, 512), AF.Copy)
                    S.dma("pool", GT[b][t0:t0 + 128, :], gt[:, :])
                S.tt(t["t1"][:, :, :], r, t["kap"][:, :, :], ALU.mult, ek="pool")
                for m in range(4):
                    S.ts(t["t1"][:, m, :], t["t1"][:, m, :], pv(PK_RK, m), ALU.mult, ek="pool")
                for m in range(4):
                    S.mm(PS[5].c(m * 2, m * 2 + 2), t["t1"][:, m, :], cst["e2"][:, :])
                bt = bont[q % 2]
                S.copy(bt[:, :], PS[5].c(0, 8))
                S.dma("pool", BON[b][d][t0:t0 + 128, :], bt[:, :])

        def head_chain(d, b, h, bank):
            fm = FM[d]
            m, hl = divmod(h, 2)
            P = slice(hl * 64, hl * 64 + 64)
            sl = h
            fb, ppf, ttf, yf_, xf_ = FB[sl], PPf[sl], TTf[sl], Yf_[sl], Xf_[sl]
            S.mm(bank.c(0, 256), fm[P, b, m, 0, :], fm[P, b, m, 2:4, :])
            S.mm(bank.c(256, 512), fm[P, b, m, 2, :], fm[P, b, m, 0:2, :])
            yield
            S.tt(V(fb.ts, fb.t[:, 0:512].rearrange("p (x c) -> p x c", c=256)[:, :, 0:128]),
                 bank.v(bank.t[:, 0:256].rearrange("p (x c) -> p x c", c=128)), mk2[:, d, 0:2, :], ALU.mult)
            S.tt(fb[:, 128:256], bank.c(256, 384), mk2[:, d, 2, :], ALU.mult)
            S.tt(fb[:, 448:576], bank.c(256, 384), mk2[:, d, 4, :], ALU.mult)
            S.tt(ARB[sl][:, :], bank.c(384, 512), mk2[:, d, 3, :], ALU.mult)
            S.copy(fb[:, 384:448], TOK[d][:, b, m, 0, hl * 64:(hl + 1) * 64], ek="pool")
            S.tt(ttf[:, :], fb[:, 128:256], identf[:, :], ALU.add, ek="pool")
            yield
            Pn, Pt_ = fb[:, 0:128], fb[:, 128:256]
            S.mm(bank.c(0, 128), Pt_, Pn)
            S.mm(bank.c(128, 256), Pn, Pt_)
            S.mm(bank.c(384, 512), fm[P, b, m, 3, :], fm[P, b, m, 1, :])
            yield
            S.act(ppf[:, :], bank.c(0, 256), AF.Copy)
            S.tt(ARK[sl][:, :], bank.c(384, 512), mk2[:, d, 3, :], ALU.mult)
            yield
            for lev in range(1, 6):
                Pn, Pt_ = ppf[:, 0:128], ppf[:, 128:256]
                S.mm(bank.c(256, 384), Pn, ttf[:, :])
                if lev < 5:
                    S.mm(bank.c(0, 128), Pt_, Pn)
                    S.mm(bank.c(128, 256), Pn, Pt_)
                yield
                S.tt(ttf[:, :], bank.c(256, 384), ttf[:, :], ALU.add)
                if lev < 5:
                    S.act(ppf[:, :], bank.c(0, 256), AF.Copy)
                yield
            S.mm(bank.c(0, 192), ttf[:, :], fb[:, 256:448])
            yield
            S.act(yf_[:, :], bank.c(0, 192), AF.Copy)
            yield
            S.mm(bank.c(256, 448), fb[:, 448:576], yf_[:, :])
            yield
            S.copy(xf_[:, :], bank.c(256, 448))
            yield
            S.mm(bank.c(0, 192), ttf[:, :], xf_[:, :])
            yield
            S.tt(Gb[sl][:, :], bank.c(0, 128), yf_[:, 0:128], ALU.add)
            S.tt(WP[sl // 2][:, hl * 64:(hl + 1) * 64], bank.c(128, 192), yf_[:, 128:192], ALU.add)
            yield

        def chunk_math(d, b, latent, t0):
            ck("B1")
            fm = FM[d]
            NCH = 6
            pending = list(range(8))
            active = []
            free_banks = [PS[i] for i in range(NCH)]
            while pending or active:
                while pending and len(active) < NCH:
                    h = pending.pop(0)
                    bk = free_banks.pop(0)
                    active.append((head_chain(d, b, h, bk), bk))
                nxt = []
                for g, bk in active:
                    try:
                        next(g)
                        nxt.append((g, bk))
                    except StopIteration:
                        free_banks.append(bk)
                active = nxt
            for h in range(8):
                S.tr(PT.c(384 + (h % 4) * 128, 512 + (h % 4) * 128), Gb[h][:, :], identb[:, :])
                if h % 4 == 3:
                    for hh in range(h - 3, h + 1):
                        q_ = hh % 4
                        if hh % 2 == 0:
                            S.act(GTs[hh][:, :], PT.c(384 + q_ * 128, 512 + q_ * 128), AF.Copy)
                        else:
                            S.copy(GTs[hh][:, :], PT.c(384 + q_ * 128, 512 + q_ * 128))
            for pr in range(4):
                S.tr(PT.c(pr * 128, (pr + 1) * 128), WP[pr][:, :], identb[:, :])
            for pr in range(4):
                S.copy(WTs[pr][:, :], PT.c(pr * 128, (pr + 1) * 128))
            ck("B4")
            psU = PS[5].c(0, 512)
            for h in range(8):
                m, hl = divmod(h, 2)
                P = slice(hl * 64, hl * 64 + 64)
                sl = h
                o = PS[5].c(h * 64, (h + 1) * 64)
                S.mm(o, WTs[sl // 2][P, :], Hb[d][P, b, m, :], start=True, stop=False)
                S.mm(o, GTs[sl][:, :], VM[d][:, b, h * 64:(h + 1) * 64], start=False, stop=True)
            S.act(Ub[:, b, :], psU, AF.Copy)
            for h in range(8):
                m, hl = divmod(h, 2)
                P = slice(hl * 64, hl * 64 + 64)
                sl = h
                if latent:
                    o = PS[6].c(h * 64, (h + 1) * 64)
                    S.mm(o, fm[P, b, m, 1, :], Hb[d][P, b, m, :], start=True, stop=False)
                    S.mm(o, ARB[sl][:, :], Ub[:, b, h * 64:(h + 1) * 64], start=False, stop=False)
                    S.mm(o, ARK[sl][:, :], VM[d][:, b, h * 64:(h + 1) * 64], start=False, stop=True)
                o = PS[0].v(PS[0].t[P, (b * 4 + m) * 64:(b * 4 + m + 1) * 64])
                S.mm(o, TOK[d][:, b, m, 1, hl * 64:(hl + 1) * 64], Ub[:, b, h * 64:(h + 1) * 64], start=True, stop=False)
                S.mm(o, TOK[d][:, b, m, 2, hl * 64:(hl + 1) * 64], VM[d][:, b, h * 64:(h + 1) * 64], start=False, stop=True)
            ck("B5")
            if latent:
                ys = Ysb[(b + d) % 2]
                S.act(ys[:, :], PS[6].c(0, 512), AF.Copy)
                S.dma("pool", YD[b][d][t0:t0 + 128, :], ys[:, :])
            cs_ = slice(b * 256, (b + 1) * 256)
            S.tt(htmp[:, cs_], PS[0].v(PS[0].t[:, cs_]), Hf[d][:, cs_], ALU.add)
            S.tt(V(Hf[d].ts, Hf[d].t[:, cs_].rearrange("p (m v) -> p m v", v=64)),
                 V(htmp.ts, htmp.t[:, cs_].rearrange("p (m v) -> p m v", v=64)),
                 V(GC[d].ts, GC[d].t[:, b, :].unsqueeze(2).broadcast_to([128, 4, 64])), ALU.mult)
            S.copy(Hb[d][:, b, :, :], V(Hf[d].ts, Hf[d].t[:, cs_].rearrange("p (m v) -> p m v", v=64)), ek="pool")
            ck("B6")

        nsteps = 2 + TL // 128
        for s_ in range(nsteps):
            for d in range(2):
                for b in range(NB):
                    if s_ < 2:
                        ci = s_ if d == 0 else 1 - s_
                        prep(d, b, U_ctx[b], ci * 128, False)
                        chunk_math(d, b, False, ci * 128)
                    else:
                        li = s_ - 2
                        ci = li if d == 0 else (TL // 128 - 1 - li)
                        prep(d, b, U_lat[b], ci * 128, True)
                        chunk_math(d, b, True, ci * 128)
        barrier(S)
        B.close()

    phase_B()
    if stop_after == "B":
        return nc, S

    def phase_C(b):
        C = Scope(nc)
        load_const(C, "c32")
        lnx = C.sb("lnx", [128, 1024], F32)
        S.dma("sp", lnx[:, :], dr["lnx_row"].v(dr["lnx_row"].t.ap().partition_broadcast(128)))
        stg = [C.sb(f"stgC{i}", [128, 8, 256], F32) for i in range(2)]
        fT = C.sb("fT", [128, 4, TL], BF16)
        Wupf = C.sb("Wupf", [128, 4, D], BF16)
        Wupr = C.sb("Wupr", [128, 4, D], BF16)
        Wout = C.sb("Wout", [128, 8, D], BF16)
        load_weight_bf16(C, Wupf, dr["w_up_f"], 4, 0, D, stg)
        load_weight_bf16(C, Wupr, dr["w_up_r"], 4, 0, D, stg)
        load_weight_bf16(C, Wout, dr["w_out"], 8, 0, D, stg)
        ga1b = C.sb("ga1b", [128, D], F32)
        S.dma("sp", ga1b[:, :], MODROW.v(MODROW.t.ap()[b:b + 1, 0, :].partition_broadcast(128)))
        dd = [C.sb(f"dd{i}", [32, 2, 4, 512], F32) for i in range(2)]
        ddv = DD[b].t.ap().rearrange("a (r c) k -> r a c k", c=64)
        c32 = cst["c32"]
        for cb in range(16):
            dt_ = dd[cb % 2]
            S.dma("sp", dt_[:, :, :, :], DD[b].v(ddv[:, :, cb * 4:(cb + 1) * 4, :]))
            bank = PS[cb % 2]
            for cl in range(4):
                for g in range(4):
                    o = bank.c((g * 4 + cl) * 32, (g * 4 + cl) * 32 + 32)
                    S.mm(o, dt_[:, 0, cl, g * 128:(g + 1) * 128], c32[:, 0, :], start=True, stop=False)
                    S.mm(o, dt_[:, 1, cl, g * 128:(g + 1) * 128], c32[:, 1, :], start=False, stop=True)
            for g in range(4):
                src = bank.v(bank.t[:, g * 128:(g + 1) * 128].rearrange("p (c r) -> p r c", r=32))
                dst = V(fT.ts, fT.t[:, g, :].rearrange("p (r c) -> p r c", c=64)[:, :, cb * 4:(cb + 1) * 4])
                if g % 2 == 0:
                    S.act(dst, src, AF.Copy)
                else:
                    S.copy(dst, src)
        yf = [C.sb(f"yf{i}", [128, 512], F32) for i in range(2)]
        yb = [C.sb(f"yb{i}", [128, 512], F32) for i in range(2)]
        vt = [C.sb(f"vt{i}", [128, 512], F32) for i in range(2)]
        gt = [C.sb(f"gt{i}", [128, 512], F32) for i in range(2)]
        bo = [C.sb(f"bo{i}", [128, 2, 8], F32) for i in range(2)]
        ysq = C.sb("ysq", [128, 512], F32)
        stt_ = [C.sb(f"stC{i}", [128, 6, 8], F32) for i in range(2)]
        obf = C.sb("obf", [128, 512], BF16)
        oT = C.sb("oT", [128, 4, 512], BF16)
        gsb = [C.sb(f"gsb{i}", [128, 2, 512], F32) for i in range(2)]
        mT = C.sb("mT", [128, 8, 512], BF16, nsplit=8)
        mtmp = [C.sb(f"mtmp{i}", [128, 512], F32) for i in range(2)]
        xin = [C.sb(f"xin{i}", [128, D], F32) for i in range(2)]
        xo = [C.sb(f"xo{i}", [128, D], F32) for i in range(2)]

        def b3(v_, n):
            return V(v_.ts, v_.ap.unsqueeze(2).broadcast_to([128, 8, n]))

        def v3(buf):
            return V(buf.ts, buf.t[:, :].rearrange("p (h v) -> p h v", v=64))

        for tt in range(TL // 512):
            for sub in range(4):
                i = tt * 4 + sub
                t0 = i * 128
                y_, yb_, v_, g_, bo_, st_ = yf[i % 2], yb[i % 2], vt[i % 2], gt[i % 2], bo[i % 2], stt_[i % 2]
                S.dma("sp", y_[:, :], YD[b][0][t0:t0 + 128, :])
                S.dma("sp", yb_[:, :], YD[b][1][t0:t0 + 128, :])
                S.dma("sp", v_[:, :], VT[b][t0:t0 + 128, :])
                S.dma("sp", g_[:, :], GT[b][t0:t0 + 128, :])
                S.dma("sp", bo_[:, 0, :], BON[b][0][t0:t0 + 128, :])
                S.dma("sp", bo_[:, 1, :], BON[b][1][t0:t0 + 128, :])
                S.tt(y_[:, :], y_[:, :], yb_[:, :], ALU.add)
                S.reduce(st_[:, 0, :], v3(y_), ALU.add, AX.X)
                S.tt(ysq[:, :], y_[:, :], y_[:, :], ALU.mult, ek="pool")
                S.reduce(st_[:, 1, :], v3(ysq), ALU.add, AX.X)
                S.ts(st_[:, 2, :], st_[:, 0, :], 1.0 / 64, ALU.mult)
                S.tt(st_[:, 3, :], st_[:, 2, :], st_[:, 2, :], ALU.mult)
                S.stt(st_[:, 3, :], st_[:, 1, :], 1.0 / 64, st_[:, 3, :], ALU.mult, ALU.subtract)
                S.act(st_[:, 4, :], st_[:, 3, :], AF.Sqrt, bias=eps_t[:, 1:2])
                S.op("dve", lambda e: e.reciprocal(st_.t[:, 5, :], st_.t[:, 4, :]), [st_[:, 5, :]], [st_[:, 4, :]])
                S.tt(v3(y_), v3(y_), b3(st_[:, 2, :], 64), ALU.subtract)
                S.tt(v3(y_), v3(y_), b3(st_[:, 5, :], 64), ALU.mult)
                S.tt(y_[:, :], y_[:, :], lnx[:, 0:512], ALU.mult)
                S.tt(y_[:, :], y_[:, :], lnx[:, 512:1024], ALU.add, ek="pool")
                S.tt(bo_[:, 0, :], bo_[:, 0, :], bo_[:, 1, :], ALU.add, ek="pool")
                S.tt(v3(v_), v3(v_), b3(bo_[:, 0, :], 64), ALU.mult)
                S.tt(y_[:, :], y_[:, :], v_[:, :], ALU.add, ek="pool")
                S.tt(obf[:, :], y_[:, :], g_[:, :], ALU.mult)
                for kc in range(4):
                    S.tr(PT.c(kc * 128, (kc + 1) * 128), obf[:, kc * 128:(kc + 1) * 128], identb[:, :])
                S.act(oT[:, :, sub * 128:(sub + 1) * 128],
                      PT.v(PT.t[:, 0:512].rearrange("p (k t) -> p k t", t=128)), AF.Copy)
            T0 = tt * 512
            for n in range(8):
                gs = gsb[n % 2]
                S.dma("sp", gs[:, 0, :], SG[b][n, :, T0:T0 + 512])
                S.dma("sp", gs[:, 1, :], SG[b][8 + n, :, T0:T0 + 512])
                pf = PS[2].c(0, 512)
                pr = PS[3].c(0, 512)
                for kc in range(4):
                    S.mm(pf, Wupf[:, kc, n * 128:(n + 1) * 128], fT[:, kc, T0:T0 + 512], start=(kc == 0), stop=(kc == 3))
                for kc in range(4):
                    S.mm(pr, Wupr[:, kc, n * 128:(n + 1) * 128], oT[:, kc, :], start=(kc == 0), stop=(kc == 3))
                mt = mtmp[n % 2]
                S.tt(mt[:, :], pf, gs[:, 0, :], ALU.mult)
                S.tt(gs[:, 1, :], pr, gs[:, 1, :], ALU.mult)
                S.tt(mT[:, n, :], mt[:, :], gs[:, 1, :], ALU.add, ek="pool")
            for sub in range(4):
                i = tt * 4 + sub
                t0 = i * 128
                xi, xo_ = xin[i % 2], xo[i % 2]
                S.dma("sp", xi[:, :], dr["x"].v(dr["x"].t[b, t0:t0 + 128, :]))
                for half in range(2):
                    o = PS[4 + half].c(0, 512)
                    for n in range(8):
                        S.mm(o, mT[:, n, sub * 128:(sub + 1) * 128], Wout[:, n, half * 512:(half + 1) * 512],
                             start=(n == 0), stop=(n == 7))
                    hs = slice(half * 512, (half + 1) * 512)
                    S.tt(xo_[:, hs], o, ga1b[:, hs], ALU.mult)
                    S.tt(xo_[:, hs], xo_[:, hs], xi[:, hs], ALU.add, ek="pool")
                S.dma("pool", X1[b][t0:t0 + 128, :], xo_[:, :])
        barrier(S)
        C.close()

    for b in range(NB):
        phase_C(b)
    if stop_after == "C":
        return nc, S

    def phase_D():
        Dd = Scope(nc)
        fng = Dd.sb("fng", [128, 1024], F32)
        S.dma("sp", fng[:, :], dr["fng_row"].v(dr["fng_row"].t.ap().partition_broadcast(128)))
        stg = [Dd.sb(f"stgD{i}", [128, 1024], F32) for i in range(2)]
        Wgu = Dd.sb("Wgu", [128, 8, 2 * DFF], BF16)
        Wdn = Dd.sb("Wdn", [128, 22, D], BF16)
        vgu = dr["w_gu"].t.ap().rearrange("(k p) n -> p k n", p=128)
        ci = 0
        for k in range(8):
            for c0_ in range(0, 2 * DFF, 1024):
                w = min(1024, 2 * DFF - c0_)
                sg_ = stg[ci % 2]
                S.dma("sp" if ci % 2 else "pool", sg_[:, 0:w], dr["w_gu"].v(vgu[:, k, c0_:c0_ + w]))
                S.copy(Wgu[:, k, c0_:c0_ + w], sg_[:, 0:w], ek="pool")
                ci += 1
        vdn = dr["w_down"].t.ap().rearrange("(k p) n -> p k n", p=128)
        for k in range(22):
            sg_ = stg[ci % 2]
            S.dma("sp" if ci % 2 else "pool", sg_[:, :], dr["w_down"].v(vdn[:, k, :]))
            S.copy(Wdn[:, k, :], sg_[:, :], ek="pool")
            ci += 1
        TD = 256
        hT2 = Dd.sb("hT2", [128, 8, TD], BF16)
        ntb = nt_bufs(Dd, "D", nbuf=1)
        actT = Dd.sb("actT", [128, 22, TD], BF16, nsplit=22)
        sil = [Dd.sb(f"sil{i}", [128, TD], F32) for i in range(2)]
        ga2b = Dd.sb("ga2b", [128, D], F32)
        x1t = [Dd.sb(f"x1t{i}", [128, D], F32) for i in range(1)]
        x2t = [Dd.sb(f"x2t{i}", [128, D], F32) for i in range(2)]
        junk2 = ntb[2]
        stf = [Dd.sb(f"stf{i}", [128, 4], F32) for i in range(2)]
        for b in range(NB):
            S.dma("sp", ga2b[:, :], MODROW.v(MODROW.t.ap()[b:b + 1, 1, :].partition_broadcast(128)))
            for tt in range(TL // TD):
                T0 = tt * TD
                x1v = X1[b].t
                norm_transpose(ntb, X1[b], lambda i: x1v[T0 + i * 128:T0 + (i + 1) * 128, :], TD // 128, A2,
                               lambda k: modT[:, 24 + k, b:b + 1], b, hT2, 0)
                for fc in range(22):
                    pg = PS[fc % 2].c(0, TD)
                    pu = PS[2 + fc % 2].c(0, TD)
                    for k in range(8):
                        S.mm(pg, Wgu[:, k, fc * 128:(fc + 1) * 128], hT2[:, k, :], start=(k == 0), stop=(k == 7))
                    for k in range(8):
                        S.mm(pu, Wgu[:, k, DFF + fc * 128:DFF + (fc + 1) * 128], hT2[:, k, :], start=(k == 0),
                             stop=(k == 7))
                    sl_ = sil[fc % 2]
                    S.act(sl_[:, :], pg, AF.Silu)
                    S.tt(actT[:, fc, :], pu, sl_[:, :], ALU.mult)
                for sub in range(TD // 128):
                    i = tt * (TD // 128) + sub
                    t0 = T0 + sub * 128
                    x1_, x2_, sf = x1t[0], x2t[i % 2], stf[i % 2]
                    S.dma("sp", x1_[:, :], X1[b][t0:t0 + 128, :])
                    for half in range(2):
                        o = PS[4 + half].c(0, 512)
                        for fc in range(22):
                            S.mm(o, actT[:, fc, sub * 128:(sub + 1) * 128], Wdn[:, fc, half * 512:(half + 1) * 512],
                                 start=(fc == 0), stop=(fc == 21))
                        hs = slice(half * 512, (half + 1) * 512)
                        S.tt(x2_[:, hs], o, ga2b[:, hs], ALU.mult)
                        S.tt(x2_[:, hs], x2_[:, hs], x1_[:, hs], ALU.add, ek="pool")
                    S.act(junk2[:, :], x2_[:, :], AF.Square, accum=sf[:, 0:1])
                    S.act(sf[:, 1:2], sf[:, 0:1], AF.Sqrt, scale=1.0 / D, bias=eps_t[:, 0:1])
                    S.op("dve", lambda e: e.reciprocal(sf.t[:, 2:3], sf.t[:, 1:2]), [sf[:, 2:3]], [sf[:, 1:2]])
                    S.act(x2_[:, :], x2_[:, :], AF.Copy, scale=sf[:, 2:3])
                    S.tt(x2_[:, :], x2_[:, :], fng[:, :], ALU.mult)
                    S.dma("pool", out[b, t0:t0 + 128, :], x2_[:, :])
        barrier(S)
        Dd.close()

    phase_D()
    barrier(S)
    return nc, S

from concourse.bass_utils import run_bass_kernel_spmd

N_CORES = 8
_CACHE = {}


def _fm(vec, nchunk):
    return np.ascontiguousarray(np.asarray(vec, np.float32).reshape(nchunk, 128).T)


def make_in_maps(inp, cores):
    consts = host_consts()
    f = lambda a: np.ascontiguousarray(np.asarray(a, np.float32))
    shared = {
        "b_adaT": _fm(inp["b_ada"][0], 48), "b_ada_row": f(inp["b_ada"][0]).reshape(1, 6144),
        "n1g": _fm(inp["norm1_g"][0], 8), "n2g": _fm(inp["norm2_g"][0], 8),
        "fng_row": f(inp["final_norm_g"]).reshape(1, D),
        "w_ada": f(inp["w_ada"][0]), "w_in": f(inp["w_in"][0]),
        "w2": np.concatenate([f(inp["w2_f"][0]), f(inp["w2_b"][0])], 0),
        "a2": np.concatenate([f(inp["a2_f"][0]), f(inp["a2_b"][0])], 0),
        "g2": f(inp["g2"][0]),
        "lnx_row": np.concatenate([f(inp["lnx_g"][0]), f(inp["lnx_b"][0])]).reshape(1, 1024),
        "w_up_r": f(inp["w_up_r"][0]), "w_up_f": f(inp["w_up_f"][0]), "w_out": f(inp["w_out"][0]),
        "w_gu": f(inp["w_gu"][0]), "w_down": f(inp["w_down"][0]),
    }
    mu3 = np.zeros((128, 3, 15), np.float32)
    mu3[:, 0, :] = _fm(inp["mu_prev"][0], 15)
    mu3[:, 1, :] = _fm(inp["mu_next"][0], 15)
    shared["mu3"] = mu3
    pvec = np.zeros((128, 9, 4), np.float32)
    for i, nm in ((0, "k_k"), (1, "k_a"), (3, "r_k"), (4, "w0_f"), (5, "w0_b"), (6, "a0_f"), (7, "a0_b")):
        pvec[:, i, :] = _fm(np.asarray(inp[nm][0]).reshape(512), 4)
    shared["pvec"] = pvec
    shared.update(consts)
    maps = []
    for c in cores:
        m = dict(shared)
        m["x"] = f(inp["x"][NB * c:NB * (c + 1)])
        m["ctx"] = f(inp["ctx"][NB * c:NB * (c + 1)])
        cT = np.zeros((128, 8, 3), np.float32)
        for j in range(NB):
            cT[:, :, j] = _fm(inp["c"][NB * c + j], 8)
        cT[:, :, 2] = _fm(inp["c_ctx"], 8)
        m["cT"] = cT
        maps.append(m)
    return maps


def kernel(**inputs):
    if "nc" not in _CACHE:
        _CACHE["nc"] = build()[0]
    nc = _CACHE["nc"]
    maps = make_in_maps(inputs, list(range(N_CORES)))
    res = run_bass_kernel_spmd(nc, maps, core_ids=list(range(N_CORES)))
    outs = [np.asarray(res.results[c]["out"], np.float32) for c in range(N_CORES)]
    return np.concatenate(outs, axis=0)
```

```python
import numpy as np
import concourse.bass as bass
import concourse.mybir as mybir

F32 = mybir.dt.float32
BF16 = mybir.dt.bfloat16
I32 = mybir.dt.int32
AF = mybir.ActivationFunctionType
ALU = mybir.AluOpType
AX = mybir.AxisListType

SEM_LIMIT = 30000


class T:
    __slots__ = ("name", "w", "r")

    def __init__(self, name=""):
        self.name = name
        self.w = None
        self.r = {}


class V:
    __slots__ = ("ts", "ap", "x")

    def __init__(self, ts, ap, x=False):
        self.ts = ts
        self.ap = ap
        self.x = x


class Buf:
    def __init__(self, tensor, name, nsplit=None, sdim=1):
        self.t = tensor
        self.name = name
        self.nsplit = nsplit
        self.sdim = sdim
        if nsplit is None:
            self.ts = [T(name)]
        else:
            self.ts = [T(f"{name}{i}") for i in range(nsplit)]

    def __getitem__(self, idx):
        ap = self.t[idx]
        if self.nsplit is None:
            return V(self.ts, ap)
        if not isinstance(idx, tuple):
            idx = (idx,)
        if len(idx) <= self.sdim:
            return V(self.ts, ap)
        s = idx[self.sdim]
        if isinstance(s, int):
            return V([self.ts[s]], ap)
        if isinstance(s, slice):
            st, sp, _ = s.indices(self.nsplit)
            return V(self.ts[st:sp], ap)
        return V(self.ts, ap)

    def v(self, ap, which=None):
        if which is None:
            return V(self.ts, ap)
        return V([self.ts[i] for i in which], ap)


class Sync:
    def __init__(self, nc, ndma=24):
        self.nc = nc
        self.engs = {"pe": nc.tensor, "act": nc.scalar, "dve": nc.vector,
                     "pool": nc.gpsimd, "sp": nc.sync}
        self.semh = {}
        self.cur = {}
        self.cnt = {}
        self.nsem = 0
        for k in self.engs:
            self._new_sem(k)
        self.seen = {k: {} for k in self.engs}
        self.ndma = ndma
        self.dma_key = []
        self.dma_val = []
        for i in range(ndma):
            key = self._alloc(f"dma{i}")
            self.dma_key.append(key)
            self.dma_val.append(0)
        self.dma_rr = 0
        self.ninst = {k: 0 for k in self.engs}

    def _alloc(self, name):
        key = f"{name}_{self.nsem}"
        self.nsem += 1
        self.semh[key] = self.nc.alloc_semaphore(name=key)
        return key

    def _new_sem(self, ek):
        self.cur[ek] = self._alloc(f"e_{ek}")
        self.cnt[ek] = 0

    def _deps(self, outs, ins):
        deps = {}

        def add(ev):
            if ev is None:
                return
            k, val = ev
            if deps.get(k, 0) < val:
                deps[k] = val
        for v in ins:
            for t in v.ts:
                add(t.w)
        for v in outs:
            for t in v.ts:
                add(t.w)
                for k, val in t.r.items():
                    add((k, val))
        return deps

    def _wait(self, ek, deps):
        eng = self.engs[ek]
        seen = self.seen[ek]
        for k, val in deps.items():
            if ek == "pe" and k.startswith("e_pe"):
                continue
            if seen.get(k, 0) < val:
                eng.wait_ge(self.semh[k], val)
                seen[k] = val
                self.ninst[ek] += 1

    def _mark(self, ev, outs, ins):
        k, val = ev
        for v in ins:
            for t in v.ts:
                if t.r.get(k, 0) < val:
                    t.r[k] = val
        for v in outs:
            for t in v.ts:
                t.w = ev
                t.r = {}

    def op(self, ek, fn, outs, ins):
        outs = list(outs) + [v for v in ins if v.x]
        self._wait(ek, self._deps(outs, ins))
        if self.cnt[ek] >= SEM_LIMIT:
            self._new_sem(ek)
        inst = fn(self.engs[ek])
        self.cnt[ek] += 1
        inst.then_inc(self.semh[self.cur[ek]], 1)
        self.ninst[ek] += 1
        self._mark((self.cur[ek], self.cnt[ek]), outs, ins)

    def dma(self, ek, out, in_, **kw):
        i = self.dma_rr
        self.dma_rr = (self.dma_rr + 1) % self.ndma
        if self.dma_val[i] >= SEM_LIMIT:
            self.dma_key[i] = self._alloc(f"dma{i}")
            self.dma_val[i] = 0
        deps = self._deps([out], [in_])
        key = self.dma_key[i]
        if self.dma_val[i] > 0:
            if deps.get(key, 0) < self.dma_val[i]:
                deps[key] = self.dma_val[i]
        self._wait(ek, deps)
        inst = self.engs[ek].dma_start(out=out.ap, in_=in_.ap, **kw)
        self.dma_val[i] += 16
        inst.then_inc(self.semh[key], 16)
        self.ninst[ek] += 1
        self._mark((key, self.dma_val[i]), [out], [in_])

    def wait_all(self, ek, views):
        self._wait(ek, self._deps([], views))

    def mm(self, out, lhsT, rhs, start=True, stop=True, **kw):
        self.op("pe", lambda e: e.matmul(out.ap, lhsT.ap, rhs.ap, start=start, stop=stop, **kw),
                [out], [lhsT, rhs] + ([] if start else [out]))

    def tr(self, out, in_, ident):
        self.op("pe", lambda e: e.transpose(out.ap, in_.ap, ident.ap), [out], [in_, ident])

    def act(self, out, in_, func, bias=None, scale=1.0, ek="act", accum=None):
        ins = [in_]
        kw = {}
        if bias is not None:
            if isinstance(bias, V):
                ins.append(bias)
                kw["bias"] = bias.ap
            else:
                kw["bias"] = bias
        if isinstance(scale, V):
            ins.append(scale)
            kw["scale"] = scale.ap
        else:
            kw["scale"] = scale
        outs = [out]
        if accum is not None:
            outs.append(accum)
            kw["accum_out"] = accum.ap
        self.op(ek, lambda e: e.activation(out.ap, in_.ap, func, **kw), outs, ins)

    def tt(self, out, a, b, op, ek="dve"):
        self.op(ek, lambda e: e.tensor_tensor(out.ap, a.ap, b.ap, op), [out], [a, b])

    def ts(self, out, a, s1, op0, s2=None, op1=None, ek="dve"):
        ins = [a]
        x1 = s1
        if isinstance(s1, V):
            ins.append(s1)
            x1 = s1.ap
        x2 = s2
        if isinstance(s2, V):
            ins.append(s2)
            x2 = s2.ap
        if op1 is None:
            self.op(ek, lambda e: e.tensor_scalar(out.ap, a.ap, x1, None, op0), [out], ins)
        else:
            self.op(ek, lambda e: e.tensor_scalar(out.ap, a.ap, x1, x2, op0, op1), [out], ins)

    def stt(self, out, a, s, b, op0, op1):
        ins = [a, b]
        x = s
        if isinstance(s, V):
            ins.append(s)
            x = s.ap
        self.op("dve", lambda e: e.scalar_tensor_tensor(out.ap, a.ap, x, b.ap, op0, op1), [out], ins)

    def copy(self, out, in_, ek="dve"):
        self.op(ek, lambda e: e.tensor_copy(out.ap, in_.ap), [out], [in_])

    def memset(self, out, val, ek="pool"):
        self.op(ek, lambda e: e.memset(out.ap, val), [out], [])

    def reduce(self, out, in_, op, axis, ek="dve"):
        self.op(ek, lambda e: e.tensor_reduce(out.ap, in_.ap, axis, op), [out], [in_])

from contextlib import ExitStack

NB = 2
D = 1024
TL = 2048
TC = 256
NIN = 4480
DFF = 2816
NORM_EPS = 1e-6
import os
EVAC_ACT_ONLY = bool(int(os.environ.get('EVAC_ACT_ONLY', '0')))
GN_EPS = 64e-5
SC = -float(np.exp(-0.5))


class Scope:
    uid = 0

    def __init__(self, nc):
        self.nc = nc
        self.es = ExitStack()
        self.n = 0

    def sb(self, name, shape, dt, nsplit=None, sdim=1):
        Scope.uid += 1
        t = self.es.enter_context(self.nc.sbuf_tensor(f"s{Scope.uid}_{name}", shape, dt))
        return Buf(t, name, nsplit, sdim)

    def close(self):
        self.es.close()


class PBank:
    def __init__(self, nc, name, dt, width):
        self.t = nc.alloc_psum_tensor(name, [128, width], dt)
        self.blk = width
        self.ts = [T(f"{name}{i}") for i in range(max(1, width // self.blk))]

    def c(self, a, b, rows=slice(None)):
        q0 = a // self.blk
        q1 = (b - 1) // self.blk
        return V(self.ts[q0:q1 + 1], self.t[rows, a:b], True)

    def v(self, ap):
        return V(self.ts, ap, True)


def barrier(S):
    for ek, eng in S.engs.items():
        deps = {}
        for k2 in S.engs:
            if S.cnt[k2] > 0:
                deps[S.cur[k2]] = S.cnt[k2]
        for i in range(S.ndma):
            if S.dma_val[i] > 0:
                deps[S.dma_key[i]] = S.dma_val[i]
        seen = S.seen[ek]
        for k, val in deps.items():
            if seen.get(k, 0) < val:
                eng.wait_ge(S.semh[k], val)
                seen[k] = val


def host_consts():
    c = {}
    i = np.arange(128)
    row = i[:, None]
    col = i[None, :]
    SL = (col < row).astype(np.float32)
    SU = (col > row).astype(np.float32)
    IL = (col <= row).astype(np.float32)
    IU = (col >= row).astype(np.float32)
    mk = np.zeros((128, 4, 256), np.float32)
    mk[:, 0, :128] = SL; mk[:, 0, 128:] = SL
    mk[:, 1, :128] = SU; mk[:, 1, 128:] = IU
    mk[:, 2, :128] = SU; mk[:, 2, 128:] = SU
    mk[:, 3, :128] = SL; mk[:, 3, 128:] = IL
    c["mkp"] = mk
    BDm = np.zeros((128, 128), np.float32)
    BDm[:64, :64] = 1.0
    BDm[64:, 64:] = 1.0
    mk2 = np.zeros((128, 2, 5, 128), np.float32)
    mk2[:, 0, 0] = SL * BDm; mk2[:, 0, 1] = SL; mk2[:, 0, 2] = SU * BDm; mk2[:, 0, 3] = IU; mk2[:, 0, 4] = SU * (1 - BDm)
    mk2[:, 1, 0] = SU * BDm; mk2[:, 1, 1] = SU; mk2[:, 1, 2] = SL * BDm; mk2[:, 1, 3] = IL; mk2[:, 1, 4] = SL * (1 - BDm)
    c["mk2"] = mk2
    c["identf"] = np.eye(128, dtype=np.float32)
    th = 2 * np.pi * np.outer(i, i) / 128.0
    cs = np.zeros((128, 256), np.float32)
    cs[:, :128] = np.cos(th) / np.sqrt(128.0)
    cs[:, 128:] = np.sin(th) / np.sqrt(128.0)
    c["cs128"] = cs
    j = np.arange(64)
    th64 = 2 * np.pi * np.outer(j, j) / 64.0
    C64 = np.cos(th64) / 8.0
    S64 = np.sin(th64) / 8.0
    bd = np.zeros((128, 3, 128), np.float32)
    for r in range(2):
        bd[r * 64:(r + 1) * 64, 0, r * 64:(r + 1) * 64] = C64
        bd[r * 64:(r + 1) * 64, 1, r * 64:(r + 1) * 64] = S64
        bd[r * 64:(r + 1) * 64, 2, r * 64:(r + 1) * 64] = -S64
    c["bd64"] = bd
    k32 = np.arange(32)
    th32 = 2 * np.pi * np.outer(k32, k32) / 32.0
    c32 = np.zeros((32, 2, 32), np.float32)
    c32[:, 0, :] = np.cos(th32) / np.sqrt(32.0)
    c32[:, 1, :] = -np.sin(th32) / np.sqrt(32.0)
    c["c32"] = c32
    ones_bd = np.zeros((128, 128), np.float32)
    ones_bd[:64, :64] = 1.0
    ones_bd[64:, 64:] = 1.0
    c["onesbd"] = ones_bd
    e2 = np.zeros((128, 2), np.float32)
    e2[:64, 0] = 1.0
    e2[64:, 1] = 1.0
    c["e2"] = e2
    c["ones"] = np.ones((128, 128), np.float32)
    return c


CONST_SHAPES = {"mk2": [128, 2, 5, 128], "mkp": [128, 4, 256], "identf": [128, 128], "cs128": [128, 256], "bd64": [128, 3, 128],
                "c32": [32, 2, 32], "onesbd": [128, 128], "e2": [128, 2], "ones": [128, 128]}

IN_SHAPES = {
    "x": [NB, TL, D], "ctx": [NB, TC, D], "cT": [128, 8, 3], "b_adaT": [128, 48], "b_ada_row": [1, 6144],
    "n1g": [128, 8], "n2g": [128, 8], "fng_row": [1, D],
    "w_ada": [D, 6144], "w_in": [D, NIN], "mu3": [128, 3, 15],
    "w2": [128, 512], "a2": [128, 512], "g2": [128, 512],
    "pvec": [128, 9, 4],
    "lnx_row": [1, 1024],
    "w_up_r": [512, D], "w_up_f": [512, D], "w_out": [D, D], "w_gu": [D, 2 * DFF], "w_down": [DFF, D],
}
IN_SHAPES.update(CONST_SHAPES)


class StopBuild(Exception):
    pass


def build(debug=(), stop_after=None):
    nc = bass.Bass("TRN2", target_bir_lowering=False)
    S = Sync(nc)
    try:
        return _build(nc, S, debug, stop_after)
    except StopBuild:
        barrier(S)
        return nc, S


def _build(nc, S, debug, stop_after):
    def ck(name):
        if stop_after == name:
            raise StopBuild()
    dr = {}
    for name, shp in IN_SHAPES.items():
        dr[name] = Buf(nc.dram_tensor(name, shp, F32, kind="ExternalInput"), name)
    out = Buf(nc.dram_tensor("out", [NB, TL, D], F32, kind="ExternalOutput"), "out", nsplit=NB, sdim=0)

    def scratch(name, shape, nsplit=None, sdim=0):
        kind = "ExternalOutput" if name in debug else "Internal"
        return Buf(nc.dram_tensor(name, shape, F32, kind=kind), name, nsplit, sdim)

    U_lat = [scratch(f"U_lat{b}", [15, 128, TL], 15, 0) for b in range(NB)]
    U_ctx = [scratch(f"U_ctx{b}", [15, 128, TC], 15, 0) for b in range(NB)]
    SG = [scratch(f"SG{b}", [16, 128, TL], 16, 0) for b in range(NB)]
    DD = [scratch(f"DD{b}", [2, TL, 512], 2, 0) for b in range(NB)]
    YD = [[scratch(f"YD{b}_{d}", [TL, 512]) for d in range(2)] for b in range(NB)]
    VT = [scratch(f"VT{b}", [TL, 512]) for b in range(NB)]
    GT = [scratch(f"GT{b}", [TL, 512]) for b in range(NB)]
    BON = [[scratch(f"BON{b}_{d}", [TL, 8]) for d in range(2)] for b in range(NB)]
    X1 = [scratch(f"X1_{b}", [TL, D]) for b in range(NB)]
    MODROW = scratch("MODROW", [3, 2, 1024])

    if os.environ.get("PTFIRST") == "1":
        PT = PBank(nc, "pst", BF16, 1024)
        PS = [PBank(nc, f"ps{i}", F32, 512) for i in range(7)]
    else:
        PS = [PBank(nc, f"ps{i}", F32, 512) for i in range(7)]
        PT = PBank(nc, "pst", BF16, 1024)

    G = Scope(nc)
    cst = {}

    def load_const(sc, name):
        shp = CONST_SHAPES[name]
        cst[name] = sc.sb("c_" + name, shp, F32)
        S.dma("sp", cst[name].v(cst[name].t[tuple(slice(None) for _ in shp)]),
              dr[name].v(dr[name].t[tuple(slice(None) for _ in shp)]))
        return cst[name]

    load_const(G, "identf")
    identf = cst["identf"]
    identb = G.sb("identb", [128, 128], BF16)
    S.copy(identb[:, :], identf[:, :])
    mu3 = G.sb("mu3", [128, 3, 15], F32)
    S.dma("sp", mu3[:, :, :], dr["mu3"][:, :, :])
    c0 = G.sb("c0", [128, 15], F32)
    S.tt(c0[:, :], mu3[:, 0, :], mu3[:, 1, :], ALU.add)
    S.ts(c0[:, :], c0[:, :], -1.0, ALU.mult, 1.0, ALU.add)
    pvec = G.sb("pvec", [128, 9, 4], F32)
    S.dma("sp", pvec[:, :, :], dr["pvec"][:, :, :])
    S.ts(pvec[:, 2, :], pvec[:, 1, :], -1.0, ALU.mult, 1.0, ALU.add)
    PK_KK, PK_KA, PK_OMKA, PK_RK, PK_W0, PK_A0 = 0, 1, 2, 3, 4, 6
    n1g = G.sb("n1g", [128, 8], F32)
    n2g = G.sb("n2g", [128, 8], F32)
    S.dma("sp", n1g[:, :], dr["n1g"][:, :])
    S.dma("sp", n2g[:, :], dr["n2g"][:, :])
    smallw = {}
    for name in ("w2", "a2", "g2"):
        smallw[name] = G.sb("bf_" + name, [128, 512], BF16)
    W2w, A2w, G2w = smallw["w2"], smallw["a2"], smallw["g2"]

    if stop_after == "G":
        barrier(S)
        return nc, S
    modT = G.sb("modT", [128, 48, 3], F32)
    A1 = G.sb("A1", [128, 8, 3], F32)
    A2 = G.sb("A2", [128, 8, 3], F32)
    M = Scope(nc)
    garow = M.sb("garow", [3, 2, 1024], F32)
    stg_small = [M.sb(f"stg_small{i}", [128, 512], F32) for i in range(3)]
    for i, name in enumerate(("w2", "a2", "g2")):
        S.dma("sp", stg_small[i][:, :], dr[name][:, :])
        S.copy(smallw[name][:, :], stg_small[i][:, :])
    cT = M.sb("cT", [128, 8, 3], F32)
    sT = M.sb("sT", [128, 8, 3], F32)
    badaT = M.sb("badaT", [128, 48], F32)
    brow = M.sb("brow", [3, 6144], F32)
    S.dma("sp", cT[:, :, :], dr["cT"][:, :, :])
    S.dma("sp", badaT[:, :], dr["b_adaT"][:, :])
    S.dma("sp", brow[:, :], dr["b_ada_row"].v(dr["b_ada_row"].t.ap().partition_broadcast(3)))
    S.act(sT[:, :, :], cT[:, :, :], AF.Silu)
    wa = [M.sb(f"wa{i}", [128, 8, 1024], F32) for i in range(2)]
    w_ada_v = dr["w_ada"].t.ap().rearrange("(k p) n -> p k n", p=128)
    for sec in range(6):
        wb = wa[sec % 2]
        S.dma("sp" if sec % 2 == 0 else "pool", wb[:, :, :], dr["w_ada"].v(w_ada_v[:, :, sec * 1024:(sec + 1) * 1024]))
        for nn in range(8):
            n = sec * 8 + nn
            o = PS[0].c(n * 3, n * 3 + 3)
            for k in range(8):
                S.mm(o, wb[:, k, nn * 128:(nn + 1) * 128], sT[:, k, :], start=(k == 0), stop=(k == 7))
        if sec in (2, 5):
            gi_ = 0 if sec == 2 else 1
            for half in range(2):
                o = PS[1].c(0, 512, rows=slice(0, 3))
                for k in range(8):
                    S.mm(o, sT[:, k, :], wb[:, k, half * 512:(half + 1) * 512], start=(k == 0), stop=(k == 7))
                S.tt(garow[:, gi_, half * 512:(half + 1) * 512], o,
                     brow[:, sec * 1024 + half * 512: sec * 1024 + (half + 1) * 512], ALU.add)
    S.tt(modT[:, :, :], PS[0].v(PS[0].t[:, 0:144].rearrange("p (n j) -> p n j", j=3)),
         V(badaT.ts, badaT.t[:, :].unsqueeze(2).broadcast_to([128, 48, 3])), ALU.add)
    S.dma("sp", MODROW[:, :, :], garow[:, :, :])
    S.ts(A1[:, :, :], modT[:, 8:16, :], 1.0, ALU.add)
    S.tt(A1[:, :, :], A1[:, :, :], V(n1g.ts, n1g.t[:, :].unsqueeze(2).broadcast_to([128, 8, 3])), ALU.mult)
    S.ts(A2[:, :, :], modT[:, 32:40, :], 1.0, ALU.add)
    S.tt(A2[:, :, :], A2[:, :, :], V(n2g.ts, n2g.t[:, :].unsqueeze(2).broadcast_to([128, 8, 3])), ALU.mult)
    barrier(S)
    M.close()
    if stop_after == "M":
        if "modT" in debug:
            dbgm = nc.dram_tensor("dbg_modT", [128, 48, 3], F32, kind="ExternalOutput")
            S.dma("sp", V([T("x")], dbgm[:, :, :]), modT[:, :, :])
        barrier(S)
        return nc, S

    def nt_bufs(sc, tag, nbuf=2):
        xt = [sc.sb(f"xt{tag}{i}", [128, 1024], F32) for i in range(nbuf)]
        xn = [sc.sb(f"xn{tag}{i}", [128, 1024], BF16) for i in range(nbuf)]
        junk = sc.sb(f"junk{tag}", [128, 1024], BF16)
        st = [sc.sb(f"st{tag}{i}", [128, 4], F32) for i in range(2)]
        return xt, xn, junk, st

    def norm_transpose(bufs, src_buf, src_rows_fn, ntiles, Aap, Bfn, j, hT, col0):
        xt, xn, junk, st = bufs
        for i in range(ntiles):
            if i == 1:
                ck("A0b")
            if i == 3:
                ck("A0c")
            x_t = xt[i % len(xt)]
            x_n = xn[i % len(xn)]
            s_ = st[i % 2]
            S.dma("sp", x_t[:, :], src_buf.v(src_rows_fn(i)))
            ck("A0a")
            S.act(junk[:, :], x_t[:, :], AF.Square, accum=s_[:, 0:1])
            S.act(s_[:, 1:2], s_[:, 0:1], AF.Sqrt, scale=1.0 / D, bias=eps_t[:, 0:1])
            S.op("dve", lambda e: e.reciprocal(s_.t[:, 2:3], s_.t[:, 1:2]), [s_[:, 2:3]], [s_[:, 1:2]])
            S.act(x_n[:, :], x_t[:, :], AF.Copy, scale=s_[:, 2:3])
            ck("A0a2")
            for k in range(8):
                S.tr(PT.c(k * 128, (k + 1) * 128), x_n[:, k * 128:(k + 1) * 128], identb[:, :])
            ck("A0a3")
            for k in range(8):
                o = hT[:, k, col0 + i * 128: col0 + (i + 1) * 128]
                if k == 1:
                    ck("A0a4")
                if os.environ.get("DBGV") == "1":
                    o = junk[:, 0:128]
                if os.environ.get("DBGV") == "2":
                    S.act(o, PT.c(k * 128, (k + 1) * 128), AF.Identity, scale=0.5, bias=eps_t[:, 0:1])
                    continue
                if os.environ.get("DBGV") == "3":
                    S.act(o, PT.c(k * 128, (k + 1) * 128), AF.Copy)
                    continue
                if k == 2:
                    ck("A0a5")
                if k % 2 == 0 and not EVAC_ACT_ONLY:
                    S.ts(o, PT.c(k * 128, (k + 1) * 128), Aap[:, k, j:j + 1], ALU.mult, Bfn(k), ALU.add)
                else:
                    S.act(o, PT.c(k * 128, (k + 1) * 128), AF.Identity, scale=Aap[:, k, j:j + 1],
                          bias=Bfn(k))

    eps_t = G.sb("eps_t", [128, 2], F32)
    S.memset(eps_t[:, 0:1], NORM_EPS)
    S.memset(eps_t[:, 1:2], GN_EPS)

    def load_weight_bf16(sc, dst, dram_buf, kchunks, c0_, c1_, stg, ei=[0]):
        v = dram_buf.t.ap().rearrange("(k p) n -> p k n", p=128)
        cc = c0_
        while cc < c1_:
            w = min(256, c1_ - cc)
            sg_ = stg[ei[0] % 2]
            ei[0] += 1
            S.dma("pool" if ei[0] % 2 else "sp", sg_[:, 0:kchunks, 0:w], dram_buf.v(v[:, :, cc:cc + w]))
            S.copy(dst[:, 0:kchunks, cc - c0_: cc - c0_ + w], sg_[:, 0:kchunks, 0:w], ek="pool")
            cc += w

    Bsh1 = modT

    def phase_A(b):
        A = Scope(nc)
        load_const(A, "cs128")
        load_const(A, "bd64")
        stg = [A.sb(f"stgA{i}", [128, 8, 256], F32) for i in range(2)]
        hT = A.sb("hT", [128, 8, TL + 2], BF16)
        hTc = A.sb("hTc", [128, 8, TC + 2], BF16)
        Win = A.sb("Win", [128, 8, 2432], BF16)
        S.memset(hT[:, :, 0:1], 0.0)
        S.memset(hT[:, :, TL + 1:TL + 2], 0.0)
        S.memset(hTc[:, :, 0:1], 0.0)
        S.memset(hTc[:, :, TC + 1:TC + 2], 0.0)
        xv = dr["x"].t
        cv = dr["ctx"].t
        ck("A0")
        ntb = nt_bufs(A, "A")
        norm_transpose(ntb, dr["x"], lambda i: xv[b, i * 128:(i + 1) * 128, :], TL // 128, A1,
                       lambda k: modT[:, k, b:b + 1], b, hT, 1)
        norm_transpose(ntb, dr["ctx"], lambda i: cv[b, i * 128:(i + 1) * 128, :], TC // 128, A1,
                       lambda k: modT[:, k, 2:3], 2, hTc, 1)
        ck("A1")
        pb = [A.sb(f"pb{i}", [128, 514], F32) for i in range(2)]
        ub = [A.sb(f"ub{i}", [128, 512], F32) for i in range(3)]
        uf = A.sb("uf", [128, 4, 512], F32, nsplit=4)
        Abuf = [A.sb(f"Abuf{i}", [128, 4, 256], F32) for i in range(2)]
        Dout = [A.sb(f"Dout{i}", [128, 2, 512], F32) for i in range(2)]
        cnt = [0]

        def gemm(hbuf, T_, TT, n_lo, n_hi, wofs, Udst):
            for t0 in range(0, T_, TT):
                for n in range(n_lo, n_hi):
                    q = cnt[0]
                    cnt[0] += 1
                    bank = PS[q % 2]
                    o = bank.c(0, TT)
                    for k in range(8):
                        S.mm(o, Win[:, k, (n - wofs) * 128:(n - wofs + 1) * 128], hbuf[:, k, 1 + t0:1 + t0 + TT],
                             start=(k == 0), stop=(k == 7))
                    if n == 1:
                        ck("A2")
                    if n == 16:
                        ck("A3")
                    if n < 15:
                        oh = PS[2 + q % 2].c(0, 2)
                        for k in range(8):
                            S.mm(oh, Win[:, k, (n - wofs) * 128:(n - wofs + 1) * 128],
                                 hbuf[:, k, t0:t0 + TT + 2:TT + 1], start=(k == 0), stop=(k == 7))
                        p_ = pb[q % 2]
                        u_ = ub[q % 3]
                        S.act(p_[:, 1:TT + 1], o, AF.Copy)
                        S.copy(p_[:, 0:TT + 2:TT + 1], oh)
                        S.act(u_[:, 0:TT], p_[:, 1:TT + 1], AF.Copy, scale=c0[:, n:n + 1])
                        S.stt(u_[:, 0:TT], p_[:, 0:TT], mu3[:, 0, n:n + 1], u_[:, 0:TT], ALU.mult, ALU.add)
                        S.stt(u_[:, 0:TT], p_[:, 2:TT + 2], mu3[:, 1, n:n + 1], u_[:, 0:TT], ALU.mult, ALU.add)
                        S.dma("pool", Udst[n, :, t0:t0 + TT], u_[:, 0:TT])
                    elif n < 19:
                        S.act(uf[:, n - 15, 0:TT], o, AF.Copy)
                    else:
                        u_ = ub[q % 3]
                        S.act(u_[:, 0:TT], o, AF.Sigmoid)
                        S.dma("pool", SG[b][n - 19, :, t0:t0 + TT], u_[:, 0:TT])
                if n_lo <= 15 and n_hi >= 19 and T_ == TL:
                    for ch in range(4):
                        ab = Abuf[ch % 2]
                        do = Dout[ch % 2]
                        for g in range(4):
                            bank = PS[4 + (g // 2)]
                            S.mm(bank.c((g % 2) * 256, (g % 2) * 256 + 256), uf[:, g, ch * 128:(ch + 1) * 128],
                                 cst["cs128"][:, :])
                        S.act(ab[:, 0:2, :], PS[4].v(PS[4].t[:, :].rearrange("p (g c) -> p g c", c=256)), AF.Copy)
                        S.copy(ab[:, 2:4, :], PS[5].v(PS[5].t[:, :].rearrange("p (g c) -> p g c", c=256)))
                        Ac = ab[:, :, 0:128]
                        As = ab[:, :, 128:256]
                        d1 = PS[6].c(0, 512)
                        S.mm(d1, cst["bd64"][:, 0, :], Ac, start=True, stop=False)
                        S.mm(d1, cst["bd64"][:, 2, :], As, start=False, stop=True)
                        S.act(do[:, 0, :], d1, AF.Copy)
                        d2 = PS[6].c(0, 512)
                        S.mm(d2, cst["bd64"][:, 0, :], As, start=True, stop=False)
                        S.mm(d2, cst["bd64"][:, 1, :], Ac, start=False, stop=True)
                        S.copy(do[:, 1, :], d2)
                        tt0 = t0 + ch * 128
                        S.dma("pool", DD[b].v(DD[b].t.ap()[:, tt0:tt0 + 128, :].rearrange("a t c -> t a c")),
                              do[:, :, :])

        load_weight_bf16(A, Win, dr["w_in"], 8, 0, 2432, stg)
        ck("A1b")
        gemm(hT, TL, 512, 0, 19, 0, U_lat[b])
        ck("A4")
        gemm(hTc, TC, 256, 0, 15, 0, U_ctx[b])
        load_weight_bf16(A, Win, dr["w_in"], 8, 2432, 4480, stg)
        gemm(hT, TL, 512, 19, 35, 19, None)
        barrier(S)
        A.close()

    for b in range(NB):
        phase_A(b)
    if stop_after == "A":
        barrier(S)
        return nc, S


    def pv(row, m):
        return pvec[:, row, m:m + 1]

    def phase_B():
        B = Scope(nc)
        mk2 = load_const(B, "mk2")
        load_const(B, "onesbd")
        load_const(B, "e2")
        load_const(B, "ones")
        uc = [B.sb(f"uc{i}", [128, 15, 128], F32) for i in range(2)]
        Hf = [B.sb(f"Hf{d}", [128, NB * 4 * 64], F32) for d in range(2)]
        Hb = [B.sb(f"Hb{d}", [128, NB, 4, 64], BF16) for d in range(2)]
        FM = [B.sb(f"FM{d}", [128, NB, 4, 4, 128], BF16, nsplit=NB, sdim=1) for d in range(2)]
        TOK = [B.sb(f"TOK{d}", [128, NB, 4, 3, 128], BF16, nsplit=NB, sdim=1) for d in range(2)]
        VM = [B.sb(f"VM{d}", [128, NB, 512], BF16, nsplit=NB, sdim=1) for d in range(2)]
        GC = [B.sb(f"GC{d}", [128, NB, 4], F32) for d in range(2)]
        tmp = {}
        for nm in ("kkraw", "sq", "rn", "kk", "a", "sg", "cf", "incl", "excl", "gi", "ge", "ginv", "kap",
                   "beta", "t1"):
            tmp[nm] = B.sb("t_" + nm, [128, 4, 128], F32)
        adb = B.sb("adb", [128, 128], BF16)
        twd = B.sb("twd", [128, 128], BF16)
        sgd = B.sb("sgd", [128, 128], BF16)
        vtok = [B.sb(f"vtok{i}", [128, 512], F32) for i in range(2)]
        gtok = [B.sb(f"gtok{i}", [128, 512], F32) for i in range(2)]
        bont = [B.sb(f"bont{i}", [128, 8], F32) for i in range(2)]
        NSLOT = 8
        FB = [B.sb(f"FB{i}", [128, 576], F32) for i in range(NSLOT)]
        PPf = [B.sb(f"PPf{i}", [128, 256], F32) for i in range(NSLOT)]
        TTf = [B.sb(f"TTf{i}", [128, 128], F32) for i in range(NSLOT)]
        Yf_ = [B.sb(f"Yf{i}", [128, 192], F32) for i in range(NSLOT)]
        Xf_ = [B.sb(f"Xf{i}", [128, 192], F32) for i in range(NSLOT)]
        ARB2 = [[B.sb(f"ARB{p_}_{i}", [128, 128], BF16) for i in range(NSLOT)] for p_ in range(2)]
        ARK2 = [[B.sb(f"ARK{p_}_{i}", [128, 128], BF16) for i in range(NSLOT)] for p_ in range(2)]
        Gb2 = [[B.sb(f"Gb{p_}_{i}", [128, 128], BF16) for i in range(NSLOT)] for p_ in range(2)]
        GTs2 = [[B.sb(f"GTs{p_}_{i}", [128, 128], BF16) for i in range(NSLOT)] for p_ in range(2)]
        WP2 = [[B.sb(f"WP{p_}_{i}", [128, 128], BF16) for i in range(NSLOT // 2)] for p_ in range(2)]
        WTs2 = [[B.sb(f"WTs{p_}_{i}", [128, 128], BF16) for i in range(NSLOT // 2)] for p_ in range(2)]
        Ub = B.sb("Ub", [128, NB, 512], BF16, nsplit=NB, sdim=1)
        Ysb = [B.sb(f"Ysb{i}", [128, 512], F32) for i in range(2)]
        htmp = B.sb("htmp", [128, NB * 4 * 64], F32)
        for d in range(2):
            S.memset(Hf[d][:, :], 0.0)
            S.memset(Hb[d][:, :, :, :], 0.0)
        ctr = [0]

        def prep(d, b, Usrc, t0, latent):
            q = ctr[0]
            ctr[0] += 1
            U = uc[q % 2]
            S.dma("sp", U[:, :, :], Usrc.v(Usrc.t.ap()[:, :, t0:t0 + 128].rearrange("n p t -> p n t")))
            Pd = slice(d * 64, d * 64 + 64)
            r = U[:, 0:4, :]
            k = U[:, 4:8, :]
            t = tmp
            for m in range(4):
                S.ts(t["kkraw"][:, m, :], U[:, 4 + m, :], pv(PK_KK, m), ALU.mult, ek="pool")
            S.tt(t["sq"][:, :, :], t["kkraw"][:, :, :], t["kkraw"][:, :, :], ALU.mult, ek="pool")
            ssp = PS[4].c(0, 512)
            S.mm(ssp, cst["onesbd"][:, :], t["sq"][:, :, :])
            S.ts(t["rn"][:, :, :], PS[4].v(PS[4].t[:, :].rearrange("p (m t) -> p m t", t=128)), 1e-24, ALU.max)
            S.act(t["rn"][:, :, :], t["rn"][:, :, :], AF.Ln)
            S.act(t["rn"][:, :, :], t["rn"][:, :, :], AF.Exp, scale=-0.5)
            S.tt(t["kk"][:, :, :], t["kkraw"][:, :, :], t["rn"][:, :, :], ALU.mult)
            ck("B1a")
            yield
            S.copy(adb[Pd, :], U[Pd, 13, :], ek="pool")
            for m in range(4):
                S.mm(PS[5].c(m * 128, (m + 1) * 128), A2w[Pd, m * 128:(m + 1) * 128], adb[Pd, :])
            for m in range(4):
                S.act(t["a"][:, m, :], PS[5].c(m * 128, (m + 1) * 128), AF.Sigmoid, bias=pv(PK_A0 + d, m))
            yield
            S.act(twd[Pd, :], U[Pd, 12, :], AF.Tanh)
            for m in range(4):
                S.mm(PS[4].c(m * 128, (m + 1) * 128), W2w[Pd, m * 128:(m + 1) * 128], twd[Pd, :])
            for m in range(4):
                S.act(t["sg"][:, m, :], PS[4].c(m * 128, (m + 1) * 128), AF.Sigmoid, bias=pv(PK_W0 + d, m))
            ck("B1b")
            yield
            for m in range(4):
                S.op("dve", lambda e: e.tensor_tensor_scan(t["cf"].t[:, m, :], cst["ones"].t[:, :], t["sg"].t[:, m, :],
                                                           0.0, ALU.mult, ALU.add),
                     [t["cf"][:, m, :]], [cst["ones"][:, :], t["sg"][:, m, :]])
            ck("B1c")
            yield
            if d == 0:
                incl = t["cf"]
                S.tt(t["excl"][:, :, :], t["cf"][:, :, :], t["sg"][:, :, :], ALU.subtract)
                excl = t["excl"]
            else:
                for m in range(4):
                    S.ts(t["excl"][:, m, :], t["cf"][:, m, :], -1.0, ALU.mult, t["cf"][:, m, 127:128], ALU.add)
                S.tt(t["incl"][:, :, :], t["excl"][:, :, :], t["sg"][:, :, :], ALU.add)
                incl = t["incl"]
                excl = t["excl"]
            S.act(t["gi"][:, :, :], incl[:, :, :], AF.Exp, scale=SC)
            S.act(t["ge"][:, :, :], excl[:, :, :], AF.Exp, scale=SC)
            S.act(t["ginv"][:, :, :], incl[:, :, :], AF.Exp, scale=-SC)
            S.act(GC[d][:, b, :], t["cf"][:, :, 127], AF.Exp, scale=SC)
            yield
            for m in range(4):
                S.ts(t["t1"][:, m, :], t["a"][:, m, :], pv(PK_KA, m), ALU.mult, pv(PK_OMKA, m), ALU.add, ek="pool")
            S.tt(t["kap"][:, :, :], k, t["t1"][:, :, :], ALU.mult, ek="pool")
            S.tt(t["beta"][:, :, :], t["kk"][:, :, :], t["a"][:, :, :], ALU.mult, ek="pool")
            fm = FM[d]
            S.tt(t["sq"][:, :, :], t["kk"][:, :, :], t["ge"][:, :, :], ALU.mult)
            S.act(fm[:, b, :, 0, :], t["sq"][:, :, :], AF.Copy, scale=-1.0)
            S.tt(fm[:, b, :, 1, :], r, t["gi"][:, :, :], ALU.mult)
            S.tt(fm[:, b, :, 2, :], t["beta"][:, :, :], t["ginv"][:, :, :], ALU.mult)
            S.tt(fm[:, b, :, 3, :], t["kap"][:, :, :], t["ginv"][:, :, :], ALU.mult)
            ck("B1d")
            yield
            for m in range(4):
                for xi, x in enumerate((0, 2, 3)):
                    S.tr(PT.c(xi * 128, (xi + 1) * 128), fm[:, b, m, x, :], identb[:, :])
                S.act(TOK[d][:, b, m, :, :], PT.v(PT.t[:, 0:384].rearrange("p (x c) -> p x c", c=128)),
                      AF.Copy)
                yield
            ck("B1e")
            yield
            for m in range(4):
                S.tr(PS[5].c(m * 128, (m + 1) * 128), U[:, 8 + m, :], identf[:, :])
            S.act(VM[d][:, b, :], PS[5].c(0, 512), AF.Copy)
            yield
            if latent:
                if d == 0:
                    vt = vtok[(q // 2) % 2]
                    S.copy(vt[:, :], PS[5].c(0, 512))
                    S.dma("pool", VT[b][t0:t0 + 128, :], vt[:, :])
                    S.act(sgd[:, :], U[:, 14, :], AF.Sigmoid)
                    S.mm(PS[4].c(0, 512), sgd[:, :], G2w[:, :])
                    gt = gtok[(q // 2) % 2]
                    S.act(gt[:, :], PS[4].c(0, 512), AF.Copy)
                    S.dma("pool", GT[b][t0:t0 + 128, :], gt[:, :])
                yield
                S.tt(t["t1"][:, :, :], r, t["kap"][:, :, :], ALU.mult, ek="pool")
                for m in range(4):
                    S.ts(t["t1"][:, m, :], t["t1"][:, m, :], pv(PK_RK, m), ALU.mult, ek="pool")
                for m in range(4):
                    S.mm(PS[5].c(m * 2, m * 2 + 2), t["t1"][:, m, :], cst["e2"][:, :])
                bt = bont[q % 2]
                S.copy(bt[:, :], PS[5].c(0, 8))
                S.dma("pool", BON[b][d][t0:t0 + 128, :], bt[:, :])

        def head_chain(d, b, h, bank, par):
            ARB, ARK, Gb, WP = ARB2[par], ARK2[par], Gb2[par], WP2[par]
            fm = FM[d]
            m, hl = divmod(h, 2)
            P = slice(hl * 64, hl * 64 + 64)
            sl = h
            fb, ppf, ttf, yf_, xf_ = FB[sl], PPf[sl], TTf[sl], Yf_[sl], Xf_[sl]
            S.mm(bank.c(0, 256), fm[P, b, m, 0, :], fm[P, b, m, 2:4, :])
            S.mm(bank.c(256, 512), fm[P, b, m, 2, :], fm[P, b, m, 0:2, :])
            yield
            S.tt(V(fb.ts, fb.t[:, 0:512].rearrange("p (x c) -> p x c", c=256)[:, :, 0:128]),
                 bank.v(bank.t[:, 0:256].rearrange("p (x c) -> p x c", c=128)), mk2[:, d, 0:2, :], ALU.mult)
            S.tt(fb[:, 128:256], bank.c(256, 384), mk2[:, d, 2, :], ALU.mult)
            S.tt(fb[:, 448:576], bank.c(256, 384), mk2[:, d, 4, :], ALU.mult)
            S.tt(ARB[sl][:, :], bank.c(384, 512), mk2[:, d, 3, :], ALU.mult)
            S.copy(fb[:, 384:448], TOK[d][:, b, m, 0, hl * 64:(hl + 1) * 64], ek="pool")
            S.tt(ttf[:, :], fb[:, 128:256], identf[:, :], ALU.add, ek="pool")
            yield
            Pn, Pt_ = fb[:, 0:128], fb[:, 128:256]
            S.mm(bank.c(0, 128), Pt_, Pn)
            S.mm(bank.c(128, 256), Pn, Pt_)
            S.mm(bank.c(384, 512), fm[P, b, m, 3, :], fm[P, b, m, 1, :])
            yield
            S.act(ppf[:, :], bank.c(0, 256), AF.Copy)
            S.tt(ARK[sl][:, :], bank.c(384, 512), mk2[:, d, 3, :], ALU.mult)
            yield
            for lev in range(1, 6):
                Pn, Pt_ = ppf[:, 0:128], ppf[:, 128:256]
                S.mm(bank.c(256, 384), Pn, ttf[:, :])
                if lev < 5:
                    S.mm(bank.c(0, 128), Pt_, Pn)
                    S.mm(bank.c(128, 256), Pn, Pt_)
                yield
                S.tt(ttf[:, :], bank.c(256, 384), ttf[:, :], ALU.add)
                if lev < 5:
                    S.act(ppf[:, :], bank.c(0, 256), AF.Copy)
                yield
            S.mm(bank.c(0, 192), ttf[:, :], fb[:, 256:448])
            yield
            S.act(yf_[:, :], bank.c(0, 192), AF.Copy)
            yield
            S.mm(bank.c(256, 448), fb[:, 448:576], yf_[:, :])
            yield
            S.copy(xf_[:, :], bank.c(256, 448))
            yield
            S.mm(bank.c(0, 192), ttf[:, :], xf_[:, :])
            yield
            S.tt(Gb[sl][:, :], bank.c(0, 128), yf_[:, 0:128], ALU.add)
            S.tt(WP[sl // 2][:, hl * 64:(hl + 1) * 64], bank.c(128, 192), yf_[:, 128:192], ALU.add)
            yield

        def chunk_math(d, b, par, extras):
            ck("B1")
            NCH = 4
            pending = list(range(8))
            active = [(e_, None) for e_ in extras if e_ is not None]
            free_banks = [PS[i] for i in range(NCH)]
            while pending or active:
                while pending and free_banks:
                    h = pending.pop(0)
                    bk = free_banks.pop(0)
                    active.append((head_chain(d, b, h, bk, par), bk))
                nxt = []
                for g, bk in active:
                    try:
                        next(g)
                        nxt.append((g, bk))
                    except StopIteration:
                        if bk is not None:
                            free_banks.append(bk)
                active = nxt

        def state_part(d, b, latent, t0, par):
            ARB, ARK, Gb, GTs, WP, WTs = ARB2[par], ARK2[par], Gb2[par], GTs2[par], WP2[par], WTs2[par]
            fm = FM[d]
            for h4 in range(2):
                for q_ in range(4):
                    S.tr(PT.c(384 + q_ * 128, 512 + q_ * 128), Gb[h4 * 4 + q_][:, :], identb[:, :])
                yield
                for q_ in range(4):
                    hh = h4 * 4 + q_
                    if hh % 2 == 0:
                        S.act(GTs[hh][:, :], PT.c(384 + q_ * 128, 512 + q_ * 128), AF.Copy)
                    else:
                        S.copy(GTs[hh][:, :], PT.c(384 + q_ * 128, 512 + q_ * 128))
                yield
            for pr in range(4):
                S.tr(PT.c(384 + pr * 128, 512 + pr * 128), WP[pr][:, :], identb[:, :])
            yield
            for pr in range(4):
                S.copy(WTs[pr][:, :], PT.c(384 + pr * 128, 512 + pr * 128))
            yield
            psU = PS[6].c(0, 512)
            for h in range(8):
                m, hl = divmod(h, 2)
                P = slice(hl * 64, hl * 64 + 64)
                o = PS[6].c(h * 64, (h + 1) * 64)
                S.mm(o, WTs[h // 2][P, :], Hb[d][P, b, m, :], start=True, stop=False)
                S.mm(o, GTs[h][:, :], VM[d][:, b, h * 64:(h + 1) * 64], start=False, stop=True)
            yield
            S.act(Ub[:, b, :], psU, AF.Copy)
            yield
            if latent:
                for h in range(8):
                    m, hl = divmod(h, 2)
                    P = slice(hl * 64, hl * 64 + 64)
                    o = PS[6].c(h * 64, (h + 1) * 64)
                    S.mm(o, fm[P, b, m, 1, :], Hb[d][P, b, m, :], start=True, stop=False)
                    S.mm(o, ARB[h][:, :], Ub[:, b, h * 64:(h + 1) * 64], start=False, stop=False)
                    S.mm(o, ARK[h][:, :], VM[d][:, b, h * 64:(h + 1) * 64], start=False, stop=True)
                yield
                ys = Ysb[(b + d) % 2]
                S.act(ys[:, :], PS[6].c(0, 512), AF.Copy)
                S.dma("pool", YD[b][d][t0:t0 + 128, :], ys[:, :])
                yield
            for h in range(8):
                m, hl = divmod(h, 2)
                P = slice(hl * 64, hl * 64 + 64)
                o = PS[6].v(PS[6].t[P, (b * 4 + m) * 64:(b * 4 + m + 1) * 64])
                S.mm(o, TOK[d][:, b, m, 1, hl * 64:(hl + 1) * 64], Ub[:, b, h * 64:(h + 1) * 64], start=True, stop=False)
                S.mm(o, TOK[d][:, b, m, 2, hl * 64:(hl + 1) * 64], VM[d][:, b, h * 64:(h + 1) * 64], start=False, stop=True)
            yield
            cs_ = slice(b * 256, (b + 1) * 256)
            S.tt(htmp[:, cs_], PS[6].v(PS[6].t[:, cs_]), Hf[d][:, cs_], ALU.add)
            S.tt(V(Hf[d].ts, Hf[d].t[:, cs_].rearrange("p (m v) -> p m v", v=64)),
                 V(htmp.ts, htmp.t[:, cs_].rearrange("p (m v) -> p m v", v=64)),
                 V(GC[d].ts, GC[d].t[:, b, :].unsqueeze(2).broadcast_to([128, 4, 64])), ALU.mult)
            S.copy(Hb[d][:, b, :, :], V(Hf[d].ts, Hf[d].t[:, cs_].rearrange("p (m v) -> p m v", v=64)), ek="pool")
            yield

        nsteps = 2 + TL // 128
        units = []
        for s_ in range(nsteps):
            for d in range(2):
                for b in range(NB):
                    if s_ < 2:
                        ci = s_ if d == 0 else 1 - s_
                        units.append((d, b, U_ctx[b], ci * 128, False))
                    else:
                        li = s_ - 2
                        ci = li if d == 0 else (TL // 128 - 1 - li)
                        units.append((d, b, U_lat[b], ci * 128, True))
        for _ in prep(*units[0]):
            pass
        prev_state = None
        for i, (d, b, Usrc, t0, latent) in enumerate(units):
            nxt_prep = prep(*units[i + 1]) if i + 1 < len(units) else None
            chunk_math(d, b, i % 2, [prev_state, nxt_prep])
            prev_state = state_part(d, b, latent, t0, i % 2)
        for _ in prev_state:
            pass
        barrier(S)
        B.close()

    phase_B()
    if stop_after == "B":
        return nc, S

    def phase_C(b):
        C = Scope(nc)
        load_const(C, "c32")
        lnx = C.sb("lnx", [128, 1024], F32)
        S.dma("sp", lnx[:, :], dr["lnx_row"].v(dr["lnx_row"].t.ap().partition_broadcast(128)))
        stg = [C.sb(f"stgC{i}", [128, 8, 256], F32) for i in range(2)]
        fT = C.sb("fT", [128, 4, TL], BF16)
        Wupf = C.sb("Wupf", [128, 4, D], BF16)
        Wupr = C.sb("Wupr", [128, 4, D], BF16)
        Wout = C.sb("Wout", [128, 8, D], BF16)
        load_weight_bf16(C, Wupf, dr["w_up_f"], 4, 0, D, stg)
        load_weight_bf16(C, Wupr, dr["w_up_r"], 4, 0, D, stg)
        load_weight_bf16(C, Wout, dr["w_out"], 8, 0, D, stg)
        ga1b = C.sb("ga1b", [128, D], F32)
        S.dma("sp", ga1b[:, :], MODROW.v(MODROW.t.ap()[b:b + 1, 0, :].partition_broadcast(128)))
        dd = [C.sb(f"dd{i}", [32, 2, 4, 512], F32) for i in range(2)]
        ddv = DD[b].t.ap().rearrange("a (r c) k -> r a c k", c=64)
        c32 = cst["c32"]
        for cb in range(16):
            dt_ = dd[cb % 2]
            S.dma("sp", dt_[:, :, :, :], DD[b].v(ddv[:, :, cb * 4:(cb + 1) * 4, :]))
            bank = PS[cb % 2]
            for cl in range(4):
                for g in range(4):
                    o = bank.c((g * 4 + cl) * 32, (g * 4 + cl) * 32 + 32)
                    S.mm(o, dt_[:, 0, cl, g * 128:(g + 1) * 128], c32[:, 0, :], start=True, stop=False)
                    S.mm(o, dt_[:, 1, cl, g * 128:(g + 1) * 128], c32[:, 1, :], start=False, stop=True)
            for g in range(4):
                src = bank.v(bank.t[:, g * 128:(g + 1) * 128].rearrange("p (c r) -> p r c", r=32))
                dst = V(fT.ts, fT.t[:, g, :].rearrange("p (r c) -> p r c", c=64)[:, :, cb * 4:(cb + 1) * 4])
                if g % 2 == 0:
                    S.act(dst, src, AF.Copy)
                else:
                    S.copy(dst, src)
        yf = [C.sb(f"yf{i}", [128, 512], F32) for i in range(2)]
        yb = [C.sb(f"yb{i}", [128, 512], F32) for i in range(2)]
        vt = [C.sb(f"vt{i}", [128, 512], F32) for i in range(2)]
        gt = [C.sb(f"gt{i}", [128, 512], F32) for i in range(2)]
        bo = [C.sb(f"bo{i}", [128, 2, 8], F32) for i in range(2)]
        ysq = C.sb("ysq", [128, 512], F32)
        stt_ = [C.sb(f"stC{i}", [128, 6, 8], F32) for i in range(2)]
        obf = C.sb("obf", [128, 512], BF16)
        oT = C.sb("oT", [128, 4, 512], BF16)
        gsb = [C.sb(f"gsb{i}", [128, 2, 512], F32) for i in range(2)]
        mT = C.sb("mT", [128, 8, 512], BF16, nsplit=8)
        mtmp = [C.sb(f"mtmp{i}", [128, 512], F32) for i in range(2)]
        xin = [C.sb(f"xin{i}", [128, D], F32) for i in range(2)]
        xo = [C.sb(f"xo{i}", [128, D], F32) for i in range(2)]

        def b3(v_, n):
            return V(v_.ts, v_.ap.unsqueeze(2).broadcast_to([128, 8, n]))

        def v3(buf):
            return V(buf.ts, buf.t[:, :].rearrange("p (h v) -> p h v", v=64))

        for tt in range(TL // 512):
            for sub in range(4):
                i = tt * 4 + sub
                t0 = i * 128
                y_, yb_, v_, g_, bo_, st_ = yf[i % 2], yb[i % 2], vt[i % 2], gt[i % 2], bo[i % 2], stt_[i % 2]
                S.dma("sp", y_[:, :], YD[b][0][t0:t0 + 128, :])
                S.dma("sp", yb_[:, :], YD[b][1][t0:t0 + 128, :])
                S.dma("sp", v_[:, :], VT[b][t0:t0 + 128, :])
                S.dma("sp", g_[:, :], GT[b][t0:t0 + 128, :])
                S.dma("sp", bo_[:, 0, :], BON[b][0][t0:t0 + 128, :])
                S.dma("sp", bo_[:, 1, :], BON[b][1][t0:t0 + 128, :])
                S.tt(y_[:, :], y_[:, :], yb_[:, :], ALU.add)
                S.reduce(st_[:, 0, :], v3(y_), ALU.add, AX.X)
                S.tt(ysq[:, :], y_[:, :], y_[:, :], ALU.mult, ek="pool")
                S.reduce(st_[:, 1, :], v3(ysq), ALU.add, AX.X)
                S.ts(st_[:, 2, :], st_[:, 0, :], 1.0 / 64, ALU.mult)
                S.tt(st_[:, 3, :], st_[:, 2, :], st_[:, 2, :], ALU.mult)
                S.stt(st_[:, 3, :], st_[:, 1, :], 1.0 / 64, st_[:, 3, :], ALU.mult, ALU.subtract)
                S.act(st_[:, 4, :], st_[:, 3, :], AF.Sqrt, bias=eps_t[:, 1:2])
                S.op("dve", lambda e: e.reciprocal(st_.t[:, 5, :], st_.t[:, 4, :]), [st_[:, 5, :]], [st_[:, 4, :]])
                S.tt(v3(y_), v3(y_), b3(st_[:, 2, :], 64), ALU.subtract)
                S.tt(v3(y_), v3(y_), b3(st_[:, 5, :], 64), ALU.mult)
                S.tt(y_[:, :], y_[:, :], lnx[:, 0:512], ALU.mult)
                S.tt(y_[:, :], y_[:, :], lnx[:, 512:1024], ALU.add, ek="pool")
                S.tt(bo_[:, 0, :], bo_[:, 0, :], bo_[:, 1, :], ALU.add, ek="pool")
                S.tt(v3(v_), v3(v_), b3(bo_[:, 0, :], 64), ALU.mult)
                S.tt(y_[:, :], y_[:, :], v_[:, :], ALU.add, ek="pool")
                S.tt(obf[:, :], y_[:, :], g_[:, :], ALU.mult)
                for kc in range(4):
                    S.tr(PT.c(kc * 128, (kc + 1) * 128), obf[:, kc * 128:(kc + 1) * 128], identb[:, :])
                S.act(oT[:, :, sub * 128:(sub + 1) * 128],
                      PT.v(PT.t[:, 0:512].rearrange("p (k t) -> p k t", t=128)), AF.Copy)
            T0 = tt * 512
            for n in range(8):
                gs = gsb[n % 2]
                S.dma("sp", gs[:, 0, :], SG[b][n, :, T0:T0 + 512])
                S.dma("sp", gs[:, 1, :], SG[b][8 + n, :, T0:T0 + 512])
                pf = PS[2].c(0, 512)
                pr = PS[3].c(0, 512)
                for kc in range(4):
                    S.mm(pf, Wupf[:, kc, n * 128:(n + 1) * 128], fT[:, kc, T0:T0 + 512], start=(kc == 0), stop=(kc == 3))
                for kc in range(4):
                    S.mm(pr, Wupr[:, kc, n * 128:(n + 1) * 128], oT[:, kc, :], start=(kc == 0), stop=(kc == 3))
                mt = mtmp[n % 2]
                S.tt(mt[:, :], pf, gs[:, 0, :], ALU.mult)
                S.tt(gs[:, 1, :], pr, gs[:, 1, :], ALU.mult)
                S.tt(mT[:, n, :], mt[:, :], gs[:, 1, :], ALU.add, ek="pool")
            for sub in range(4):
                i = tt * 4 + sub
                t0 = i * 128
                xi, xo_ = xin[i % 2], xo[i % 2]
                S.dma("sp", xi[:, :], dr["x"].v(dr["x"].t[b, t0:t0 + 128, :]))
                for half in range(2):
                    o = PS[4 + half].c(0, 512)
                    for n in range(8):
                        S.mm(o, mT[:, n, sub * 128:(sub + 1) * 128], Wout[:, n, half * 512:(half + 1) * 512],
                             start=(n == 0), stop=(n == 7))
                    hs = slice(half * 512, (half + 1) * 512)
                    S.tt(xo_[:, hs], o, ga1b[:, hs], ALU.mult)
                    S.tt(xo_[:, hs], xo_[:, hs], xi[:, hs], ALU.add, ek="pool")
                S.dma("pool", X1[b][t0:t0 + 128, :], xo_[:, :])
        barrier(S)
        C.close()

    for b in range(NB):
        phase_C(b)
    if stop_after == "C":
        return nc, S

    def phase_D():
        Dd = Scope(nc)
        fng = Dd.sb("fng", [128, 1024], F32)
        S.dma("sp", fng[:, :], dr["fng_row"].v(dr["fng_row"].t.ap().partition_broadcast(128)))
        stg = [Dd.sb(f"stgD{i}", [128, 1024], F32) for i in range(2)]
        Wgu = Dd.sb("Wgu", [128, 8, 2 * DFF], BF16)
        Wdn = Dd.sb("Wdn", [128, 22, D], BF16)
        vgu = dr["w_gu"].t.ap().rearrange("(k p) n -> p k n", p=128)
        ci = 0
        for k in range(8):
            for c0_ in range(0, 2 * DFF, 1024):
                w = min(1024, 2 * DFF - c0_)
                sg_ = stg[ci % 2]
                S.dma("sp" if ci % 2 else "pool", sg_[:, 0:w], dr["w_gu"].v(vgu[:, k, c0_:c0_ + w]))
                S.copy(Wgu[:, k, c0_:c0_ + w], sg_[:, 0:w], ek="pool")
                ci += 1
        vdn = dr["w_down"].t.ap().rearrange("(k p) n -> p k n", p=128)
        for k in range(22):
            sg_ = stg[ci % 2]
            S.dma("sp" if ci % 2 else "pool", sg_[:, :], dr["w_down"].v(vdn[:, k, :]))
            S.copy(Wdn[:, k, :], sg_[:, :], ek="pool")
            ci += 1
        TD = 256
        hT2 = Dd.sb("hT2", [128, 8, TD], BF16)
        ntb = nt_bufs(Dd, "D", nbuf=1)
        actT = Dd.sb("actT", [128, 22, TD], BF16, nsplit=22)
        sil = [Dd.sb(f"sil{i}", [128, TD], F32) for i in range(2)]
        ga2b = Dd.sb("ga2b", [128, D], F32)
        x1t = [Dd.sb(f"x1t{i}", [128, D], F32) for i in range(1)]
        x2t = [Dd.sb(f"x2t{i}", [128, D], F32) for i in range(2)]
        junk2 = ntb[2]
        stf = [Dd.sb(f"stf{i}", [128, 4], F32) for i in range(2)]
        for b in range(NB):
            S.dma("sp", ga2b[:, :], MODROW.v(MODROW.t.ap()[b:b + 1, 1, :].partition_broadcast(128)))
            for tt in range(TL // TD):
                T0 = tt * TD
                x1v = X1[b].t
                norm_transpose(ntb, X1[b], lambda i: x1v[T0 + i * 128:T0 + (i + 1) * 128, :], TD // 128, A2,
                               lambda k: modT[:, 24 + k, b:b + 1], b, hT2, 0)
                for fc in range(22):
                    pg = PS[fc % 2].c(0, TD)
                    pu = PS[2 + fc % 2].c(0, TD)
                    for k in range(8):
                        S.mm(pg, Wgu[:, k, fc * 128:(fc + 1) * 128], hT2[:, k, :], start=(k == 0), stop=(k == 7))
                    for k in range(8):
                        S.mm(pu, Wgu[:, k, DFF + fc * 128:DFF + (fc + 1) * 128], hT2[:, k, :], start=(k == 0),
                             stop=(k == 7))
                    sl_ = sil[fc % 2]
                    S.act(sl_[:, :], pg, AF.Silu)
                    S.tt(actT[:, fc, :], pu, sl_[:, :], ALU.mult)
                for sub in range(TD // 128):
                    i = tt * (TD // 128) + sub
                    t0 = T0 + sub * 128
                    x1_, x2_, sf = x1t[0], x2t[i % 2], stf[i % 2]
                    S.dma("sp", x1_[:, :], X1[b][t0:t0 + 128, :])
                    for half in range(2):
                        o = PS[4 + half].c(0, 512)
                        for fc in range(22):
                            S.mm(o, actT[:, fc, sub * 128:(sub + 1) * 128], Wdn[:, fc, half * 512:(half + 1) * 512],
                                 start=(fc == 0), stop=(fc == 21))
                        hs = slice(half * 512, (half + 1) * 512)
                        S.tt(x2_[:, hs], o, ga2b[:, hs], ALU.mult)
                        S.tt(x2_[:, hs], x2_[:, hs], x1_[:, hs], ALU.add, ek="pool")
                    S.act(junk2[:, :], x2_[:, :], AF.Square, accum=sf[:, 0:1])
                    S.act(sf[:, 1:2], sf[:, 0:1], AF.Sqrt, scale=1.0 / D, bias=eps_t[:, 0:1])
                    S.op("dve", lambda e: e.reciprocal(sf.t[:, 2:3], sf.t[:, 1:2]), [sf[:, 2:3]], [sf[:, 1:2]])
                    S.act(x2_[:, :], x2_[:, :], AF.Copy, scale=sf[:, 2:3])
                    S.tt(x2_[:, :], x2_[:, :], fng[:, :], ALU.mult)
                    S.dma("pool", out[b, t0:t0 + 128, :], x2_[:, :])
        barrier(S)
        Dd.close()

    phase_D()
    barrier(S)
    return nc, S

from concourse.bass_utils import run_bass_kernel_spmd

N_CORES = 8
_CACHE = {}


def _fm(vec, nchunk):
    return np.ascontiguousarray(np.asarray(vec, np.float32).reshape(nchunk, 128).T)


def make_in_maps(inp, cores):
    consts = host_consts()
    f = lambda a: np.ascontiguousarray(np.asarray(a, np.float32))
    shared = {
        "b_adaT": _fm(inp["b_ada"][0], 48), "b_ada_row": f(inp["b_ada"][0]).reshape(1, 6144),
        "n1g": _fm(inp["norm1_g"][0], 8), "n2g": _fm(inp["norm2_g"][0], 8),
        "fng_row": f(inp["final_norm_g"]).reshape(1, D),
        "w_ada": f(inp["w_ada"][0]), "w_in": f(inp["w_in"][0]),
        "w2": np.concatenate([f(inp["w2_f"][0]), f(inp["w2_b"][0])], 0),
        "a2": np.concatenate([f(inp["a2_f"][0]), f(inp["a2_b"][0])], 0),
        "g2": f(inp["g2"][0]),
        "lnx_row": np.concatenate([f(inp["lnx_g"][0]), f(inp["lnx_b"][0])]).reshape(1, 1024),
        "w_up_r": f(inp["w_up_r"][0]), "w_up_f": f(inp["w_up_f"][0]), "w_out": f(inp["w_out"][0]),
        "w_gu": f(inp["w_gu"][0]), "w_down": f(inp["w_down"][0]),
    }
    mu3 = np.zeros((128, 3, 15), np.float32)
    mu3[:, 0, :] = _fm(inp["mu_prev"][0], 15)
    mu3[:, 1, :] = _fm(inp["mu_next"][0], 15)
    shared["mu3"] = mu3
    pvec = np.zeros((128, 9, 4), np.float32)
    for i, nm in ((0, "k_k"), (1, "k_a"), (3, "r_k"), (4, "w0_f"), (5, "w0_b"), (6, "a0_f"), (7, "a0_b")):
        pvec[:, i, :] = _fm(np.asarray(inp[nm][0]).reshape(512), 4)
    shared["pvec"] = pvec
    shared.update(consts)
    maps = []
    for c in cores:
        m = dict(shared)
        m["x"] = f(inp["x"][NB * c:NB * (c + 1)])
        m["ctx"] = f(inp["ctx"][NB * c:NB * (c + 1)])
        cT = np.zeros((128, 8, 3), np.float32)
        for j in range(NB):
            cT[:, :, j] = _fm(inp["c"][NB * c + j], 8)
        cT[:, :, 2] = _fm(inp["c_ctx"], 8)
        m["cT"] = cT
        maps.append(m)
    return maps


def kernel(**inputs):
    if "nc" not in _CACHE:
        _CACHE["nc"] = build()[0]
    nc = _CACHE["nc"]
    maps = make_in_maps(inputs, list(range(N_CORES)))
    res = run_bass_kernel_spmd(nc, maps, core_ids=list(range(N_CORES)))
    outs = [np.asarray(res.results[c]["out"], np.float32) for c in range(N_CORES)]
    return np.concatenate(outs, axis=0)
```

```python
import numpy as np
import concourse.bass as bass
import concourse.mybir as mybir

F32 = mybir.dt.float32
BF16 = mybir.dt.bfloat16
I32 = mybir.dt.int32
AF = mybir.ActivationFunctionType
ALU = mybir.AluOpType
AX = mybir.AxisListType

SEM_LIMIT = 30000


class T:
    __slots__ = ("name", "w", "r")

    def __init__(self, name=""):
        self.name = name
        self.w = None
        self.r = {}


class V:
    __slots__ = ("ts", "ap", "x")

    def __init__(self, ts, ap, x=False):
        self.ts = ts
        self.ap = ap
        self.x = x


class Buf:
    def __init__(self, tensor, name, nsplit=None, sdim=1):
        self.t = tensor
        self.name = name
        self.nsplit = nsplit
        self.sdim = sdim
        if nsplit is None:
            self.ts = [T(name)]
        else:
            self.ts = [T(f"{name}{i}") for i in range(nsplit)]

    def __getitem__(self, idx):
        ap = self.t[idx]
        if self.nsplit is None:
            return V(self.ts, ap)
        if not isinstance(idx, tuple):
            idx = (idx,)
        if len(idx) <= self.sdim:
            return V(self.ts, ap)
        s = idx[self.sdim]
        if isinstance(s, int):
            return V([self.ts[s]], ap)
        if isinstance(s, slice):
            st, sp, _ = s.indices(self.nsplit)
            return V(self.ts[st:sp], ap)
        return V(self.ts, ap)

    def v(self, ap, which=None):
        if which is None:
            return V(self.ts, ap)
        return V([self.ts[i] for i in which], ap)


class Sync:
    def __init__(self, nc, ndma=24):
        self.nc = nc
        self.engs = {"pe": nc.tensor, "act": nc.scalar, "dve": nc.vector,
                     "pool": nc.gpsimd, "sp": nc.sync}
        self.semh = {}
        self.cur = {}
        self.cnt = {}
        self.nsem = 0
        for k in self.engs:
            self._new_sem(k)
        self.seen = {k: {} for k in self.engs}
        self.ndma = ndma
        self.dma_key = []
        self.dma_val = []
        for i in range(ndma):
            key = self._alloc(f"dma{i}")
            self.dma_key.append(key)
            self.dma_val.append(0)
        self.dma_rr = 0
        self.ninst = {k: 0 for k in self.engs}

    def _alloc(self, name):
        key = f"{name}_{self.nsem}"
        self.nsem += 1
        self.semh[key] = self.nc.alloc_semaphore(name=key)
        return key

    def _new_sem(self, ek):
        self.cur[ek] = self._alloc(f"e_{ek}")
        self.cnt[ek] = 0

    def _deps(self, outs, ins):
        deps = {}

        def add(ev):
            if ev is None:
                return
            k, val = ev
            if deps.get(k, 0) < val:
                deps[k] = val
        for v in ins:
            for t in v.ts:
                add(t.w)
        for v in outs:
            for t in v.ts:
                add(t.w)
                for k, val in t.r.items():
                    add((k, val))
        return deps

    def _wait(self, ek, deps):
        eng = self.engs[ek]
        seen = self.seen[ek]
        for k, val in deps.items():
            if ek == "pe" and k.startswith("e_pe"):
                continue
            if seen.get(k, 0) < val:
                eng.wait_ge(self.semh[k], val)
                seen[k] = val
                self.ninst[ek] += 1

    def _mark(self, ev, outs, ins):
        k, val = ev
        for v in ins:
            for t in v.ts:
                if t.r.get(k, 0) < val:
                    t.r[k] = val
        for v in outs:
            for t in v.ts:
                t.w = ev
                t.r = {}

    def op(self, ek, fn, outs, ins):
        deps = self._deps(outs, ins)
        own = self.cur[ek]
        for v in ins:
            if v.x:
                for t in v.ts:
                    for k, val in t.r.items():
                        if k != own and deps.get(k, 0) < val:
                            deps[k] = val
        self._wait(ek, deps)
        if self.cnt[ek] >= SEM_LIMIT:
            self._new_sem(ek)
        inst = fn(self.engs[ek])
        self.cnt[ek] += 1
        inst.then_inc(self.semh[self.cur[ek]], 1)
        self.ninst[ek] += 1
        self._mark((self.cur[ek], self.cnt[ek]), outs, ins)

    def dma(self, ek, out, in_, **kw):
        i = self.dma_rr
        self.dma_rr = (self.dma_rr + 1) % self.ndma
        if self.dma_val[i] >= SEM_LIMIT:
            self.dma_key[i] = self._alloc(f"dma{i}")
            self.dma_val[i] = 0
        deps = self._deps([out], [in_])
        key = self.dma_key[i]
        if self.dma_val[i] > 0:
            if deps.get(key, 0) < self.dma_val[i]:
                deps[key] = self.dma_val[i]
        self._wait(ek, deps)
        inst = self.engs[ek].dma_start(out=out.ap, in_=in_.ap, **kw)
        self.dma_val[i] += 16
        inst.then_inc(self.semh[key], 16)
        self.ninst[ek] += 1
        self._mark((key, self.dma_val[i]), [out], [in_])

    def wait_all(self, ek, views):
        self._wait(ek, self._deps([], views))

    def mm(self, out, lhsT, rhs, start=True, stop=True, **kw):
        self.op("pe", lambda e: e.matmul(out.ap, lhsT.ap, rhs.ap, start=start, stop=stop, **kw),
                [out], [lhsT, rhs] + ([] if start else [out]))

    def tr(self, out, in_, ident):
        self.op("pe", lambda e: e.transpose(out.ap, in_.ap, ident.ap), [out], [in_, ident])

    def act(self, out, in_, func, bias=None, scale=1.0, ek="act", accum=None):
        ins = [in_]
        kw = {}
        if bias is not None:
            if isinstance(bias, V):
                ins.append(bias)
                kw["bias"] = bias.ap
            else:
                kw["bias"] = bias
        if isinstance(scale, V):
            ins.append(scale)
            kw["scale"] = scale.ap
        else:
            kw["scale"] = scale
        outs = [out]
        if accum is not None:
            outs.append(accum)
            kw["accum_out"] = accum.ap
        self.op(ek, lambda e: e.activation(out.ap, in_.ap, func, **kw), outs, ins)

    def tt(self, out, a, b, op, ek="dve"):
        self.op(ek, lambda e: e.tensor_tensor(out.ap, a.ap, b.ap, op), [out], [a, b])

    def ts(self, out, a, s1, op0, s2=None, op1=None, ek="dve"):
        ins = [a]
        x1 = s1
        if isinstance(s1, V):
            ins.append(s1)
            x1 = s1.ap
        x2 = s2
        if isinstance(s2, V):
            ins.append(s2)
            x2 = s2.ap
        if op1 is None:
            self.op(ek, lambda e: e.tensor_scalar(out.ap, a.ap, x1, None, op0), [out], ins)
        else:
            self.op(ek, lambda e: e.tensor_scalar(out.ap, a.ap, x1, x2, op0, op1), [out], ins)

    def stt(self, out, a, s, b, op0, op1):
        ins = [a, b]
        x = s
        if isinstance(s, V):
            ins.append(s)
            x = s.ap
        self.op("dve", lambda e: e.scalar_tensor_tensor(out.ap, a.ap, x, b.ap, op0, op1), [out], ins)

    def copy(self, out, in_, ek="dve"):
        self.op(ek, lambda e: e.tensor_copy(out.ap, in_.ap), [out], [in_])

    def memset(self, out, val, ek="pool"):
        self.op(ek, lambda e: e.memset(out.ap, val), [out], [])

    def reduce(self, out, in_, op, axis, ek="dve"):
        self.op(ek, lambda e: e.tensor_reduce(out.ap, in_.ap, axis, op), [out], [in_])

from contextlib import ExitStack

NB = 2
D = 1024
TL = 2048
TC = 256
NIN = 4480
DFF = 2816
NORM_EPS = 1e-6
import os
EVAC_ACT_ONLY = bool(int(os.environ.get('EVAC_ACT_ONLY', '0')))
GN_EPS = 64e-5
SC = -float(np.exp(-0.5))


class Scope:
    uid = 0

    def __init__(self, nc):
        self.nc = nc
        self.es = ExitStack()
        self.n = 0

    def sb(self, name, shape, dt, nsplit=None, sdim=1):
        Scope.uid += 1
        t = self.es.enter_context(self.nc.sbuf_tensor(f"s{Scope.uid}_{name}", shape, dt))
        return Buf(t, name, nsplit, sdim)

    def close(self):
        self.es.close()


class PBank:
    def __init__(self, nc, name, dt, width):
        self.t = nc.alloc_psum_tensor(name, [128, width], dt)
        self.blk = width
        self.ts = [T(f"{name}{i}") for i in range(max(1, width // self.blk))]

    def c(self, a, b, rows=slice(None)):
        q0 = a // self.blk
        q1 = (b - 1) // self.blk
        return V(self.ts[q0:q1 + 1], self.t[rows, a:b], True)

    def v(self, ap):
        return V(self.ts, ap, True)


def barrier(S):
    for ek, eng in S.engs.items():
        deps = {}
        for k2 in S.engs:
            if S.cnt[k2] > 0:
                deps[S.cur[k2]] = S.cnt[k2]
        for i in range(S.ndma):
            if S.dma_val[i] > 0:
                deps[S.dma_key[i]] = S.dma_val[i]
        seen = S.seen[ek]
        for k, val in deps.items():
            if seen.get(k, 0) < val:
                eng.wait_ge(S.semh[k], val)
                seen[k] = val


def host_consts():
    c = {}
    i = np.arange(128)
    row = i[:, None]
    col = i[None, :]
    SL = (col < row).astype(np.float32)
    SU = (col > row).astype(np.float32)
    IL = (col <= row).astype(np.float32)
    IU = (col >= row).astype(np.float32)
    mk = np.zeros((128, 4, 256), np.float32)
    mk[:, 0, :128] = SL; mk[:, 0, 128:] = SL
    mk[:, 1, :128] = SU; mk[:, 1, 128:] = IU
    mk[:, 2, :128] = SU; mk[:, 2, 128:] = SU
    mk[:, 3, :128] = SL; mk[:, 3, 128:] = IL
    c["mkp"] = mk
    BDm = np.zeros((128, 128), np.float32)
    BDm[:64, :64] = 1.0
    BDm[64:, 64:] = 1.0
    mk2 = np.zeros((128, 2, 5, 128), np.float32)
    mk2[:, 0, 0] = SL * BDm; mk2[:, 0, 1] = SL; mk2[:, 0, 2] = SU * BDm; mk2[:, 0, 3] = IU; mk2[:, 0, 4] = SU * (1 - BDm)
    mk2[:, 1, 0] = SU * BDm; mk2[:, 1, 1] = SU; mk2[:, 1, 2] = SL * BDm; mk2[:, 1, 3] = IL; mk2[:, 1, 4] = SL * (1 - BDm)
    c["mk2"] = mk2
    c["identf"] = np.eye(128, dtype=np.float32)
    th = 2 * np.pi * np.outer(i, i) / 128.0
    cs = np.zeros((128, 256), np.float32)
    cs[:, :128] = np.cos(th) / np.sqrt(128.0)
    cs[:, 128:] = np.sin(th) / np.sqrt(128.0)
    c["cs128"] = cs
    j = np.arange(64)
    th64 = 2 * np.pi * np.outer(j, j) / 64.0
    C64 = np.cos(th64) / 8.0
    S64 = np.sin(th64) / 8.0
    bd = np.zeros((128, 3, 128), np.float32)
    for r in range(2):
        bd[r * 64:(r + 1) * 64, 0, r * 64:(r + 1) * 64] = C64
        bd[r * 64:(r + 1) * 64, 1, r * 64:(r + 1) * 64] = S64
        bd[r * 64:(r + 1) * 64, 2, r * 64:(r + 1) * 64] = -S64
    c["bd64"] = bd
    k32 = np.arange(32)
    th32 = 2 * np.pi * np.outer(k32, k32) / 32.0
    c32 = np.zeros((32, 2, 32), np.float32)
    c32[:, 0, :] = np.cos(th32) / np.sqrt(32.0)
    c32[:, 1, :] = -np.sin(th32) / np.sqrt(32.0)
    c["c32"] = c32
    ones_bd = np.zeros((128, 128), np.float32)
    ones_bd[:64, :64] = 1.0
    ones_bd[64:, 64:] = 1.0
    c["onesbd"] = ones_bd
    e2 = np.zeros((128, 2), np.float32)
    e2[:64, 0] = 1.0
    e2[64:, 1] = 1.0
    c["e2"] = e2
    c["ones"] = np.ones((128, 128), np.float32)
    return c


CONST_SHAPES = {"mk2": [128, 2, 5, 128], "mkp": [128, 4, 256], "identf": [128, 128], "cs128": [128, 256], "bd64": [128, 3, 128],
                "c32": [32, 2, 32], "onesbd": [128, 128], "e2": [128, 2], "ones": [128, 128]}

IN_SHAPES = {
    "x": [NB, TL, D], "ctx": [NB, TC, D], "cT": [128, 8, 3], "b_adaT": [128, 48], "b_ada_row": [1, 6144],
    "n1g": [128, 8], "n2g": [128, 8], "fng_row": [1, D],
    "w_ada": [D, 6144], "w_in": [D, NIN], "mu3": [128, 3, 15],
    "w2": [128, 512], "a2": [128, 512], "g2": [128, 512],
    "pvec": [128, 9, 4],
    "lnx_row": [1, 1024],
    "w_up_r": [512, D], "w_up_f": [512, D], "w_out": [D, D], "w_gu": [D, 2 * DFF], "w_down": [DFF, D],
}
IN_SHAPES.update(CONST_SHAPES)


class StopBuild(Exception):
    pass


def build(debug=(), stop_after=None):
    nc = bass.Bass("TRN2", target_bir_lowering=False)
    S = Sync(nc)
    try:
        return _build(nc, S, debug, stop_after)
    except StopBuild:
        barrier(S)
        return nc, S


def _build(nc, S, debug, stop_after):
    def ck(name):
        if stop_after == name:
            raise StopBuild()
    dr = {}
    for name, shp in IN_SHAPES.items():
        dr[name] = Buf(nc.dram_tensor(name, shp, F32, kind="ExternalInput"), name)
    out = Buf(nc.dram_tensor("out", [NB, TL, D], F32, kind="ExternalOutput"), "out", nsplit=NB, sdim=0)

    def scratch(name, shape, nsplit=None, sdim=0):
        kind = "ExternalOutput" if name in debug else "Internal"
        return Buf(nc.dram_tensor(name, shape, F32, kind=kind), name, nsplit, sdim)

    U_lat = [scratch(f"U_lat{b}", [15, 128, TL], 15, 0) for b in range(NB)]
    U_ctx = [scratch(f"U_ctx{b}", [15, 128, TC], 15, 0) for b in range(NB)]
    SG = [scratch(f"SG{b}", [16, 128, TL], 16, 0) for b in range(NB)]
    DD = [scratch(f"DD{b}", [2, TL, 512], 2, 0) for b in range(NB)]
    YD = [[scratch(f"YD{b}_{d}", [TL, 512]) for d in range(2)] for b in range(NB)]
    VT = [scratch(f"VT{b}", [TL, 512]) for b in range(NB)]
    GT = [scratch(f"GT{b}", [TL, 512]) for b in range(NB)]
    BON = [[scratch(f"BON{b}_{d}", [TL, 8]) for d in range(2)] for b in range(NB)]
    X1 = [scratch(f"X1_{b}", [TL, D]) for b in range(NB)]
    MODROW = scratch("MODROW", [3, 2, 1024])

    if os.environ.get("PTFIRST") == "1":
        PT = PBank(nc, "pst", BF16, 1024)
        PS = [PBank(nc, f"ps{i}", F32, 512) for i in range(7)]
    else:
        PS = [PBank(nc, f"ps{i}", F32, 512) for i in range(7)]
        PT = PBank(nc, "pst", BF16, 1024)

    G = Scope(nc)
    cst = {}

    def load_const(sc, name):
        shp = CONST_SHAPES[name]
        cst[name] = sc.sb("c_" + name, shp, F32)
        S.dma("sp", cst[name].v(cst[name].t[tuple(slice(None) for _ in shp)]),
              dr[name].v(dr[name].t[tuple(slice(None) for _ in shp)]))
        return cst[name]

    load_const(G, "identf")
    identf = cst["identf"]
    identb = G.sb("identb", [128, 128], BF16)
    S.copy(identb[:, :], identf[:, :])
    mu3 = G.sb("mu3", [128, 3, 15], F32)
    S.dma("sp", mu3[:, :, :], dr["mu3"][:, :, :])
    c0 = G.sb("c0", [128, 15], F32)
    S.tt(c0[:, :], mu3[:, 0, :], mu3[:, 1, :], ALU.add)
    S.ts(c0[:, :], c0[:, :], -1.0, ALU.mult, 1.0, ALU.add)
    pvec = G.sb("pvec", [128, 9, 4], F32)
    S.dma("sp", pvec[:, :, :], dr["pvec"][:, :, :])
    S.ts(pvec[:, 2, :], pvec[:, 1, :], -1.0, ALU.mult, 1.0, ALU.add)
    PK_KK, PK_KA, PK_OMKA, PK_RK, PK_W0, PK_A0 = 0, 1, 2, 3, 4, 6
    n1g = G.sb("n1g", [128, 8], F32)
    n2g = G.sb("n2g", [128, 8], F32)
    S.dma("sp", n1g[:, :], dr["n1g"][:, :])
    S.dma("sp", n2g[:, :], dr["n2g"][:, :])
    smallw = {}
    for name in ("w2", "a2", "g2"):
        smallw[name] = G.sb("bf_" + name, [128, 512], BF16)
    W2w, A2w, G2w = smallw["w2"], smallw["a2"], smallw["g2"]

    if stop_after == "G":
        barrier(S)
        return nc, S
    modT = G.sb("modT", [128, 48, 3], F32)
    A1 = G.sb("A1", [128, 8, 3], F32)
    A2 = G.sb("A2", [128, 8, 3], F32)
    M = Scope(nc)
    garow = M.sb("garow", [3, 2, 1024], F32)
    stg_small = [M.sb(f"stg_small{i}", [128, 512], F32) for i in range(3)]
    for i, name in enumerate(("w2", "a2", "g2")):
        S.dma("sp", stg_small[i][:, :], dr[name][:, :])
        S.copy(smallw[name][:, :], stg_small[i][:, :])
    cT = M.sb("cT", [128, 8, 3], F32)
    sT = M.sb("sT", [128, 8, 3], F32)
    badaT = M.sb("badaT", [128, 48], F32)
    brow = M.sb("brow", [3, 6144], F32)
    S.dma("sp", cT[:, :, :], dr["cT"][:, :, :])
    S.dma("sp", badaT[:, :], dr["b_adaT"][:, :])
    S.dma("sp", brow[:, :], dr["b_ada_row"].v(dr["b_ada_row"].t.ap().partition_broadcast(3)))
    S.act(sT[:, :, :], cT[:, :, :], AF.Silu)
    wa = [M.sb(f"wa{i}", [128, 8, 1024], F32) for i in range(2)]
    w_ada_v = dr["w_ada"].t.ap().rearrange("(k p) n -> p k n", p=128)
    for sec in range(6):
        wb = wa[sec % 2]
        S.dma("sp" if sec % 2 == 0 else "pool", wb[:, :, :], dr["w_ada"].v(w_ada_v[:, :, sec * 1024:(sec + 1) * 1024]))
        for nn in range(8):
            n = sec * 8 + nn
            o = PS[0].c(n * 3, n * 3 + 3)
            for k in range(8):
                S.mm(o, wb[:, k, nn * 128:(nn + 1) * 128], sT[:, k, :], start=(k == 0), stop=(k == 7))
        if sec in (2, 5):
            gi_ = 0 if sec == 2 else 1
            for half in range(2):
                o = PS[1].c(0, 512, rows=slice(0, 3))
                for k in range(8):
                    S.mm(o, sT[:, k, :], wb[:, k, half * 512:(half + 1) * 512], start=(k == 0), stop=(k == 7))
                S.tt(garow[:, gi_, half * 512:(half + 1) * 512], o,
                     brow[:, sec * 1024 + half * 512: sec * 1024 + (half + 1) * 512], ALU.add)
    S.tt(modT[:, :, :], PS[0].v(PS[0].t[:, 0:144].rearrange("p (n j) -> p n j", j=3)),
         V(badaT.ts, badaT.t[:, :].unsqueeze(2).broadcast_to([128, 48, 3])), ALU.add)
    S.dma("sp", MODROW[:, :, :], garow[:, :, :])
    S.ts(A1[:, :, :], modT[:, 8:16, :], 1.0, ALU.add)
    S.tt(A1[:, :, :], A1[:, :, :], V(n1g.ts, n1g.t[:, :].unsqueeze(2).broadcast_to([128, 8, 3])), ALU.mult)
    S.ts(A2[:, :, :], modT[:, 32:40, :], 1.0, ALU.add)
    S.tt(A2[:, :, :], A2[:, :, :], V(n2g.ts, n2g.t[:, :].unsqueeze(2).broadcast_to([128, 8, 3])), ALU.mult)
    barrier(S)
    M.close()
    if stop_after == "M":
        if "modT" in debug:
            dbgm = nc.dram_tensor("dbg_modT", [128, 48, 3], F32, kind="ExternalOutput")
            S.dma("sp", V([T("x")], dbgm[:, :, :]), modT[:, :, :])
        barrier(S)
        return nc, S

    def nt_bufs(sc, tag, nbuf=2):
        xt = [sc.sb(f"xt{tag}{i}", [128, 1024], F32) for i in range(nbuf)]
        xn = [sc.sb(f"xn{tag}{i}", [128, 1024], BF16) for i in range(nbuf)]
        junk = sc.sb(f"junk{tag}", [128, 1024], BF16)
        st = [sc.sb(f"st{tag}{i}", [128, 4], F32) for i in range(2)]
        return xt, xn, junk, st

    def norm_transpose(bufs, src_buf, src_rows_fn, ntiles, Aap, Bfn, j, hT, col0):
        xt, xn, junk, st = bufs
        for i in range(ntiles):
            if i == 1:
                ck("A0b")
            if i == 3:
                ck("A0c")
            x_t = xt[i % len(xt)]
            x_n = xn[i % len(xn)]
            s_ = st[i % 2]
            S.dma("sp", x_t[:, :], src_buf.v(src_rows_fn(i)))
            ck("A0a")
            S.act(junk[:, :], x_t[:, :], AF.Square, accum=s_[:, 0:1])
            S.act(s_[:, 1:2], s_[:, 0:1], AF.Sqrt, scale=1.0 / D, bias=eps_t[:, 0:1])
            S.op("dve", lambda e: e.reciprocal(s_.t[:, 2:3], s_.t[:, 1:2]), [s_[:, 2:3]], [s_[:, 1:2]])
            S.act(x_n[:, :], x_t[:, :], AF.Copy, scale=s_[:, 2:3])
            ck("A0a2")
            for k in range(8):
                S.tr(PT.c(k * 128, (k + 1) * 128), x_n[:, k * 128:(k + 1) * 128], identb[:, :])
            ck("A0a3")
            for k in range(8):
                o = hT[:, k, col0 + i * 128: col0 + (i + 1) * 128]
                if k == 1:
                    ck("A0a4")
                if os.environ.get("DBGV") == "1":
                    o = junk[:, 0:128]
                if os.environ.get("DBGV") == "2":
                    S.act(o, PT.c(k * 128, (k + 1) * 128), AF.Identity, scale=0.5, bias=eps_t[:, 0:1])
                    continue
                if os.environ.get("DBGV") == "3":
                    S.act(o, PT.c(k * 128, (k + 1) * 128), AF.Copy)
                    continue
                if k == 2:
                    ck("A0a5")
                if k % 2 == 0 and not EVAC_ACT_ONLY:
                    S.ts(o, PT.c(k * 128, (k + 1) * 128), Aap[:, k, j:j + 1], ALU.mult, Bfn(k), ALU.add)
                else:
                    S.act(o, PT.c(k * 128, (k + 1) * 128), AF.Identity, scale=Aap[:, k, j:j + 1],
                          bias=Bfn(k))

    eps_t = G.sb("eps_t", [128, 2], F32)
    S.memset(eps_t[:, 0:1], NORM_EPS)
    S.memset(eps_t[:, 1:2], GN_EPS)

    def load_weight_bf16(sc, dst, dram_buf, kchunks, c0_, c1_, stg, ei=[0]):
        v = dram_buf.t.ap().rearrange("(k p) n -> p k n", p=128)
        cc = c0_
        while cc < c1_:
            w = min(256, c1_ - cc)
            sg_ = stg[ei[0] % 2]
            ei[0] += 1
            S.dma("pool" if ei[0] % 2 else "sp", sg_[:, 0:kchunks, 0:w], dram_buf.v(v[:, :, cc:cc + w]))
            S.copy(dst[:, 0:kchunks, cc - c0_: cc - c0_ + w], sg_[:, 0:kchunks, 0:w], ek="pool")
            cc += w

    Bsh1 = modT

    def phase_A(b):
        A = Scope(nc)
        load_const(A, "cs128")
        load_const(A, "bd64")
        stg = [A.sb(f"stgA{i}", [128, 8, 256], F32) for i in range(2)]
        hT = A.sb("hT", [128, 8, TL + 2], BF16)
        hTc = A.sb("hTc", [128, 8, TC + 2], BF16)
        Win = A.sb("Win", [128, 8, 2432], BF16)
        S.memset(hT[:, :, 0:1], 0.0)
        S.memset(hT[:, :, TL + 1:TL + 2], 0.0)
        S.memset(hTc[:, :, 0:1], 0.0)
        S.memset(hTc[:, :, TC + 1:TC + 2], 0.0)
        xv = dr["x"].t
        cv = dr["ctx"].t
        ck("A0")
        ntb = nt_bufs(A, "A")
        norm_transpose(ntb, dr["x"], lambda i: xv[b, i * 128:(i + 1) * 128, :], TL // 128, A1,
                       lambda k: modT[:, k, b:b + 1], b, hT, 1)
        norm_transpose(ntb, dr["ctx"], lambda i: cv[b, i * 128:(i + 1) * 128, :], TC // 128, A1,
                       lambda k: modT[:, k, 2:3], 2, hTc, 1)
        ck("A1")
        pb = [A.sb(f"pb{i}", [128, 514], F32) for i in range(2)]
        ub = [A.sb(f"ub{i}", [128, 512], F32) for i in range(3)]
        uf = A.sb("uf", [128, 4, 512], F32, nsplit=4)
        Abuf = [A.sb(f"Abuf{i}", [128, 4, 256], F32) for i in range(2)]
        Dout = [A.sb(f"Dout{i}", [128, 2, 512], F32) for i in range(2)]
        cnt = [0]

        def gemm(hbuf, T_, TT, n_lo, n_hi, wofs, Udst):
            for t0 in range(0, T_, TT):
                for n in range(n_lo, n_hi):
                    q = cnt[0]
                    cnt[0] += 1
                    bank = PS[q % 2]
                    o = bank.c(0, TT)
                    for k in range(8):
                        S.mm(o, Win[:, k, (n - wofs) * 128:(n - wofs + 1) * 128], hbuf[:, k, 1 + t0:1 + t0 + TT],
                             start=(k == 0), stop=(k == 7))
                    if n == 1:
                        ck("A2")
                    if n == 16:
                        ck("A3")
                    if n < 15:
                        oh = PS[2 + q % 2].c(0, 2)
                        for k in range(8):
                            S.mm(oh, Win[:, k, (n - wofs) * 128:(n - wofs + 1) * 128],
                                 hbuf[:, k, t0:t0 + TT + 2:TT + 1], start=(k == 0), stop=(k == 7))
                        p_ = pb[q % 2]
                        u_ = ub[q % 3]
                        S.act(p_[:, 1:TT + 1], o, AF.Copy)
                        S.copy(p_[:, 0:TT + 2:TT + 1], oh)
                        S.act(u_[:, 0:TT], p_[:, 1:TT + 1], AF.Copy, scale=c0[:, n:n + 1])
                        S.stt(u_[:, 0:TT], p_[:, 0:TT], mu3[:, 0, n:n + 1], u_[:, 0:TT], ALU.mult, ALU.add)
                        S.stt(u_[:, 0:TT], p_[:, 2:TT + 2], mu3[:, 1, n:n + 1], u_[:, 0:TT], ALU.mult, ALU.add)
                        S.dma("pool", Udst[n, :, t0:t0 + TT], u_[:, 0:TT])
                    elif n < 19:
                        S.act(uf[:, n - 15, 0:TT], o, AF.Copy)
                    else:
                        u_ = ub[q % 3]
                        S.act(u_[:, 0:TT], o, AF.Sigmoid)
                        S.dma("pool", SG[b][n - 19, :, t0:t0 + TT], u_[:, 0:TT])
                if n_lo <= 15 and n_hi >= 19 and T_ == TL:
                    for ch in range(4):
                        ab = Abuf[ch % 2]
                        do = Dout[ch % 2]
                        for g in range(4):
                            bank = PS[4 + (g // 2)]
                            S.mm(bank.c((g % 2) * 256, (g % 2) * 256 + 256), uf[:, g, ch * 128:(ch + 1) * 128],
                                 cst["cs128"][:, :])
                        S.act(ab[:, 0:2, :], PS[4].v(PS[4].t[:, :].rearrange("p (g c) -> p g c", c=256)), AF.Copy)
                        S.copy(ab[:, 2:4, :], PS[5].v(PS[5].t[:, :].rearrange("p (g c) -> p g c", c=256)))
                        Ac = ab[:, :, 0:128]
                        As = ab[:, :, 128:256]
                        d1 = PS[6].c(0, 512)
                        S.mm(d1, cst["bd64"][:, 0, :], Ac, start=True, stop=False)
                        S.mm(d1, cst["bd64"][:, 2, :], As, start=False, stop=True)
                        S.act(do[:, 0, :], d1, AF.Copy)
                        d2 = PS[6].c(0, 512)
                        S.mm(d2, cst["bd64"][:, 0, :], As, start=True, stop=False)
                        S.mm(d2, cst["bd64"][:, 1, :], Ac, start=False, stop=True)
                        S.copy(do[:, 1, :], d2)
                        tt0 = t0 + ch * 128
                        S.dma("pool", DD[b].v(DD[b].t.ap()[:, tt0:tt0 + 128, :].rearrange("a t c -> t a c")),
                              do[:, :, :])

        load_weight_bf16(A, Win, dr["w_in"], 8, 0, 2432, stg)
        ck("A1b")
        gemm(hT, TL, 512, 0, 19, 0, U_lat[b])
        ck("A4")
        gemm(hTc, TC, 256, 0, 15, 0, U_ctx[b])
        load_weight_bf16(A, Win, dr["w_in"], 8, 2432, 4480, stg)
        gemm(hT, TL, 512, 19, 35, 19, None)
        barrier(S)
        A.close()

    for b in range(NB):
        phase_A(b)
    if stop_after == "A":
        barrier(S)
        return nc, S


    def pv(row, m):
        return pvec[:, row, m:m + 1]

    def phase_B():
        B = Scope(nc)
        mk2 = load_const(B, "mk2")
        load_const(B, "onesbd")
        load_const(B, "e2")
        load_const(B, "ones")
        uc = [B.sb(f"uc{i}", [128, 15, 128], F32) for i in range(2)]
        Hf = [B.sb(f"Hf{d}", [128, NB * 4 * 64], F32) for d in range(2)]
        Hb = [B.sb(f"Hb{d}", [128, NB, 4, 64], BF16) for d in range(2)]
        FM = [B.sb(f"FM{d}", [128, NB, 4, 4, 128], BF16, nsplit=NB, sdim=1) for d in range(2)]
        TOK = [B.sb(f"TOK{d}", [128, NB, 4, 3, 128], BF16, nsplit=NB, sdim=1) for d in range(2)]
        VM = [B.sb(f"VM{d}", [128, NB, 512], BF16, nsplit=NB, sdim=1) for d in range(2)]
        GC = [B.sb(f"GC{d}", [128, NB, 4], F32) for d in range(2)]
        tmp = {}
        for nm in ("kkraw", "sq", "rn", "kk", "a", "sg", "cf", "incl", "excl", "gi", "ge", "ginv", "kap",
                   "beta", "t1"):
            tmp[nm] = B.sb("t_" + nm, [128, 4, 128], F32)
        adb = B.sb("adb", [128, 128], BF16)
        twd = B.sb("twd", [128, 128], BF16)
        sgd = B.sb("sgd", [128, 128], BF16)
        vtok = [B.sb(f"vtok{i}", [128, 512], F32) for i in range(2)]
        gtok = [B.sb(f"gtok{i}", [128, 512], F32) for i in range(2)]
        bont = [B.sb(f"bont{i}", [128, 8], F32) for i in range(2)]
        NSLOT = 8
        FB = [B.sb(f"FB{i}", [128, 576], F32) for i in range(NSLOT)]
        PPf = [B.sb(f"PPf{i}", [128, 256], F32) for i in range(NSLOT)]
        TTf = [B.sb(f"TTf{i}", [128, 128], F32) for i in range(NSLOT)]
        Yf_ = [B.sb(f"Yf{i}", [128, 192], BF16) for i in range(NSLOT)]
        Xf_ = [B.sb(f"Xf{i}", [128, 192], BF16) for i in range(NSLOT)]
        Zb_ = [B.sb(f"Zb{i}", [128, 192], BF16) for i in range(NSLOT)]
        Ltb_ = [B.sb(f"Ltb{i}", [128, 128], BF16) for i in range(NSLOT)]
        TTb_ = [B.sb(f"TTb{i}", [128, 128], BF16) for i in range(NSLOT)]
        ARB2 = [[B.sb(f"ARB{p_}_{i}", [128, 128], BF16) for i in range(NSLOT)] for p_ in range(2)]
        ARK2 = [[B.sb(f"ARK{p_}_{i}", [128, 128], BF16) for i in range(NSLOT)] for p_ in range(2)]
        Gb2 = [[B.sb(f"Gb{p_}_{i}", [128, 128], BF16) for i in range(NSLOT)] for p_ in range(2)]
        GTs2 = [[B.sb(f"GTs{p_}_{i}", [128, 128], BF16) for i in range(NSLOT)] for p_ in range(2)]
        WP2 = [[B.sb(f"WP{p_}_{i}", [128, 128], BF16) for i in range(NSLOT // 2)] for p_ in range(2)]
        WTs2 = [[B.sb(f"WTs{p_}_{i}", [128, 128], BF16) for i in range(NSLOT // 2)] for p_ in range(2)]
        Ub = B.sb("Ub", [128, NB, 512], BF16, nsplit=NB, sdim=1)
        Ysb = [B.sb(f"Ysb{i}", [128, 512], F32) for i in range(2)]
        htmp = B.sb("htmp", [128, NB * 4 * 64], F32)
        for d in range(2):
            S.memset(Hf[d][:, :], 0.0)
            S.memset(Hb[d][:, :, :, :], 0.0)
        ctr = [0]

        def prep(d, b, Usrc, t0, latent):
            q = ctr[0]
            ctr[0] += 1
            U = uc[q % 2]
            S.dma("sp", U[:, :, :], Usrc.v(Usrc.t.ap()[:, :, t0:t0 + 128].rearrange("n p t -> p n t")))
            Pd = slice(d * 64, d * 64 + 64)
            r = U[:, 0:4, :]
            k = U[:, 4:8, :]
            t = tmp
            for m in range(4):
                S.ts(t["kkraw"][:, m, :], U[:, 4 + m, :], pv(PK_KK, m), ALU.mult, ek="pool")
            S.tt(t["sq"][:, :, :], t["kkraw"][:, :, :], t["kkraw"][:, :, :], ALU.mult, ek="pool")
            ssp = PS[5].c(0, 512)
            S.mm(ssp, cst["onesbd"][:, :], t["sq"][:, :, :])
            S.ts(t["rn"][:, :, :], PS[5].v(PS[5].t[:, :].rearrange("p (m t) -> p m t", t=128)), 1e-24, ALU.max)
            S.act(t["rn"][:, :, :], t["rn"][:, :, :], AF.Ln)
            S.act(t["rn"][:, :, :], t["rn"][:, :, :], AF.Exp, scale=-0.5)
            S.tt(t["kk"][:, :, :], t["kkraw"][:, :, :], t["rn"][:, :, :], ALU.mult)
            ck("B1a")
            yield
            S.copy(adb[Pd, :], U[Pd, 13, :], ek="pool")
            for m in range(4):
                S.mm(PS[5].c(m * 128, (m + 1) * 128), A2w[Pd, m * 128:(m + 1) * 128], adb[Pd, :])
            for m in range(4):
                S.act(t["a"][:, m, :], PS[5].c(m * 128, (m + 1) * 128), AF.Sigmoid, bias=pv(PK_A0 + d, m))
            yield
            S.act(twd[Pd, :], U[Pd, 12, :], AF.Tanh)
            for m in range(4):
                S.mm(PS[5].c(m * 128, (m + 1) * 128), W2w[Pd, m * 128:(m + 1) * 128], twd[Pd, :])
            for m in range(4):
                S.act(t["sg"][:, m, :], PS[5].c(m * 128, (m + 1) * 128), AF.Sigmoid, bias=pv(PK_W0 + d, m))
            ck("B1b")
            yield
            for m in range(4):
                S.op("dve", lambda e: e.tensor_tensor_scan(t["cf"].t[:, m, :], cst["ones"].t[:, :], t["sg"].t[:, m, :],
                                                           0.0, ALU.mult, ALU.add),
                     [t["cf"][:, m, :]], [cst["ones"][:, :], t["sg"][:, m, :]])
            ck("B1c")
            yield
            if d == 0:
                incl = t["cf"]
                S.tt(t["excl"][:, :, :], t["cf"][:, :, :], t["sg"][:, :, :], ALU.subtract)
                excl = t["excl"]
            else:
                for m in range(4):
                    S.ts(t["excl"][:, m, :], t["cf"][:, m, :], -1.0, ALU.mult, t["cf"][:, m, 127:128], ALU.add)
                S.tt(t["incl"][:, :, :], t["excl"][:, :, :], t["sg"][:, :, :], ALU.add)
                incl = t["incl"]
                excl = t["excl"]
            S.act(t["gi"][:, :, :], incl[:, :, :], AF.Exp, scale=SC)
            S.act(t["ge"][:, :, :], excl[:, :, :], AF.Exp, scale=SC)
            S.act(t["ginv"][:, :, :], incl[:, :, :], AF.Exp, scale=-SC)
            S.act(GC[d][:, b, :], t["cf"][:, :, 127], AF.Exp, scale=SC)
            yield
            for m in range(4):
                S.ts(t["t1"][:, m, :], t["a"][:, m, :], pv(PK_KA, m), ALU.mult, pv(PK_OMKA, m), ALU.add, ek="pool")
            S.tt(t["kap"][:, :, :], k, t["t1"][:, :, :], ALU.mult, ek="pool")
            S.tt(t["beta"][:, :, :], t["kk"][:, :, :], t["a"][:, :, :], ALU.mult, ek="pool")
            fm = FM[d]
            S.tt(t["sq"][:, :, :], t["kk"][:, :, :], t["ge"][:, :, :], ALU.mult)
            S.act(fm[:, b, :, 0, :], t["sq"][:, :, :], AF.Copy, scale=-1.0)
            S.tt(fm[:, b, :, 1, :], r, t["gi"][:, :, :], ALU.mult)
            S.tt(fm[:, b, :, 2, :], t["beta"][:, :, :], t["ginv"][:, :, :], ALU.mult)
            S.tt(fm[:, b, :, 3, :], t["kap"][:, :, :], t["ginv"][:, :, :], ALU.mult)
            ck("B1d")
            yield
            for m in range(4):
                for xi, x in enumerate((0, 2, 3)):
                    S.tr(PT.c(xi * 128, (xi + 1) * 128), fm[:, b, m, x, :], identb[:, :])
                S.act(TOK[d][:, b, m, :, :], PT.v(PT.t[:, 0:384].rearrange("p (x c) -> p x c", c=128)),
                      AF.Copy)
                yield
            ck("B1e")
            yield
            for m in range(4):
                S.tr(PS[5].c(m * 128, (m + 1) * 128), U[:, 8 + m, :], identf[:, :])
            S.act(VM[d][:, b, :], PS[5].c(0, 512), AF.Copy)
            yield
            if latent:
                if d == 0:
                    vt = vtok[(q // 2) % 2]
                    S.copy(vt[:, :], PS[5].c(0, 512))
                    S.dma("pool", VT[b][t0:t0 + 128, :], vt[:, :])
                    S.act(sgd[:, :], U[:, 14, :], AF.Sigmoid)
                    S.mm(PS[5].c(0, 512), sgd[:, :], G2w[:, :])
                    gt = gtok[(q // 2) % 2]
                    S.act(gt[:, :], PS[5].c(0, 512), AF.Copy)
                    S.dma("pool", GT[b][t0:t0 + 128, :], gt[:, :])
                yield
                S.tt(t["t1"][:, :, :], r, t["kap"][:, :, :], ALU.mult, ek="pool")
                for m in range(4):
                    S.ts(t["t1"][:, m, :], t["t1"][:, m, :], pv(PK_RK, m), ALU.mult, ek="pool")
                for m in range(4):
                    S.mm(PS[5].c(m * 2, m * 2 + 2), t["t1"][:, m, :], cst["e2"][:, :])
                bt = bont[q % 2]
                S.copy(bt[:, :], PS[5].c(0, 8))
                S.dma("pool", BON[b][d][t0:t0 + 128, :], bt[:, :])

        def head_chain(d, b, h, bank, par):
            ARB, ARK, Gb, WP = ARB2[par], ARK2[par], Gb2[par], WP2[par]
            fm = FM[d]
            m, hl = divmod(h, 2)
            P = slice(hl * 64, hl * 64 + 64)
            sl = h
            fb, ppf, ttf, yf_, xf_ = FB[sl], PPf[sl], TTf[sl], Yf_[sl], Xf_[sl]
            zb, ltb, ttb = Zb_[sl], Ltb_[sl], TTb_[sl]
            S.mm(bank.c(0, 256), fm[P, b, m, 0, :], fm[P, b, m, 2:4, :])
            S.mm(bank.c(256, 512), fm[P, b, m, 2, :], fm[P, b, m, 0:2, :])
            yield
            S.tt(fb[:, 0:128], bank.c(0, 128), mk2[:, d, 0, :], ALU.mult)
            S.tt(zb[:, 0:128], bank.c(128, 256), mk2[:, d, 1, :], ALU.mult)
            S.tt(fb[:, 128:256], bank.c(256, 384), mk2[:, d, 2, :], ALU.mult)
            S.tt(ltb[:, :], bank.c(256, 384), mk2[:, d, 4, :], ALU.mult)
            S.tt(ARB[sl][:, :], bank.c(384, 512), mk2[:, d, 3, :], ALU.mult)
            S.copy(zb[:, 128:192], TOK[d][:, b, m, 0, hl * 64:(hl + 1) * 64], ek="pool")
            S.tt(ttf[:, :], fb[:, 128:256], identf[:, :], ALU.add, ek="pool")
            yield
            Pn, Pt_ = fb[:, 0:128], fb[:, 128:256]
            S.mm(bank.c(0, 128), Pt_, Pn)
            S.mm(bank.c(128, 256), Pn, Pt_)
            S.mm(bank.c(384, 512), fm[P, b, m, 3, :], fm[P, b, m, 1, :])
            yield
            S.act(ppf[:, :], bank.c(0, 256), AF.Copy)
            S.tt(ARK[sl][:, :], bank.c(384, 512), mk2[:, d, 3, :], ALU.mult)
            yield
            for lev in range(1, 6):
                Pn, Pt_ = ppf[:, 0:128], ppf[:, 128:256]
                S.mm(bank.c(256, 384), Pn, ttf[:, :])
                if lev < 5:
                    S.mm(bank.c(0, 128), Pt_, Pn)
                    S.mm(bank.c(128, 256), Pn, Pt_)
                yield
                S.tt(ttf[:, :], bank.c(256, 384), ttf[:, :], ALU.add)
                if lev < 5:
                    S.act(ppf[:, :], bank.c(0, 256), AF.Copy)
                yield
            S.act(ttb[:, :], ttf[:, :], AF.Copy)
            yield
            S.mm(bank.c(0, 192), ttb[:, :], zb[:, :])
            yield
            S.act(yf_[:, :], bank.c(0, 192), AF.Copy)
            yield
            S.mm(bank.c(256, 448), ltb[:, :], yf_[:, :])
            yield
            S.copy(xf_[:, :], bank.c(256, 448))
            yield
            S.mm(bank.c(0, 192), ttb[:, :], xf_[:, :])
            yield
            S.tt(Gb[sl][:, :], bank.c(0, 128), yf_[:, 0:128], ALU.add)
            S.tt(WP[sl // 2][:, hl * 64:(hl + 1) * 64], bank.c(128, 192), yf_[:, 128:192], ALU.add)
            yield

        def chunk_math(d, b, par, extras):
            ck("B1")
            NCH = 5
            pending = list(range(8))
            active = [(e_, None) for e_ in extras if e_ is not None]
            free_banks = [PS[i] for i in range(NCH)]
            while pending or active:
                while pending and free_banks:
                    h = pending.pop(0)
                    bk = free_banks.pop(0)
                    active.append((head_chain(d, b, h, bk, par), bk))
                nxt = []
                for g, bk in active:
                    try:
                        next(g)
                        nxt.append((g, bk))
                    except StopIteration:
                        if bk is not None:
                            free_banks.append(bk)
                active = nxt

        def state_part(d, b, latent, t0, par):
            ARB, ARK, Gb, GTs, WP, WTs = ARB2[par], ARK2[par], Gb2[par], GTs2[par], WP2[par], WTs2[par]
            fm = FM[d]
            for h4 in range(2):
                for q_ in range(4):
                    S.tr(PT.c(384 + q_ * 128, 512 + q_ * 128), Gb[h4 * 4 + q_][:, :], identb[:, :])
                yield
                for q_ in range(4):
                    hh = h4 * 4 + q_
                    if hh % 2 == 0:
                        S.act(GTs[hh][:, :], PT.c(384 + q_ * 128, 512 + q_ * 128), AF.Copy)
                    else:
                        S.copy(GTs[hh][:, :], PT.c(384 + q_ * 128, 512 + q_ * 128))
                yield
            for pr in range(4):
                S.tr(PT.c(384 + pr * 128, 512 + pr * 128), WP[pr][:, :], identb[:, :])
            yield
            for pr in range(4):
                S.copy(WTs[pr][:, :], PT.c(384 + pr * 128, 512 + pr * 128))
            yield
            psU = PS[6].c(0, 512)
            for h in range(8):
                m, hl = divmod(h, 2)
                P = slice(hl * 64, hl * 64 + 64)
                o = PS[6].c(h * 64, (h + 1) * 64)
                S.mm(o, WTs[h // 2][P, :], Hb[d][P, b, m, :], start=True, stop=False)
                S.mm(o, GTs[h][:, :], VM[d][:, b, h * 64:(h + 1) * 64], start=False, stop=True)
            yield
            S.act(Ub[:, b, :], psU, AF.Copy)
            yield
            if latent:
                for h in range(8):
                    m, hl = divmod(h, 2)
                    P = slice(hl * 64, hl * 64 + 64)
                    o = PS[6].c(h * 64, (h + 1) * 64)
                    S.mm(o, fm[P, b, m, 1, :], Hb[d][P, b, m, :], start=True, stop=False)
                    S.mm(o, ARB[h][:, :], Ub[:, b, h * 64:(h + 1) * 64], start=False, stop=False)
                    S.mm(o, ARK[h][:, :], VM[d][:, b, h * 64:(h + 1) * 64], start=False, stop=True)
                yield
                ys = Ysb[(b + d) % 2]
                S.act(ys[:, :], PS[6].c(0, 512), AF.Copy)
                S.dma("pool", YD[b][d][t0:t0 + 128, :], ys[:, :])
                yield
            for h in range(8):
                m, hl = divmod(h, 2)
                P = slice(hl * 64, hl * 64 + 64)
                o = PS[6].v(PS[6].t[P, (b * 4 + m) * 64:(b * 4 + m + 1) * 64])
                S.mm(o, TOK[d][:, b, m, 1, hl * 64:(hl + 1) * 64], Ub[:, b, h * 64:(h + 1) * 64], start=True, stop=False)
                S.mm(o, TOK[d][:, b, m, 2, hl * 64:(hl + 1) * 64], VM[d][:, b, h * 64:(h + 1) * 64], start=False, stop=True)
            yield
            cs_ = slice(b * 256, (b + 1) * 256)
            S.tt(htmp[:, cs_], PS[6].v(PS[6].t[:, cs_]), Hf[d][:, cs_], ALU.add)
            S.tt(V(Hf[d].ts, Hf[d].t[:, cs_].rearrange("p (m v) -> p m v", v=64)),
                 V(htmp.ts, htmp.t[:, cs_].rearrange("p (m v) -> p m v", v=64)),
                 V(GC[d].ts, GC[d].t[:, b, :].unsqueeze(2).broadcast_to([128, 4, 64])), ALU.mult)
            S.copy(Hb[d][:, b, :, :], V(Hf[d].ts, Hf[d].t[:, cs_].rearrange("p (m v) -> p m v", v=64)), ek="pool")
            yield

        nsteps = 2 + TL // 128
        units = []
        for s_ in range(nsteps):
            for d in range(2):
                for b in range(NB):
                    if s_ < 2:
                        ci = s_ if d == 0 else 1 - s_
                        units.append((d, b, U_ctx[b], ci * 128, False))
                    else:
                        li = s_ - 2
                        ci = li if d == 0 else (TL // 128 - 1 - li)
                        units.append((d, b, U_lat[b], ci * 128, True))
        for _ in prep(*units[0]):
            pass
        prev_state = None
        for i, (d, b, Usrc, t0, latent) in enumerate(units):
            nxt_prep = prep(*units[i + 1]) if i + 1 < len(units) else None
            chunk_math(d, b, i % 2, [prev_state, nxt_prep])
            prev_state = state_part(d, b, latent, t0, i % 2)
        for _ in prev_state:
            pass
        barrier(S)
        B.close()

    phase_B()
    if stop_after == "B":
        return nc, S

    def phase_C(b):
        C = Scope(nc)
        load_const(C, "c32")
        lnx = C.sb("lnx", [128, 1024], F32)
        S.dma("sp", lnx[:, :], dr["lnx_row"].v(dr["lnx_row"].t.ap().partition_broadcast(128)))
        stg = [C.sb(f"stgC{i}", [128, 8, 256], F32) for i in range(2)]
        fT = C.sb("fT", [128, 4, TL], BF16)
        Wupf = C.sb("Wupf", [128, 4, D], BF16)
        Wupr = C.sb("Wupr", [128, 4, D], BF16)
        Wout = C.sb("Wout", [128, 8, D], BF16)
        load_weight_bf16(C, Wupf, dr["w_up_f"], 4, 0, D, stg)
        load_weight_bf16(C, Wupr, dr["w_up_r"], 4, 0, D, stg)
        load_weight_bf16(C, Wout, dr["w_out"], 8, 0, D, stg)
        ga1b = C.sb("ga1b", [128, D], F32)
        S.dma("sp", ga1b[:, :], MODROW.v(MODROW.t.ap()[b:b + 1, 0, :].partition_broadcast(128)))
        dd = [C.sb(f"dd{i}", [32, 2, 4, 512], F32) for i in range(2)]
        ddv = DD[b].t.ap().rearrange("a (r c) k -> r a c k", c=64)
        c32 = cst["c32"]
        for cb in range(16):
            dt_ = dd[cb % 2]
            S.dma("sp", dt_[:, :, :, :], DD[b].v(ddv[:, :, cb * 4:(cb + 1) * 4, :]))
            bank = PS[cb % 2]
            for cl in range(4):
                for g in range(4):
                    o = bank.c((g * 4 + cl) * 32, (g * 4 + cl) * 32 + 32)
                    S.mm(o, dt_[:, 0, cl, g * 128:(g + 1) * 128], c32[:, 0, :], start=True, stop=False)
                    S.mm(o, dt_[:, 1, cl, g * 128:(g + 1) * 128], c32[:, 1, :], start=False, stop=True)
            for g in range(4):
                src = bank.v(bank.t[:, g * 128:(g + 1) * 128].rearrange("p (c r) -> p r c", r=32))
                dst = V(fT.ts, fT.t[:, g, :].rearrange("p (r c) -> p r c", c=64)[:, :, cb * 4:(cb + 1) * 4])
                if g % 2 == 0:
                    S.act(dst, src, AF.Copy)
                else:
                    S.copy(dst, src)
        yf = [C.sb(f"yf{i}", [128, 512], F32) for i in range(2)]
        yb = [C.sb(f"yb{i}", [128, 512], F32) for i in range(2)]
        vt = [C.sb(f"vt{i}", [128, 512], F32) for i in range(2)]
        gt = [C.sb(f"gt{i}", [128, 512], F32) for i in range(2)]
        bo = [C.sb(f"bo{i}", [128, 2, 8], F32) for i in range(2)]
        ysq = C.sb("ysq", [128, 512], F32)
        stt_ = [C.sb(f"stC{i}", [128, 6, 8], F32) for i in range(2)]
        obf = C.sb("obf", [128, 512], BF16)
        oT = C.sb("oT", [128, 4, 512], BF16)
        gsb = [C.sb(f"gsb{i}", [128, 2, 512], F32) for i in range(2)]
        mT = C.sb("mT", [128, 8, 512], BF16, nsplit=8)
        mtmp = [C.sb(f"mtmp{i}", [128, 512], F32) for i in range(2)]
        xin = [C.sb(f"xin{i}", [128, D], F32) for i in range(2)]
        xo = [C.sb(f"xo{i}", [128, D], F32) for i in range(2)]

        def b3(v_, n):
            return V(v_.ts, v_.ap.unsqueeze(2).broadcast_to([128, 8, n]))

        def v3(buf):
            return V(buf.ts, buf.t[:, :].rearrange("p (h v) -> p h v", v=64))

        for tt in range(TL // 512):
            for sub in range(4):
                i = tt * 4 + sub
                t0 = i * 128
                y_, yb_, v_, g_, bo_, st_ = yf[i % 2], yb[i % 2], vt[i % 2], gt[i % 2], bo[i % 2], stt_[i % 2]
                S.dma("sp", y_[:, :], YD[b][0][t0:t0 + 128, :])
                S.dma("sp", yb_[:, :], YD[b][1][t0:t0 + 128, :])
                S.dma("sp", v_[:, :], VT[b][t0:t0 + 128, :])
                S.dma("sp", g_[:, :], GT[b][t0:t0 + 128, :])
                S.dma("sp", bo_[:, 0, :], BON[b][0][t0:t0 + 128, :])
                S.dma("sp", bo_[:, 1, :], BON[b][1][t0:t0 + 128, :])
                S.tt(y_[:, :], y_[:, :], yb_[:, :], ALU.add)
                S.reduce(st_[:, 0, :], v3(y_), ALU.add, AX.X)
                S.tt(ysq[:, :], y_[:, :], y_[:, :], ALU.mult, ek="pool")
                S.reduce(st_[:, 1, :], v3(ysq), ALU.add, AX.X)
                S.ts(st_[:, 2, :], st_[:, 0, :], 1.0 / 64, ALU.mult)
                S.tt(st_[:, 3, :], st_[:, 2, :], st_[:, 2, :], ALU.mult)
                S.stt(st_[:, 3, :], st_[:, 1, :], 1.0 / 64, st_[:, 3, :], ALU.mult, ALU.subtract)
                S.act(st_[:, 4, :], st_[:, 3, :], AF.Sqrt, bias=eps_t[:, 1:2])
                S.op("dve", lambda e: e.reciprocal(st_.t[:, 5, :], st_.t[:, 4, :]), [st_[:, 5, :]], [st_[:, 4, :]])
                S.tt(v3(y_), v3(y_), b3(st_[:, 2, :], 64), ALU.subtract)
                S.tt(v3(y_), v3(y_), b3(st_[:, 5, :], 64), ALU.mult)
                S.tt(y_[:, :], y_[:, :], lnx[:, 0:512], ALU.mult)
                S.tt(y_[:, :], y_[:, :], lnx[:, 512:1024], ALU.add, ek="pool")
                S.tt(bo_[:, 0, :], bo_[:, 0, :], bo_[:, 1, :], ALU.add, ek="pool")
                S.tt(v3(v_), v3(v_), b3(bo_[:, 0, :], 64), ALU.mult)
                S.tt(y_[:, :], y_[:, :], v_[:, :], ALU.add, ek="pool")
                S.tt(obf[:, :], y_[:, :], g_[:, :], ALU.mult)
                for kc in range(4):
                    S.tr(PT.c(kc * 128, (kc + 1) * 128), obf[:, kc * 128:(kc + 1) * 128], identb[:, :])
                S.act(oT[:, :, sub * 128:(sub + 1) * 128],
                      PT.v(PT.t[:, 0:512].rearrange("p (k t) -> p k t", t=128)), AF.Copy)
            T0 = tt * 512
            for n in range(8):
                gs = gsb[n % 2]
                S.dma("sp", gs[:, 0, :], SG[b][n, :, T0:T0 + 512])
                S.dma("sp", gs[:, 1, :], SG[b][8 + n, :, T0:T0 + 512])
                pf = PS[2].c(0, 512)
                pr = PS[3].c(0, 512)
                for kc in range(4):
                    S.mm(pf, Wupf[:, kc, n * 128:(n + 1) * 128], fT[:, kc, T0:T0 + 512], start=(kc == 0), stop=(kc == 3))
                for kc in range(4):
                    S.mm(pr, Wupr[:, kc, n * 128:(n + 1) * 128], oT[:, kc, :], start=(kc == 0), stop=(kc == 3))
                mt = mtmp[n % 2]
                S.tt(mt[:, :], pf, gs[:, 0, :], ALU.mult)
                S.tt(gs[:, 1, :], pr, gs[:, 1, :], ALU.mult)
                S.tt(mT[:, n, :], mt[:, :], gs[:, 1, :], ALU.add, ek="pool")
            for sub in range(4):
                i = tt * 4 + sub
                t0 = i * 128
                xi, xo_ = xin[i % 2], xo[i % 2]
                S.dma("sp", xi[:, :], dr["x"].v(dr["x"].t[b, t0:t0 + 128, :]))
                for half in range(2):
                    o = PS[4 + half].c(0, 512)
                    for n in range(8):
                        S.mm(o, mT[:, n, sub * 128:(sub + 1) * 128], Wout[:, n, half * 512:(half + 1) * 512],
                             start=(n == 0), stop=(n == 7))
                    hs = slice(half * 512, (half + 1) * 512)
                    S.tt(xo_[:, hs], o, ga1b[:, hs], ALU.mult)
                    S.tt(xo_[:, hs], xo_[:, hs], xi[:, hs], ALU.add, ek="pool")
                S.dma("pool", X1[b][t0:t0 + 128, :], xo_[:, :])
        barrier(S)
        C.close()

    for b in range(NB):
        phase_C(b)
    if stop_after == "C":
        return nc, S

    def phase_D():
        Dd = Scope(nc)
        fng = Dd.sb("fng", [128, 1024], F32)
        S.dma("sp", fng[:, :], dr["fng_row"].v(dr["fng_row"].t.ap().partition_broadcast(128)))
        stg = [Dd.sb(f"stgD{i}", [128, 1024], F32) for i in range(2)]
        Wgu = Dd.sb("Wgu", [128, 8, 2 * DFF], BF16)
        Wdn = Dd.sb("Wdn", [128, 22, D], BF16)
        vgu = dr["w_gu"].t.ap().rearrange("(k p) n -> p k n", p=128)
        ci = 0
        for k in range(8):
            for c0_ in range(0, 2 * DFF, 1024):
                w = min(1024, 2 * DFF - c0_)
                sg_ = stg[ci % 2]
                S.dma("sp" if ci % 2 else "pool", sg_[:, 0:w], dr["w_gu"].v(vgu[:, k, c0_:c0_ + w]))
                S.copy(Wgu[:, k, c0_:c0_ + w], sg_[:, 0:w], ek="pool")
                ci += 1
        vdn = dr["w_down"].t.ap().rearrange("(k p) n -> p k n", p=128)
        for k in range(22):
            sg_ = stg[ci % 2]
            S.dma("sp" if ci % 2 else "pool", sg_[:, :], dr["w_down"].v(vdn[:, k, :]))
            S.copy(Wdn[:, k, :], sg_[:, :], ek="pool")
            ci += 1
        TD = 256
        hT2 = Dd.sb("hT2", [128, 8, TD], BF16)
        ntb = nt_bufs(Dd, "D", nbuf=1)
        actT = Dd.sb("actT", [128, 22, TD], BF16, nsplit=22)
        sil = [Dd.sb(f"sil{i}", [128, TD], F32) for i in range(2)]
        ga2b = Dd.sb("ga2b", [128, D], F32)
        x1t = [Dd.sb(f"x1t{i}", [128, D], F32) for i in range(1)]
        x2t = [Dd.sb(f"x2t{i}", [128, D], F32) for i in range(2)]
        junk2 = ntb[2]
        stf = [Dd.sb(f"stf{i}", [128, 4], F32) for i in range(2)]
        for b in range(NB):
            S.dma("sp", ga2b[:, :], MODROW.v(MODROW.t.ap()[b:b + 1, 1, :].partition_broadcast(128)))
            for tt in range(TL // TD):
                T0 = tt * TD
                x1v = X1[b].t
                norm_transpose(ntb, X1[b], lambda i: x1v[T0 + i * 128:T0 + (i + 1) * 128, :], TD // 128, A2,
                               lambda k: modT[:, 24 + k, b:b + 1], b, hT2, 0)
                for fc in range(22):
                    pg = PS[fc % 2].c(0, TD)
                    pu = PS[2 + fc % 2].c(0, TD)
                    for k in range(8):
                        S.mm(pg, Wgu[:, k, fc * 128:(fc + 1) * 128], hT2[:, k, :], start=(k == 0), stop=(k == 7))
                    for k in range(8):
                        S.mm(pu, Wgu[:, k, DFF + fc * 128:DFF + (fc + 1) * 128], hT2[:, k, :], start=(k == 0),
                             stop=(k == 7))
                    sl_ = sil[fc % 2]
                    S.act(sl_[:, :], pg, AF.Silu)
                    S.tt(actT[:, fc, :], pu, sl_[:, :], ALU.mult)
                for sub in range(TD // 128):
                    i = tt * (TD // 128) + sub
                    t0 = T0 + sub * 128
                    x1_, x2_, sf = x1t[0], x2t[i % 2], stf[i % 2]
                    S.dma("sp", x1_[:, :], X1[b][t0:t0 + 128, :])
                    for half in range(2):
                        o = PS[4 + half].c(0, 512)
                        for fc in range(22):
                            S.mm(o, actT[:, fc, sub * 128:(sub + 1) * 128], Wdn[:, fc, half * 512:(half + 1) * 512],
                                 start=(fc == 0), stop=(fc == 21))
                        hs = slice(half * 512, (half + 1) * 512)
                        S.tt(x2_[:, hs], o, ga2b[:, hs], ALU.mult)
                        S.tt(x2_[:, hs], x2_[:, hs], x1_[:, hs], ALU.add, ek="pool")
                    S.act(junk2[:, :], x2_[:, :], AF.Square, accum=sf[:, 0:1])
                    S.act(sf[:, 1:2], sf[:, 0:1], AF.Sqrt, scale=1.0 / D, bias=eps_t[:, 0:1])
                    S.op("dve", lambda e: e.reciprocal(sf.t[:, 2:3], sf.t[:, 1:2]), [sf[:, 2:3]], [sf[:, 1:2]])
                    S.act(x2_[:, :], x2_[:, :], AF.Copy, scale=sf[:, 2:3])
                    S.tt(x2_[:, :], x2_[:, :], fng[:, :], ALU.mult)
                    S.dma("pool", out[b, t0:t0 + 128, :], x2_[:, :])
        barrier(S)
        Dd.close()

    phase_D()
    barrier(S)
    return nc, S

from concourse.bass_utils import run_bass_kernel_spmd

N_CORES = 8
_CACHE = {}


def _fm(vec, nchunk):
    return np.ascontiguousarray(np.asarray(vec, np.float32).reshape(nchunk, 128).T)


def make_in_maps(inp, cores):
    consts = host_consts()
    f = lambda a: np.ascontiguousarray(np.asarray(a, np.float32))
    shared = {
        "b_adaT": _fm(inp["b_ada"][0], 48), "b_ada_row": f(inp["b_ada"][0]).reshape(1, 6144),
        "n1g": _fm(inp["norm1_g"][0], 8), "n2g": _fm(inp["norm2_g"][0], 8),
        "fng_row": f(inp["final_norm_g"]).reshape(1, D),
        "w_ada": f(inp["w_ada"][0]), "w_in": f(inp["w_in"][0]),
        "w2": np.concatenate([f(inp["w2_f"][0]), f(inp["w2_b"][0])], 0),
        "a2": np.concatenate([f(inp["a2_f"][0]), f(inp["a2_b"][0])], 0),
        "g2": f(inp["g2"][0]),
        "lnx_row": np.concatenate([f(inp["lnx_g"][0]), f(inp["lnx_b"][0])]).reshape(1, 1024),
        "w_up_r": f(inp["w_up_r"][0]), "w_up_f": f(inp["w_up_f"][0]), "w_out": f(inp["w_out"][0]),
        "w_gu": f(inp["w_gu"][0]), "w_down": f(inp["w_down"][0]),
    }
    mu3 = np.zeros((128, 3, 15), np.float32)
    mu3[:, 0, :] = _fm(inp["mu_prev"][0], 15)
    mu3[:, 1, :] = _fm(inp["mu_next"][0], 15)
    shared["mu3"] = mu3
    pvec = np.zeros((128, 9, 4), np.float32)
    for i, nm in ((0, "k_k"), (1, "k_a"), (3, "r_k"), (4, "w0_f"), (5, "w0_b"), (6, "a0_f"), (7, "a0_b")):
        pvec[:, i, :] = _fm(np.asarray(inp[nm][0]).reshape(512), 4)
    shared["pvec"] = pvec
    shared.update(consts)
    maps = []
    for c in cores:
        m = dict(shared)
        m["x"] = f(inp["x"][NB * c:NB * (c + 1)])
        m["ctx"] = f(inp["ctx"][NB * c:NB * (c + 1)])
        cT = np.zeros((128, 8, 3), np.float32)
        for j in range(NB):
            cT[:, :, j] = _fm(inp["c"][NB * c + j], 8)
        cT[:, :, 2] = _fm(inp["c_ctx"], 8)
        m["cT"] = cT
        maps.append(m)
    return maps


def kernel(**inputs):
    if "nc" not in _CACHE:
        _CACHE["nc"] = build()[0]
    nc = _CACHE["nc"]
    maps = make_in_maps(inputs, list(range(N_CORES)))
    res = run_bass_kernel_spmd(nc, maps, core_ids=list(range(N_CORES)))
    outs = [np.asarray(res.results[c]["out"], np.float32) for c in range(N_CORES)]
    return np.concatenate(outs, axis=0)
```

```python
import numpy as np
import concourse.bass as bass
import concourse.mybir as mybir

F32 = mybir.dt.float32
BF16 = mybir.dt.bfloat16
I32 = mybir.dt.int32
AF = mybir.ActivationFunctionType
ALU = mybir.AluOpType
AX = mybir.AxisListType

SEM_LIMIT = 30000


class T:
    __slots__ = ("name", "w", "r")

    def __init__(self, name=""):
        self.name = name
        self.w = None
        self.r = {}


class V:
    __slots__ = ("ts", "ap", "x")

    def __init__(self, ts, ap, x=False):
        self.ts = ts
        self.ap = ap
        self.x = x


class Buf:
    def __init__(self, tensor, name, nsplit=None, sdim=1):
        self.t = tensor
        self.name = name
        self.nsplit = nsplit
        self.sdim = sdim
        if nsplit is None:
            self.ts = [T(name)]
        else:
            self.ts = [T(f"{name}{i}") for i in range(nsplit)]

    def __getitem__(self, idx):
        ap = self.t[idx]
        if self.nsplit is None:
            return V(self.ts, ap)
        if not isinstance(idx, tuple):
            idx = (idx,)
        if len(idx) <= self.sdim:
            return V(self.ts, ap)
        s = idx[self.sdim]
        if isinstance(s, int):
            return V([self.ts[s]], ap)
        if isinstance(s, slice):
            st, sp, _ = s.indices(self.nsplit)
            return V(self.ts[st:sp], ap)
        return V(self.ts, ap)

    def v(self, ap, which=None):
        if which is None:
            return V(self.ts, ap)
        return V([self.ts[i] for i in which], ap)


class Sync:
    def __init__(self, nc, ndma=24):
        self.nc = nc
        self.engs = {"pe": nc.tensor, "act": nc.scalar, "dve": nc.vector,
                     "pool": nc.gpsimd, "sp": nc.sync}
        self.semh = {}
        self.cur = {}
        self.cnt = {}
        self.nsem = 0
        for k in self.engs:
            self._new_sem(k)
        self.seen = {k: {} for k in self.engs}
        self.ndma = ndma
        self.dma_key = []
        self.dma_val = []
        for i in range(ndma):
            key = self._alloc(f"dma{i}")
            self.dma_key.append(key)
            self.dma_val.append(0)
        self.dma_rr = 0
        self.ninst = {k: 0 for k in self.engs}

    def _alloc(self, name):
        key = f"{name}_{self.nsem}"
        self.nsem += 1
        self.semh[key] = self.nc.alloc_semaphore(name=key)
        return key

    def _new_sem(self, ek):
        self.cur[ek] = self._alloc(f"e_{ek}")
        self.cnt[ek] = 0

    def _deps(self, outs, ins):
        deps = {}

        def add(ev):
            if ev is None:
                return
            k, val = ev
            if deps.get(k, 0) < val:
                deps[k] = val
        for v in ins:
            for t in v.ts:
                add(t.w)
        for v in outs:
            for t in v.ts:
                add(t.w)
                for k, val in t.r.items():
                    add((k, val))
        return deps

    def _wait(self, ek, deps):
        eng = self.engs[ek]
        seen = self.seen[ek]
        for k, val in deps.items():
            if ek == "pe" and k.startswith("e_pe"):
                continue
            if seen.get(k, 0) < val:
                eng.wait_ge(self.semh[k], val)
                seen[k] = val
                self.ninst[ek] += 1

    def _mark(self, ev, outs, ins):
        k, val = ev
        for v in ins:
            for t in v.ts:
                if t.r.get(k, 0) < val:
                    t.r[k] = val
        for v in outs:
            for t in v.ts:
                t.w = ev
                t.r = {}

    def op(self, ek, fn, outs, ins):
        deps = self._deps(outs, ins)
        own = self.cur[ek]
        for v in ins:
            if v.x:
                for t in v.ts:
                    for k, val in t.r.items():
                        if k != own and deps.get(k, 0) < val:
                            deps[k] = val
        self._wait(ek, deps)
        if self.cnt[ek] >= SEM_LIMIT:
            self._new_sem(ek)
        inst = fn(self.engs[ek])
        self.cnt[ek] += 1
        inst.then_inc(self.semh[self.cur[ek]], 1)
        self.ninst[ek] += 1
        self._mark((self.cur[ek], self.cnt[ek]), outs, ins)

    def dma(self, ek, out, in_, **kw):
        i = self.dma_rr
        self.dma_rr = (self.dma_rr + 1) % self.ndma
        if self.dma_val[i] >= SEM_LIMIT:
            self.dma_key[i] = self._alloc(f"dma{i}")
            self.dma_val[i] = 0
        deps = self._deps([out], [in_])
        key = self.dma_key[i]
        if self.dma_val[i] > 0:
            if deps.get(key, 0) < self.dma_val[i]:
                deps[key] = self.dma_val[i]
        self._wait(ek, deps)
        inst = self.engs[ek].dma_start(out=out.ap, in_=in_.ap, **kw)
        self.dma_val[i] += 16
        inst.then_inc(self.semh[key], 16)
        self.ninst[ek] += 1
        self._mark((key, self.dma_val[i]), [out], [in_])

    def wait_all(self, ek, views):
        self._wait(ek, self._deps([], views))

    def mm(self, out, lhsT, rhs, start=True, stop=True, **kw):
        self.op("pe", lambda e: e.matmul(out.ap, lhsT.ap, rhs.ap, start=start, stop=stop, **kw),
                [out], [lhsT, rhs] + ([] if start else [out]))

    def tr(self, out, in_, ident):
        self.op("pe", lambda e: e.transpose(out.ap, in_.ap, ident.ap), [out], [in_, ident])

    def act(self, out, in_, func, bias=None, scale=1.0, ek="act", accum=None):
        ins = [in_]
        kw = {}
        if bias is not None:
            if isinstance(bias, V):
                ins.append(bias)
                kw["bias"] = bias.ap
            else:
                kw["bias"] = bias
        if isinstance(scale, V):
            ins.append(scale)
            kw["scale"] = scale.ap
        else:
            kw["scale"] = scale
        outs = [out]
        if accum is not None:
            outs.append(accum)
            kw["accum_out"] = accum.ap
        self.op(ek, lambda e: e.activation(out.ap, in_.ap, func, **kw), outs, ins)

    def tt(self, out, a, b, op, ek="dve"):
        self.op(ek, lambda e: e.tensor_tensor(out.ap, a.ap, b.ap, op), [out], [a, b])

    def ts(self, out, a, s1, op0, s2=None, op1=None, ek="dve"):
        ins = [a]
        x1 = s1
        if isinstance(s1, V):
            ins.append(s1)
            x1 = s1.ap
        x2 = s2
        if isinstance(s2, V):
            ins.append(s2)
            x2 = s2.ap
        if op1 is None:
            self.op(ek, lambda e: e.tensor_scalar(out.ap, a.ap, x1, None, op0), [out], ins)
        else:
            self.op(ek, lambda e: e.tensor_scalar(out.ap, a.ap, x1, x2, op0, op1), [out], ins)

    def stt(self, out, a, s, b, op0, op1):
        ins = [a, b]
        x = s
        if isinstance(s, V):
            ins.append(s)
            x = s.ap
        self.op("dve", lambda e: e.scalar_tensor_tensor(out.ap, a.ap, x, b.ap, op0, op1), [out], ins)

    def copy(self, out, in_, ek="dve"):
        self.op(ek, lambda e: e.tensor_copy(out.ap, in_.ap), [out], [in_])

    def memset(self, out, val, ek="pool"):
        self.op(ek, lambda e: e.memset(out.ap, val), [out], [])

    def reduce(self, out, in_, op, axis, ek="dve"):
        self.op(ek, lambda e: e.tensor_reduce(out.ap, in_.ap, axis, op), [out], [in_])

from contextlib import ExitStack

NB = 2
D = 1024
TL = 2048
TC = 256
NIN = 4480
DFF = 2816
NORM_EPS = 1e-6
import os
EVAC_ACT_ONLY = bool(int(os.environ.get('EVAC_ACT_ONLY', '0')))
GN_EPS = 64e-5
SC = -float(np.exp(-0.5))


class Scope:
    uid = 0

    def __init__(self, nc):
        self.nc = nc
        self.es = ExitStack()
        self.n = 0

    def sb(self, name, shape, dt, nsplit=None, sdim=1):
        Scope.uid += 1
        t = self.es.enter_context(self.nc.sbuf_tensor(f"s{Scope.uid}_{name}", shape, dt))
        return Buf(t, name, nsplit, sdim)

    def close(self):
        self.es.close()


class PBank:
    def __init__(self, nc, name, dt, width):
        self.t = nc.alloc_psum_tensor(name, [128, width], dt)
        self.blk = width
        self.ts = [T(f"{name}{i}") for i in range(max(1, width // self.blk))]

    def c(self, a, b, rows=slice(None)):
        q0 = a // self.blk
        q1 = (b - 1) // self.blk
        return V(self.ts[q0:q1 + 1], self.t[rows, a:b], True)

    def v(self, ap):
        return V(self.ts, ap, True)


def barrier(S):
    for ek, eng in S.engs.items():
        deps = {}
        for k2 in S.engs:
            if S.cnt[k2] > 0:
                deps[S.cur[k2]] = S.cnt[k2]
        for i in range(S.ndma):
            if S.dma_val[i] > 0:
                deps[S.dma_key[i]] = S.dma_val[i]
        seen = S.seen[ek]
        for k, val in deps.items():
            if seen.get(k, 0) < val:
                eng.wait_ge(S.semh[k], val)
                seen[k] = val


def host_consts():
    c = {}
    i = np.arange(128)
    row = i[:, None]
    col = i[None, :]
    SL = (col < row).astype(np.float32)
    SU = (col > row).astype(np.float32)
    IL = (col <= row).astype(np.float32)
    IU = (col >= row).astype(np.float32)
    mk = np.zeros((128, 4, 256), np.float32)
    mk[:, 0, :128] = SL; mk[:, 0, 128:] = SL
    mk[:, 1, :128] = SU; mk[:, 1, 128:] = IU
    mk[:, 2, :128] = SU; mk[:, 2, 128:] = SU
    mk[:, 3, :128] = SL; mk[:, 3, 128:] = IL
    c["mkp"] = mk
    BDm = np.zeros((128, 128), np.float32)
    BDm[:64, :64] = 1.0
    BDm[64:, 64:] = 1.0
    mk2 = np.zeros((128, 2, 5, 128), np.float32)
    mk2[:, 0, 0] = SL * BDm; mk2[:, 0, 1] = SL; mk2[:, 0, 2] = SU * BDm; mk2[:, 0, 3] = IU; mk2[:, 0, 4] = SU * (1 - BDm)
    mk2[:, 1, 0] = SU * BDm; mk2[:, 1, 1] = SU; mk2[:, 1, 2] = SL * BDm; mk2[:, 1, 3] = IL; mk2[:, 1, 4] = SL * (1 - BDm)
    c["mk2"] = mk2
    c["identf"] = np.eye(128, dtype=np.float32)
    th = 2 * np.pi * np.outer(i, i) / 128.0
    cs = np.zeros((128, 256), np.float32)
    cs[:, :128] = np.cos(th) / np.sqrt(128.0)
    cs[:, 128:] = np.sin(th) / np.sqrt(128.0)
    c["cs128"] = cs
    j = np.arange(64)
    th64 = 2 * np.pi * np.outer(j, j) / 64.0
    C64 = np.cos(th64) / 8.0
    S64 = np.sin(th64) / 8.0
    bd = np.zeros((128, 3, 128), np.float32)
    for r in range(2):
        bd[r * 64:(r + 1) * 64, 0, r * 64:(r + 1) * 64] = C64
        bd[r * 64:(r + 1) * 64, 1, r * 64:(r + 1) * 64] = S64
        bd[r * 64:(r + 1) * 64, 2, r * 64:(r + 1) * 64] = -S64
    c["bd64"] = bd
    k32 = np.arange(32)
    th32 = 2 * np.pi * np.outer(k32, k32) / 32.0
    c32 = np.zeros((32, 2, 32), np.float32)
    c32[:, 0, :] = np.cos(th32) / np.sqrt(32.0)
    c32[:, 1, :] = -np.sin(th32) / np.sqrt(32.0)
    c["c32"] = c32
    ones_bd = np.zeros((128, 128), np.float32)
    ones_bd[:64, :64] = 1.0
    ones_bd[64:, 64:] = 1.0
    c["onesbd"] = ones_bd
    e2 = np.zeros((128, 2), np.float32)
    e2[:64, 0] = 1.0
    e2[64:, 1] = 1.0
    c["e2"] = e2
    c["ones"] = np.ones((128, 128), np.float32)
    return c


CONST_SHAPES = {"mk2": [128, 2, 5, 128], "mkp": [128, 4, 256], "identf": [128, 128], "cs128": [128, 256], "bd64": [128, 3, 128],
                "c32": [32, 2, 32], "onesbd": [128, 128], "e2": [128, 2], "ones": [128, 128]}

IN_SHAPES = {
    "x": [NB, TL, D], "ctx": [NB, TC, D], "cT": [128, 8, 3], "b_adaT": [128, 48], "b_ada_row": [1, 6144],
    "n1g": [128, 8], "n2g": [128, 8], "fng_row": [1, D],
    "w_ada": [D, 6144], "w_in": [D, NIN], "mu3": [128, 3, 15],
    "w2": [128, 512], "a2": [128, 512], "g2": [128, 512],
    "pvec": [128, 9, 4],
    "lnx_row": [1, 1024],
    "w_up_r": [512, D], "w_up_f": [512, D], "w_out": [D, D], "w_gu": [D, 2 * DFF], "w_down": [DFF, D],
}
IN_SHAPES.update(CONST_SHAPES)


class StopBuild(Exception):
    pass


def build(debug=(), stop_after=None):
    nc = bass.Bass("TRN2", target_bir_lowering=False)
    S = Sync(nc)
    try:
        return _build(nc, S, debug, stop_after)
    except StopBuild:
        barrier(S)
        return nc, S


def _build(nc, S, debug, stop_after):
    def ck(name):
        if stop_after == name:
            raise StopBuild()
    dr = {}
    for name, shp in IN_SHAPES.items():
        dr[name] = Buf(nc.dram_tensor(name, shp, F32, kind="ExternalInput"), name)
    out = Buf(nc.dram_tensor("out", [NB, TL, D], F32, kind="ExternalOutput"), "out", nsplit=NB, sdim=0)

    def scratch(name, shape, nsplit=None, sdim=0):
        kind = "ExternalOutput" if name in debug else "Internal"
        return Buf(nc.dram_tensor(name, shape, F32, kind=kind), name, nsplit, sdim)

    U_lat = [scratch(f"U_lat{b}", [15, 128, TL], 15, 0) for b in range(NB)]
    U_ctx = [scratch(f"U_ctx{b}", [15, 128, TC], 15, 0) for b in range(NB)]
    SG = [scratch(f"SG{b}", [16, 128, TL], 16, 0) for b in range(NB)]
    DD = [scratch(f"DD{b}", [2, TL, 512], 2, 0) for b in range(NB)]
    YD = [[scratch(f"YD{b}_{d}", [TL, 512]) for d in range(2)] for b in range(NB)]
    VT = [scratch(f"VT{b}", [TL, 512]) for b in range(NB)]
    GT = [scratch(f"GT{b}", [TL, 512]) for b in range(NB)]
    BON = [[scratch(f"BON{b}_{d}", [TL, 8]) for d in range(2)] for b in range(NB)]
    X1 = [scratch(f"X1_{b}", [TL, D]) for b in range(NB)]
    MODROW = scratch("MODROW", [3, 2, 1024])

    if os.environ.get("PTFIRST") == "1":
        PT = PBank(nc, "pst", BF16, 1024)
        PS = [PBank(nc, f"ps{i}", F32, 512) for i in range(7)]
    else:
        PS = [PBank(nc, f"ps{i}", F32, 512) for i in range(7)]
        PT = PBank(nc, "pst", BF16, 1024)

    G = Scope(nc)
    cst = {}

    def load_const(sc, name):
        shp = CONST_SHAPES[name]
        cst[name] = sc.sb("c_" + name, shp, F32)
        S.dma("sp", cst[name].v(cst[name].t[tuple(slice(None) for _ in shp)]),
              dr[name].v(dr[name].t[tuple(slice(None) for _ in shp)]))
        return cst[name]

    load_const(G, "identf")
    identf = cst["identf"]
    identb = G.sb("identb", [128, 128], BF16)
    S.copy(identb[:, :], identf[:, :])
    mu3 = G.sb("mu3", [128, 3, 15], F32)
    S.dma("sp", mu3[:, :, :], dr["mu3"][:, :, :])
    c0 = G.sb("c0", [128, 15], F32)
    S.tt(c0[:, :], mu3[:, 0, :], mu3[:, 1, :], ALU.add)
    S.ts(c0[:, :], c0[:, :], -1.0, ALU.mult, 1.0, ALU.add)
    pvec = G.sb("pvec", [128, 9, 4], F32)
    S.dma("sp", pvec[:, :, :], dr["pvec"][:, :, :])
    S.ts(pvec[:, 2, :], pvec[:, 1, :], -1.0, ALU.mult, 1.0, ALU.add)
    PK_KK, PK_KA, PK_OMKA, PK_RK, PK_W0, PK_A0 = 0, 1, 2, 3, 4, 6
    n1g = G.sb("n1g", [128, 8], F32)
    n2g = G.sb("n2g", [128, 8], F32)
    S.dma("sp", n1g[:, :], dr["n1g"][:, :])
    S.dma("sp", n2g[:, :], dr["n2g"][:, :])
    smallw = {}
    for name in ("w2", "a2", "g2"):
        smallw[name] = G.sb("bf_" + name, [128, 512], BF16)
    W2w, A2w, G2w = smallw["w2"], smallw["a2"], smallw["g2"]

    if stop_after == "G":
        barrier(S)
        return nc, S
    modT = G.sb("modT", [128, 48, 3], F32)
    A1 = G.sb("A1", [128, 8, 3], F32)
    A2 = G.sb("A2", [128, 8, 3], F32)
    M = Scope(nc)
    garow = M.sb("garow", [3, 2, 1024], F32)
    stg_small = [M.sb(f"stg_small{i}", [128, 512], F32) for i in range(3)]
    for i, name in enumerate(("w2", "a2", "g2")):
        S.dma("sp", stg_small[i][:, :], dr[name][:, :])
        S.copy(smallw[name][:, :], stg_small[i][:, :])
    cT = M.sb("cT", [128, 8, 3], F32)
    sT = M.sb("sT", [128, 8, 3], F32)
    badaT = M.sb("badaT", [128, 48], F32)
    brow = M.sb("brow", [3, 6144], F32)
    S.dma("sp", cT[:, :, :], dr["cT"][:, :, :])
    S.dma("sp", badaT[:, :], dr["b_adaT"][:, :])
    S.dma("sp", brow[:, :], dr["b_ada_row"].v(dr["b_ada_row"].t.ap().partition_broadcast(3)))
    S.act(sT[:, :, :], cT[:, :, :], AF.Silu)
    wa = [M.sb(f"wa{i}", [128, 8, 1024], F32) for i in range(2)]
    w_ada_v = dr["w_ada"].t.ap().rearrange("(k p) n -> p k n", p=128)
    for sec in range(6):
        wb = wa[sec % 2]
        S.dma("sp" if sec % 2 == 0 else "pool", wb[:, :, :], dr["w_ada"].v(w_ada_v[:, :, sec * 1024:(sec + 1) * 1024]))
        for nn in range(8):
            n = sec * 8 + nn
            o = PS[0].c(n * 3, n * 3 + 3)
            for k in range(8):
                S.mm(o, wb[:, k, nn * 128:(nn + 1) * 128], sT[:, k, :], start=(k == 0), stop=(k == 7))
        if sec in (2, 5):
            gi_ = 0 if sec == 2 else 1
            for half in range(2):
                o = PS[1].c(0, 512, rows=slice(0, 3))
                for k in range(8):
                    S.mm(o, sT[:, k, :], wb[:, k, half * 512:(half + 1) * 512], start=(k == 0), stop=(k == 7))
                S.tt(garow[:, gi_, half * 512:(half + 1) * 512], o,
                     brow[:, sec * 1024 + half * 512: sec * 1024 + (half + 1) * 512], ALU.add)
    S.tt(modT[:, :, :], PS[0].v(PS[0].t[:, 0:144].rearrange("p (n j) -> p n j", j=3)),
         V(badaT.ts, badaT.t[:, :].unsqueeze(2).broadcast_to([128, 48, 3])), ALU.add)
    S.dma("sp", MODROW[:, :, :], garow[:, :, :])
    S.ts(A1[:, :, :], modT[:, 8:16, :], 1.0, ALU.add)
    S.tt(A1[:, :, :], A1[:, :, :], V(n1g.ts, n1g.t[:, :].unsqueeze(2).broadcast_to([128, 8, 3])), ALU.mult)
    S.ts(A2[:, :, :], modT[:, 32:40, :], 1.0, ALU.add)
    S.tt(A2[:, :, :], A2[:, :, :], V(n2g.ts, n2g.t[:, :].unsqueeze(2).broadcast_to([128, 8, 3])), ALU.mult)
    barrier(S)
    M.close()
    if stop_after == "M":
        if "modT" in debug:
            dbgm = nc.dram_tensor("dbg_modT", [128, 48, 3], F32, kind="ExternalOutput")
            S.dma("sp", V([T("x")], dbgm[:, :, :]), modT[:, :, :])
        barrier(S)
        return nc, S

    def nt_bufs(sc, tag, nbuf=2):
        xt = [sc.sb(f"xt{tag}{i}", [128, 1024], F32) for i in range(nbuf)]
        xn = [sc.sb(f"xn{tag}{i}", [128, 1024], BF16) for i in range(nbuf)]
        junk = sc.sb(f"junk{tag}", [128, 1024], BF16)
        st = [sc.sb(f"st{tag}{i}", [128, 4], F32) for i in range(2)]
        return xt, xn, junk, st

    def norm_transpose(bufs, src_buf, src_rows_fn, ntiles, Aap, Bfn, j, hT, col0):
        xt, xn, junk, st = bufs
        for i in range(ntiles):
            if i == 1:
                ck("A0b")
            if i == 3:
                ck("A0c")
            x_t = xt[i % len(xt)]
            x_n = xn[i % len(xn)]
            s_ = st[i % 2]
            S.dma("sp", x_t[:, :], src_buf.v(src_rows_fn(i)))
            ck("A0a")
            S.act(junk[:, :], x_t[:, :], AF.Square, accum=s_[:, 0:1])
            S.act(s_[:, 1:2], s_[:, 0:1], AF.Sqrt, scale=1.0 / D, bias=eps_t[:, 0:1])
            S.op("dve", lambda e: e.reciprocal(s_.t[:, 2:3], s_.t[:, 1:2]), [s_[:, 2:3]], [s_[:, 1:2]])
            S.act(x_n[:, :], x_t[:, :], AF.Copy, scale=s_[:, 2:3])
            ck("A0a2")
            for k in range(8):
                S.tr(PT.c(k * 128, (k + 1) * 128), x_n[:, k * 128:(k + 1) * 128], identb[:, :])
            ck("A0a3")
            for k in range(8):
                o = hT[:, k, col0 + i * 128: col0 + (i + 1) * 128]
                if k == 1:
                    ck("A0a4")
                if os.environ.get("DBGV") == "1":
                    o = junk[:, 0:128]
                if os.environ.get("DBGV") == "2":
                    S.act(o, PT.c(k * 128, (k + 1) * 128), AF.Identity, scale=0.5, bias=eps_t[:, 0:1])
                    continue
                if os.environ.get("DBGV") == "3":
                    S.act(o, PT.c(k * 128, (k + 1) * 128), AF.Copy)
                    continue
                if k == 2:
                    ck("A0a5")
                if k % 2 == 0 and not EVAC_ACT_ONLY:
                    S.ts(o, PT.c(k * 128, (k + 1) * 128), Aap[:, k, j:j + 1], ALU.mult, Bfn(k), ALU.add)
                else:
                    S.act(o, PT.c(k * 128, (k + 1) * 128), AF.Identity, scale=Aap[:, k, j:j + 1],
                          bias=Bfn(k))

    eps_t = G.sb("eps_t", [128, 2], F32)
    S.memset(eps_t[:, 0:1], NORM_EPS)
    S.memset(eps_t[:, 1:2], GN_EPS)

    def load_weight_bf16(sc, dst, dram_buf, kchunks, c0_, c1_, stg=None):
        v = dram_buf.t.ap().rearrange("(k p) n -> p k n", p=128)
        for k in range(kchunks):
            S.dma("pool", dst[:, k, 0:c1_ - c0_], dram_buf.v(v[:, k, c0_:c1_]), max_dma_last_dim=4096)

    Bsh1 = modT

    def phase_A(b):
        A = Scope(nc)
        load_const(A, "cs128")
        load_const(A, "bd64")
        stg = None
        hT = A.sb("hT", [128, 8, TL + 2], BF16)
        hTc = A.sb("hTc", [128, 8, TC + 2], BF16)
        Win = A.sb("Win", [128, 8, 2432], BF16)
        S.memset(hT[:, :, 0:1], 0.0)
        S.memset(hT[:, :, TL + 1:TL + 2], 0.0)
        S.memset(hTc[:, :, 0:1], 0.0)
        S.memset(hTc[:, :, TC + 1:TC + 2], 0.0)
        xv = dr["x"].t
        cv = dr["ctx"].t
        ck("A0")
        ntb = nt_bufs(A, "A")
        norm_transpose(ntb, dr["x"], lambda i: xv[b, i * 128:(i + 1) * 128, :], TL // 128, A1,
                       lambda k: modT[:, k, b:b + 1], b, hT, 1)
        norm_transpose(ntb, dr["ctx"], lambda i: cv[b, i * 128:(i + 1) * 128, :], TC // 128, A1,
                       lambda k: modT[:, k, 2:3], 2, hTc, 1)
        ck("A1")
        pb = [A.sb(f"pb{i}", [128, 514], F32) for i in range(2)]
        ub = [A.sb(f"ub{i}", [128, 512], F32) for i in range(3)]
        uf = A.sb("uf", [128, 4, 512], F32, nsplit=4)
        Abuf = [A.sb(f"Abuf{i}", [128, 4, 256], F32) for i in range(2)]
        Dout = [A.sb(f"Dout{i}", [128, 2, 512], F32) for i in range(2)]
        cnt = [0]

        def gemm(hbuf, T_, TT, n_lo, n_hi, wofs, Udst):
            for t0 in range(0, T_, TT):
                for n in range(n_lo, n_hi):
                    q = cnt[0]
                    cnt[0] += 1
                    bank = PS[q % 2]
                    o = bank.c(0, TT)
                    for k in range(8):
                        S.mm(o, Win[:, k, (n - wofs) * 128:(n - wofs + 1) * 128], hbuf[:, k, 1 + t0:1 + t0 + TT],
                             start=(k == 0), stop=(k == 7))
                    if n == 1:
                        ck("A2")
                    if n == 16:
                        ck("A3")
                    if n < 15:
                        oh = PS[2 + q % 2].c(0, 2)
                        for k in range(8):
                            S.mm(oh, Win[:, k, (n - wofs) * 128:(n - wofs + 1) * 128],
                                 hbuf[:, k, t0:t0 + TT + 2:TT + 1], start=(k == 0), stop=(k == 7))
                        p_ = pb[q % 2]
                        u_ = ub[q % 3]
                        S.act(p_[:, 1:TT + 1], o, AF.Copy)
                        S.copy(p_[:, 0:TT + 2:TT + 1], oh)
                        S.act(u_[:, 0:TT], p_[:, 1:TT + 1], AF.Copy, scale=c0[:, n:n + 1])
                        S.stt(u_[:, 0:TT], p_[:, 0:TT], mu3[:, 0, n:n + 1], u_[:, 0:TT], ALU.mult, ALU.add)
                        S.stt(u_[:, 0:TT], p_[:, 2:TT + 2], mu3[:, 1, n:n + 1], u_[:, 0:TT], ALU.mult, ALU.add)
                        S.dma("pool", Udst[n, :, t0:t0 + TT], u_[:, 0:TT])
                    elif n < 19:
                        S.act(uf[:, n - 15, 0:TT], o, AF.Copy)
                    else:
                        u_ = ub[q % 3]
                        S.act(u_[:, 0:TT], o, AF.Sigmoid)
                        S.dma("pool", SG[b][n - 19, :, t0:t0 + TT], u_[:, 0:TT])
                if n_lo <= 15 and n_hi >= 19 and T_ == TL:
                    for ch in range(4):
                        ab = Abuf[ch % 2]
                        do = Dout[ch % 2]
                        for g in range(4):
                            bank = PS[4 + (g // 2)]
                            S.mm(bank.c((g % 2) * 256, (g % 2) * 256 + 256), uf[:, g, ch * 128:(ch + 1) * 128],
                                 cst["cs128"][:, :])
                        S.act(ab[:, 0:2, :], PS[4].v(PS[4].t[:, :].rearrange("p (g c) -> p g c", c=256)), AF.Copy)
                        S.copy(ab[:, 2:4, :], PS[5].v(PS[5].t[:, :].rearrange("p (g c) -> p g c", c=256)))
                        Ac = ab[:, :, 0:128]
                        As = ab[:, :, 128:256]
                        d1 = PS[6].c(0, 512)
                        S.mm(d1, cst["bd64"][:, 0, :], Ac, start=True, stop=False)
                        S.mm(d1, cst["bd64"][:, 2, :], As, start=False, stop=True)
                        S.act(do[:, 0, :], d1, AF.Copy)
                        d2 = PS[6].c(0, 512)
                        S.mm(d2, cst["bd64"][:, 0, :], As, start=True, stop=False)
                        S.mm(d2, cst["bd64"][:, 1, :], Ac, start=False, stop=True)
                        S.copy(do[:, 1, :], d2)
                        tt0 = t0 + ch * 128
                        S.dma("pool", DD[b].v(DD[b].t.ap()[:, tt0:tt0 + 128, :].rearrange("a t c -> t a c")),
                              do[:, :, :])

        load_weight_bf16(A, Win, dr["w_in"], 8, 0, 2432, stg)
        ck("A1b")
        gemm(hT, TL, 512, 0, 19, 0, U_lat[b])
        ck("A4")
        gemm(hTc, TC, 256, 0, 15, 0, U_ctx[b])
        load_weight_bf16(A, Win, dr["w_in"], 8, 2432, 4480, stg)
        gemm(hT, TL, 512, 19, 35, 19, None)
        barrier(S)
        A.close()

    for b in range(NB):
        phase_A(b)
    if stop_after == "A":
        barrier(S)
        return nc, S


    def pv(row, m):
        return pvec[:, row, m:m + 1]

    def phase_B():
        B = Scope(nc)
        mk2 = load_const(B, "mk2")
        load_const(B, "onesbd")
        load_const(B, "e2")
        load_const(B, "ones")
        uc = [B.sb(f"uc{i}", [128, 15, 128], F32) for i in range(2)]
        Hf = [B.sb(f"Hf{d}", [128, NB * 4 * 64], F32) for d in range(2)]
        Hb = [B.sb(f"Hb{d}", [128, NB, 4, 64], BF16) for d in range(2)]
        FM = [B.sb(f"FM{d}", [128, NB, 4, 4, 128], BF16, nsplit=NB, sdim=1) for d in range(2)]
        TOK = [B.sb(f"TOK{d}", [128, NB, 4, 3, 128], BF16, nsplit=NB, sdim=1) for d in range(2)]
        VM = [B.sb(f"VM{d}", [128, NB, 512], BF16, nsplit=NB, sdim=1) for d in range(2)]
        GC = [B.sb(f"GC{d}", [128, NB, 4], F32) for d in range(2)]
        tmp = {}
        for nm in ("kkraw", "sq", "rn", "kk", "a", "sg", "cf", "incl", "excl", "gi", "ge", "ginv", "kap",
                   "beta", "t1"):
            tmp[nm] = B.sb("t_" + nm, [128, 4, 128], F32)
        adb = B.sb("adb", [128, 128], BF16)
        twd = B.sb("twd", [128, 128], BF16)
        sgd = B.sb("sgd", [128, 128], BF16)
        vtok = [B.sb(f"vtok{i}", [128, 512], F32) for i in range(2)]
        gtok = [B.sb(f"gtok{i}", [128, 512], F32) for i in range(2)]
        bont = [B.sb(f"bont{i}", [128, 8], F32) for i in range(2)]
        NSLOT = 8
        FB = [B.sb(f"FB{i}", [128, 576], F32) for i in range(NSLOT)]
        PPf = [B.sb(f"PPf{i}", [128, 256], F32) for i in range(NSLOT)]
        TTf = [B.sb(f"TTf{i}", [128, 128], F32) for i in range(NSLOT)]
        Yf_ = [B.sb(f"Yf{i}", [128, 192], BF16) for i in range(NSLOT)]
        Xf_ = [B.sb(f"Xf{i}", [128, 192], BF16) for i in range(NSLOT)]
        Zb_ = [B.sb(f"Zb{i}", [128, 192], BF16) for i in range(NSLOT)]
        Ltb_ = [B.sb(f"Ltb{i}", [128, 128], BF16) for i in range(NSLOT)]
        TTb_ = [B.sb(f"TTb{i}", [128, 128], BF16) for i in range(NSLOT)]
        ARB2 = [[B.sb(f"ARB{p_}_{i}", [128, 128], BF16) for i in range(NSLOT)] for p_ in range(2)]
        ARK2 = [[B.sb(f"ARK{p_}_{i}", [128, 128], BF16) for i in range(NSLOT)] for p_ in range(2)]
        Gb2 = [[B.sb(f"Gb{p_}_{i}", [128, 128], BF16) for i in range(NSLOT)] for p_ in range(2)]
        GTs2 = [[B.sb(f"GTs{p_}_{i}", [128, 128], BF16) for i in range(NSLOT)] for p_ in range(2)]
        WP2 = [[B.sb(f"WP{p_}_{i}", [128, 128], BF16) for i in range(NSLOT // 2)] for p_ in range(2)]
        WTs2 = [[B.sb(f"WTs{p_}_{i}", [128, 128], BF16) for i in range(NSLOT // 2)] for p_ in range(2)]
        Ub = B.sb("Ub", [128, NB, 512], BF16, nsplit=NB, sdim=1)
        Ysb = [B.sb(f"Ysb{i}", [128, 512], F32) for i in range(2)]
        htmp = B.sb("htmp", [128, NB * 4 * 64], F32)
        for d in range(2):
            S.memset(Hf[d][:, :], 0.0)
            S.memset(Hb[d][:, :, :, :], 0.0)
        ctr = [0]

        def prep(d, b, Usrc, t0, latent):
            q = ctr[0]
            ctr[0] += 1
            U = uc[q % 2]
            S.dma("sp", U[:, :, :], Usrc.v(Usrc.t.ap()[:, :, t0:t0 + 128].rearrange("n p t -> p n t")))
            Pd = slice(d * 64, d * 64 + 64)
            r = U[:, 0:4, :]
            k = U[:, 4:8, :]
            t = tmp
            for m in range(4):
                S.ts(t["kkraw"][:, m, :], U[:, 4 + m, :], pv(PK_KK, m), ALU.mult, ek="pool")
            S.tt(t["sq"][:, :, :], t["kkraw"][:, :, :], t["kkraw"][:, :, :], ALU.mult, ek="pool")
            ssp = PS[5].c(0, 512)
            S.mm(ssp, cst["onesbd"][:, :], t["sq"][:, :, :])
            S.ts(t["rn"][:, :, :], PS[5].v(PS[5].t[:, :].rearrange("p (m t) -> p m t", t=128)), 1e-24, ALU.max)
            S.act(t["rn"][:, :, :], t["rn"][:, :, :], AF.Ln)
            S.act(t["rn"][:, :, :], t["rn"][:, :, :], AF.Exp, scale=-0.5)
            S.tt(t["kk"][:, :, :], t["kkraw"][:, :, :], t["rn"][:, :, :], ALU.mult)
            ck("B1a")
            yield
            S.copy(adb[Pd, :], U[Pd, 13, :], ek="pool")
            for m in range(4):
                S.mm(PS[5].c(m * 128, (m + 1) * 128), A2w[Pd, m * 128:(m + 1) * 128], adb[Pd, :])
            for m in range(4):
                S.act(t["a"][:, m, :], PS[5].c(m * 128, (m + 1) * 128), AF.Sigmoid, bias=pv(PK_A0 + d, m))
            yield
            S.act(twd[Pd, :], U[Pd, 12, :], AF.Tanh)
            for m in range(4):
                S.mm(PS[5].c(m * 128, (m + 1) * 128), W2w[Pd, m * 128:(m + 1) * 128], twd[Pd, :])
            for m in range(4):
                S.act(t["sg"][:, m, :], PS[5].c(m * 128, (m + 1) * 128), AF.Sigmoid, bias=pv(PK_W0 + d, m))
            ck("B1b")
            yield
            for m in range(4):
                S.op("dve", lambda e: e.tensor_tensor_scan(t["cf"].t[:, m, :], cst["ones"].t[:, :], t["sg"].t[:, m, :],
                                                           0.0, ALU.mult, ALU.add),
                     [t["cf"][:, m, :]], [cst["ones"][:, :], t["sg"][:, m, :]])
            ck("B1c")
            yield
            if d == 0:
                incl = t["cf"]
                S.tt(t["excl"][:, :, :], t["cf"][:, :, :], t["sg"][:, :, :], ALU.subtract)
                excl = t["excl"]
            else:
                for m in range(4):
                    S.ts(t["excl"][:, m, :], t["cf"][:, m, :], -1.0, ALU.mult, t["cf"][:, m, 127:128], ALU.add)
                S.tt(t["incl"][:, :, :], t["excl"][:, :, :], t["sg"][:, :, :], ALU.add)
                incl = t["incl"]
                excl = t["excl"]
            S.act(t["gi"][:, :, :], incl[:, :, :], AF.Exp, scale=SC)
            S.act(t["ge"][:, :, :], excl[:, :, :], AF.Exp, scale=SC)
            S.act(t["ginv"][:, :, :], incl[:, :, :], AF.Exp, scale=-SC)
            S.act(GC[d][:, b, :], t["cf"][:, :, 127], AF.Exp, scale=SC)
            yield
            for m in range(4):
                S.ts(t["t1"][:, m, :], t["a"][:, m, :], pv(PK_KA, m), ALU.mult, pv(PK_OMKA, m), ALU.add, ek="pool")
            S.tt(t["kap"][:, :, :], k, t["t1"][:, :, :], ALU.mult, ek="pool")
            S.tt(t["beta"][:, :, :], t["kk"][:, :, :], t["a"][:, :, :], ALU.mult, ek="pool")
            fm = FM[d]
            S.tt(t["sq"][:, :, :], t["kk"][:, :, :], t["ge"][:, :, :], ALU.mult)
            S.act(fm[:, b, :, 0, :], t["sq"][:, :, :], AF.Copy, scale=-1.0)
            S.tt(fm[:, b, :, 1, :], r, t["gi"][:, :, :], ALU.mult)
            S.tt(fm[:, b, :, 2, :], t["beta"][:, :, :], t["ginv"][:, :, :], ALU.mult)
            S.tt(fm[:, b, :, 3, :], t["kap"][:, :, :], t["ginv"][:, :, :], ALU.mult)
            ck("B1d")
            yield
            for m in range(4):
                for xi, x in enumerate((0, 2, 3)):
                    S.tr(PT.c(xi * 128, (xi + 1) * 128), fm[:, b, m, x, :], identb[:, :])
                S.act(TOK[d][:, b, m, :, :], PT.v(PT.t[:, 0:384].rearrange("p (x c) -> p x c", c=128)),
                      AF.Copy)
                yield
            ck("B1e")
            yield
            for m in range(4):
                S.tr(PS[5].c(m * 128, (m + 1) * 128), U[:, 8 + m, :], identf[:, :])
            S.act(VM[d][:, b, :], PS[5].c(0, 512), AF.Copy)
            yield
            if latent:
                if d == 0:
                    vt = vtok[(q // 2) % 2]
                    S.copy(vt[:, :], PS[5].c(0, 512))
                    S.dma("pool", VT[b][t0:t0 + 128, :], vt[:, :])
                    S.act(sgd[:, :], U[:, 14, :], AF.Sigmoid)
                    S.mm(PS[5].c(0, 512), sgd[:, :], G2w[:, :])
                    gt = gtok[(q // 2) % 2]
                    S.act(gt[:, :], PS[5].c(0, 512), AF.Copy)
                    S.dma("pool", GT[b][t0:t0 + 128, :], gt[:, :])
                yield
                S.tt(t["t1"][:, :, :], r, t["kap"][:, :, :], ALU.mult, ek="pool")
                for m in range(4):
                    S.ts(t["t1"][:, m, :], t["t1"][:, m, :], pv(PK_RK, m), ALU.mult, ek="pool")
                for m in range(4):
                    S.mm(PS[5].c(m * 2, m * 2 + 2), t["t1"][:, m, :], cst["e2"][:, :])
                bt = bont[q % 2]
                S.copy(bt[:, :], PS[5].c(0, 8))
                S.dma("pool", BON[b][d][t0:t0 + 128, :], bt[:, :])

        def head_chain(d, b, h, bank, par):
            ARB, ARK, Gb, WP = ARB2[par], ARK2[par], Gb2[par], WP2[par]
            fm = FM[d]
            m, hl = divmod(h, 2)
            P = slice(hl * 64, hl * 64 + 64)
            sl = h
            fb, ppf, ttf, yf_, xf_ = FB[sl], PPf[sl], TTf[sl], Yf_[sl], Xf_[sl]
            zb, ltb, ttb = Zb_[sl], Ltb_[sl], TTb_[sl]
            S.mm(bank.c(0, 256), fm[P, b, m, 0, :], fm[P, b, m, 2:4, :])
            S.mm(bank.c(256, 512), fm[P, b, m, 2, :], fm[P, b, m, 0:2, :])
            yield
            S.tt(fb[:, 0:128], bank.c(0, 128), mk2[:, d, 0, :], ALU.mult)
            S.tt(zb[:, 0:128], bank.c(128, 256), mk2[:, d, 1, :], ALU.mult)
            S.tt(fb[:, 128:256], bank.c(256, 384), mk2[:, d, 2, :], ALU.mult)
            S.tt(ltb[:, :], bank.c(256, 384), mk2[:, d, 4, :], ALU.mult)
            S.tt(ARB[sl][:, :], bank.c(384, 512), mk2[:, d, 3, :], ALU.mult)
            S.copy(zb[:, 128:192], TOK[d][:, b, m, 0, hl * 64:(hl + 1) * 64], ek="pool")
            S.tt(ttf[:, :], fb[:, 128:256], identf[:, :], ALU.add, ek="pool")
            yield
            Pn, Pt_ = fb[:, 0:128], fb[:, 128:256]
            S.mm(bank.c(0, 128), Pt_, Pn)
            S.mm(bank.c(128, 256), Pn, Pt_)
            S.mm(bank.c(384, 512), fm[P, b, m, 3, :], fm[P, b, m, 1, :])
            yield
            S.act(ppf[:, :], bank.c(0, 256), AF.Copy)
            S.tt(ARK[sl][:, :], bank.c(384, 512), mk2[:, d, 3, :], ALU.mult)
            yield
            for lev in range(1, 6):
                Pn, Pt_ = ppf[:, 0:128], ppf[:, 128:256]
                S.mm(bank.c(256, 384), Pn, ttf[:, :])
                if lev < 5:
                    S.mm(bank.c(0, 128), Pt_, Pn)
                    S.mm(bank.c(128, 256), Pn, Pt_)
                yield
                S.tt(ttf[:, :], bank.c(256, 384), ttf[:, :], ALU.add)
                if lev < 5:
                    S.act(ppf[:, :], bank.c(0, 256), AF.Copy)
                yield
            S.act(ttb[:, :], ttf[:, :], AF.Copy)
            yield
            S.mm(bank.c(0, 192), ttb[:, :], zb[:, :])
            yield
            S.act(yf_[:, :], bank.c(0, 192), AF.Copy)
            yield
            S.mm(bank.c(256, 448), ltb[:, :], yf_[:, :])
            yield
            S.copy(xf_[:, :], bank.c(256, 448))
            yield
            S.mm(bank.c(0, 192), ttb[:, :], xf_[:, :])
            yield
            S.tt(Gb[sl][:, :], bank.c(0, 128), yf_[:, 0:128], ALU.add)
            S.tt(WP[sl // 2][:, hl * 64:(hl + 1) * 64], bank.c(128, 192), yf_[:, 128:192], ALU.add)
            yield

        def chunk_math(d, b, par, extras):
            ck("B1")
            NCH = 5
            pending = list(range(8))
            active = [(e_, None) for e_ in extras if e_ is not None]
            free_banks = [PS[i] for i in range(NCH)]
            while pending or active:
                while pending and free_banks:
                    h = pending.pop(0)
                    bk = free_banks.pop(0)
                    active.append((head_chain(d, b, h, bk, par), bk))
                nxt = []
                for g, bk in active:
                    try:
                        next(g)
                        nxt.append((g, bk))
                    except StopIteration:
                        if bk is not None:
                            free_banks.append(bk)
                active = nxt

        def state_part(d, b, latent, t0, par):
            ARB, ARK, Gb, GTs, WP, WTs = ARB2[par], ARK2[par], Gb2[par], GTs2[par], WP2[par], WTs2[par]
            fm = FM[d]
            for h4 in range(2):
                for q_ in range(4):
                    S.tr(PT.c(384 + q_ * 128, 512 + q_ * 128), Gb[h4 * 4 + q_][:, :], identb[:, :])
                yield
                for q_ in range(4):
                    hh = h4 * 4 + q_
                    if hh % 2 == 0:
                        S.act(GTs[hh][:, :], PT.c(384 + q_ * 128, 512 + q_ * 128), AF.Copy)
                    else:
                        S.copy(GTs[hh][:, :], PT.c(384 + q_ * 128, 512 + q_ * 128))
                yield
            for pr in range(4):
                S.tr(PT.c(384 + pr * 128, 512 + pr * 128), WP[pr][:, :], identb[:, :])
            yield
            for pr in range(4):
                S.copy(WTs[pr][:, :], PT.c(384 + pr * 128, 512 + pr * 128))
            yield
            psU = PS[6].c(0, 512)
            for h in range(8):
                m, hl = divmod(h, 2)
                P = slice(hl * 64, hl * 64 + 64)
                o = PS[6].c(h * 64, (h + 1) * 64)
                S.mm(o, WTs[h // 2][P, :], Hb[d][P, b, m, :], start=True, stop=False)
                S.mm(o, GTs[h][:, :], VM[d][:, b, h * 64:(h + 1) * 64], start=False, stop=True)
            yield
            S.act(Ub[:, b, :], psU, AF.Copy)
            yield
            if latent:
                for h in range(8):
                    m, hl = divmod(h, 2)
                    P = slice(hl * 64, hl * 64 + 64)
                    o = PS[6].c(h * 64, (h + 1) * 64)
                    S.mm(o, fm[P, b, m, 1, :], Hb[d][P, b, m, :], start=True, stop=False)
                    S.mm(o, ARB[h][:, :], Ub[:, b, h * 64:(h + 1) * 64], start=False, stop=False)
                    S.mm(o, ARK[h][:, :], VM[d][:, b, h * 64:(h + 1) * 64], start=False, stop=True)
                yield
                ys = Ysb[(b + d) % 2]
                S.act(ys[:, :], PS[6].c(0, 512), AF.Copy)
                S.dma("pool", YD[b][d][t0:t0 + 128, :], ys[:, :])
                yield
            for h in range(8):
                m, hl = divmod(h, 2)
                P = slice(hl * 64, hl * 64 + 64)
                o = PS[6].v(PS[6].t[P, (b * 4 + m) * 64:(b * 4 + m + 1) * 64])
                S.mm(o, TOK[d][:, b, m, 1, hl * 64:(hl + 1) * 64], Ub[:, b, h * 64:(h + 1) * 64], start=True, stop=False)
                S.mm(o, TOK[d][:, b, m, 2, hl * 64:(hl + 1) * 64], VM[d][:, b, h * 64:(h + 1) * 64], start=False, stop=True)
            yield
            cs_ = slice(b * 256, (b + 1) * 256)
            S.tt(htmp[:, cs_], PS[6].v(PS[6].t[:, cs_]), Hf[d][:, cs_], ALU.add)
            S.tt(V(Hf[d].ts, Hf[d].t[:, cs_].rearrange("p (m v) -> p m v", v=64)),
                 V(htmp.ts, htmp.t[:, cs_].rearrange("p (m v) -> p m v", v=64)),
                 V(GC[d].ts, GC[d].t[:, b, :].unsqueeze(2).broadcast_to([128, 4, 64])), ALU.mult)
            S.copy(Hb[d][:, b, :, :], V(Hf[d].ts, Hf[d].t[:, cs_].rearrange("p (m v) -> p m v", v=64)), ek="pool")
            yield

        nsteps = 2 + TL // 128
        units = []
        for s_ in range(nsteps):
            for d in range(2):
                for b in range(NB):
                    if s_ < 2:
                        ci = s_ if d == 0 else 1 - s_
                        units.append((d, b, U_ctx[b], ci * 128, False))
                    else:
                        li = s_ - 2
                        ci = li if d == 0 else (TL // 128 - 1 - li)
                        units.append((d, b, U_lat[b], ci * 128, True))
        for _ in prep(*units[0]):
            pass
        prev_state = None
        for i, (d, b, Usrc, t0, latent) in enumerate(units):
            nxt_prep = prep(*units[i + 1]) if i + 1 < len(units) else None
            chunk_math(d, b, i % 2, [prev_state, nxt_prep])
            prev_state = state_part(d, b, latent, t0, i % 2)
        for _ in prev_state:
            pass
        barrier(S)
        B.close()

    phase_B()
    if stop_after == "B":
        return nc, S

    def phase_C(b):
        C = Scope(nc)
        load_const(C, "c32")
        lnx = C.sb("lnx", [128, 1024], F32)
        S.dma("sp", lnx[:, :], dr["lnx_row"].v(dr["lnx_row"].t.ap().partition_broadcast(128)))
        stg = None
        fT = C.sb("fT", [128, 4, TL], BF16)
        Wupf = C.sb("Wupf", [128, 4, D], BF16)
        Wupr = C.sb("Wupr", [128, 4, D], BF16)
        Wout = C.sb("Wout", [128, 8, D], BF16)
        load_weight_bf16(C, Wupf, dr["w_up_f"], 4, 0, D, stg)
        load_weight_bf16(C, Wupr, dr["w_up_r"], 4, 0, D, stg)
        load_weight_bf16(C, Wout, dr["w_out"], 8, 0, D, stg)
        ga1b = C.sb("ga1b", [128, D], F32)
        S.dma("sp", ga1b[:, :], MODROW.v(MODROW.t.ap()[b:b + 1, 0, :].partition_broadcast(128)))
        dd = [C.sb(f"dd{i}", [32, 2, 4, 512], F32) for i in range(2)]
        ddv = DD[b].t.ap().rearrange("a (r c) k -> r a c k", c=64)
        c32 = cst["c32"]
        for cb in range(16):
            dt_ = dd[cb % 2]
            S.dma("sp", dt_[:, :, :, :], DD[b].v(ddv[:, :, cb * 4:(cb + 1) * 4, :]))
            bank = PS[cb % 2]
            for cl in range(4):
                for g in range(4):
                    o = bank.c((g * 4 + cl) * 32, (g * 4 + cl) * 32 + 32)
                    S.mm(o, dt_[:, 0, cl, g * 128:(g + 1) * 128], c32[:, 0, :], start=True, stop=False)
                    S.mm(o, dt_[:, 1, cl, g * 128:(g + 1) * 128], c32[:, 1, :], start=False, stop=True)
            for g in range(4):
                src = bank.v(bank.t[:, g * 128:(g + 1) * 128].rearrange("p (c r) -> p r c", r=32))
                dst = V(fT.ts, fT.t[:, g, :].rearrange("p (r c) -> p r c", c=64)[:, :, cb * 4:(cb + 1) * 4])
                if g % 2 == 0:
                    S.act(dst, src, AF.Copy)
                else:
                    S.copy(dst, src)
        yf = [C.sb(f"yf{i}", [128, 512], F32) for i in range(2)]
        yb = [C.sb(f"yb{i}", [128, 512], F32) for i in range(2)]
        vt = [C.sb(f"vt{i}", [128, 512], F32) for i in range(2)]
        gt = [C.sb(f"gt{i}", [128, 512], F32) for i in range(2)]
        bo = [C.sb(f"bo{i}", [128, 2, 8], F32) for i in range(2)]
        ysq = C.sb("ysq", [128, 512], F32)
        stt_ = [C.sb(f"stC{i}", [128, 6, 8], F32) for i in range(2)]
        obf = C.sb("obf", [128, 512], BF16)
        oT = C.sb("oT", [128, 4, 512], BF16)
        gsb = [C.sb(f"gsb{i}", [128, 2, 512], F32) for i in range(2)]
        mT = C.sb("mT", [128, 8, 512], BF16, nsplit=8)
        mtmp = [C.sb(f"mtmp{i}", [128, 512], F32) for i in range(2)]
        xin = [C.sb(f"xin{i}", [128, D], F32) for i in range(2)]
        xo = [C.sb(f"xo{i}", [128, D], F32) for i in range(2)]

        def b3(v_, n):
            return V(v_.ts, v_.ap.unsqueeze(2).broadcast_to([128, 8, n]))

        def v3(buf):
            return V(buf.ts, buf.t[:, :].rearrange("p (h v) -> p h v", v=64))

        for tt in range(TL // 512):
            for sub in range(4):
                i = tt * 4 + sub
                t0 = i * 128
                y_, yb_, v_, g_, bo_, st_ = yf[i % 2], yb[i % 2], vt[i % 2], gt[i % 2], bo[i % 2], stt_[i % 2]
                S.dma("sp", y_[:, :], YD[b][0][t0:t0 + 128, :])
                S.dma("sp", yb_[:, :], YD[b][1][t0:t0 + 128, :])
                S.dma("sp", v_[:, :], VT[b][t0:t0 + 128, :])
                S.dma("sp", g_[:, :], GT[b][t0:t0 + 128, :])
                S.dma("sp", bo_[:, 0, :], BON[b][0][t0:t0 + 128, :])
                S.dma("sp", bo_[:, 1, :], BON[b][1][t0:t0 + 128, :])
                S.tt(y_[:, :], y_[:, :], yb_[:, :], ALU.add)
                S.reduce(st_[:, 0, :], v3(y_), ALU.add, AX.X)
                S.tt(ysq[:, :], y_[:, :], y_[:, :], ALU.mult, ek="pool")
                S.reduce(st_[:, 1, :], v3(ysq), ALU.add, AX.X)
                S.ts(st_[:, 2, :], st_[:, 0, :], 1.0 / 64, ALU.mult)
                S.tt(st_[:, 3, :], st_[:, 2, :], st_[:, 2, :], ALU.mult)
                S.stt(st_[:, 3, :], st_[:, 1, :], 1.0 / 64, st_[:, 3, :], ALU.mult, ALU.subtract)
                S.act(st_[:, 4, :], st_[:, 3, :], AF.Sqrt, bias=eps_t[:, 1:2])
                S.op("dve", lambda e: e.reciprocal(st_.t[:, 5, :], st_.t[:, 4, :]), [st_[:, 5, :]], [st_[:, 4, :]])
                S.tt(v3(y_), v3(y_), b3(st_[:, 2, :], 64), ALU.subtract)
                S.tt(v3(y_), v3(y_), b3(st_[:, 5, :], 64), ALU.mult)
                S.tt(y_[:, :], y_[:, :], lnx[:, 0:512], ALU.mult)
                S.tt(y_[:, :], y_[:, :], lnx[:, 512:1024], ALU.add, ek="pool")
                S.tt(bo_[:, 0, :], bo_[:, 0, :], bo_[:, 1, :], ALU.add, ek="pool")
                S.tt(v3(v_), v3(v_), b3(bo_[:, 0, :], 64), ALU.mult)
                S.tt(y_[:, :], y_[:, :], v_[:, :], ALU.add, ek="pool")
                S.tt(obf[:, :], y_[:, :], g_[:, :], ALU.mult)
                for kc in range(4):
                    S.tr(PT.c(kc * 128, (kc + 1) * 128), obf[:, kc * 128:(kc + 1) * 128], identb[:, :])
                S.act(oT[:, :, sub * 128:(sub + 1) * 128],
                      PT.v(PT.t[:, 0:512].rearrange("p (k t) -> p k t", t=128)), AF.Copy)
            T0 = tt * 512
            for n in range(8):
                gs = gsb[n % 2]
                S.dma("sp", gs[:, 0, :], SG[b][n, :, T0:T0 + 512])
                S.dma("sp", gs[:, 1, :], SG[b][8 + n, :, T0:T0 + 512])
                pf = PS[2].c(0, 512)
                pr = PS[3].c(0, 512)
                for kc in range(4):
                    S.mm(pf, Wupf[:, kc, n * 128:(n + 1) * 128], fT[:, kc, T0:T0 + 512], start=(kc == 0), stop=(kc == 3))
                for kc in range(4):
                    S.mm(pr, Wupr[:, kc, n * 128:(n + 1) * 128], oT[:, kc, :], start=(kc == 0), stop=(kc == 3))
                mt = mtmp[n % 2]
                S.tt(mt[:, :], pf, gs[:, 0, :], ALU.mult)
                S.tt(gs[:, 1, :], pr, gs[:, 1, :], ALU.mult)
                S.tt(mT[:, n, :], mt[:, :], gs[:, 1, :], ALU.add, ek="pool")
            for sub in range(4):
                i = tt * 4 + sub
                t0 = i * 128
                xi, xo_ = xin[i % 2], xo[i % 2]
                S.dma("sp", xi[:, :], dr["x"].v(dr["x"].t[b, t0:t0 + 128, :]))
                for half in range(2):
                    o = PS[4 + half].c(0, 512)
                    for n in range(8):
                        S.mm(o, mT[:, n, sub * 128:(sub + 1) * 128], Wout[:, n, half * 512:(half + 1) * 512],
                             start=(n == 0), stop=(n == 7))
                    hs = slice(half * 512, (half + 1) * 512)
                    S.tt(xo_[:, hs], o, ga1b[:, hs], ALU.mult)
                    S.tt(xo_[:, hs], xo_[:, hs], xi[:, hs], ALU.add, ek="pool")
                S.dma("pool", X1[b][t0:t0 + 128, :], xo_[:, :])
        barrier(S)
        C.close()

    for b in range(NB):
        phase_C(b)
    if stop_after == "C":
        return nc, S

    def phase_D():
        Dd = Scope(nc)
        fng = Dd.sb("fng", [128, 1024], F32)
        S.dma("sp", fng[:, :], dr["fng_row"].v(dr["fng_row"].t.ap().partition_broadcast(128)))
        Wgu = Dd.sb("Wgu", [128, 8, 2 * DFF], BF16)
        Wdn = Dd.sb("Wdn", [128, 22, D], BF16)
        load_weight_bf16(Dd, Wgu, dr["w_gu"], 8, 0, 2 * DFF)
        load_weight_bf16(Dd, Wdn, dr["w_down"], 22, 0, D)
        TD = 256
        hT2 = Dd.sb("hT2", [128, 8, TD], BF16)
        ntb = nt_bufs(Dd, "D", nbuf=1)
        actT = Dd.sb("actT", [128, 22, TD], BF16, nsplit=22)
        sil = [Dd.sb(f"sil{i}", [128, TD], F32) for i in range(2)]
        ga2b = Dd.sb("ga2b", [128, D], F32)
        x1t = [Dd.sb(f"x1t{i}", [128, D], F32) for i in range(1)]
        x2t = [Dd.sb(f"x2t{i}", [128, D], F32) for i in range(2)]
        junk2 = ntb[2]
        stf = [Dd.sb(f"stf{i}", [128, 4], F32) for i in range(2)]
        for b in range(NB):
            S.dma("sp", ga2b[:, :], MODROW.v(MODROW.t.ap()[b:b + 1, 1, :].partition_broadcast(128)))
            for tt in range(TL // TD):
                T0 = tt * TD
                x1v = X1[b].t
                norm_transpose(ntb, X1[b], lambda i: x1v[T0 + i * 128:T0 + (i + 1) * 128, :], TD // 128, A2,
                               lambda k: modT[:, 24 + k, b:b + 1], b, hT2, 0)
                for fc in range(22):
                    pg = PS[fc % 2].c(0, TD)
                    pu = PS[2 + fc % 2].c(0, TD)
                    for k in range(8):
                        S.mm(pg, Wgu[:, k, fc * 128:(fc + 1) * 128], hT2[:, k, :], start=(k == 0), stop=(k == 7))
                    for k in range(8):
                        S.mm(pu, Wgu[:, k, DFF + fc * 128:DFF + (fc + 1) * 128], hT2[:, k, :], start=(k == 0),
                             stop=(k == 7))
                    sl_ = sil[fc % 2]
                    S.act(sl_[:, :], pg, AF.Silu)
                    S.tt(actT[:, fc, :], pu, sl_[:, :], ALU.mult)
                for sub in range(TD // 128):
                    i = tt * (TD // 128) + sub
                    t0 = T0 + sub * 128
                    x1_, x2_, sf = x1t[0], x2t[i % 2], stf[i % 2]
                    S.dma("sp", x1_[:, :], X1[b][t0:t0 + 128, :])
                    for half in range(2):
                        o = PS[4 + half].c(0, 512)
                        for fc in range(22):
                            S.mm(o, actT[:, fc, sub * 128:(sub + 1) * 128], Wdn[:, fc, half * 512:(half + 1) * 512],
                                 start=(fc == 0), stop=(fc == 21))
                        hs = slice(half * 512, (half + 1) * 512)
                        S.tt(x2_[:, hs], o, ga2b[:, hs], ALU.mult)
                        S.tt(x2_[:, hs], x2_[:, hs], x1_[:, hs], ALU.add, ek="pool")
                    S.act(junk2[:, :], x2_[:, :], AF.Square, accum=sf[:, 0:1])
                    S.act(sf[:, 1:2], sf[:, 0:1], AF.Sqrt, scale=1.0 / D, bias=eps_t[:, 0:1])
                    S.op("dve", lambda e: e.reciprocal(sf.t[:, 2:3], sf.t[:, 1:2]), [sf[:, 2:3]], [sf[:, 1:2]])
                    S.act(x2_[:, :], x2_[:, :], AF.Copy, scale=sf[:, 2:3])
                    S.tt(x2_[:, :], x2_[:, :], fng[:, :], ALU.mult)
                    S.dma("pool", out[b, t0:t0 + 128, :], x2_[:, :])
        barrier(S)
        Dd.close()

    phase_D()
    barrier(S)
    return nc, S

from concourse.bass_utils import run_bass_kernel_spmd

N_CORES = 8
_CACHE = {}


def _fm(vec, nchunk):
    return np.ascontiguousarray(np.asarray(vec, np.float32).reshape(nchunk, 128).T)


def make_in_maps(inp, cores):
    consts = host_consts()
    f = lambda a: np.ascontiguousarray(np.asarray(a, np.float32))
    shared = {
        "b_adaT": _fm(inp["b_ada"][0], 48), "b_ada_row": f(inp["b_ada"][0]).reshape(1, 6144),
        "n1g": _fm(inp["norm1_g"][0], 8), "n2g": _fm(inp["norm2_g"][0], 8),
        "fng_row": f(inp["final_norm_g"]).reshape(1, D),
        "w_ada": f(inp["w_ada"][0]), "w_in": f(inp["w_in"][0]),
        "w2": np.concatenate([f(inp["w2_f"][0]), f(inp["w2_b"][0])], 0),
        "a2": np.concatenate([f(inp["a2_f"][0]), f(inp["a2_b"][0])], 0),
        "g2": f(inp["g2"][0]),
        "lnx_row": np.concatenate([f(inp["lnx_g"][0]), f(inp["lnx_b"][0])]).reshape(1, 1024),
        "w_up_r": f(inp["w_up_r"][0]), "w_up_f": f(inp["w_up_f"][0]), "w_out": f(inp["w_out"][0]),
        "w_gu": f(inp["w_gu"][0]), "w_down": f(inp["w_down"][0]),
    }
    mu3 = np.zeros((128, 3, 15), np.float32)
    mu3[:, 0, :] = _fm(inp["mu_prev"][0], 15)
    mu3[:, 1, :] = _fm(inp["mu_next"][0], 15)
    shared["mu3"] = mu3
    pvec = np.zeros((128, 9, 4), np.float32)
    for i, nm in ((0, "k_k"), (1, "k_a"), (3, "r_k"), (4, "w0_f"), (5, "w0_b"), (6, "a0_f"), (7, "a0_b")):
        pvec[:, i, :] = _fm(np.asarray(inp[nm][0]).reshape(512), 4)
    shared["pvec"] = pvec
    shared.update(consts)
    maps = []
    for c in cores:
        m = dict(shared)
        m["x"] = f(inp["x"][NB * c:NB * (c + 1)])
        m["ctx"] = f(inp["ctx"][NB * c:NB * (c + 1)])
        cT = np.zeros((128, 8, 3), np.float32)
        for j in range(NB):
            cT[:, :, j] = _fm(inp["c"][NB * c + j], 8)
        cT[:, :, 2] = _fm(inp["c_ctx"], 8)
        m["cT"] = cT
        maps.append(m)
    return maps


def kernel(**inputs):
    if "nc" not in _CACHE:
        _CACHE["nc"] = build()[0]
    nc = _CACHE["nc"]
    maps = make_in_maps(inputs, list(range(N_CORES)))
    res = run_bass_kernel_spmd(nc, maps, core_ids=list(range(N_CORES)))
    outs = [np.asarray(res.results[c]["out"], np.float32) for c in range(N_CORES)]
    return np.concatenate(outs, axis=0)
```

```python
import numpy as np
import concourse.bass as bass
import concourse.mybir as mybir

F32 = mybir.dt.float32
BF16 = mybir.dt.bfloat16
I32 = mybir.dt.int32
AF = mybir.ActivationFunctionType
ALU = mybir.AluOpType
AX = mybir.AxisListType

SEM_LIMIT = 30000


class T:
    __slots__ = ("name", "w", "r")

    def __init__(self, name=""):
        self.name = name
        self.w = None
        self.r = {}


class V:
    __slots__ = ("ts", "ap", "x")

    def __init__(self, ts, ap, x=False):
        self.ts = ts
        self.ap = ap
        self.x = x


class Buf:
    def __init__(self, tensor, name, nsplit=None, sdim=1):
        self.t = tensor
        self.name = name
        self.nsplit = nsplit
        self.sdim = sdim
        if nsplit is None:
            self.ts = [T(name)]
        else:
            self.ts = [T(f"{name}{i}") for i in range(nsplit)]

    def __getitem__(self, idx):
        ap = self.t[idx]
        if self.nsplit is None:
            return V(self.ts, ap)
        if not isinstance(idx, tuple):
            idx = (idx,)
        if len(idx) <= self.sdim:
            return V(self.ts, ap)
        s = idx[self.sdim]
        if isinstance(s, int):
            return V([self.ts[s]], ap)
        if isinstance(s, slice):
            st, sp, _ = s.indices(self.nsplit)
            return V(self.ts[st:sp], ap)
        return V(self.ts, ap)

    def v(self, ap, which=None):
        if which is None:
            return V(self.ts, ap)
        return V([self.ts[i] for i in which], ap)


class Sync:
    def __init__(self, nc, ndma=24):
        self.nc = nc
        self.engs = {"pe": nc.tensor, "act": nc.scalar, "dve": nc.vector,
                     "pool": nc.gpsimd, "sp": nc.sync}
        self.semh = {}
        self.cur = {}
        self.cnt = {}
        self.nsem = 0
        for k in self.engs:
            self._new_sem(k)
        self.seen = {k: {} for k in self.engs}
        self.ndma = ndma
        self.dma_key = []
        self.dma_val = []
        for i in range(ndma):
            key = self._alloc(f"dma{i}")
            self.dma_key.append(key)
            self.dma_val.append(0)
        self.dma_rr = 0
        self.ninst = {k: 0 for k in self.engs}

    def _alloc(self, name):
        key = f"{name}_{self.nsem}"
        self.nsem += 1
        self.semh[key] = self.nc.alloc_semaphore(name=key)
        return key

    def _new_sem(self, ek):
        self.cur[ek] = self._alloc(f"e_{ek}")
        self.cnt[ek] = 0

    def _deps(self, outs, ins):
        deps = {}

        def add(ev):
            if ev is None:
                return
            k, val = ev
            if deps.get(k, 0) < val:
                deps[k] = val
        for v in ins:
            for t in v.ts:
                add(t.w)
        for v in outs:
            for t in v.ts:
                add(t.w)
                for k, val in t.r.items():
                    add((k, val))
        return deps

    def _wait(self, ek, deps):
        eng = self.engs[ek]
        seen = self.seen[ek]
        for k, val in deps.items():
            if ek == "pe" and k.startswith("e_pe"):
                continue
            if seen.get(k, 0) < val:
                eng.wait_ge(self.semh[k], val)
                seen[k] = val
                self.ninst[ek] += 1

    def _mark(self, ev, outs, ins):
        k, val = ev
        for v in ins:
            for t in v.ts:
                if t.r.get(k, 0) < val:
                    t.r[k] = val
        for v in outs:
            for t in v.ts:
                t.w = ev
                t.r = {}

    def op(self, ek, fn, outs, ins):
        deps = self._deps(outs, ins)
        own = self.cur[ek]
        for v in ins:
            if v.x:
                for t in v.ts:
                    for k, val in t.r.items():
                        if k != own and deps.get(k, 0) < val:
                            deps[k] = val
        self._wait(ek, deps)
        if self.cnt[ek] >= SEM_LIMIT:
            self._new_sem(ek)
        inst = fn(self.engs[ek])
        self.cnt[ek] += 1
        inst.then_inc(self.semh[self.cur[ek]], 1)
        self.ninst[ek] += 1
        self._mark((self.cur[ek], self.cnt[ek]), outs, ins)

    def dma(self, ek, out, in_, **kw):
        i = self.dma_rr
        self.dma_rr = (self.dma_rr + 1) % self.ndma
        if self.dma_val[i] >= SEM_LIMIT:
            self.dma_key[i] = self._alloc(f"dma{i}")
            self.dma_val[i] = 0
        deps = self._deps([out], [in_])
        key = self.dma_key[i]
        if self.dma_val[i] > 0:
            if deps.get(key, 0) < self.dma_val[i]:
                deps[key] = self.dma_val[i]
        self._wait(ek, deps)
        inst = self.engs[ek].dma_start(out=out.ap, in_=in_.ap, **kw)
        self.dma_val[i] += 16
        inst.then_inc(self.semh[key], 16)
        self.ninst[ek] += 1
        self._mark((key, self.dma_val[i]), [out], [in_])

    def wait_all(self, ek, views):
        self._wait(ek, self._deps([], views))

    def mm(self, out, lhsT, rhs, start=True, stop=True, **kw):
        self.op("pe", lambda e: e.matmul(out.ap, lhsT.ap, rhs.ap, start=start, stop=stop, **kw),
                [out], [lhsT, rhs] + ([] if start else [out]))

    def tr(self, out, in_, ident):
        self.op("pe", lambda e: e.transpose(out.ap, in_.ap, ident.ap), [out], [in_, ident])

    def act(self, out, in_, func, bias=None, scale=1.0, ek="act", accum=None):
        ins = [in_]
        kw = {}
        if bias is not None:
            if isinstance(bias, V):
                ins.append(bias)
                kw["bias"] = bias.ap
            else:
                kw["bias"] = bias
        if isinstance(scale, V):
            ins.append(scale)
            kw["scale"] = scale.ap
        else:
            kw["scale"] = scale
        outs = [out]
        if accum is not None:
            outs.append(accum)
            kw["accum_out"] = accum.ap
        self.op(ek, lambda e: e.activation(out.ap, in_.ap, func, **kw), outs, ins)

    def tt(self, out, a, b, op, ek="dve"):
        self.op(ek, lambda e: e.tensor_tensor(out.ap, a.ap, b.ap, op), [out], [a, b])

    def ts(self, out, a, s1, op0, s2=None, op1=None, ek="dve"):
        ins = [a]
        x1 = s1
        if isinstance(s1, V):
            ins.append(s1)
            x1 = s1.ap
        x2 = s2
        if isinstance(s2, V):
            ins.append(s2)
            x2 = s2.ap
        if op1 is None:
            self.op(ek, lambda e: e.tensor_scalar(out.ap, a.ap, x1, None, op0), [out], ins)
        else:
            self.op(ek, lambda e: e.tensor_scalar(out.ap, a.ap, x1, x2, op0, op1), [out], ins)

    def stt(self, out, a, s, b, op0, op1):
        ins = [a, b]
        x = s
        if isinstance(s, V):
            ins.append(s)
            x = s.ap
        self.op("dve", lambda e: e.scalar_tensor_tensor(out.ap, a.ap, x, b.ap, op0, op1), [out], ins)

    def copy(self, out, in_, ek="dve"):
        self.op(ek, lambda e: e.tensor_copy(out.ap, in_.ap), [out], [in_])

    def memset(self, out, val, ek="pool"):
        self.op(ek, lambda e: e.memset(out.ap, val), [out], [])

    def reduce(self, out, in_, op, axis, ek="dve"):
        self.op(ek, lambda e: e.tensor_reduce(out.ap, in_.ap, axis, op), [out], [in_])

from contextlib import ExitStack

NB = 2
D = 1024
TL = 2048
TC = 256
NIN = 4480
DFF = 2816
NORM_EPS = 1e-6
import os
EVAC_ACT_ONLY = bool(int(os.environ.get('EVAC_ACT_ONLY', '0')))
GN_EPS = 64e-5
SC = -float(np.exp(-0.5))


class Scope:
    uid = 0

    def __init__(self, nc):
        self.nc = nc
        self.es = ExitStack()
        self.n = 0

    def sb(self, name, shape, dt, nsplit=None, sdim=1):
        Scope.uid += 1
        t = self.es.enter_context(self.nc.sbuf_tensor(f"s{Scope.uid}_{name}", shape, dt))
        return Buf(t, name, nsplit, sdim)

    def close(self):
        self.es.close()


class PBank:
    def __init__(self, nc, name, dt, width):
        self.t = nc.alloc_psum_tensor(name, [128, width], dt)
        self.blk = width
        self.ts = [T(f"{name}{i}") for i in range(max(1, width // self.blk))]

    def c(self, a, b, rows=slice(None)):
        q0 = a // self.blk
        q1 = (b - 1) // self.blk
        return V(self.ts[q0:q1 + 1], self.t[rows, a:b], True)

    def v(self, ap):
        return V(self.ts, ap, True)


def barrier(S):
    for ek, eng in S.engs.items():
        deps = {}
        for k2 in S.engs:
            if S.cnt[k2] > 0:
                deps[S.cur[k2]] = S.cnt[k2]
        for i in range(S.ndma):
            if S.dma_val[i] > 0:
                deps[S.dma_key[i]] = S.dma_val[i]
        seen = S.seen[ek]
        for k, val in deps.items():
            if seen.get(k, 0) < val:
                eng.wait_ge(S.semh[k], val)
                seen[k] = val


def host_consts():
    c = {}
    i = np.arange(128)
    row = i[:, None]
    col = i[None, :]
    SL = (col < row).astype(np.float32)
    SU = (col > row).astype(np.float32)
    IL = (col <= row).astype(np.float32)
    IU = (col >= row).astype(np.float32)
    mk = np.zeros((128, 4, 256), np.float32)
    mk[:, 0, :128] = SL; mk[:, 0, 128:] = SL
    mk[:, 1, :128] = SU; mk[:, 1, 128:] = IU
    mk[:, 2, :128] = SU; mk[:, 2, 128:] = SU
    mk[:, 3, :128] = SL; mk[:, 3, 128:] = IL
    c["mkp"] = mk
    BDm = np.zeros((128, 128), np.float32)
    BDm[:64, :64] = 1.0
    BDm[64:, 64:] = 1.0
    mk2 = np.zeros((128, 2, 5, 128), np.float32)
    mk2[:, 0, 0] = SL * BDm; mk2[:, 0, 1] = SL; mk2[:, 0, 2] = SU * BDm; mk2[:, 0, 3] = IU; mk2[:, 0, 4] = SU * (1 - BDm)
    mk2[:, 1, 0] = SU * BDm; mk2[:, 1, 1] = SU; mk2[:, 1, 2] = SL * BDm; mk2[:, 1, 3] = IL; mk2[:, 1, 4] = SL * (1 - BDm)
    c["mk2"] = mk2
    c["identf"] = np.eye(128, dtype=np.float32)
    th = 2 * np.pi * np.outer(i, i) / 128.0
    cs = np.zeros((128, 256), np.float32)
    cs[:, :128] = np.cos(th) / np.sqrt(128.0)
    cs[:, 128:] = np.sin(th) / np.sqrt(128.0)
    c["cs128"] = cs
    j = np.arange(64)
    th64 = 2 * np.pi * np.outer(j, j) / 64.0
    C64 = np.cos(th64) / 8.0
    S64 = np.sin(th64) / 8.0
    bd = np.zeros((128, 3, 128), np.float32)
    for r in range(2):
        bd[r * 64:(r + 1) * 64, 0, r * 64:(r + 1) * 64] = C64
        bd[r * 64:(r + 1) * 64, 1, r * 64:(r + 1) * 64] = S64
        bd[r * 64:(r + 1) * 64, 2, r * 64:(r + 1) * 64] = -S64
    c["bd64"] = bd
    k32 = np.arange(32)
    th32 = 2 * np.pi * np.outer(k32, k32) / 32.0
    c32 = np.zeros((32, 2, 32), np.float32)
    c32[:, 0, :] = np.cos(th32) / np.sqrt(32.0)
    c32[:, 1, :] = -np.sin(th32) / np.sqrt(32.0)
    c["c32"] = c32
    ones_bd = np.zeros((128, 128), np.float32)
    ones_bd[:64, :64] = 1.0
    ones_bd[64:, 64:] = 1.0
    c["onesbd"] = ones_bd
    e2 = np.zeros((128, 2), np.float32)
    e2[:64, 0] = 1.0
    e2[64:, 1] = 1.0
    c["e2"] = e2
    c["ones"] = np.ones((128, 128), np.float32)
    return c


CONST_SHAPES = {"mk2": [128, 2, 5, 128], "mkp": [128, 4, 256], "identf": [128, 128], "cs128": [128, 256], "bd64": [128, 3, 128],
                "c32": [32, 2, 32], "onesbd": [128, 128], "e2": [128, 2], "ones": [128, 128]}

IN_SHAPES = {
    "x": [NB, TL, D], "ctx": [NB, TC, D], "cT": [128, 8, 3], "b_adaT": [128, 48], "b_ada_row": [1, 6144],
    "n1g": [128, 8], "n2g": [128, 8], "fng_row": [1, D],
    "w_ada": [D, 6144], "w_in": [D, NIN], "mu3": [128, 3, 15],
    "w2": [128, 512], "a2": [128, 512], "g2": [128, 512],
    "pvec": [128, 9, 4],
    "lnx_row": [1, 1024],
    "w_up_r": [512, D], "w_up_f": [512, D], "w_out": [D, D], "w_gu": [D, 2 * DFF], "w_down": [DFF, D],
}
IN_SHAPES.update(CONST_SHAPES)


class StopBuild(Exception):
    pass


def build(debug=(), stop_after=None):
    nc = bass.Bass("TRN2", target_bir_lowering=False)
    S = Sync(nc)
    try:
        return _build(nc, S, debug, stop_after)
    except StopBuild:
        barrier(S)
        return nc, S


def _build(nc, S, debug, stop_after):
    def ck(name):
        if stop_after == name:
            raise StopBuild()
    dr = {}
    for name, shp in IN_SHAPES.items():
        dr[name] = Buf(nc.dram_tensor(name, shp, F32, kind="ExternalInput"), name)
    out = Buf(nc.dram_tensor("out", [NB, TL, D], F32, kind="ExternalOutput"), "out", nsplit=NB, sdim=0)

    def scratch(name, shape, nsplit=None, sdim=0):
        kind = "ExternalOutput" if name in debug else "Internal"
        return Buf(nc.dram_tensor(name, shape, F32, kind=kind), name, nsplit, sdim)

    U_lat = [scratch(f"U_lat{b}", [15, 128, TL], 15, 0) for b in range(NB)]
    U_ctx = [scratch(f"U_ctx{b}", [15, 128, TC], 15, 0) for b in range(NB)]
    SG = [scratch(f"SG{b}", [16, 128, TL], 16, 0) for b in range(NB)]
    DD = [scratch(f"DD{b}", [2, TL, 512], 2, 0) for b in range(NB)]
    YD = [[scratch(f"YD{b}_{d}", [TL, 512]) for d in range(2)] for b in range(NB)]
    VT = [scratch(f"VT{b}", [TL, 512]) for b in range(NB)]
    GT = [scratch(f"GT{b}", [TL, 512]) for b in range(NB)]
    BON = [[scratch(f"BON{b}_{d}", [TL, 8]) for d in range(2)] for b in range(NB)]
    X1 = [scratch(f"X1_{b}", [TL, D]) for b in range(NB)]
    MODROW = scratch("MODROW", [3, 2, 1024])

    if os.environ.get("PTFIRST") == "1":
        PT = PBank(nc, "pst", BF16, 1024)
        PS = [PBank(nc, f"ps{i}", F32, 512) for i in range(7)]
    else:
        PS = [PBank(nc, f"ps{i}", F32, 512) for i in range(7)]
        PT = PBank(nc, "pst", BF16, 1024)

    G = Scope(nc)
    cst = {}

    def load_const(sc, name):
        shp = CONST_SHAPES[name]
        cst[name] = sc.sb("c_" + name, shp, F32)
        S.dma("sp", cst[name].v(cst[name].t[tuple(slice(None) for _ in shp)]),
              dr[name].v(dr[name].t[tuple(slice(None) for _ in shp)]))
        return cst[name]

    load_const(G, "identf")
    identf = cst["identf"]
    identb = G.sb("identb", [128, 128], BF16)
    S.copy(identb[:, :], identf[:, :])
    mu3 = G.sb("mu3", [128, 3, 15], F32)
    S.dma("sp", mu3[:, :, :], dr["mu3"][:, :, :])
    c0 = G.sb("c0", [128, 15], F32)
    S.tt(c0[:, :], mu3[:, 0, :], mu3[:, 1, :], ALU.add)
    S.ts(c0[:, :], c0[:, :], -1.0, ALU.mult, 1.0, ALU.add)
    pvec = G.sb("pvec", [128, 9, 4], F32)
    S.dma("sp", pvec[:, :, :], dr["pvec"][:, :, :])
    S.ts(pvec[:, 2, :], pvec[:, 1, :], -1.0, ALU.mult, 1.0, ALU.add)
    PK_KK, PK_KA, PK_OMKA, PK_RK, PK_W0, PK_A0 = 0, 1, 2, 3, 4, 6
    n1g = G.sb("n1g", [128, 8], F32)
    n2g = G.sb("n2g", [128, 8], F32)
    S.dma("sp", n1g[:, :], dr["n1g"][:, :])
    S.dma("sp", n2g[:, :], dr["n2g"][:, :])
    smallw = {}
    for name in ("w2", "a2", "g2"):
        smallw[name] = G.sb("bf_" + name, [128, 512], BF16)
    W2w, A2w, G2w = smallw["w2"], smallw["a2"], smallw["g2"]

    if stop_after == "G":
        barrier(S)
        return nc, S
    modT = G.sb("modT", [128, 48, 3], F32)
    A1 = G.sb("A1", [128, 8, 3], F32)
    A2 = G.sb("A2", [128, 8, 3], F32)
    M = Scope(nc)
    garow = M.sb("garow", [3, 2, 1024], F32)
    stg_small = [M.sb(f"stg_small{i}", [128, 512], F32) for i in range(3)]
    for i, name in enumerate(("w2", "a2", "g2")):
        S.dma("sp", stg_small[i][:, :], dr[name][:, :])
        S.copy(smallw[name][:, :], stg_small[i][:, :])
    cT = M.sb("cT", [128, 8, 3], F32)
    sT = M.sb("sT", [128, 8, 3], F32)
    badaT = M.sb("badaT", [128, 48], F32)
    brow = M.sb("brow", [3, 6144], F32)
    S.dma("sp", cT[:, :, :], dr["cT"][:, :, :])
    S.dma("sp", badaT[:, :], dr["b_adaT"][:, :])
    S.dma("sp", brow[:, :], dr["b_ada_row"].v(dr["b_ada_row"].t.ap().partition_broadcast(3)))
    S.act(sT[:, :, :], cT[:, :, :], AF.Silu)
    wa = [M.sb(f"wa{i}", [128, 8, 1024], F32) for i in range(2)]
    w_ada_v = dr["w_ada"].t.ap().rearrange("(k p) n -> p k n", p=128)
    for sec in range(6):
        wb = wa[sec % 2]
        S.dma("sp" if sec % 2 == 0 else "pool", wb[:, :, :], dr["w_ada"].v(w_ada_v[:, :, sec * 1024:(sec + 1) * 1024]))
        for nn in range(8):
            n = sec * 8 + nn
            o = PS[0].c(n * 3, n * 3 + 3)
            for k in range(8):
                S.mm(o, wb[:, k, nn * 128:(nn + 1) * 128], sT[:, k, :], start=(k == 0), stop=(k == 7))
        if sec in (2, 5):
            gi_ = 0 if sec == 2 else 1
            for half in range(2):
                o = PS[1].c(0, 512, rows=slice(0, 3))
                for k in range(8):
                    S.mm(o, sT[:, k, :], wb[:, k, half * 512:(half + 1) * 512], start=(k == 0), stop=(k == 7))
                S.tt(garow[:, gi_, half * 512:(half + 1) * 512], o,
                     brow[:, sec * 1024 + half * 512: sec * 1024 + (half + 1) * 512], ALU.add)
    S.tt(modT[:, :, :], PS[0].v(PS[0].t[:, 0:144].rearrange("p (n j) -> p n j", j=3)),
         V(badaT.ts, badaT.t[:, :].unsqueeze(2).broadcast_to([128, 48, 3])), ALU.add)
    S.dma("sp", MODROW[:, :, :], garow[:, :, :])
    S.ts(A1[:, :, :], modT[:, 8:16, :], 1.0, ALU.add)
    S.tt(A1[:, :, :], A1[:, :, :], V(n1g.ts, n1g.t[:, :].unsqueeze(2).broadcast_to([128, 8, 3])), ALU.mult)
    S.ts(A2[:, :, :], modT[:, 32:40, :], 1.0, ALU.add)
    S.tt(A2[:, :, :], A2[:, :, :], V(n2g.ts, n2g.t[:, :].unsqueeze(2).broadcast_to([128, 8, 3])), ALU.mult)
    barrier(S)
    M.close()
    if stop_after == "M":
        if "modT" in debug:
            dbgm = nc.dram_tensor("dbg_modT", [128, 48, 3], F32, kind="ExternalOutput")
            S.dma("sp", V([T("x")], dbgm[:, :, :]), modT[:, :, :])
        barrier(S)
        return nc, S

    def nt_bufs(sc, tag, nbuf=2):
        xt = [sc.sb(f"xt{tag}{i}", [128, 1024], F32) for i in range(nbuf)]
        xn = [sc.sb(f"xn{tag}{i}", [128, 1024], BF16) for i in range(nbuf)]
        junk = sc.sb(f"junk{tag}", [128, 1024], BF16)
        st = [sc.sb(f"st{tag}{i}", [128, 4], F32) for i in range(2)]
        return xt, xn, junk, st

    def norm_transpose(bufs, src_buf, src_rows_fn, ntiles, Aap, Bfn, j, hT, col0):
        xt, xn, junk, st = bufs
        for i in range(ntiles):
            if i == 1:
                ck("A0b")
            if i == 3:
                ck("A0c")
            x_t = xt[i % len(xt)]
            x_n = xn[i % len(xn)]
            s_ = st[i % 2]
            S.dma("sp", x_t[:, :], src_buf.v(src_rows_fn(i)))
            ck("A0a")
            S.act(junk[:, :], x_t[:, :], AF.Square, accum=s_[:, 0:1])
            S.act(s_[:, 1:2], s_[:, 0:1], AF.Sqrt, scale=1.0 / D, bias=eps_t[:, 0:1])
            S.op("dve", lambda e: e.reciprocal(s_.t[:, 2:3], s_.t[:, 1:2]), [s_[:, 2:3]], [s_[:, 1:2]])
            S.act(x_n[:, :], x_t[:, :], AF.Copy, scale=s_[:, 2:3])
            ck("A0a2")
            for k in range(8):
                S.tr(PT.c(k * 128, (k + 1) * 128), x_n[:, k * 128:(k + 1) * 128], identb[:, :])
            ck("A0a3")
            for k in range(8):
                o = hT[:, k, col0 + i * 128: col0 + (i + 1) * 128]
                if k == 1:
                    ck("A0a4")
                if os.environ.get("DBGV") == "1":
                    o = junk[:, 0:128]
                if os.environ.get("DBGV") == "2":
                    S.act(o, PT.c(k * 128, (k + 1) * 128), AF.Identity, scale=0.5, bias=eps_t[:, 0:1])
                    continue
                if os.environ.get("DBGV") == "3":
                    S.act(o, PT.c(k * 128, (k + 1) * 128), AF.Copy)
                    continue
                if k == 2:
                    ck("A0a5")
                if k % 2 == 0 and not EVAC_ACT_ONLY:
                    S.ts(o, PT.c(k * 128, (k + 1) * 128), Aap[:, k, j:j + 1], ALU.mult, Bfn(k), ALU.add)
                else:
                    S.act(o, PT.c(k * 128, (k + 1) * 128), AF.Identity, scale=Aap[:, k, j:j + 1],
                          bias=Bfn(k))

    eps_t = G.sb("eps_t", [128, 2], F32)
    S.memset(eps_t[:, 0:1], NORM_EPS)
    S.memset(eps_t[:, 1:2], GN_EPS)

    def load_weight_bf16(sc, dst, dram_buf, kchunks, c0_, c1_, stg=None):
        v = dram_buf.t.ap().rearrange("(k p) n -> p k n", p=128)
        for k in range(kchunks):
            S.dma("pool", dst[:, k, 0:c1_ - c0_], dram_buf.v(v[:, k, c0_:c1_]), max_dma_last_dim=4096)

    Bsh1 = modT

    def phase_A(b):
        A = Scope(nc)
        load_const(A, "cs128")
        load_const(A, "bd64")
        stg = None
        hT = A.sb("hT", [128, 8, TL + 2], BF16)
        hTc = A.sb("hTc", [128, 8, TC + 2], BF16)
        Win = A.sb("Win", [128, 8, 2432], BF16)
        S.memset(hT[:, :, 0:1], 0.0)
        S.memset(hT[:, :, TL + 1:TL + 2], 0.0)
        S.memset(hTc[:, :, 0:1], 0.0)
        S.memset(hTc[:, :, TC + 1:TC + 2], 0.0)
        xv = dr["x"].t
        cv = dr["ctx"].t
        ck("A0")
        ntb = nt_bufs(A, "A")
        norm_transpose(ntb, dr["x"], lambda i: xv[b, i * 128:(i + 1) * 128, :], TL // 128, A1,
                       lambda k: modT[:, k, b:b + 1], b, hT, 1)
        norm_transpose(ntb, dr["ctx"], lambda i: cv[b, i * 128:(i + 1) * 128, :], TC // 128, A1,
                       lambda k: modT[:, k, 2:3], 2, hTc, 1)
        ck("A1")
        pb = [A.sb(f"pb{i}", [128, 514], F32) for i in range(2)]
        ub = [A.sb(f"ub{i}", [128, 512], F32) for i in range(3)]
        uf = A.sb("uf", [128, 4, 512], F32, nsplit=4)
        Abuf = [A.sb(f"Abuf{i}", [128, 4, 256], F32) for i in range(2)]
        Dout = [A.sb(f"Dout{i}", [128, 2, 512], F32) for i in range(2)]
        cnt = [0]

        def gemm(hbuf, T_, TT, n_lo, n_hi, wofs, Udst):
            for t0 in range(0, T_, TT):
                for n in range(n_lo, n_hi):
                    q = cnt[0]
                    cnt[0] += 1
                    bank = PS[q % 2]
                    o = bank.c(0, TT)
                    for k in range(8):
                        S.mm(o, Win[:, k, (n - wofs) * 128:(n - wofs + 1) * 128], hbuf[:, k, 1 + t0:1 + t0 + TT],
                             start=(k == 0), stop=(k == 7))
                    if n == 1:
                        ck("A2")
                    if n == 16:
                        ck("A3")
                    if n < 15:
                        oh = PS[2 + q % 2].c(0, 2)
                        for k in range(8):
                            S.mm(oh, Win[:, k, (n - wofs) * 128:(n - wofs + 1) * 128],
                                 hbuf[:, k, t0:t0 + TT + 2:TT + 1], start=(k == 0), stop=(k == 7))
                        p_ = pb[q % 2]
                        u_ = ub[q % 3]
                        S.act(p_[:, 1:TT + 1], o, AF.Copy)
                        S.copy(p_[:, 0:TT + 2:TT + 1], oh)
                        S.act(u_[:, 0:TT], p_[:, 1:TT + 1], AF.Copy, scale=c0[:, n:n + 1])
                        S.stt(u_[:, 0:TT], p_[:, 0:TT], mu3[:, 0, n:n + 1], u_[:, 0:TT], ALU.mult, ALU.add)
                        S.stt(u_[:, 0:TT], p_[:, 2:TT + 2], mu3[:, 1, n:n + 1], u_[:, 0:TT], ALU.mult, ALU.add)
                        S.dma("pool", Udst[n, :, t0:t0 + TT], u_[:, 0:TT])
                    elif n < 19:
                        S.act(uf[:, n - 15, 0:TT], o, AF.Copy)
                    else:
                        u_ = ub[q % 3]
                        S.act(u_[:, 0:TT], o, AF.Sigmoid)
                        S.dma("pool", SG[b][n - 19, :, t0:t0 + TT], u_[:, 0:TT])
                if n_lo <= 15 and n_hi >= 19 and T_ == TL:
                    for ch in range(4):
                        ab = Abuf[ch % 2]
                        do = Dout[ch % 2]
                        for g in range(4):
                            bank = PS[4 + (g // 2)]
                            S.mm(bank.c((g % 2) * 256, (g % 2) * 256 + 256), uf[:, g, ch * 128:(ch + 1) * 128],
                                 cst["cs128"][:, :])
                        S.act(ab[:, 0:2, :], PS[4].v(PS[4].t[:, :].rearrange("p (g c) -> p g c", c=256)), AF.Copy)
                        S.copy(ab[:, 2:4, :], PS[5].v(PS[5].t[:, :].rearrange("p (g c) -> p g c", c=256)))
                        Ac = ab[:, :, 0:128]
                        As = ab[:, :, 128:256]
                        d1 = PS[6].c(0, 512)
                        S.mm(d1, cst["bd64"][:, 0, :], Ac, start=True, stop=False)
                        S.mm(d1, cst["bd64"][:, 2, :], As, start=False, stop=True)
                        S.act(do[:, 0, :], d1, AF.Copy)
                        d2 = PS[6].c(0, 512)
                        S.mm(d2, cst["bd64"][:, 0, :], As, start=True, stop=False)
                        S.mm(d2, cst["bd64"][:, 1, :], Ac, start=False, stop=True)
                        S.copy(do[:, 1, :], d2)
                        tt0 = t0 + ch * 128
                        S.dma("pool", DD[b].v(DD[b].t.ap()[:, tt0:tt0 + 128, :].rearrange("a t c -> t a c")),
                              do[:, :, :])

        load_weight_bf16(A, Win, dr["w_in"], 8, 0, 2432, stg)
        ck("A1b")
        gemm(hT, TL, 512, 0, 19, 0, U_lat[b])
        ck("A4")
        gemm(hTc, TC, 256, 0, 15, 0, U_ctx[b])
        load_weight_bf16(A, Win, dr["w_in"], 8, 2432, 4480, stg)
        gemm(hT, TL, 512, 19, 35, 19, None)
        barrier(S)
        A.close()

    for b in range(NB):
        phase_A(b)
    if stop_after == "A":
        barrier(S)
        return nc, S


    def pv(row, m):
        return pvec[:, row, m:m + 1]

    def phase_B():
        B = Scope(nc)
        mk2 = load_const(B, "mk2")
        load_const(B, "onesbd")
        load_const(B, "e2")
        load_const(B, "ones")
        uc = [B.sb(f"uc{i}", [128, 15, 128], F32) for i in range(2)]
        Hf = [B.sb(f"Hf{d}", [128, NB * 4 * 64], F32) for d in range(2)]
        Hb = [B.sb(f"Hb{d}", [128, NB, 4, 64], BF16) for d in range(2)]
        FM = [B.sb(f"FM{d}", [128, NB, 4, 4, 128], BF16, nsplit=NB, sdim=1) for d in range(2)]
        TOK = [B.sb(f"TOK{d}", [128, NB, 4, 3, 128], BF16, nsplit=NB, sdim=1) for d in range(2)]
        VM = [B.sb(f"VM{d}", [128, NB, 512], BF16, nsplit=NB, sdim=1) for d in range(2)]
        GC = [B.sb(f"GC{d}", [128, NB, 4], F32) for d in range(2)]
        tmp = {}
        for nm in ("kkraw", "sq", "rn", "kk", "a", "sg", "cf", "incl", "excl", "gi", "ge", "ginv", "kap",
                   "beta", "t1"):
            tmp[nm] = B.sb("t_" + nm, [128, 4, 128], F32)
        adb = B.sb("adb", [128, 128], BF16)
        twd = B.sb("twd", [128, 128], BF16)
        sgd = B.sb("sgd", [128, 128], BF16)
        vtok = [B.sb(f"vtok{i}", [128, 512], F32) for i in range(2)]
        gtok = [B.sb(f"gtok{i}", [128, 512], F32) for i in range(2)]
        bont = [B.sb(f"bont{i}", [128, 8], F32) for i in range(2)]
        NSLOT = 8
        FB = [B.sb(f"FB{i}", [128, 576], F32) for i in range(NSLOT)]
        PPf = [B.sb(f"PPf{i}", [128, 256], F32) for i in range(NSLOT)]
        TTf = [B.sb(f"TTf{i}", [128, 128], F32) for i in range(NSLOT)]
        Yf_ = [B.sb(f"Yf{i}", [128, 192], BF16) for i in range(NSLOT)]
        Xf_ = [B.sb(f"Xf{i}", [128, 192], BF16) for i in range(NSLOT)]
        Zb_ = [B.sb(f"Zb{i}", [128, 192], BF16) for i in range(NSLOT)]
        Ltb_ = [B.sb(f"Ltb{i}", [128, 128], BF16) for i in range(NSLOT)]
        TTb_ = [B.sb(f"TTb{i}", [128, 128], BF16) for i in range(NSLOT)]
        ARB2 = [[B.sb(f"ARB{p_}_{i}", [128, 128], BF16) for i in range(NSLOT)] for p_ in range(2)]
        ARK2 = [[B.sb(f"ARK{p_}_{i}", [128, 128], BF16) for i in range(NSLOT)] for p_ in range(2)]
        Gb2 = [[B.sb(f"Gb{p_}_{i}", [128, 128], BF16) for i in range(NSLOT)] for p_ in range(2)]
        GTs2 = [[B.sb(f"GTs{p_}_{i}", [128, 128], BF16) for i in range(NSLOT)] for p_ in range(2)]
        WP2 = [[B.sb(f"WP{p_}_{i}", [128, 128], BF16) for i in range(NSLOT // 2)] for p_ in range(2)]
        WTs2 = [[B.sb(f"WTs{p_}_{i}", [128, 128], BF16) for i in range(NSLOT // 2)] for p_ in range(2)]
        Ub = B.sb("Ub", [128, NB, 512], BF16, nsplit=NB, sdim=1)
        Ysb = [B.sb(f"Ysb{i}", [128, 512], F32) for i in range(2)]
        htmp = B.sb("htmp", [128, NB * 4 * 64], F32)
        for d in range(2):
            S.memset(Hf[d][:, :], 0.0)
            S.memset(Hb[d][:, :, :, :], 0.0)
        ctr = [0]

        def prep(d, b, Usrc, t0, latent):
            q = ctr[0]
            ctr[0] += 1
            U = uc[q % 2]
            S.dma("sp", U[:, :, :], Usrc.v(Usrc.t.ap()[:, :, t0:t0 + 128].rearrange("n p t -> p n t")))
            Pd = slice(d * 64, d * 64 + 64)
            r = U[:, 0:4, :]
            k = U[:, 4:8, :]
            t = tmp
            for m in range(4):
                S.ts(t["kkraw"][:, m, :], U[:, 4 + m, :], pv(PK_KK, m), ALU.mult, ek="pool")
            S.tt(t["sq"][:, :, :], t["kkraw"][:, :, :], t["kkraw"][:, :, :], ALU.mult, ek="pool")
            ssp = PS[5].c(0, 512)
            S.mm(ssp, cst["onesbd"][:, :], t["sq"][:, :, :])
            S.ts(t["rn"][:, :, :], PS[5].v(PS[5].t[:, :].rearrange("p (m t) -> p m t", t=128)), 1e-24, ALU.max)
            S.act(t["rn"][:, :, :], t["rn"][:, :, :], AF.Ln)
            S.act(t["rn"][:, :, :], t["rn"][:, :, :], AF.Exp, scale=-0.5)
            S.tt(t["kk"][:, :, :], t["kkraw"][:, :, :], t["rn"][:, :, :], ALU.mult)
            ck("B1a")
            yield
            S.copy(adb[Pd, :], U[Pd, 13, :], ek="pool")
            for m in range(4):
                S.mm(PS[5].c(m * 128, (m + 1) * 128), A2w[Pd, m * 128:(m + 1) * 128], adb[Pd, :])
            for m in range(4):
                S.act(t["a"][:, m, :], PS[5].c(m * 128, (m + 1) * 128), AF.Sigmoid, bias=pv(PK_A0 + d, m))
            yield
            S.act(twd[Pd, :], U[Pd, 12, :], AF.Tanh)
            for m in range(4):
                S.mm(PS[5].c(m * 128, (m + 1) * 128), W2w[Pd, m * 128:(m + 1) * 128], twd[Pd, :])
            for m in range(4):
                S.act(t["sg"][:, m, :], PS[5].c(m * 128, (m + 1) * 128), AF.Sigmoid, bias=pv(PK_W0 + d, m))
            ck("B1b")
            yield
            for m in range(4):
                S.op("dve", lambda e: e.tensor_tensor_scan(t["cf"].t[:, m, :], cst["ones"].t[:, :], t["sg"].t[:, m, :],
                                                           0.0, ALU.mult, ALU.add),
                     [t["cf"][:, m, :]], [cst["ones"][:, :], t["sg"][:, m, :]])
            ck("B1c")
            yield
            if d == 0:
                incl = t["cf"]
                S.tt(t["excl"][:, :, :], t["cf"][:, :, :], t["sg"][:, :, :], ALU.subtract)
                excl = t["excl"]
            else:
                for m in range(4):
                    S.ts(t["excl"][:, m, :], t["cf"][:, m, :], -1.0, ALU.mult, t["cf"][:, m, 127:128], ALU.add)
                S.tt(t["incl"][:, :, :], t["excl"][:, :, :], t["sg"][:, :, :], ALU.add)
                incl = t["incl"]
                excl = t["excl"]
            S.act(t["gi"][:, :, :], incl[:, :, :], AF.Exp, scale=SC)
            S.act(t["ge"][:, :, :], excl[:, :, :], AF.Exp, scale=SC)
            S.act(t["ginv"][:, :, :], incl[:, :, :], AF.Exp, scale=-SC)
            S.act(GC[d][:, b, :], t["cf"][:, :, 127], AF.Exp, scale=SC)
            yield
            for m in range(4):
                S.ts(t["t1"][:, m, :], t["a"][:, m, :], pv(PK_KA, m), ALU.mult, pv(PK_OMKA, m), ALU.add, ek="pool")
            S.tt(t["kap"][:, :, :], k, t["t1"][:, :, :], ALU.mult, ek="pool")
            S.tt(t["beta"][:, :, :], t["kk"][:, :, :], t["a"][:, :, :], ALU.mult, ek="pool")
            fm = FM[d]
            S.tt(t["sq"][:, :, :], t["kk"][:, :, :], t["ge"][:, :, :], ALU.mult)
            S.act(fm[:, b, :, 0, :], t["sq"][:, :, :], AF.Copy, scale=-1.0)
            S.tt(fm[:, b, :, 1, :], r, t["gi"][:, :, :], ALU.mult)
            S.tt(fm[:, b, :, 2, :], t["beta"][:, :, :], t["ginv"][:, :, :], ALU.mult)
            S.tt(fm[:, b, :, 3, :], t["kap"][:, :, :], t["ginv"][:, :, :], ALU.mult)
            ck("B1d")
            yield
            for m in range(4):
                for xi, x in enumerate((0, 2, 3)):
                    S.tr(PT.c(xi * 128, (xi + 1) * 128), fm[:, b, m, x, :], identb[:, :])
                S.act(TOK[d][:, b, m, :, :], PT.v(PT.t[:, 0:384].rearrange("p (x c) -> p x c", c=128)),
                      AF.Copy)
                yield
            ck("B1e")
            yield
            for m in range(4):
                S.tr(PS[5].c(m * 128, (m + 1) * 128), U[:, 8 + m, :], identf[:, :])
            S.act(VM[d][:, b, :], PS[5].c(0, 512), AF.Copy)
            yield
            if latent:
                if d == 0:
                    vt = vtok[(q // 2) % 2]
                    S.copy(vt[:, :], PS[5].c(0, 512))
                    S.dma("pool", VT[b][t0:t0 + 128, :], vt[:, :])
                    S.act(sgd[:, :], U[:, 14, :], AF.Sigmoid)
                    S.mm(PS[5].c(0, 512), sgd[:, :], G2w[:, :])
                    gt = gtok[(q // 2) % 2]
                    S.act(gt[:, :], PS[5].c(0, 512), AF.Copy)
                    S.dma("pool", GT[b][t0:t0 + 128, :], gt[:, :])
                yield
                S.tt(t["t1"][:, :, :], r, t["kap"][:, :, :], ALU.mult, ek="pool")
                for m in range(4):
                    S.ts(t["t1"][:, m, :], t["t1"][:, m, :], pv(PK_RK, m), ALU.mult, ek="pool")
                for m in range(4):
                    S.mm(PS[5].c(m * 2, m * 2 + 2), t["t1"][:, m, :], cst["e2"][:, :])
                bt = bont[q % 2]
                S.copy(bt[:, :], PS[5].c(0, 8))
                S.dma("pool", BON[b][d][t0:t0 + 128, :], bt[:, :])

        def head_chain(d, b, h, bank, par):
            ARB, ARK, Gb, WP = ARB2[par], ARK2[par], Gb2[par], WP2[par]
            fm = FM[d]
            m, hl = divmod(h, 2)
            P = slice(hl * 64, hl * 64 + 64)
            sl = h
            fb, ppf, ttf, yf_, xf_ = FB[sl], PPf[sl], TTf[sl], Yf_[sl], Xf_[sl]
            zb, ltb, ttb = Zb_[sl], Ltb_[sl], TTb_[sl]
            S.mm(bank.c(0, 256), fm[P, b, m, 0, :], fm[P, b, m, 2:4, :])
            S.mm(bank.c(256, 512), fm[P, b, m, 2, :], fm[P, b, m, 0:2, :])
            yield
            S.tt(fb[:, 0:128], bank.c(0, 128), mk2[:, d, 0, :], ALU.mult)
            S.tt(zb[:, 0:128], bank.c(128, 256), mk2[:, d, 1, :], ALU.mult)
            S.tt(fb[:, 128:256], bank.c(256, 384), mk2[:, d, 2, :], ALU.mult)
            S.tt(ltb[:, :], bank.c(256, 384), mk2[:, d, 4, :], ALU.mult)
            S.tt(ARB[sl][:, :], bank.c(384, 512), mk2[:, d, 3, :], ALU.mult)
            S.copy(zb[:, 128:192], TOK[d][:, b, m, 0, hl * 64:(hl + 1) * 64], ek="pool")
            S.tt(ttf[:, :], fb[:, 128:256], identf[:, :], ALU.add, ek="pool")
            yield
            Pn, Pt_ = fb[:, 0:128], fb[:, 128:256]
            S.mm(bank.c(0, 128), Pt_, Pn)
            S.mm(bank.c(128, 256), Pn, Pt_)
            S.mm(bank.c(384, 512), fm[P, b, m, 3, :], fm[P, b, m, 1, :])
            yield
            S.act(ppf[:, :], bank.c(0, 256), AF.Copy)
            S.tt(ARK[sl][:, :], bank.c(384, 512), mk2[:, d, 3, :], ALU.mult)
            yield
            for lev in range(1, 6):
                Pn, Pt_ = ppf[:, 0:128], ppf[:, 128:256]
                S.mm(bank.c(256, 384), Pn, ttf[:, :])
                if lev < 5:
                    S.mm(bank.c(0, 128), Pt_, Pn)
                    S.mm(bank.c(128, 256), Pn, Pt_)
                yield
                S.tt(ttf[:, :], bank.c(256, 384), ttf[:, :], ALU.add)
                if lev < 5:
                    S.act(ppf[:, :], bank.c(0, 256), AF.Copy)
                yield
            S.act(ttb[:, :], ttf[:, :], AF.Copy)
            yield
            S.mm(bank.c(0, 192), ttb[:, :], zb[:, :])
            yield
            S.act(yf_[:, :], bank.c(0, 192), AF.Copy)
            yield
            S.mm(bank.c(256, 448), ltb[:, :], yf_[:, :])
            yield
            S.copy(xf_[:, :], bank.c(256, 448))
            yield
            S.mm(bank.c(0, 192), ttb[:, :], xf_[:, :])
            yield
            S.tt(Gb[sl][:, :], bank.c(0, 128), yf_[:, 0:128], ALU.add)
            S.tt(WP[sl // 2][:, hl * 64:(hl + 1) * 64], bank.c(128, 192), yf_[:, 128:192], ALU.add)
            yield

        def chunk_math(d, b, par, extras):
            ck("B1")
            NCH = 5
            pending = list(range(8))
            active = [(e_, None) for e_ in extras if e_ is not None]
            free_banks = [PS[i] for i in range(NCH)]
            while pending or active:
                while pending and free_banks:
                    h = pending.pop(0)
                    bk = free_banks.pop(0)
                    active.append((head_chain(d, b, h, bk, par), bk))
                nxt = []
                for g, bk in active:
                    try:
                        next(g)
                        nxt.append((g, bk))
                    except StopIteration:
                        if bk is not None:
                            free_banks.append(bk)
                active = nxt

        def state_part(d, b, latent, t0, par):
            ARB, ARK, Gb, GTs, WP, WTs = ARB2[par], ARK2[par], Gb2[par], GTs2[par], WP2[par], WTs2[par]
            fm = FM[d]
            for h4 in range(2):
                for q_ in range(4):
                    S.tr(PT.c(384 + q_ * 128, 512 + q_ * 128), Gb[h4 * 4 + q_][:, :], identb[:, :])
                yield
                for q_ in range(4):
                    hh = h4 * 4 + q_
                    if hh % 2 == 0:
                        S.act(GTs[hh][:, :], PT.c(384 + q_ * 128, 512 + q_ * 128), AF.Copy)
                    else:
                        S.copy(GTs[hh][:, :], PT.c(384 + q_ * 128, 512 + q_ * 128))
                yield
            for pr in range(4):
                S.tr(PT.c(384 + pr * 128, 512 + pr * 128), WP[pr][:, :], identb[:, :])
            yield
            for pr in range(4):
                S.copy(WTs[pr][:, :], PT.c(384 + pr * 128, 512 + pr * 128))
            yield
            psU = PS[6].c(0, 512)
            for h in range(8):
                m, hl = divmod(h, 2)
                P = slice(hl * 64, hl * 64 + 64)
                o = PS[6].c(h * 64, (h + 1) * 64)
                S.mm(o, WTs[h // 2][P, :], Hb[d][P, b, m, :], start=True, stop=False)
                S.mm(o, GTs[h][:, :], VM[d][:, b, h * 64:(h + 1) * 64], start=False, stop=True)
            yield
            S.act(Ub[:, b, :], psU, AF.Copy)
            yield
            if latent:
                for h in range(8):
                    m, hl = divmod(h, 2)
                    P = slice(hl * 64, hl * 64 + 64)
                    o = PS[6].c(h * 64, (h + 1) * 64)
                    S.mm(o, fm[P, b, m, 1, :], Hb[d][P, b, m, :], start=True, stop=False)
                    S.mm(o, ARB[h][:, :], Ub[:, b, h * 64:(h + 1) * 64], start=False, stop=False)
                    S.mm(o, ARK[h][:, :], VM[d][:, b, h * 64:(h + 1) * 64], start=False, stop=True)
                yield
                ys = Ysb[(b + d) % 2]
                S.act(ys[:, :], PS[6].c(0, 512), AF.Copy)
                S.dma("pool", YD[b][d][t0:t0 + 128, :], ys[:, :])
                yield
            for h in range(8):
                m, hl = divmod(h, 2)
                P = slice(hl * 64, hl * 64 + 64)
                o = PS[6].v(PS[6].t[P, (b * 4 + m) * 64:(b * 4 + m + 1) * 64])
                S.mm(o, TOK[d][:, b, m, 1, hl * 64:(hl + 1) * 64], Ub[:, b, h * 64:(h + 1) * 64], start=True, stop=False)
                S.mm(o, TOK[d][:, b, m, 2, hl * 64:(hl + 1) * 64], VM[d][:, b, h * 64:(h + 1) * 64], start=False, stop=True)
            yield
            cs_ = slice(b * 256, (b + 1) * 256)
            S.tt(htmp[:, cs_], PS[6].v(PS[6].t[:, cs_]), Hf[d][:, cs_], ALU.add)
            S.tt(V(Hf[d].ts, Hf[d].t[:, cs_].rearrange("p (m v) -> p m v", v=64)),
                 V(htmp.ts, htmp.t[:, cs_].rearrange("p (m v) -> p m v", v=64)),
                 V(GC[d].ts, GC[d].t[:, b, :].unsqueeze(2).broadcast_to([128, 4, 64])), ALU.mult)
            S.copy(Hb[d][:, b, :, :], V(Hf[d].ts, Hf[d].t[:, cs_].rearrange("p (m v) -> p m v", v=64)), ek="pool")
            yield

        nsteps = 2 + TL // 128
        units = []
        for s_ in range(nsteps):
            for d in range(2):
                for b in range(NB):
                    if s_ < 2:
                        ci = s_ if d == 0 else 1 - s_
                        units.append((d, b, U_ctx[b], ci * 128, False))
                    else:
                        li = s_ - 2
                        ci = li if d == 0 else (TL // 128 - 1 - li)
                        units.append((d, b, U_lat[b], ci * 128, True))
        for _ in prep(*units[0]):
            pass
        prev_state = None
        for i, (d, b, Usrc, t0, latent) in enumerate(units):
            nxt_prep = prep(*units[i + 1]) if i + 1 < len(units) else None
            chunk_math(d, b, i % 2, [prev_state, nxt_prep])
            prev_state = state_part(d, b, latent, t0, i % 2)
        for _ in prev_state:
            pass
        barrier(S)
        B.close()

    phase_B()
    if stop_after == "B":
        return nc, S

    def phase_C(b):
        C = Scope(nc)
        load_const(C, "c32")
        lnx = C.sb("lnx", [128, 1024], F32)
        S.dma("sp", lnx[:, :], dr["lnx_row"].v(dr["lnx_row"].t.ap().partition_broadcast(128)))
        stg = None
        fT = C.sb("fT", [128, 4, TL], BF16)
        Wupf = C.sb("Wupf", [128, 4, D], BF16)
        Wupr = C.sb("Wupr", [128, 4, D], BF16)
        Wout = C.sb("Wout", [128, 8, D], BF16)
        load_weight_bf16(C, Wupf, dr["w_up_f"], 4, 0, D, stg)
        load_weight_bf16(C, Wupr, dr["w_up_r"], 4, 0, D, stg)
        load_weight_bf16(C, Wout, dr["w_out"], 8, 0, D, stg)
        ga1b = C.sb("ga1b", [128, D], F32)
        S.dma("sp", ga1b[:, :], MODROW.v(MODROW.t.ap()[b:b + 1, 0, :].partition_broadcast(128)))
        dd = [C.sb(f"dd{i}", [32, 2, 4, 512], F32) for i in range(2)]
        ddv = DD[b].t.ap().rearrange("a (r c) k -> r a c k", c=64)
        c32 = cst["c32"]
        for cb in range(16):
            dt_ = dd[cb % 2]
            S.dma("sp", dt_[:, :, :, :], DD[b].v(ddv[:, :, cb * 4:(cb + 1) * 4, :]))
            bank = PS[cb % 2]
            for cl in range(4):
                for g in range(4):
                    o = bank.c((g * 4 + cl) * 32, (g * 4 + cl) * 32 + 32)
                    S.mm(o, dt_[:, 0, cl, g * 128:(g + 1) * 128], c32[:, 0, :], start=True, stop=False)
                    S.mm(o, dt_[:, 1, cl, g * 128:(g + 1) * 128], c32[:, 1, :], start=False, stop=True)
            for g in range(4):
                src = bank.v(bank.t[:, g * 128:(g + 1) * 128].rearrange("p (c r) -> p r c", r=32))
                dst = V(fT.ts, fT.t[:, g, :].rearrange("p (r c) -> p r c", c=64)[:, :, cb * 4:(cb + 1) * 4])
                if g % 2 == 0:
                    S.act(dst, src, AF.Copy)
                else:
                    S.copy(dst, src)
        yf = [C.sb(f"yf{i}", [128, 512], F32) for i in range(2)]
        yb = [C.sb(f"yb{i}", [128, 512], F32) for i in range(2)]
        vt = [C.sb(f"vt{i}", [128, 512], F32) for i in range(2)]
        gt = [C.sb(f"gt{i}", [128, 512], F32) for i in range(2)]
        bo = [C.sb(f"bo{i}", [128, 2, 8], F32) for i in range(2)]
        ysq = C.sb("ysq", [128, 512], F32)
        stt_ = [C.sb(f"stC{i}", [128, 6, 8], F32) for i in range(2)]
        obf = C.sb("obf", [128, 512], BF16)
        oT2 = [C.sb(f"oT{i}", [128, 4, 512], BF16) for i in range(2)]
        gsb = [C.sb(f"gsb{i}", [128, 2, 512], F32) for i in range(2)]
        mT = C.sb("mT", [128, 8, 512], BF16, nsplit=8)
        mtmp = [C.sb(f"mtmp{i}", [128, 512], F32) for i in range(2)]
        xin = [C.sb(f"xin{i}", [128, D], F32) for i in range(2)]
        xo = [C.sb(f"xo{i}", [128, D], F32) for i in range(2)]

        def b3(v_, n):
            return V(v_.ts, v_.ap.unsqueeze(2).broadcast_to([128, 8, n]))

        def v3(buf):
            return V(buf.ts, buf.t[:, :].rearrange("p (h v) -> p h v", v=64))

        def epi_gen(tt):
            for sub in range(4):
                i = tt * 4 + sub
                t0 = i * 128
                y_, yb_, v_, g_, bo_, st_ = yf[i % 2], yb[i % 2], vt[i % 2], gt[i % 2], bo[i % 2], stt_[i % 2]
                S.dma("sp", y_[:, :], YD[b][0][t0:t0 + 128, :])
                S.dma("sp", yb_[:, :], YD[b][1][t0:t0 + 128, :])
                S.dma("sp", v_[:, :], VT[b][t0:t0 + 128, :])
                S.dma("sp", g_[:, :], GT[b][t0:t0 + 128, :])
                S.dma("sp", bo_[:, 0, :], BON[b][0][t0:t0 + 128, :])
                S.dma("sp", bo_[:, 1, :], BON[b][1][t0:t0 + 128, :])
                yield
                S.tt(y_[:, :], y_[:, :], yb_[:, :], ALU.add)
                S.reduce(st_[:, 0, :], v3(y_), ALU.add, AX.X)
                S.tt(ysq[:, :], y_[:, :], y_[:, :], ALU.mult, ek="pool")
                S.reduce(st_[:, 1, :], v3(ysq), ALU.add, AX.X)
                yield
                S.ts(st_[:, 2, :], st_[:, 0, :], 1.0 / 64, ALU.mult)
                S.tt(st_[:, 3, :], st_[:, 2, :], st_[:, 2, :], ALU.mult)
                S.stt(st_[:, 3, :], st_[:, 1, :], 1.0 / 64, st_[:, 3, :], ALU.mult, ALU.subtract)
                S.act(st_[:, 4, :], st_[:, 3, :], AF.Sqrt, bias=eps_t[:, 1:2])
                S.op("dve", lambda e: e.reciprocal(st_.t[:, 5, :], st_.t[:, 4, :]), [st_[:, 5, :]], [st_[:, 4, :]])
                yield
                S.tt(v3(y_), v3(y_), b3(st_[:, 2, :], 64), ALU.subtract)
                S.tt(v3(y_), v3(y_), b3(st_[:, 5, :], 64), ALU.mult)
                S.tt(y_[:, :], y_[:, :], lnx[:, 0:512], ALU.mult)
                S.tt(y_[:, :], y_[:, :], lnx[:, 512:1024], ALU.add, ek="pool")
                yield
                S.tt(bo_[:, 0, :], bo_[:, 0, :], bo_[:, 1, :], ALU.add, ek="pool")
                S.tt(v3(v_), v3(v_), b3(bo_[:, 0, :], 64), ALU.mult)
                S.tt(y_[:, :], y_[:, :], v_[:, :], ALU.add, ek="pool")
                S.tt(obf[:, :], y_[:, :], g_[:, :], ALU.mult)
                yield
                for kc in range(4):
                    S.tr(PT.c(kc * 128, (kc + 1) * 128), obf[:, kc * 128:(kc + 1) * 128], identb[:, :])
                S.act(oT2[tt % 2][:, :, sub * 128:(sub + 1) * 128],
                      PT.v(PT.t[:, 0:512].rearrange("p (k t) -> p k t", t=128)), AF.Copy)
                yield

        def mw_gen(tt):
            T0 = tt * 512
            for n in range(8):
                gs = gsb[n % 2]
                S.dma("sp", gs[:, 0, :], SG[b][n, :, T0:T0 + 512])
                S.dma("sp", gs[:, 1, :], SG[b][8 + n, :, T0:T0 + 512])
                pf = PS[2].c(0, 512)
                pr = PS[3].c(0, 512)
                for kc in range(4):
                    S.mm(pf, Wupf[:, kc, n * 128:(n + 1) * 128], fT[:, kc, T0:T0 + 512], start=(kc == 0), stop=(kc == 3))
                for kc in range(4):
                    S.mm(pr, Wupr[:, kc, n * 128:(n + 1) * 128], oT2[tt % 2][:, kc, :], start=(kc == 0), stop=(kc == 3))
                mt = mtmp[n % 2]
                S.tt(mt[:, :], pf, gs[:, 0, :], ALU.mult)
                S.tt(gs[:, 1, :], pr, gs[:, 1, :], ALU.mult)
                S.tt(mT[:, n, :], mt[:, :], gs[:, 1, :], ALU.add, ek="pool")
                yield
            for sub in range(4):
                i = tt * 4 + sub
                t0 = i * 128
                xi, xo_ = xin[i % 2], xo[i % 2]
                S.dma("sp", xi[:, :], dr["x"].v(dr["x"].t[b, t0:t0 + 128, :]))
                for half in range(2):
                    o = PS[4 + half].c(0, 512)
                    for n in range(8):
                        S.mm(o, mT[:, n, sub * 128:(sub + 1) * 128], Wout[:, n, half * 512:(half + 1) * 512],
                             start=(n == 0), stop=(n == 7))
                    hs = slice(half * 512, (half + 1) * 512)
                    S.tt(xo_[:, hs], o, ga1b[:, hs], ALU.mult)
                    S.tt(xo_[:, hs], xo_[:, hs], xi[:, hs], ALU.add, ek="pool")
                S.dma("pool", X1[b][t0:t0 + 128, :], xo_[:, :])
                yield

        def run_rr(gens):
            gens = [g for g in gens if g is not None]
            while gens:
                nx = []
                for g in gens:
                    try:
                        next(g)
                        nx.append(g)
                    except StopIteration:
                        pass
                gens = nx

        NT4 = TL // 512
        run_rr([epi_gen(0)])
        for tt in range(NT4):
            run_rr([mw_gen(tt), epi_gen(tt + 1) if tt + 1 < NT4 else None])
        barrier(S)
        C.close()

    for b in range(NB):
        phase_C(b)
    if stop_after == "C":
        return nc, S

    def phase_D():
        Dd = Scope(nc)
        fng = Dd.sb("fng", [128, 1024], F32)
        S.dma("sp", fng[:, :], dr["fng_row"].v(dr["fng_row"].t.ap().partition_broadcast(128)))
        Wgu = Dd.sb("Wgu", [128, 8, 2 * DFF], BF16)
        Wdn = Dd.sb("Wdn", [128, 22, D], BF16)
        load_weight_bf16(Dd, Wgu, dr["w_gu"], 8, 0, 2 * DFF)
        load_weight_bf16(Dd, Wdn, dr["w_down"], 22, 0, D)
        TD = 256
        hT2 = Dd.sb("hT2", [128, 8, TD], BF16)
        ntb = nt_bufs(Dd, "D", nbuf=1)
        actT = Dd.sb("actT", [128, 22, TD], BF16, nsplit=22)
        sil = [Dd.sb(f"sil{i}", [128, TD], F32) for i in range(2)]
        ga2b = Dd.sb("ga2b", [128, D], F32)
        x1t = [Dd.sb(f"x1t{i}", [128, D], F32) for i in range(1)]
        x2t = [Dd.sb(f"x2t{i}", [128, D], F32) for i in range(2)]
        junk2 = ntb[2]
        stf = [Dd.sb(f"stf{i}", [128, 4], F32) for i in range(2)]
        for b in range(NB):
            S.dma("sp", ga2b[:, :], MODROW.v(MODROW.t.ap()[b:b + 1, 1, :].partition_broadcast(128)))
            for tt in range(TL // TD):
                T0 = tt * TD
                x1v = X1[b].t
                norm_transpose(ntb, X1[b], lambda i: x1v[T0 + i * 128:T0 + (i + 1) * 128, :], TD // 128, A2,
                               lambda k: modT[:, 24 + k, b:b + 1], b, hT2, 0)
                for fc in range(22):
                    pg = PS[fc % 2].c(0, TD)
                    pu = PS[2 + fc % 2].c(0, TD)
                    for k in range(8):
                        S.mm(pg, Wgu[:, k, fc * 128:(fc + 1) * 128], hT2[:, k, :], start=(k == 0), stop=(k == 7))
                    for k in range(8):
                        S.mm(pu, Wgu[:, k, DFF + fc * 128:DFF + (fc + 1) * 128], hT2[:, k, :], start=(k == 0),
                             stop=(k == 7))
                    sl_ = sil[fc % 2]
                    S.act(sl_[:, :], pg, AF.Silu)
                    S.tt(actT[:, fc, :], pu, sl_[:, :], ALU.mult)
                for sub in range(TD // 128):
                    i = tt * (TD // 128) + sub
                    t0 = T0 + sub * 128
                    x1_, x2_, sf = x1t[0], x2t[i % 2], stf[i % 2]
                    S.dma("sp", x1_[:, :], X1[b][t0:t0 + 128, :])
                    for half in range(2):
                        o = PS[4 + half].c(0, 512)
                        for fc in range(22):
                            S.mm(o, actT[:, fc, sub * 128:(sub + 1) * 128], Wdn[:, fc, half * 512:(half + 1) * 512],
                                 start=(fc == 0), stop=(fc == 21))
                        hs = slice(half * 512, (half + 1) * 512)
                        S.tt(x2_[:, hs], o, ga2b[:, hs], ALU.mult)
                        S.tt(x2_[:, hs], x2_[:, hs], x1_[:, hs], ALU.add, ek="pool")
                    S.act(junk2[:, :], x2_[:, :], AF.Square, accum=sf[:, 0:1])
                    S.act(sf[:, 1:2], sf[:, 0:1], AF.Sqrt, scale=1.0 / D, bias=eps_t[:, 0:1])
                    S.op("dve", lambda e: e.reciprocal(sf.t[:, 2:3], sf.t[:, 1:2]), [sf[:, 2:3]], [sf[:, 1:2]])
                    S.act(x2_[:, :], x2_[:, :], AF.Copy, scale=sf[:, 2:3])
                    S.tt(x2_[:, :], x2_[:, :], fng[:, :], ALU.mult)
                    S.dma("pool", out[b, t0:t0 + 128, :], x2_[:, :])
        barrier(S)
        Dd.close()

    phase_D()
    barrier(S)
    return nc, S

from concourse.bass_utils import run_bass_kernel_spmd

N_CORES = 8
_CACHE = {}


def _fm(vec, nchunk):
    return np.ascontiguousarray(np.asarray(vec, np.float32).reshape(nchunk, 128).T)


def make_in_maps(inp, cores):
    consts = host_consts()
    f = lambda a: np.ascontiguousarray(np.asarray(a, np.float32))
    shared = {
        "b_adaT": _fm(inp["b_ada"][0], 48), "b_ada_row": f(inp["b_ada"][0]).reshape(1, 6144),
        "n1g": _fm(inp["norm1_g"][0], 8), "n2g": _fm(inp["norm2_g"][0], 8),
        "fng_row": f(inp["final_norm_g"]).reshape(1, D),
        "w_ada": f(inp["w_ada"][0]), "w_in": f(inp["w_in"][0]),
        "w2": np.concatenate([f(inp["w2_f"][0]), f(inp["w2_b"][0])], 0),
        "a2": np.concatenate([f(inp["a2_f"][0]), f(inp["a2_b"][0])], 0),
        "g2": f(inp["g2"][0]),
        "lnx_row": np.concatenate([f(inp["lnx_g"][0]), f(inp["lnx_b"][0])]).reshape(1, 1024),
        "w_up_r": f(inp["w_up_r"][0]), "w_up_f": f(inp["w_up_f"][0]), "w_out": f(inp["w_out"][0]),
        "w_gu": f(inp["w_gu"][0]), "w_down": f(inp["w_down"][0]),
    }
    mu3 = np.zeros((128, 3, 15), np.float32)
    mu3[:, 0, :] = _fm(inp["mu_prev"][0], 15)
    mu3[:, 1, :] = _fm(inp["mu_next"][0], 15)
    shared["mu3"] = mu3
    pvec = np.zeros((128, 9, 4), np.float32)
    for i, nm in ((0, "k_k"), (1, "k_a"), (3, "r_k"), (4, "w0_f"), (5, "w0_b"), (6, "a0_f"), (7, "a0_b")):
        pvec[:, i, :] = _fm(np.asarray(inp[nm][0]).reshape(512), 4)
    shared["pvec"] = pvec
    shared.update(consts)
    maps = []
    for c in cores:
        m = dict(shared)
        m["x"] = f(inp["x"][NB * c:NB * (c + 1)])
        m["ctx"] = f(inp["ctx"][NB * c:NB * (c + 1)])
        cT = np.zeros((128, 8, 3), np.float32)
        for j in range(NB):
            cT[:, :, j] = _fm(inp["c"][NB * c + j], 8)
        cT[:, :, 2] = _fm(inp["c_ctx"], 8)
        m["cT"] = cT
        maps.append(m)
    return maps


def kernel(**inputs):
    if "nc" not in _CACHE:
        _CACHE["nc"] = build()[0]
    nc = _CACHE["nc"]
    maps = make_in_maps(inputs, list(range(N_CORES)))
    res = run_bass_kernel_spmd(nc, maps, core_ids=list(range(N_CORES)))
    outs = [np.asarray(res.results[c]["out"], np.float32) for c in range(N_CORES)]
    return np.concatenate(outs, axis=0)
```
